# Optimizing a Trainium2 kernel written in Bass

```python
import math
import numpy as np
import jax
import jax.numpy as jnp
from jax import lax

D_MODEL = 1024
BATCH = 8
SEQ = 4096
DEPTH = 2

GRID_W = 64
CTX_LEN = 256
EPS = 1e-6

F_GROUPS = 4
F_GROUP_DIM = 128
F_WIDTH = F_GROUPS * F_GROUP_DIM

DN_HEADS = 4
DN_DK = 128
DN_DV = 128
DN_WIDTH = DN_HEADS * DN_DV
DN_CONV = 3
DN_CHUNK = 64

MLA_HEADS = 8
MLA_Q_LORA = 384
MLA_KV_LORA = 256
MLA_NOPE = 64
MLA_ROPE = 32
MLA_V = 64
MLA_WIDTH = MLA_HEADS * MLA_V
ROPE_BASE = 10000.0
Q_BLOCK = 128

N_BRANCH = 3
BRANCH_WIDTH = 512

IN_WIDTHS = (F_WIDTH, F_WIDTH,
             DN_HEADS * DN_DK, DN_HEADS * DN_DK, DN_WIDTH, DN_WIDTH, 4 * DN_HEADS,
             MLA_Q_LORA, MLA_KV_LORA, MLA_ROPE, MLA_WIDTH,
             N_BRANCH * D_MODEL)
D_IN = 2 * F_WIDTH + 2 * DN_HEADS * DN_DK + 2 * DN_WIDTH + 4 * DN_HEADS + MLA_Q_LORA + MLA_KV_LORA + MLA_ROPE + MLA_WIDTH + N_BRANCH * D_MODEL

kernel_name = 'hybrid_fourier_deltanet_mla_diffusion_trunk'


def rmsnorm(x, g):
    xf = x.astype(jnp.float32)
    y = xf * lax.rsqrt(jnp.mean(xf * xf, axis=-1, keepdims=True) + EPS)
    return (y * g.astype(jnp.float32)).astype(x.dtype)


def l2norm(x):
    xf = x.astype(jnp.float32)
    return xf * lax.rsqrt(jnp.sum(xf * xf, axis=-1, keepdims=True) + EPS)


def split_in(p):
    return jnp.split(p, np.cumsum(IN_WIDTHS)[:-1].tolist(), axis=-1)


def fourier_mix(u, f_w):
    B, L, _ = u.shape
    ug = u.astype(jnp.float32).reshape(B, L, F_GROUPS, F_GROUP_DIM)
    spec = jnp.fft.fft2(ug, axes=(1, 3), norm='ortho').real.astype(u.dtype)
    out = jnp.einsum('blgc,gcd->blgd', spec, f_w)
    return out.reshape(B, L, F_WIDTH)


def depthwise_conv(u, w):
    K, C = w.shape
    pad = K // 2
    return lax.conv_general_dilated(u, w[:, None, :].astype(u.dtype), window_strides=(1,),
                                    padding=[(pad, K - 1 - pad)],
                                    dimension_numbers=('NWC', 'WIO', 'NWC'),
                                    feature_group_count=C)


def gdn_prepare(q, k, v, ab, conv_w, a_log, dt_bias):
    B, L, _ = q.shape
    qkv = jax.nn.silu(depthwise_conv(jnp.concatenate([q, k, v], axis=-1), conv_w))
    q, k, v = jnp.split(qkv, [DN_HEADS * DN_DK, 2 * DN_HEADS * DN_DK], axis=-1)
    q = l2norm(q.reshape(B, L, DN_HEADS, DN_DK)).transpose(0, 2, 1, 3)
    k = l2norm(k.reshape(B, L, DN_HEADS, DN_DK)).transpose(0, 2, 1, 3)
    v = v.reshape(B, L, DN_HEADS, DN_DV).transpose(0, 2, 1, 3)
    ab = ab.astype(jnp.float32).reshape(B, L, 4, DN_HEADS).transpose(2, 0, 3, 1)
    a_log = a_log.astype(jnp.float32)
    dt_bias = dt_bias.astype(jnp.float32)
    g = -jnp.exp(a_log)[:, None, :, None] * jax.nn.softplus(ab[:2] + dt_bias[:, None, :, None])
    beta = jax.nn.sigmoid(ab[2:])
    return q, k, v, g, beta


def gdn_chunk_scan(q, k, v, g, beta, s0):
    B, H, L, dk = q.shape
    dv = v.shape[-1]
    C = DN_CHUNK
    n = L // C
    f32 = jnp.float32
    out_dtype = v.dtype
    q = q.astype(f32).reshape(B, H, n, C, dk) * (dk ** -0.5)
    k = k.astype(f32).reshape(B, H, n, C, dk)
    v = v.astype(f32).reshape(B, H, n, C, dv)
    beta = beta.astype(f32).reshape(B, H, n, C)
    G = jnp.cumsum(g.astype(f32).reshape(B, H, n, C), axis=-1)
    incl = jnp.tril(jnp.ones((C, C), dtype=bool))
    strict = jnp.tril(jnp.ones((C, C), dtype=bool), -1)
    diff = G[..., :, None] - G[..., None, :]
    decay = jnp.where(incl, jnp.exp(jnp.where(incl, diff, 0.0)), 0.0)
    kk = jnp.einsum('bhnid,bhnjd->bhnij', k, k)
    a_mat = jnp.eye(C, dtype=f32) + jnp.where(strict, beta[..., :, None] * kk * decay, 0.0)
    rhs = jnp.concatenate([(beta * jnp.exp(G))[..., None] * k, beta[..., None] * v], axis=-1)
    sol = lax.linalg.triangular_solve(a_mat, rhs, left_side=True, lower=True, unit_diagonal=True)
    w_c, u_c = sol[..., :dk], sol[..., dk:]
    qk = jnp.einsum('bhnid,bhnjd->bhnij', q, k) * decay
    q_g = q * jnp.exp(G)[..., None]
    k_d = k * jnp.exp(G[..., -1:] - G)[..., None]
    g_last = jnp.exp(G[..., -1])

    def step(S, inp):
        w_n, u_n, qg_n, qk_n, kd_n, gl_n = inp
        u_new = u_n - jnp.einsum('bhcd,bhde->bhce', w_n, S)
        o = jnp.einsum('bhcd,bhde->bhce', qg_n, S) + jnp.einsum('bhcj,bhje->bhce', qk_n, u_new)
        S = S * gl_n[..., None, None] + jnp.einsum('bhcd,bhce->bhde', kd_n, u_new)
        return S, o

    xs = tuple(jnp.moveaxis(t, 2, 0) for t in (w_c, u_c, q_g, qk, k_d, g_last))
    s_final, o = lax.scan(step, s0.astype(f32), xs)
    o = jnp.moveaxis(o, 0, 2).reshape(B, H, L, dv)
    return o.astype(out_dtype), s_final


def gdn_direction(lat, ctx, reverse):
    if reverse:
        lat = tuple(jnp.flip(t, axis=2) for t in lat)
        ctx = tuple(jnp.flip(t, axis=2) for t in ctx)
    B, H = lat[0].shape[:2]
    s0 = jnp.zeros((B, H, DN_DK, DN_DV), jnp.float32)
    o_ctx, s_ctx = gdn_chunk_scan(*ctx, s0)
    o_lat, _ = gdn_chunk_scan(*lat, s_ctx)
    if reverse:
        o_lat = jnp.flip(o_lat, axis=2)
        o_ctx = jnp.flip(o_ctx, axis=2)
    return o_lat, o_ctx


def gdn_output(o, norm_w, z):
    B, H, L, dv = o.shape
    o = rmsnorm(o.transpose(0, 2, 1, 3).astype(z.dtype), norm_w).reshape(B, L, H * dv)
    return o * jax.nn.silu(z)


def axial_rope_tables(L):
    rows = L // GRID_W
    row = jnp.repeat(jnp.arange(rows, dtype=jnp.float32), GRID_W)
    col = jnp.tile(jnp.arange(GRID_W, dtype=jnp.float32), rows)
    n_freq = MLA_ROPE // 4
    inv = ROPE_BASE ** (-jnp.arange(n_freq, dtype=jnp.float32) / n_freq)
    ang_r = row[:, None] * inv
    ang_c = col[:, None] * inv
    ang = jnp.concatenate([ang_r, ang_r, ang_c, ang_c], axis=-1)
    return jnp.cos(ang), jnp.sin(ang)


def rotate_axial(x):
    x0, x1, x2, x3 = jnp.split(x, 4, axis=-1)
    return jnp.concatenate([-x1, x0, -x3, x2], axis=-1)


def apply_rope(x, cos, sin):
    return (x * cos + rotate_axial(x) * sin).astype(x.dtype)


def mla_qkv(cq, ckv, kpe, q_norm, w_uq, kv_norm, w_ukv, rope):
    B, L, _ = cq.shape
    q = (rmsnorm(cq, q_norm) @ w_uq).reshape(B, L, MLA_HEADS, MLA_NOPE + MLA_ROPE)
    kv = (rmsnorm(ckv, kv_norm) @ w_ukv).reshape(B, L, MLA_HEADS, MLA_NOPE + MLA_V)
    q_nope, q_pe = q[..., :MLA_NOPE], q[..., MLA_NOPE:]
    k_nope, v = kv[..., :MLA_NOPE], kv[..., MLA_NOPE:]
    if rope is not None:
        cos, sin = rope
        q_pe = apply_rope(q_pe, cos[:, None, :], sin[:, None, :])
        kpe = apply_rope(kpe, cos, sin)
    k_pe = jnp.broadcast_to(kpe[:, :, None, :], (B, L, MLA_HEADS, MLA_ROPE)).astype(k_nope.dtype)
    q = jnp.concatenate([q_nope, q_pe.astype(q_nope.dtype)], axis=-1)
    k = jnp.concatenate([k_nope, k_pe], axis=-1)
    return q, k, v


def attend(q, k, v):
    B, L, H, dqk = q.shape
    nb = L // Q_BLOCK
    scale = dqk ** -0.5
    qb = jnp.moveaxis(q.reshape(B, nb, Q_BLOCK, H, dqk), 1, 0)

    def block(qi):
        s = jnp.einsum('bqhd,bkhd->bhqk', qi, k).astype(jnp.float32) * scale
        p = jax.nn.softmax(s, axis=-1).astype(v.dtype)
        return jnp.einsum('bhqk,bkhd->bqhd', p, v)

    o = lax.map(block, qb)
    return jnp.moveaxis(o, 0, 1).reshape(B, L, H * v.shape[-1])


def merge_branches(y_a, y_b, y_c, gate_logits, w_branch, w_out):
    g_a, g_b, g_c = jnp.split(gate_logits, N_BRANCH, axis=-1)
    merged = (jax.nn.sigmoid(g_a) * (y_a @ w_branch[0])
              + jax.nn.sigmoid(g_b) * (y_b @ w_branch[1])
              + jax.nn.sigmoid(g_c) * (y_c @ w_branch[2]))
    return merged @ w_out


def mixer(h, hc, w_in, f_w, dn_conv, dn_a_log, dn_dt_bias, dn_norm,
          mla_q_norm, mla_w_uq, mla_kv_norm, mla_w_ukv, w_branch, w_out, need_ctx_out):
    B, L, _ = h.shape
    Lc = hc.shape[1]
    (fa_x, fa_z, dn_q, dn_k, dn_v, dn_z, dn_ab, cq, ckv, kpe, mla_z, gate_logits) = split_in(h @ w_in)
    (fa_xc, fa_zc, dn_qc, dn_kc, dn_vc, dn_zc, dn_abc, cqc, ckvc, kpec, mla_zc, gate_logits_c) = split_in(hc @ w_in)

    y_a = fourier_mix(fa_x, f_w) * jax.nn.silu(fa_z)

    q, k, v, g, beta = gdn_prepare(dn_q, dn_k, dn_v, dn_ab, dn_conv, dn_a_log, dn_dt_bias)
    qc, kc, vc, gc, betac = gdn_prepare(dn_qc, dn_kc, dn_vc, dn_abc, dn_conv, dn_a_log, dn_dt_bias)
    o_f, oc_f = gdn_direction((q, k, v, g[0], beta[0]), (qc, kc, vc, gc[0], betac[0]), False)
    o_b, oc_b = gdn_direction((q, k, v, g[1], beta[1]), (qc, kc, vc, gc[1], betac[1]), True)
    y_b = gdn_output(o_f + o_b, dn_norm, dn_z)

    rope = axial_rope_tables(L)
    q_m, k_m, v_m = mla_qkv(cq, ckv, kpe, mla_q_norm, mla_w_uq, mla_kv_norm, mla_w_ukv, rope)
    qc_m, kc_m, vc_m = mla_qkv(cqc, ckvc, kpec, mla_q_norm, mla_w_uq, mla_kv_norm, mla_w_ukv, None)
    o_m = attend(q_m, jnp.concatenate([k_m, kc_m], axis=1), jnp.concatenate([v_m, vc_m], axis=1))
    y_c = o_m * jax.nn.silu(mla_z)

    y = merge_branches(y_a, y_b, y_c, gate_logits, w_branch, w_out)

    yc = None
    if need_ctx_out:
        yc_a = fourier_mix(fa_xc, f_w) * jax.nn.silu(fa_zc)
        yc_b = gdn_output(oc_f + oc_b, dn_norm, dn_zc)
        yc_c = attend(qc_m, kc_m, vc_m) * jax.nn.silu(mla_zc)
        yc = merge_branches(yc_a, yc_b, yc_c, gate_logits_c, w_branch, w_out)
    return y, yc


def trunk_layer(x, ctx, c, c_ctx, w_mod, b_mod, g_pre, g_post, w_in, f_w, dn_conv, dn_a_log, dn_dt_bias,
                dn_norm, mla_q_norm, mla_w_uq, mla_kv_norm, mla_w_ukv, w_branch, w_out, need_ctx_out):
    mod = jax.nn.silu(c) @ w_mod + b_mod
    shift, scale, gate = jnp.split(mod[:, None, :], 3, axis=-1)
    mod_c = jax.nn.silu(c_ctx) @ w_mod + b_mod
    shift_c, scale_c, gate_c = jnp.split(mod_c, 3, axis=-1)
    h = rmsnorm(x, g_pre) * (1.0 + scale) + shift
    hc = rmsnorm(ctx, g_pre) * (1.0 + scale_c) + shift_c
    y, yc = mixer(h, hc, w_in, f_w, dn_conv, dn_a_log, dn_dt_bias, dn_norm,
                  mla_q_norm, mla_w_uq, mla_kv_norm, mla_w_ukv, w_branch, w_out, need_ctx_out)
    x = x + gate * rmsnorm(y, g_post)
    if need_ctx_out:
        ctx = ctx + gate_c * rmsnorm(yc, g_post)
    return x, ctx


def setup_inputs(seed: int = 0) -> dict:
    key = jax.random.key(seed)
    ks = jax.random.split(key, 24)
    f32 = jnp.float32

    def nrm(k, shape, fan_in):
        return jax.random.normal(k, shape, f32) * (fan_in ** -0.5)

    def gain(k, shape):
        return 1.0 + 0.02 * jax.random.normal(k, shape, f32)

    dt = jnp.exp(jax.random.uniform(ks[10], (DEPTH, 2, DN_HEADS), f32, math.log(1e-3), math.log(1e-1)))
    dn_dt_bias = dt + jnp.log(-jnp.expm1(-dt))
    dn_a_log = jnp.log(jax.random.uniform(ks[11], (DEPTH, 2, DN_HEADS), f32, 1.0, 16.0))
    return {
        'x': jax.random.normal(ks[0], (BATCH, SEQ, D_MODEL), f32),
        'c': jax.random.normal(ks[1], (BATCH, D_MODEL), f32),
        'ctx': jax.random.normal(ks[2], (BATCH, CTX_LEN, D_MODEL), f32),
        'c_ctx': jax.random.normal(ks[3], (D_MODEL,), f32),
        'w_mod': nrm(ks[4], (DEPTH, D_MODEL, 3 * D_MODEL), D_MODEL),
        'b_mod': 0.02 * jax.random.normal(ks[5], (DEPTH, 3 * D_MODEL), f32),
        'g_pre': gain(ks[6], (DEPTH, D_MODEL)),
        'g_post': gain(ks[7], (DEPTH, D_MODEL)),
        'w_in': nrm(ks[8], (DEPTH, D_MODEL, D_IN), D_MODEL),
        'f_w': nrm(ks[9], (DEPTH, F_GROUPS, F_GROUP_DIM, F_GROUP_DIM), F_GROUP_DIM),
        'dn_conv': nrm(ks[12], (DEPTH, DN_CONV, 2 * DN_HEADS * DN_DK + DN_WIDTH), DN_CONV),
        'dn_a_log': dn_a_log,
        'dn_dt_bias': dn_dt_bias,
        'dn_norm': gain(ks[13], (DEPTH, DN_DV)),
        'mla_q_norm': gain(ks[14], (DEPTH, MLA_Q_LORA)),
        'mla_w_uq': nrm(ks[15], (DEPTH, MLA_Q_LORA, MLA_HEADS * (MLA_NOPE + MLA_ROPE)), MLA_Q_LORA),
        'mla_kv_norm': gain(ks[16], (DEPTH, MLA_KV_LORA)),
        'mla_w_ukv': nrm(ks[17], (DEPTH, MLA_KV_LORA, MLA_HEADS * (MLA_NOPE + MLA_V)), MLA_KV_LORA),
        'w_branch': nrm(ks[18], (DEPTH, N_BRANCH, BRANCH_WIDTH, D_MODEL), BRANCH_WIDTH),
        'w_out': nrm(ks[19], (DEPTH, D_MODEL, D_MODEL), D_MODEL),
    }


def reference(x, c, ctx, c_ctx, w_mod, b_mod, g_pre, g_post, w_in, f_w, dn_conv, dn_a_log, dn_dt_bias,
              dn_norm, mla_q_norm, mla_w_uq, mla_kv_norm, mla_w_ukv, w_branch, w_out):
    for l in range(DEPTH):
        x, ctx = trunk_layer(x, ctx, c, c_ctx, w_mod[l], b_mod[l], g_pre[l], g_post[l], w_in[l], f_w[l],
                             dn_conv[l], dn_a_log[l], dn_dt_bias[l], dn_norm[l], mla_q_norm[l], mla_w_uq[l],
                             mla_kv_norm[l], mla_w_ukv[l], w_branch[l], w_out[l],
                             need_ctx_out=(l < DEPTH - 1))
    return x
```

```python
import math
import numpy as np
import ml_dtypes
import concourse.bass as bass
import concourse.mybir as mybir
from concourse.bass_utils import run_bass_kernel_spmd

F32 = mybir.dt.float32
BF16 = mybir.dt.bfloat16
AF = mybir.ActivationFunctionType
ALU = mybir.AluOpType

TL = 4096
TC = 256
TT = TL + TC
NT = TT // 128
CHUNKS = [(i * 512, 512) for i in range(8)] + [(TL, TC)]
NB = 58
EPS = 1e-6
NEG = -30000.0


class Buf:
    __slots__ = ("name", "writers", "readers", "waw", "dsem", "excl")

    def __init__(self, name, waw=True):
        self.name = name
        self.excl = False
        self.writers = {}
        self.readers = {}
        self.waw = waw
        self.dsem = None


class V:
    __slots__ = ("ap", "buf")

    def __init__(self, ap, buf):
        self.ap = ap
        self.buf = buf


class T:
    def __init__(self, handle, buf, bufs=None):
        self.h = handle
        self.buf = buf
        self.bufs = bufs

    def __getitem__(self, key):
        return V(self.h[key], self.buf)

    def sub(self, i):
        return T(self.h, self.bufs[i])


class FW:
    def __init__(self, nc, strict_same=True):
        self.nc = nc
        self.eng = {"pe": nc.tensor, "dve": nc.vector, "act": nc.scalar, "pool": nc.gpsimd, "sp": nc.sync}
        self.sems = {}
        self.count = {}
        for e in self.eng:
            self.sems[e] = nc.alloc_semaphore("s_" + e)
            self.count[e] = 0
        self.seen = {e: {} for e in self.eng}
        self.strict_same = strict_same
        self.nops = {e: 0 for e in self.eng}
        self.nwait = 0
        self.uid = 0
        self.scopes = [[]]
        self.scope_bufs = [[]]
        self.free_dsems = []
        self.bar_tile = V(nc.alloc_sbuf_tensor("bar_tile", [128, 8], F32)[:, :], Buf("bar"))

    def sb(self, name, shape, dtype, nsub=0, waw=True):
        self.uid += 1
        name = "%s_u%d" % (name, self.uid)
        g = self.nc.sbuf_tensor(name, list(shape), dtype)
        h = g.__enter__()
        self.scopes[-1].append(g)
        bufs = [Buf(name + "_%d" % i, waw) for i in range(nsub)] if nsub else None
        t = T(h, Buf(name, waw), bufs)
        self.scope_bufs[-1].extend([t.buf] + (bufs or []))
        return t

    def push(self):
        self.scopes.append([])
        self.scope_bufs.append([])

    def pop(self):
        self.barrier()
        for g in reversed(self.scopes.pop()):
            g.__exit__(None, None, None)
        for b in self.scope_bufs.pop():
            if b.dsem is not None:
                self.free_dsems.append(b.dsem)
                b.dsem = None

    def barrier(self):
        need = {k: v for k, v in self.count.items() if v > 0}
        self._waits("pool", need)
        ins = self.eng["pool"].memset(self.bar_tile.ap, 0.0)
        self.count["pool"] += 1
        ins.then_inc(self.sems["pool"], 1)
        val = self.count["pool"]
        for e in self.eng:
            if e == "pool":
                continue
            self._waits(e, {"pool": val})
            for k, v in need.items():
                self.seen[e][k] = max(self.seen[e].get(k, 0), v)

    def ps(self, name, shape, dtype=F32):
        h = self.nc.alloc_psum_tensor(name, list(shape), dtype)
        b = Buf(name)
        b.excl = True
        return T(h, b)

    def dram(self, name, shape, dtype, kind="Internal", nsub=0):
        h = self.nc.dram_tensor(name, list(shape), dtype, kind=kind)
        bufs = [Buf(name + "_%d" % i, False) for i in range(nsub)] if nsub else None
        return T(h.ap(), Buf(name, waw=False), bufs)

    def _need(self, reads, writes):
        need = {}
        for v in reads:
            for k, val in v.buf.writers.items():
                if need.get(k, 0) < val:
                    need[k] = val
            if v.buf.excl:
                for k, val in v.buf.readers.items():
                    if need.get(k, 0) < val:
                        need[k] = val
        for v in writes:
            b = v.buf
            if b.waw:
                for k, val in b.writers.items():
                    if need.get(k, 0) < val:
                        need[k] = val
            for k, val in b.readers.items():
                if need.get(k, 0) < val:
                    need[k] = val
        return need

    def _waits(self, e, need):
        seen = self.seen[e]
        for k, val in need.items():
            if k == e and (e == "pe" or not self.strict_same):
                continue
            if seen.get(k, 0) >= val:
                continue
            self.eng[e].wait_ge(self.sems[k], val)
            seen[k] = val
            self.nwait += 1

    def op(self, e, fn, reads=(), writes=()):
        self._waits(e, self._need(reads, writes))
        ins = fn(self.eng[e])
        self.count[e] += 1
        val = self.count[e]
        ins.then_inc(self.sems[e], 1)
        self.nops[e] += 1
        for v in reads:
            b = v.buf
            if b.readers.get(e, 0) < val:
                b.readers[e] = val
        for v in writes:
            b = v.buf
            if b.waw:
                b.writers = {e: val}
            else:
                b.writers[e] = val
            b.readers = {}
        return ins

    def dma(self, q, out, in_, semof=None, **kw):
        sbuf = semof if semof is not None else out.buf
        if sbuf.dsem is None:
            if self.free_dsems:
                key = self.free_dsems.pop()
            else:
                key = "d%d" % len(self.sems)
                self.sems[key] = self.nc.alloc_semaphore(key)
                self.count[key] = 0
            sbuf.dsem = key
        key = sbuf.dsem
        self._waits(q, self._need([in_], [out]))
        ins = self.eng[q].dma_start(out=out.ap, in_=in_.ap, **kw)
        self.count[key] += 16
        val = self.count[key]
        ins.then_inc(self.sems[key], 16)
        self.nops[q] += 1
        b = in_.buf
        if b.readers.get(key, 0) < val:
            b.readers[key] = val
        b = out.buf
        if b.waw:
            b.writers = {key: val}
        else:
            b.writers[key] = val
        b.readers = {}
        return ins

    def finish(self, bufs, e="sp"):
        need = {}
        for b in bufs:
            for k, val in b.buf.writers.items():
                if need.get(k, 0) < val:
                    need[k] = val
        self._waits(e, need)

    def mm(self, out, lhsT, rhs, start=True, stop=True):
        reads = [lhsT, rhs] + ([] if start else [out])
        return self.op("pe", lambda e: e.matmul(out.ap, lhsT.ap, rhs.ap, start=start, stop=stop), reads, [out])

    def tr(self, out, in_, ident):
        return self.op("pe", lambda e: e.transpose(out.ap, in_.ap, ident.ap), [in_, ident], [out])

    def act(self, out, in_, func, bias=None, scale=None, accum_out=None):
        reads = [in_]
        kw = {}
        if bias is not None:
            if isinstance(bias, V):
                reads.append(bias)
                kw["bias"] = bias.ap
            else:
                kw["bias"] = bias
        if scale is not None:
            if isinstance(scale, V):
                reads.append(scale)
                kw["scale"] = scale.ap
            else:
                kw["scale"] = scale
        writes = [out]
        if accum_out is not None:
            kw["accum_out"] = accum_out.ap
            writes.append(accum_out)
        return self.op("act", lambda e: e.activation(out.ap, in_.ap, func, **kw), reads, writes)

    def tt(self, out, in0, in1, op, e="dve"):
        return self.op(e, lambda g: g.tensor_tensor(out.ap, in0.ap, in1.ap, op), [in0, in1], [out])

    def ts(self, out, in0, s1, s2=None, op0=ALU.mult, op1=None, e="dve"):
        reads = [in0]
        a1 = s1
        if isinstance(s1, V):
            reads.append(s1)
            a1 = s1.ap
        a2 = s2
        if isinstance(s2, V):
            reads.append(s2)
            a2 = s2.ap
        if op1 is None:
            return self.op(e, lambda g: g.tensor_scalar(out.ap, in0.ap, a1, None, op0), reads, [out])
        return self.op(e, lambda g: g.tensor_scalar(out.ap, in0.ap, a1, a2, op0, op1), reads, [out])

    def stt(self, out, in0, s, in1, op0, op1):
        reads = [in0, in1]
        a = s
        if isinstance(s, V):
            reads.append(s)
            a = s.ap
        return self.op("dve", lambda g: g.scalar_tensor_tensor(out.ap, in0.ap, a, in1.ap, op0, op1), reads, [out])

    def copy(self, out, in_, e="dve"):
        if e == "act":
            return self.op("act", lambda g: g.activation(out.ap, in_.ap, AF.Copy), [in_], [out])
        return self.op(e, lambda g: g.tensor_copy(out.ap, in_.ap), [in_], [out])

    def recip(self, out, in_):
        return self.op("dve", lambda g: g.reciprocal(out.ap, in_.ap), [in_], [out])


def _bf(a):
    return np.ascontiguousarray(a).astype(ml_dtypes.bfloat16)


_CONST = {}


def host_constants():
    if _CONST:
        return _CONST
    c = {}
    t = np.arange(TL, dtype=np.int64)
    ph = (np.outer(t, t) % TL).astype(np.float64) * (2 * np.pi / TL)
    c["CL"] = _bf(np.cos(ph))
    c["NSL"] = _bf(-np.sin(ph))
    tcx = np.arange(TC, dtype=np.int64)
    phc = (np.outer(tcx, tcx) % TC).astype(np.float64) * (2 * np.pi / TC)
    c["CLC"] = _bf(np.cos(phc))
    c["NSLC"] = _bf(-np.sin(phc))
    ch = np.arange(128, dtype=np.int64)
    phd = (np.outer(ch, ch) % 128).astype(np.float64) * (2 * np.pi / 128)
    c["CSC"] = _bf(np.concatenate([np.cos(phd), np.sin(phd)], axis=1))
    rows = np.repeat(np.arange(64, dtype=np.float32), 64)
    cols = np.tile(np.arange(64, dtype=np.float32), 64)
    inv = (10000.0 ** (-np.arange(8, dtype=np.float32) / 8)).astype(np.float32)
    ang_r = rows[:, None] * inv
    ang_c = cols[:, None] * inv
    ang = np.concatenate([ang_r, ang_r, ang_c, ang_c], axis=-1)
    sgn = np.array([-1.0] * 8 + [1.0] * 8 + [-1.0] * 8 + [1.0] * 8, dtype=np.float32)
    C = np.ones((96, TT), np.float32)
    S = np.zeros((96, TT), np.float32)
    C[64:96, :TL] = np.cos(ang).T
    S[64:96, :TL] = (np.sin(ang) * sgn[None, :]).T
    c["ROPEC"] = C
    c["ROPES"] = S
    k = np.arange(128)[:, None]
    m = np.arange(128)[None, :]
    same = (k // 64) == (m // 64)
    ident = (k == m).astype(np.float32)
    ones = np.ones((128, 128), np.float32)
    triF = (same & (k <= m)).astype(np.float32)
    restF = (same & (k > m)).astype(np.float32)
    triB = (same & (k >= m)).astype(np.float32)
    restB = (same & (k < m)).astype(np.float32)
    tot0 = np.broadcast_to((k < 64), (128, 128)).astype(np.float32)
    tot1 = np.broadcast_to((k >= 64), (128, 128)).astype(np.float32)
    j = k
    i = m
    nmF = np.where(same & (i >= j), 0.0, NEG).astype(np.float32)
    nmB = np.where(same & (i <= j), 0.0, NEG).astype(np.float32)
    stF = (same & (i > j)).astype(np.float32)
    stB = (same & (i < j)).astype(np.float32)
    c["MSK"] = np.ascontiguousarray(np.concatenate(
        [ident, ones, triF, restF, tot0, tot1, triB, restB,
         np.tile(nmF, (1, 4)), np.tile(nmB, (1, 4)), np.tile(ident, (1, 4))], axis=1)).astype(np.float32)
    d16 = ((k // 16) == (m // 16)).astype(np.float32)
    c32 = (((k // 32) == (m // 32)) & ((k // 16) != (m // 16))).astype(np.float32)
    c64 = (same & ((k // 32) != (m // 32))).astype(np.float32)
    c["MSKB"] = _bf(np.concatenate([ident, ones, np.tile(stF, (1, 4)), np.tile(stB, (1, 4)), np.tile(ident, (1, 4)),
                                    np.tile(d16, (1, 4)), np.tile(c32, (1, 4)), np.tile(c64, (1, 4))], axis=1))
    _CONST.update(c)
    return c


IN_W = (512, 512, 512, 512, 512, 512, 16, 384, 256, 32, 512, 3072)
IN_OFF = np.concatenate([[0], np.cumsum(IN_W)]).tolist()


def _col(v, nblk):
    return np.ascontiguousarray(v.reshape(nblk, 128).T)


def host_layout(inp, b):
    d = {}
    d["xin"] = np.ascontiguousarray(np.concatenate([inp["x"][b], inp["ctx"][b]], axis=0))
    cc = np.stack([_col(inp["c"][b], 8), _col(inp["c_ctx"], 8)], axis=-1)
    d["ccol"] = np.ascontiguousarray(cc)
    return d


def host_layout_shared(inp):
    d = {}
    o = IN_OFF
    perm = np.concatenate([np.arange(8, 16), np.arange(0, 8), np.arange(24, 32), np.arange(16, 24)])
    w_in = inp["w_in"]
    kpe = w_in[:, :, o[9]:o[10]]
    cols = np.concatenate([
        w_in[:, :, o[0]:o[6]],
        w_in[:, :, o[7]:o[9]],
        w_in[:, :, o[10]:o[11]],
        w_in[:, :, o[11]:o[12]],
        kpe, kpe[:, :, perm],
        np.zeros((2, 1024, 64), np.float32),
    ], axis=-1)
    assert cols.shape[-1] == NB * 128
    d["w_in_r"] = np.ascontiguousarray(cols.reshape(2, 8, 128, NB, 128).transpose(0, 3, 2, 1, 4))
    wab = w_in[:, :, o[6]:o[7]]
    d["w_ab"] = np.ascontiguousarray(wab.reshape(2, 8, 128, 16).transpose(0, 2, 1, 3))
    d["w_mod"] = inp["w_mod"]
    d["bmod"] = np.ascontiguousarray(np.stack([_col(inp["b_mod"][l], 24) for l in range(2)]))
    d["gpre"] = np.ascontiguousarray(np.stack([_col(inp["g_pre"][l], 8) for l in range(2)]))
    d["gpost"] = np.ascontiguousarray(np.stack([_col(inp["g_post"][l], 8) for l in range(2)]))
    d["f_w"] = np.ascontiguousarray(inp["f_w"].transpose(0, 2, 1, 3))
    cw = inp["dn_conv"]
    d["convw"] = np.ascontiguousarray(cw.reshape(2, 3, 12, 128).transpose(0, 3, 2, 1))
    d["alog"] = np.ascontiguousarray(np.broadcast_to(inp["dn_a_log"].reshape(2, 1, 1, 8), (2, 128, NT, 8)))
    d["dtb"] = np.ascontiguousarray(np.broadcast_to(inp["dn_dt_bias"].reshape(2, 1, 1, 8), (2, 128, NT, 8)))
    d["dnorm"] = np.ascontiguousarray(inp["dn_norm"].reshape(2, 128, 1))
    d["qnorm"] = np.ascontiguousarray(np.stack([_col(inp["mla_q_norm"][l], 3) for l in range(2)]))
    d["kvnorm"] = np.ascontiguousarray(np.stack([_col(inp["mla_kv_norm"][l], 2) for l in range(2)]))
    wuq = inp["mla_w_uq"]
    hp = np.concatenate([np.arange(64), 64 + perm])
    permc = np.concatenate([h * 96 + hp for h in range(8)])
    both = np.concatenate([wuq, wuq[:, :, permc]], axis=-1)
    d["w_uq"] = np.ascontiguousarray(both.reshape(2, 3, 128, 1536).transpose(0, 2, 1, 3))
    d["w_ukv"] = np.ascontiguousarray(inp["mla_w_ukv"].reshape(2, 2, 128, 1024).transpose(0, 2, 1, 3))
    d["w_br"] = np.ascontiguousarray(inp["w_branch"].reshape(2, 12, 128, 1024).transpose(0, 2, 1, 3))
    d["w_out"] = np.ascontiguousarray(inp["w_out"].reshape(2, 8, 128, 1024).transpose(0, 2, 1, 3))
    return d


def build(n_layers=2, dbg=(), stop_after=None):
    nc = bass.Bass("TRN2", target_bir_lowering=False)
    fw = FW(nc)
    EI = "ExternalInput"

    def skind(name):
        return "ExternalOutput" if name in dbg else "Internal"

    xin = fw.dram("xin", [TT, 1024], F32, EI)
    ccol_d = fw.dram("ccol", [128, 8, 2], F32, EI)
    w_in_d = fw.dram("w_in_r", [2, NB, 128, 8, 128], F32, EI)
    w_ab_d = fw.dram("w_ab", [2, 128, 8, 16], F32, EI)
    w_mod_d = fw.dram("w_mod", [2, 1024, 3072], F32, EI)
    bmod_d = fw.dram("bmod", [2, 128, 24], F32, EI)
    gpre_d = fw.dram("gpre", [2, 128, 8], F32, EI)
    gpost_d = fw.dram("gpost", [2, 128, 8], F32, EI)
    f_w_d = fw.dram("f_w", [2, 128, 4, 128], F32, EI)
    convw_d = fw.dram("convw", [2, 128, 12, 3], F32, EI)
    alog_d = fw.dram("alog", [2, 128, NT, 8], F32, EI)
    dtb_d = fw.dram("dtb", [2, 128, NT, 8], F32, EI)
    dnorm_d = fw.dram("dnorm", [2, 128, 1], F32, EI)
    qnorm_d = fw.dram("qnorm", [2, 128, 3], F32, EI)
    kvnorm_d = fw.dram("kvnorm", [2, 128, 2], F32, EI)
    w_uq_d = fw.dram("w_uq", [2, 128, 3, 1536], F32, EI)
    w_ukv_d = fw.dram("w_ukv", [2, 128, 2, 1024], F32, EI)
    w_br_d = fw.dram("w_br", [2, 128, 12, 1024], F32, EI)
    w_out_d = fw.dram("w_out", [2, 128, 8, 1024], F32, EI)
    CL_d = fw.dram("CL", [TL, TL], BF16, EI)
    NSL_d = fw.dram("NSL", [TL, TL], BF16, EI)
    CLC_d = fw.dram("CLC", [TC, TC], BF16, EI)
    NSLC_d = fw.dram("NSLC", [TC, TC], BF16, EI)
    CSC_d = fw.dram("CSC", [128, 256], BF16, EI)
    ROPEC_d = fw.dram("ROPEC", [96, TT], F32, EI)
    ROPES_d = fw.dram("ROPES", [96, TT], F32, EI)
    MSK_d = fw.dram("MSK", [128, 8 * 128 + 1536], F32, EI)
    MSKB_d = fw.dram("MSKB", [128, 2 * 128 + 6 * 512], BF16, EI)

    out_d = fw.dram("out", [TL, 1024], F32, "ExternalOutput")
    xs_d = fw.dram("xs", [TT, 1024], F32, skind("xs"))
    pT_d = fw.dram("pT", [NB * 128, TT], BF16, skind("pT"))
    yaT_d = fw.dram("yaT", [512, TT], BF16, skind("yaT"))
    ycT_d = fw.dram("ycT", [512, TT], BF16, skind("ycT"))
    oT_d = [fw.dram("oT%d" % d, [512, TT], F32, skind("oT%d" % d)) for d in range(2)]
    dbg_d = {}

    msk = fw.sb("msk", [128, 8 * 128 + 1536], F32)
    mskb = fw.sb("mskb", [128, 2 * 128 + 6 * 512], BF16)
    fw.dma("sp", msk[:], MSK_d[:])
    fw.dma("sp", mskb[:], MSKB_d[:])

    def mcol(i):
        return msk[:, i * 128:(i + 1) * 128]
    identF, onesF, triF, restF, tot0, tot1, triB, restB = [mcol(i) for i in range(8)]
    negmask = [msk[:, 1024:1536], msk[:, 1536:2048]]
    ident4F = msk[:, 2048:2560]
    identB = mskb[:, 0:128]
    onesB = mskb[:, 128:256]
    strictB = [mskb[:, 256:768], mskb[:, 768:1280]]
    ident4B = mskb[:, 1280:1792]
    mD16 = mskb[:, 1792:2304]
    mC32 = mskb[:, 2304:2816]
    mC64 = mskb[:, 2816:3328]

    PF = [fw.ps("pf%d" % i, [128, 512], F32) for i in range(7)]
    PB = fw.ps("pb", [128, 1024], BF16)

    ccol = fw.sb("ccol_sb", [128, 8, 2], F32)
    sc = fw.sb("sc", [128, 8, 2], F32)
    modc = fw.sb("modc", [128, 24, 2], F32)
    Acol = fw.sb("Acol", [128, 8, 2], F32)
    ggcol = fw.sb("ggcol", [128, 8, 2], F32)
    ggbc = [fw.sb("ggbc%d" % j, [128, 1024], F32) for j in range(2)]
    bmod = fw.sb("bmod_sb", [128, 24], F32)
    gpre = fw.sb("gpre_sb", [128, 8], F32)
    gpost = fw.sb("gpost_sb", [128, 8], F32)
    abT = fw.sb("abT", [128, NT, 16], F32)
    stat = fw.sb("stat", [128, 4 * NT], F32, nsub=NT)
    fw.dma("sp", ccol[:], ccol_d[:])

    ring_ctr = {}

    def ring(lst, key):
        i = ring_ctr.get(key, 0)
        ring_ctr[key] = i + 1
        return lst[i % len(lst)]

    pfc = [0]

    def pf(lo=0, hi=7):
        i = pfc[0]
        pfc[0] += 1
        return PF[lo + i % (hi - lo)]

    S = {}

    def alloc_s1():
        fw.push()
        S["hT"] = fw.sb("hT", [128, 8, TT], BF16)
        S["w32"] = [fw.sb("w32_%d" % i, [128, 4096], F32) for i in range(2)]
        S["wbf"] = [fw.sb("wbf_%d" % i, [128, 1024], BF16) for i in range(3)]
        S["stg"] = [fw.sb("stg_%d" % i, [128, TT], BF16) for i in range(3)]
        S["xring"] = [fw.sb("xr%d" % i, [128, 1024], F32) for i in range(3)]
        S["hnring"] = [fw.sb("hn%d" % i, [128, 1024], BF16) for i in range(2)]

    def phase_mod(l):
        fw.dma("sp", bmod[:], bmod_d[l])
        fw.dma("sp", gpre[:], gpre_d[l])
        fw.dma("sp", gpost[:], gpost_d[l])
        fw.act(sc[:], ccol[:], AF.Silu)
        pm = PF[0]
        wv = w_mod_d.h[l].rearrange("(kc k) n -> k kc n", k=128)
        for nch in range(6):
            slot = ring(S["w32"], "w32")
            sv = V(slot.h[:, :].rearrange("p (kc n) -> p kc n", kc=8), slot.buf)
            fw.dma("sp", sv, V(wv[:, :, nch * 512:(nch + 1) * 512], w_mod_d.buf))
            for j in range(4):
                blk = nch * 4 + j
                for kc in range(8):
                    fw.mm(pm[:, blk * 2:blk * 2 + 2],
                          V(slot.h[:, kc * 512 + j * 128: kc * 512 + (j + 1) * 128], slot.buf),
                          sc[:, kc, :], start=(kc == 0), stop=(kc == 7))
        for j in range(2):
            fw.tt(modc[:, :, j], V(pm.h[:, 0:48].rearrange("p (b j) -> p b j", j=2)[:, :, j], pm.buf), bmod[:], ALU.add)
            fw.stt(Acol[:, :, j], modc[:, 8:16, j], 1.0, gpre[:], ALU.add, ALU.mult)
            fw.tt(ggcol[:, :, j], modc[:, 16:24, j], gpost[:], ALU.mult)
        for j in range(2):
            for half in range(2):
                pg = pf(1, 7)
                for q in range(4):
                    kc = half * 4 + q
                    D = ring(S["w32"], "w32")
                    fw.ts(D[:, 0:128], identF, ggcol[:, kc, j:j + 1])
                    fw.mm(pg[:, q * 128:(q + 1) * 128], onesF, D[:, 0:128])
                fw.copy(ggbc[j][:, half * 512:(half + 1) * 512], pg[:])

    def phase_h(l):
        src = xin if l == 0 else xs_d
        for tt in range(NT):
            j = 0 if tt < 32 else 1
            xt = ring(S["xring"], "xr")
            fw.dma("sp", xt[:], src[tt * 128:(tt + 1) * 128, :])
            st = stat.sub(tt)
            hn = ring(S["hnring"], "hn")
            fw.act(hn[:], xt[:], AF.Square, accum_out=st[:, 4 * tt:4 * tt + 1])
            fw.act(st[:, 4 * tt + 1:4 * tt + 2], st[:, 4 * tt:4 * tt + 1], AF.Sqrt, bias=EPS, scale=1.0 / 1024)
            fw.recip(st[:, 4 * tt + 2:4 * tt + 3], st[:, 4 * tt + 1:4 * tt + 2])
            fw.ts(hn[:], xt[:], st[:, 4 * tt + 2:4 * tt + 3])
            for kc in range(8):
                fw.tr(PB[:, kc * 128:(kc + 1) * 128], hn[:, kc * 128:(kc + 1) * 128], identB)
            for kc in range(8):
                o = S["hT"][:, kc, tt * 128:(tt + 1) * 128]
                i = PB[:, kc * 128:(kc + 1) * 128]
                if kc % 2 == 0:
                    fw.ts(o, i, Acol[:, kc, j:j + 1], modc[:, kc, j:j + 1], ALU.mult, ALU.add)
                else:
                    fw.act(o, i, AF.Identity, bias=modc[:, kc, j:j + 1], scale=Acol[:, kc, j:j + 1])

    SILU_BLK = list(range(4, 8)) + list(range(20, 24)) + list(range(29, 33))
    SIG_BLK = list(range(33, 57))
    COPY_BLK = [b for b in range(NB) if b not in SILU_BLK and b not in SIG_BLK]

    def phase_proj(l):
        wab32 = ring(S["w32"], "w32")
        fw.dma("sp", wab32[:, 0:128], V(w_ab_d.h[l].rearrange("p kc n -> p (kc n)"), w_ab_d.buf))
        wabb = ring(S["wbf"], "wbf")
        fw.copy(wabb[:, 0:128], wab32[:, 0:128], e="pool")
        pa = [PF[5], PF[6]]
        for tt in range(NT):
            dst = pa[0][:, tt * 16:(tt + 1) * 16] if tt < 32 else pa[1][:, (tt - 32) * 16:(tt - 31) * 16]
            for kc in range(8):
                fw.mm(dst, S["hT"][:, kc, tt * 128:(tt + 1) * 128], wabb[:, kc * 16:(kc + 1) * 16], start=(kc == 0), stop=(kc == 7))
        fw.copy(V(abT.h[:, 0:32, :].rearrange("p a b -> p (a b)"), abT.buf), pa[0][:, 0:512])
        fw.copy(V(abT.h[:, 32:34, :].rearrange("p a b -> p (a b)"), abT.buf), pa[1][:, 0:32])
        for blk in COPY_BLK + SILU_BLK + SIG_BLK:
            ws = ring(S["w32"], "w32")
            fw.dma("sp", ws[:, 0:1024], V(w_in_d.h[l, blk].rearrange("p kc n -> p (kc n)"), w_in_d.buf))
            wb = ring(S["wbf"], "wbf")
            fw.copy(wb[:, 0:1024], ws[:, 0:1024], e="pool")
            sg = ring(S["stg"], "stg")
            for ci, (t0, n) in enumerate(CHUNKS):
                ps = pf(0, 5)
                for kc in range(8):
                    fw.mm(ps[:, 0:n], wb[:, kc * 128:(kc + 1) * 128], S["hT"][:, kc, t0:t0 + n], start=(kc == 0), stop=(kc == 7))
                if blk in SILU_BLK:
                    fw.act(sg[:, t0:t0 + n], ps[:, 0:n], AF.Silu)
                elif blk in SIG_BLK:
                    fw.act(sg[:, t0:t0 + n], ps[:, 0:n], AF.Sigmoid)
                else:
                    fw.copy(sg[:, t0:t0 + n], ps[:, 0:n])
            fw.dma("pool", pT_d[blk * 128:(blk + 1) * 128, :], sg[:], semof=sg.buf)


    def v3(t, n):
        return t.h[:, :].rearrange("p (a n) -> p a n", n=n)

    def phase_fourier(l, with_ctx):
        fw.push()
        UT = fw.sb("UT", [128, 4, TT], BF16)
        ABs = fw.sb("ABs", [128, NT, 1024], BF16)
        csc = fw.sb("csc", [128, 256], BF16)
        fw32 = fw.sb("fw32", [128, 512], F32)
        fwb = fw.sb("fwb", [128, 512], BF16)
        tbC = [fw.sb("tbC%d" % i, [128, 8, 512], BF16) for i in range(2)]
        tbS = [fw.sb("tbS%d" % i, [128, 8, 512], BF16) for i in range(2)]
        specb = [fw.sb("specb%d" % i, [128, 512], BF16) for i in range(2)]
        szr = [fw.sb("szr%d" % i, [128, 4, 512], BF16) for i in range(2)]
        yast = [fw.sb("yast%d" % i, [128, 4, 512], BF16) for i in range(2)]
        fw.dma("sp", csc[:], CSC_d[:])
        fw.dma("sp", fw32[:], V(f_w_d.h[l].rearrange("p g d -> p (g d)"), f_w_d.buf))
        fw.copy(fwb[:], fw32[:], e="pool")
        for g in range(4):
            fw.dma("sp", UT[:, g, :], pT_d[g * 128:(g + 1) * 128, :])
        for tt in range(NT if with_ctx else 32):
            for half in range(2):
                ps = pf(4, 7)
                for gg in range(2):
                    g = half * 2 + gg
                    fw.mm(ps[:, gg * 256:(gg + 1) * 256], UT[:, g, tt * 128:(tt + 1) * 128], csc[:])
                fw.copy(ABs[:, tt, half * 512:(half + 1) * 512], ps[:], e=("dve" if half == 0 else "act"))
        osc = 1.0 / math.sqrt(128.0)
        jobs = [(ci, t0, n, 0, 32, CL_d, NSL_d, TL) for ci, (t0, n) in enumerate(CHUNKS[:8])]
        if with_ctx:
            jobs.append((8, TL, TC, 32, 2, CLC_d, NSLC_d, TC))
        for (ci, t0, n, tt0, ntile, Cd, Sd, Lseq) in jobs:
            sz = ring(szr, "szr")
            fw.dma("sp", sz[:, :, 0:n], V(pT_d.h[512:1024, :].rearrange("(g p) t -> p g t", p=128)[:, :, t0:t0 + n], pT_d.buf))
            c0 = t0 - tt0 * 128
            nq = (ntile + 7) // 8
            for qd in range(nq):
                na = min(8, ntile - qd * 8)
                tc_ = ring(tbC, "tbC")
                tsn = ring(tbS, "tbS")
                fw.dma("sp", tc_[:, 0:na, 0:n], V(Cd.h[qd * 1024: qd * 1024 + na * 128, :].rearrange("(a p) n -> p a n", p=128)[:, :, c0:c0 + n], Cd.buf))
                fw.dma("sp", tsn[:, 0:na, 0:n], V(Sd.h[qd * 1024: qd * 1024 + na * 128, :].rearrange("(a p) n -> p a n", p=128)[:, :, c0:c0 + n], Sd.buf))
                for g in range(4):
                    for a in range(na):
                        tt = tt0 + qd * 8 + a
                        first = (qd == 0 and a == 0)
                        last = (qd == nq - 1 and a == na - 1)
                        fw.mm(PF[g][:, 0:n], ABs[:, tt, g * 256: g * 256 + 128], tc_[:, a, 0:n], start=first, stop=False)
                        fw.mm(PF[g][:, 0:n], ABs[:, tt, g * 256 + 128: g * 256 + 256], tsn[:, a, 0:n], start=False, stop=last)
            ya = ring(yast, "yast")
            scl = osc / math.sqrt(float(Lseq))
            for g in range(4):
                sb_ = ring(specb, "specb")
                fw.act(sb_[:, 0:n], PF[g][:, 0:n], AF.Copy, scale=scl)
                po = pf(4, 7)
                fw.mm(po[:, 0:n], fwb[:, g * 128:(g + 1) * 128], sb_[:, 0:n])
                fw.tt(ya[:, g, 0:n], po[:, 0:n], sz[:, g, 0:n], ALU.mult)
            fw.dma("pool", V(yaT_d.h.rearrange("(g p) t -> p g t", p=128)[:, :, t0:t0 + n], yaT_d.buf), ya[:, :, 0:n], semof=ya.buf)
        fw.pop()

    def phase_gdn(l):
        fw.push()
        qT = fw.sb("qT", [128, 4, TT], BF16)
        kT = fw.sb("kT", [128, 4, TT], BF16)
        vT = fw.sb("vT", [128, 4, TT], BF16)
        convw = fw.sb("convw", [128, 12, 3], F32)
        fw.dma("sp", convw[:], convw_d[l])
        fw.push()
        cin = [fw.sb("cin%d" % i, [128, TT], BF16) for i in range(2)]
        cy = [fw.sb("cy%d" % i, [128, TT], F32) for i in range(2)]
        sqr = [fw.sb("sqr%d" % i, [128, 512], BF16) for i in range(2)]
        rr = [fw.sb("rr%d" % i, [128, 512], F32) for i in range(2)]
        for blk in range(12):
            xi = ring(cin, "cin")
            y = ring(cy, "cy")
            fw.dma("sp", xi[:], pT_d[1024 + blk * 128: 1024 + (blk + 1) * 128, :])
            for (a, b) in ((0, TL), (TL, TT)):
                fw.ts(y[:, a:b], xi[:, a:b], convw[:, blk, 1:2])
                fw.stt(y[:, a + 1:b], xi[:, a:b - 1], convw[:, blk, 0:1], y[:, a + 1:b], ALU.mult, ALU.add)
                fw.stt(y[:, a:b - 1], xi[:, a + 1:b], convw[:, blk, 2:3], y[:, a:b - 1], ALU.mult, ALU.add)
            h = blk % 4
            if blk >= 8:
                fw.act(vT[:, h, :], y[:], AF.Silu)
                continue
            dst = qT if blk < 4 else kT
            fw.act(y[:], y[:], AF.Silu)
            for (t0, n) in CHUNKS:
                sq = ring(sqr, "sqr")
                r = ring(rr, "rr")
                fw.tt(sq[:, 0:n], y[:, t0:t0 + n], y[:, t0:t0 + n], ALU.mult, e="pool")
                ps = pf(0, 7)
                fw.mm(ps[:, 0:n], onesB, sq[:, 0:n])
                fw.act(r[:, 0:n], ps[:, 0:n], AF.Ln, bias=EPS)
                fw.act(r[:, 0:n], r[:, 0:n], AF.Exp, scale=-0.5)
                fw.tt(dst[:, h, t0:t0 + n], y[:, t0:t0 + n], r[:, 0:n], ALU.mult)
        fw.pop()
        if "qkv" in dbg and l == 0:
            for nm, t_ in (("qT", qT), ("kT", kT), ("vT", vT)):
                dbg_d[nm] = fw.dram(nm + "_o", [128, 4, TT], BF16, "ExternalOutput")
                fw.dma("pool", dbg_d[nm][:], t_[:], semof=t_.buf)

        if "gdn_d1only" in dbg:
            fw.pop()
            return
        NC = NT * 4
        bb = fw.sb("bb", [128, NT, 8], F32)
        gd = [fw.sb("gd%d" % d, [128, NC], F32) for d in range(2)]
        names = ("Gs", "nG", "EG", "ER", "GL0", "GL1")
        GA = [{nm: fw.sb("%s%d" % (nm, d), [128, NC], F32) for nm in names} for d in range(2)]
        fw.push()
        alog = fw.sb("alog", [128, NT, 8], F32)
        dtb = fw.sb("dtb", [128, NT, 8], F32)
        fw.dma("sp", alog[:], alog_d[l])
        fw.dma("sp", dtb[:], dtb_d[l])
        gz = fw.sb("gz", [128, NT, 8], F32)
        fw.tt(gz[:], abT[:, :, 0:8], dtb[:], ALU.add)
        fw.act(gz[:], gz[:], AF.Exp)
        fw.act(gz[:], gz[:], AF.Ln, bias=1.0)
        fw.act(alog[:], alog[:], AF.Exp)
        fw.stt(gz[:], gz[:], -1.0, alog[:], ALU.mult, ALU.mult)
        fw.act(bb[:], abT[:, :, 8:16], AF.Exp, scale=-1.0)
        fw.ts(bb[:], bb[:], 1.0, None, ALU.add)
        fw.recip(bb[:], bb[:])
        for d in range(2):
            fw.copy(V(v3(gd[d], 4), gd[d].buf), gz[:, :, d * 4:(d + 1) * 4])
        fw.pop()
        for d in range(2):
            tri = triF if d == 0 else triB
            rest = restF if d == 0 else restB
            p1 = pf(0, 7)
            fw.mm(p1[:, 0:NC], tri, gd[d][:])
            if "ab2b" in dbg:
                fw.pop()
                return
            fw.copy(GA[d]["Gs"][:], p1[:, 0:NC])
            if "ab2c" in dbg:
                fw.pop()
                return
            fw.ts(GA[d]["nG"][:], p1[:, 0:NC], -1.0)
            fw.act(GA[d]["EG"][:], p1[:, 0:NC], AF.Exp)
            p2 = pf(0, 7)
            fw.mm(p2[:, 0:NC], rest, gd[d][:])
            fw.act(GA[d]["ER"][:], p2[:, 0:NC], AF.Exp)
            p3 = pf(0, 7)
            fw.mm(p3[:, 0:NC], tot0, gd[d][:])
            fw.act(GA[d]["GL0"][:], p3[:, 0:NC], AF.Exp)
            p4 = pf(0, 7)
            fw.mm(p4[:, 0:NC], tot1, gd[d][:])
            fw.act(GA[d]["GL1"][:], p4[:, 0:NC], AF.Exp)

        if "ab2" in dbg:
            fw.pop()
            return

        def bc4(t_, tt, off=0, rows=slice(0, 128), stride4=True):
            ap = t_.h[rows, tt * 4 + off: tt * 4 + off + 4].unsqueeze(2)
            nrows = rows.stop - rows.start
            return V(ap.broadcast_to([nrows, 4, 128]), t_.buf)

        def bcb(tt, d, rows=slice(0, 128)):
            ap = bb.h[rows, tt, d * 4:(d + 1) * 4].unsqueeze(2)
            nrows = rows.stop - rows.start
            return V(ap.broadcast_to([nrows, 4, 128]), bb.buf)

        NS = 2
        slots = []
        for i in range(NS):
            slots.append({
                "w0T": fw.sb("w0T%d" % i, [128, 512], BF16), "qkdT": fw.sb("qkdT%d" % i, [128, 512], BF16),
                "qgT": fw.sb("qgT%d" % i, [128, 512], BF16), "kd": fw.sb("kd%d" % i, [128, 512], BF16),
                "ub": fw.sb("ub%d" % i, [128, 512], F32)})
        tmps = []
        for i in range(1):
            tm_ = {
                "EGr": fw.sb("EGr%d" % i, [128, 512], F32),
                "tD": fw.sb("tD%d" % i, [128, 512], F32), "dec": fw.sb("dec%d" % i, [128, 512], F32),
                "kEG": fw.sb("kEG%d" % i, [128, 512], BF16), "vtok": fw.sb("vtok%d" % i, [128, 512], BF16),
                "TTb": fw.sb("TTb%d" % i, [128, 512], BF16)}
            for nm in ("M0", "MT0", "Qa", "QTa", "Qb", "QTb", "Pa", "PTa", "Pb", "PTb", "C32", "C32T", "C64", "C64T"):
                tm_[nm] = fw.sb("%s_%d" % (nm, i), [128, 512], F32)
            tm_["Rp"] = tm_["dec"]
            tm_["Mf"] = tm_["tD"]
            tmps.append(tm_)
        S32 = [fw.sb("S32_%d" % d, [128, 512], F32) for d in range(2)]
        Sb = [fw.sb("Sb_%d" % d, [128, 512], BF16) for d in range(2)]
        un = [fw.sb("un_%d" % d, [128, 512], BF16) for d in range(2)]
        t5_ = fw.sb("t5", [128, 512], F32)
        t5 = [t5_, t5_]
        ost = [fw.sb("ost%d" % i, [128, 512], F32) for i in range(2)]
        for d in range(2):
            fw.op("pool", lambda g, d=d: g.memset(S32[d].h[:, :], 0.0), [], [S32[d][:]])
            fw.op("pool", lambda g, d=d: g.memset(Sb[d].h[:, :], 0.0), [], [Sb[d][:]])
        qscale = 128.0 ** -0.5

        def pg():
            i = pfc[0]
            pfc[0] += 1
            return PF[(0, 1, 2, 6)[i % 4]]

        def prep(tt, d, sl, tm):
            tok = slice(tt * 128, (tt + 1) * 128)
            G = GA[d]
            fw.tt(V(v3(tm["Rp"], 128), tm["Rp"].buf), V(ident4F.ap.rearrange("p (a n) -> p a n", n=128), ident4F.buf),
                  bc4(G["Gs"], tt), ALU.mult, e="pool")
            p1 = pg()
            fw.mm(p1[:], onesF, tm["Rp"][:])
            fw.act(tm["EGr"][:], p1[:], AF.Exp, bias=math.log(qscale))
            fw.tt(tm["tD"][:], p1[:], negmask[d], ALU.add)
            fw.tt(V(v3(tm["tD"], 128), tm["tD"].buf), V(v3(tm["tD"], 128), tm["tD"].buf), bc4(G["nG"], tt), ALU.add)
            fw.act(tm["dec"][:], tm["tD"][:], AF.Exp)
            pk = pg()
            for h in range(4):
                fw.mm(pk[:, h * 128:(h + 1) * 128], kT[:, h, tok], kT[:, h, tok])
            fw.tt(tm["Mf"][:], pk[:], tm["dec"][:], ALU.mult)
            fw.tt(V(v3(tm["Mf"], 128), tm["Mf"].buf), V(v3(tm["Mf"], 128), tm["Mf"].buf), bcb(tt, d), ALU.mult)
            M0, MT0 = tm["M0"], tm["MT0"]
            fw.tt(M0[:], tm["Mf"][:], strictB[d], ALU.mult, e="pool")
            ptr = pg()
            for h in range(4):
                fw.tr(ptr[:, h * 128:(h + 1) * 128], M0[:, h * 128:(h + 1) * 128], identF)
            fw.copy(MT0[:], ptr[:], e="act")
            Q, QT, P, PT = tm["Qa"], tm["QTa"], tm["Pa"], tm["PTa"]
            Qn, QTn, Pn, PTn = tm["Qb"], tm["QTb"], tm["Pb"], tm["PTb"]
            fw.tt(Q[:], M0[:], mD16, ALU.mult, e="pool")
            fw.tt(QT[:], MT0[:], mD16, ALU.mult, e="pool")
            fw.tt(tm["C32"][:], M0[:], mC32, ALU.mult, e="pool")
            fw.tt(tm["C32T"][:], MT0[:], mC32, ALU.mult, e="pool")
            fw.tt(tm["C64"][:], M0[:], mC64, ALU.mult, e="pool")
            fw.tt(tm["C64T"][:], MT0[:], mC64, ALU.mult, e="pool")
            fw.tt(P[:], ident4F, Q[:], ALU.subtract)
            fw.tt(PT[:], ident4F, QT[:], ALU.subtract)

            def mm4(lhsT, rhs):
                p_ = pg()
                for h in range(4):
                    hs = slice(h * 128, (h + 1) * 128)
                    fw.mm(p_[:, hs], lhsT[:, hs], rhs[:, hs])
                return p_

            for lev in range(3):
                pq = mm4(QT, Q)
                fw.copy(Qn[:], pq[:], e="dve")
                pqt = mm4(Q, QT)
                fw.copy(QTn[:], pqt[:], e="act")
                pp = mm4(QTn, P)
                fw.tt(Pn[:], pp[:], P[:], ALU.add)
                ppt = mm4(Qn, PT)
                fw.tt(PTn[:], ppt[:], PT[:], ALU.add)
                Q, Qn = Qn, Q
                QT, QTn = QTn, QT
                P, Pn = Pn, P
                PT, PTn = PTn, PT
            X, XT = P, PT
            py = mm4(tm["C32T"], X)
            fw.copy(Qn[:], py[:], e="act")
            pyp = mm4(tm["C32"], XT)
            fw.copy(QTn[:], pyp[:], e="dve")
            px = mm4(XT, Qn)
            fw.tt(Pn[:], X[:], px[:], ALU.subtract)
            pxt = mm4(X, QTn)
            fw.tt(PTn[:], XT[:], pxt[:], ALU.subtract)
            X, XT = Pn, PTn
            py = mm4(tm["C64T"], X)
            fw.copy(Q[:], py[:], e="act")
            px = mm4(XT, Q)
            fw.tt(tm["TTb"][:], X[:], px[:], ALU.subtract)
            TTm = tm["TTb"]
            for h in range(4):
                fw.tr(PB[:, h * 128:(h + 1) * 128], kT[:, h, tok], identB)
            fw.tt(V(v3(tm["kEG"], 128), tm["kEG"].buf), V(PB.h[:, 0:512].rearrange("p (a n) -> p a n", n=128), PB.buf), bc4(G["EG"], tt), ALU.mult)
            fw.tt(V(v3(sl["kd"], 128), sl["kd"].buf), V(PB.h[:, 0:512].rearrange("p (a n) -> p a n", n=128), PB.buf), bc4(G["ER"], tt), ALU.mult)
            for h in range(4):
                fw.tr(PB[:, 512 + h * 128: 512 + (h + 1) * 128], vT[:, h, tok], identB)
            fw.copy(tm["vtok"][:], PB[:, 512:1024], e="act")
            po = pg()
            for h in range(4):
                hs = slice(h * 128, (h + 1) * 128)
                fw.mm(po[:, hs], tm["kEG"][:, hs], TTm[:, hs])
            fw.copy(sl["w0T"][:], po[:], e="act")
            po2 = pg()
            for h in range(4):
                hs = slice(h * 128, (h + 1) * 128)
                fw.mm(po2[:, hs], TTm[:, hs], tm["vtok"][:, hs])
            fw.tt(V(v3(sl["ub"], 128), sl["ub"].buf), V(po2.h[:, :].rearrange("p (a n) -> p a n", n=128), po2.buf), bcb(tt, d), ALU.mult)
            po3 = pg()
            for h in range(4):
                hs = slice(h * 128, (h + 1) * 128)
                fw.mm(po3[:, hs], kT[:, h, tok], qT[:, h, tok])
            fw.stt(sl["qkdT"][:], po3[:], qscale, tm["dec"][:], ALU.mult, ALU.mult)
            fw.tt(V(v3(sl["qgT"], 128), sl["qgT"].buf), qT[:, :, tok], V(v3(tm["EGr"], 128), tm["EGr"].buf), ALU.mult, e="pool")

        def scan(tt, d, sl):
            G = GA[d]
            for ci in ((0, 1) if d == 0 else (1, 0)):
                cs = slice(64 * ci, 64 * ci + 64)
                for h in range(4):
                    hs = slice(h * 128, (h + 1) * 128)
                    fw.mm(PF[3][cs, hs], sl["w0T"][:, h * 128 + 64 * ci: h * 128 + 64 * ci + 64], Sb[d][:, hs])
                t5v = V(t5[d].h[cs, :].rearrange("p (a n) -> p a n", n=128), t5[d].buf)
                fw.tt(t5v, V(PF[3].h[cs, :].rearrange("p (a n) -> p a n", n=128), PF[3].buf), bcb(tt, d, cs), ALU.mult)
                fw.tt(un[d][cs, :], sl["ub"][cs, :], t5[d][cs, :], ALU.subtract)
                for h in range(4):
                    oc = slice(h * 128 + 64 * ci, h * 128 + 64 * ci + 64)
                    hs = slice(h * 128, (h + 1) * 128)
                    fw.mm(PF[4][:, oc], Sb[d][:, hs], sl["qgT"][:, oc], start=True, stop=False)
                    fw.mm(PF[4][:, oc], un[d][cs, hs], sl["qkdT"][cs, oc], start=False, stop=True)
                for h in range(4):
                    hs = slice(h * 128, (h + 1) * 128)
                    fw.mm(PF[5][:, hs], sl["kd"][cs, hs], un[d][cs, hs])
                gl = G["GL0"] if ci == 0 else G["GL1"]
                fw.tt(V(v3(S32[d], 128), S32[d].buf), V(v3(S32[d], 128), S32[d].buf), bc4(gl, tt), ALU.mult)
                fw.tt(S32[d][:], S32[d][:], PF[5][:], ALU.add)
                fw.copy(Sb[d][:], S32[d][:], e="act")
            o = ring(ost, "ost")
            fw.copy(o[:], PF[4][:], e="act")
            fw.dma("pool", V(oT_d[d].h.rearrange("(h p) t -> p h t", p=128)[:, :, tt * 128:(tt + 1) * 128], oT_d[d].buf),
                   V(v3(o, 128), o.buf), semof=o.buf)

        order_f = [32, 33] + list(range(32))
        order_b = [33, 32] + list(range(31, -1, -1))
        k = 0
        for s_ in range(NT):
            for d, tt in ((0, order_f[s_]), (1, order_b[s_])):
                sl = slots[k % NS]
                tm = tmps[0]
                k += 1
                if "gdn_abonly" in dbg:
                    continue
                prep(tt, d, sl, tm)
                if "gdn_noscan" not in dbg:
                    scan(tt, d, sl)
        fw.pop()


    def phase_mla(l, with_ctx):
        fw.push()
        cqn = fw.sb("cqn", [128, 3, TT], BF16)
        ckvn = fw.sb("ckvn", [128, 2, TT], BF16)
        ropeC = fw.sb("ropeC", [96, TT], BF16)
        ropeS = fw.sb("ropeS", [96, TT], BF16)
        kper = fw.sb("kper", [96, TT], BF16)
        wuq = fw.sb("wuq", [128, 3, 1536], BF16)
        wukv = fw.sb("wukv", [128, 2, 1024], BF16)
        qn = fw.sb("qn", [128, 3], F32)
        kvn = fw.sb("kvn", [128, 2], F32)
        fw.dma("sp", qn[:], qnorm_d[l])
        fw.dma("sp", kvn[:], kvnorm_d[l])
        fw.push()
        st32 = fw.sb("st32", [128, 4608], F32)
        fw.dma("sp", st32[:, 0:4608], V(w_uq_d.h[l].rearrange("p a n -> p (a n)"), w_uq_d.buf))
        fw.copy(V(wuq.h[:, :, :].rearrange("p a n -> p (a n)"), wuq.buf), st32[:, 0:4608], e="pool")
        fw.dma("sp", st32[:, 0:2048], V(w_ukv_d.h[l].rearrange("p a n -> p (a n)"), w_ukv_d.buf))
        fw.copy(V(wukv.h[:, :, :].rearrange("p a n -> p (a n)"), wukv.buf), st32[:, 0:2048], e="pool")
        fw.dma("sp", st32[0:96, 0:TT], ROPEC_d[:])
        fw.copy(ropeC[:], st32[0:96, 0:TT], e="pool")
        fw.dma("sp", st32[0:96, 0:TT], ROPES_d[:])
        fw.copy(ropeS[:], st32[0:96, 0:TT], e="pool")
        kpa = fw.sb("kpa", [96, TT], BF16)
        kpb = fw.sb("kpb", [96, TT], BF16)
        fw.dma("sp", kpa[64:96, :], pT_d[7296:7328, :])
        fw.dma("sp", kpb[64:96, :], pT_d[7328:7360, :])
        fw.tt(st32[64:96, 0:TT], kpa[64:96, :], ropeC[64:96, :], ALU.mult)
        fw.tt(kpb[64:96, :], kpb[64:96, :], ropeS[64:96, :], ALU.mult)
        fw.tt(kper[64:96, :], st32[64:96, 0:TT], kpb[64:96, :], ALU.add)
        sqr = [fw.sb("msq%d" % i, [128, 512], BF16) for i in range(3)]
        rr = [fw.sb("mrr%d" % i, [128, 512], F32) for i in range(2)]
        for (dst, nb_, row0, nrm, width) in ((cqn, 3, 3072, qn, 384.0), (ckvn, 2, 3456, kvn, 256.0)):
            for b_ in range(nb_):
                fw.dma("sp", dst[:, b_, :], pT_d[row0 + b_ * 128: row0 + (b_ + 1) * 128, :])
            for (t0, n) in CHUNKS:
                ps = pf(0, 5)
                sqs = []
                for b_ in range(nb_):
                    sq = ring(sqr, "msq")
                    fw.tt(sq[:, 0:n], dst[:, b_, t0:t0 + n], dst[:, b_, t0:t0 + n], ALU.mult, e="pool")
                    sqs.append(sq)
                for b_ in range(nb_):
                    fw.mm(ps[:, 0:n], onesB, sqs[b_][:, 0:n], start=(b_ == 0), stop=(b_ == nb_ - 1))
                r = ring(rr, "mrr")
                fw.act(r[:, 0:n], ps[:, 0:n], AF.Ln, bias=EPS, scale=1.0 / width)
                fw.act(r[:, 0:n], r[:, 0:n], AF.Exp, scale=-0.5)
                for b_ in range(nb_):
                    fw.stt(dst[:, b_, t0:t0 + n], dst[:, b_, t0:t0 + n], nrm[:, b_:b_ + 1], r[:, 0:n], ALU.mult, ALU.mult)
        fw.pop()

        kTh = [fw.sb("kTh%d" % i, [96, TT], BF16) for i in range(2)]
        qTh = [fw.sb("qTh%d" % i, [96, TT], BF16) for i in range(2)]
        Vaug = [fw.sb("Vaug%d" % i, [128, NT, 128], BF16) for i in range(2)]
        for i in range(2):
            fw.op("pool", lambda g, i=i: g.memset(Vaug[i].h[:, :, 64:128], 1.0), [], [Vaug[i][:, :, 64:128]])
        PTr = [fw.sb("PTr%d" % i, [128, 512], BF16) for i in range(4)]
        q1 = [fw.sb("q1_%d" % i, [96, 512], F32) for i in range(2)]
        q2 = [fw.sb("q2_%d" % i, [96, 512], F32) for i in range(2)]
        rden = [fw.sb("rden%d" % i, [128, 512], F32) for i in range(2)]
        otmp = [fw.sb("otmp%d" % i, [64, 512], F32) for i in range(2)]
        zc = [fw.sb("zc%d" % i, [64, 512], BF16) for i in range(2)]
        ycs = [fw.sb("ycs%d" % i, [64, 512], BF16) for i in range(2)]
        ascale = 96.0 ** -0.5
        qchunks = CHUNKS if with_ctx else CHUNKS[:8]
        for h in range(8):
            kt_ = kTh[h % 2]
            qt_ = qTh[h % 2]
            va = Vaug[h % 2]
            for (t0, n) in CHUNKS:
                ps = pf(0, 5)
                for kc in range(2):
                    fw.mm(ps[0:64, 0:n], wukv[:, kc, h * 128: h * 128 + 64], ckvn[:, kc, t0:t0 + n], start=(kc == 0), stop=(kc == 1))
                fw.copy(kt_[0:64, t0:t0 + n], ps[0:64, 0:n], e="dve")
            fw.copy(kt_[64:96, :], kper[64:96, :], e="pool")
            for t8 in range(0, NT, 8):
                nt8 = min(8, NT - t8)
                ps = pf(0, 5)
                for a in range(nt8):
                    tt = t8 + a
                    for kc in range(2):
                        fw.mm(ps[:, a * 64:(a + 1) * 64], ckvn[:, kc, tt * 128:(tt + 1) * 128], wukv[:, kc, h * 128 + 64: h * 128 + 128],
                              start=(kc == 0), stop=(kc == 1))
                fw.copy(va[:, t8:t8 + nt8, 0:64], V(ps.h[:, 0:nt8 * 64].rearrange("p (a n) -> p a n", n=64), ps.buf), e="dve")
            for (t0, n) in qchunks:
                pa = pf(0, 5)
                for kc in range(3):
                    fw.mm(pa[0:96, 0:n], wuq[:, kc, h * 96:(h + 1) * 96], cqn[:, kc, t0:t0 + n], start=(kc == 0), stop=(kc == 2))
                pb_ = pf(0, 5)
                for kc in range(3):
                    fw.mm(pb_[0:96, 0:n], wuq[:, kc, 768 + h * 96: 768 + (h + 1) * 96], cqn[:, kc, t0:t0 + n], start=(kc == 0), stop=(kc == 2))
                a1 = ring(q1, "q1")
                a2 = ring(q2, "q2")
                fw.tt(a1[:, 0:n], pa[0:96, 0:n], ropeC[:, t0:t0 + n], ALU.mult)
                fw.tt(a2[:, 0:n], pb_[0:96, 0:n], ropeS[:, t0:t0 + n], ALU.mult)
                fw.tt(qt_[:, t0:t0 + n], a1[:, 0:n], a2[:, 0:n], ALU.add, e="pool")
            for qi, (t0, n) in enumerate(qchunks):
                ktiles = list(range(NT)) if t0 < TL else [32, 33]
                acc = PF[5 + qi % 2]
                zt = ring(zc, "zc")
                fw.dma("sp", zt[:, 0:n], pT_d[3712 + h * 64: 3712 + (h + 1) * 64, t0:t0 + n])
                for ki, kt in enumerate(ktiles):
                    ps = pf(0, 5)
                    fw.mm(ps[:, 0:n], kt_[:, kt * 128:(kt + 1) * 128], qt_[:, t0:t0 + n])
                    pt = ring(PTr, "PTr")
                    fw.act(pt[:, 0:n], ps[:, 0:n], AF.Exp, scale=ascale)
                    fw.mm(acc[:, 0:n], va[:, kt, :], pt[:, 0:n], start=(ki == 0), stop=(ki == len(ktiles) - 1))
                rd = ring(rden, "rden")
                fw.recip(rd[64:128, 0:n], acc[64:128, 0:n])
                ot = ring(otmp, "otmp")
                fw.tt(ot[:, 0:n], acc[0:64, 0:n], rd[64:128, 0:n], ALU.mult)
                yc = ring(ycs, "ycs")
                fw.tt(yc[:, 0:n], ot[:, 0:n], zt[:, 0:n], ALU.mult, e="pool")
                fw.dma("pool", ycT_d[h * 64:(h + 1) * 64, t0:t0 + n], yc[:, 0:n], semof=yc.buf)
        fw.pop()

    def phase_final(l, with_ctx, last):
        fw.push()
        wbr = fw.sb("wbr", [128, 12, 1024], BF16)
        wout = fw.sb("wout", [128, 8, 1024], BF16)
        dnrm = fw.sb("dnrm", [128, 1], F32)
        fw.dma("sp", dnrm[:], dnorm_d[l])
        fw.push()
        st32 = [fw.sb("fst32_%d" % i, [128, 4096], F32) for i in range(2)]
        for q in range(3):
            st = ring(st32, "fst32")
            fw.dma("sp", st[:], V(w_br_d.h[l, :, q * 4:(q + 1) * 4, :].rearrange("p a n -> p (a n)"), w_br_d.buf))
            fw.copy(V(wbr.h[:, q * 4:(q + 1) * 4, :].rearrange("p a n -> p (a n)"), wbr.buf), st[:], e="pool")
        for q in range(2):
            st = ring(st32, "fst32")
            fw.dma("sp", st[:], V(w_out_d.h[l, :, q * 4:(q + 1) * 4, :].rearrange("p a n -> p (a n)"), w_out_d.buf))
            fw.copy(V(wout.h[:, q * 4:(q + 1) * 4, :].rearrange("p a n -> p (a n)"), wout.buf), st[:], e="pool")
        fw.pop()
        of_ = fw.sb("of", [128, 4, 512], F32)
        ob_ = fw.sb("ob", [128, 4, 512], F32)
        sqb = fw.sb("sqb", [128, 4, 512], BF16)
        rr = [fw.sb("frr%d" % i, [128, 512], F32) for i in range(2)]
        szb = fw.sb("szb", [128, 4, 512], BF16)
        ybT = fw.sb("ybT", [128, 4, 512], BF16)
        yaT = fw.sb("yaTt", [128, 4, 512], BF16)
        ycT = fw.sb("ycTt", [128, 4, 512], BF16)
        gts = [fw.sb("gts%d" % i, [128, 8, 512], BF16) for i in range(2)]
        merged = fw.sb("merged", [128, 8, 512], F32)
        mergedb = fw.sb("mergedb", [128, 8, 512], BF16)
        tmpf = [fw.sb("tmpf%d" % i, [128, 512], F32) for i in range(2)]
        xr = [fw.sb("fxr%d" % i, [128, 1024], F32) for i in range(2)]
        xo = [fw.sb("fxo%d" % i, [128, 1024], F32) for i in range(2)]
        junk = fw.sb("junk", [128, 512], BF16)
        st4 = fw.sb("st4", [128, 8 * NT], F32, nsub=NT)
        xsrc = xin if l == 0 else xs_d
        chunks = CHUNKS if with_ctx else CHUNKS[:8]
        for (t0, n) in chunks:
            j = 0 if t0 < TL else 1
            for d, dstt in ((0, of_), (1, ob_)):
                fw.dma("sp", dstt[:, :, 0:n], V(oT_d[d].h.rearrange("(h p) t -> p h t", p=128)[:, :, t0:t0 + n], oT_d[d].buf))
            fw.dma("sp", szb[:, :, 0:n], V(pT_d.h[2560:3072, :].rearrange("(g p) t -> p g t", p=128)[:, :, t0:t0 + n], pT_d.buf))
            fw.dma("sp", yaT[:, :, 0:n], V(yaT_d.h.rearrange("(g p) t -> p g t", p=128)[:, :, t0:t0 + n], yaT_d.buf))
            fw.dma("sp", ycT[:, :, 0:n], V(ycT_d.h.rearrange("(g p) t -> p g t", p=128)[:, :, t0:t0 + n], ycT_d.buf))
            fw.tt(of_[:, :, 0:n], of_[:, :, 0:n], ob_[:, :, 0:n], ALU.add, e="pool")
            fw.act(sqb[:, :, 0:n], of_[:, :, 0:n], AF.Square)
            for h in range(4):
                ps = pf()
                fw.mm(ps[:, 0:n], onesB, sqb[:, h, 0:n])
                r = ring(rr, "frr")
                fw.act(r[:, 0:n], ps[:, 0:n], AF.Ln, bias=EPS, scale=1.0 / 128)
                fw.act(r[:, 0:n], r[:, 0:n], AF.Exp, scale=-0.5)
                fw.stt(r[:, 0:n], of_[:, h, 0:n], dnrm[:, 0:1], r[:, 0:n], ALU.mult, ALU.mult)
                fw.tt(ybT[:, h, 0:n], r[:, 0:n], szb[:, h, 0:n], ALU.mult, e="pool")
            for br, src in enumerate((yaT, ybT, ycT)):
                gt = ring(gts, "gts")
                fw.dma("sp", gt[:, :, 0:n], V(pT_d.h[4224 + br * 1024: 4224 + (br + 1) * 1024, :].rearrange("(g p) t -> p g t", p=128)[:, :, t0:t0 + n], pT_d.buf))
                for jb in range(8):
                    ps = pf()
                    for kc in range(4):
                        fw.mm(ps[:, 0:n], wbr[:, br * 4 + kc, jb * 128:(jb + 1) * 128], src[:, kc, 0:n], start=(kc == 0), stop=(kc == 3))
                    if br == 0:
                        fw.tt(merged[:, jb, 0:n], ps[:, 0:n], gt[:, jb, 0:n], ALU.mult)
                    else:
                        tf = ring(tmpf, "tmpf")
                        fw.tt(tf[:, 0:n], ps[:, 0:n], gt[:, jb, 0:n], ALU.mult)
                        fw.tt(merged[:, jb, 0:n], merged[:, jb, 0:n], tf[:, 0:n], ALU.add, e="pool")
            fw.copy(mergedb[:, :, 0:n], merged[:, :, 0:n], e="act")
            for ts_ in range(n // 128):
                tt = (t0 // 128) + ts_
                xt = ring(xr, "fxr")
                fw.dma("sp", xt[:], xsrc[tt * 128:(tt + 1) * 128, :])
                st = st4.sub(tt)
                c0 = 8 * tt
                phs = []
                for half in range(2):
                    ps = pf()
                    for kc in range(8):
                        fw.mm(ps[:], mergedb[:, kc, ts_ * 128:(ts_ + 1) * 128], wout[:, kc, half * 512:(half + 1) * 512], start=(kc == 0), stop=(kc == 7))
                    fw.act(junk[:], ps[:], AF.Square, accum_out=st[:, c0 + half: c0 + half + 1])
                    phs.append(ps)
                fw.tt(st[:, c0 + 2:c0 + 3], st[:, c0:c0 + 1], st[:, c0 + 1:c0 + 2], ALU.add)
                fw.act(st[:, c0 + 3:c0 + 4], st[:, c0 + 2:c0 + 3], AF.Sqrt, bias=EPS, scale=1.0 / 1024)
                fw.recip(st[:, c0 + 4:c0 + 5], st[:, c0 + 3:c0 + 4])
                o = ring(xo, "fxo")
                for half in range(2):
                    hs = slice(half * 512, (half + 1) * 512)
                    fw.stt(o[:, hs], phs[half][:], st[:, c0 + 4:c0 + 5], ggbc[j][:, hs], ALU.mult, ALU.mult)
                fw.tt(o[:], o[:], xt[:], ALU.add, e="pool")
                if last:
                    fw.dma("pool", out_d[tt * 128:(tt + 1) * 128, :], o[:], semof=o.buf)
                else:
                    fw.dma("pool", xs_d[tt * 128:(tt + 1) * 128, :], o[:], semof=o.buf)
        fw.pop()

    for l in range(n_layers):
        alloc_s1()
        phase_mod(l)
        phase_h(l)
        if "hT" in dbg and l == 0:
            dbg_d["hT"] = fw.dram("hT_o", [128, 8, TT], BF16, "ExternalOutput")
            fw.dma("pool", dbg_d["hT"][:], S["hT"][:], semof=S["hT"].buf)
        if stop_after == "h":
            fw.pop()
            break
        phase_proj(l)
        fw.pop()
        if stop_after == "proj":
            break
        if "nofourier" not in dbg:
            phase_fourier(l, with_ctx=(l < n_layers - 1 or "ctxall" in dbg))
        if stop_after == "fourier":
            break
        if "nogdn" not in dbg:
            phase_gdn(l)
        if stop_after == "gdn":
            break
        wc = (l < n_layers - 1 or "ctxall" in dbg)
        phase_mla(l, with_ctx=wc)
        if stop_after == "mla":
            break
        phase_final(l, with_ctx=wc, last=(l == n_layers - 1 and "ctxall" not in dbg))

    if "abT" in dbg:
        dbg_d["abT"] = fw.dram("abT_o", [128, NT, 16], F32, "ExternalOutput")
        fw.dma("pool", dbg_d["abT"][:], abT[:], semof=abT.buf)
    outs = [out_d, xs_d, pT_d, yaT_d, ycT_d] + oT_d + list(dbg_d.values())
    fw.finish(outs, e="pool")
    return nc, fw


def kernel(**inputs):
    inp = {k: np.asarray(v) for k, v in inputs.items()}
    consts = host_constants()
    shared = host_layout_shared(inp)
    nc, fw = build()
    in_maps = []
    for b in range(8):
        m = dict(consts)
        m.update(shared)
        m.update(host_layout(inp, b))
        in_maps.append(m)
    res = run_bass_kernel_spmd(nc, in_maps, core_ids=list(range(8)))
    return np.stack([np.asarray(r["out"]) for r in res.results], axis=0).astype(np.float32)
```

```python
import math
import numpy as np
import ml_dtypes
import concourse.bass as bass
import concourse.mybir as mybir
from concourse.bass_utils import run_bass_kernel_spmd

F32 = mybir.dt.float32
BF16 = mybir.dt.bfloat16
AF = mybir.ActivationFunctionType
ALU = mybir.AluOpType

TL = 4096
TC = 256
TT = TL + TC
NT = TT // 128
CHUNKS = [(i * 512, 512) for i in range(8)] + [(TL, TC)]
NB = 58
EPS = 1e-6
NEG = -30000.0


class Buf:
    __slots__ = ("name", "writers", "readers", "waw", "dsem", "excl")

    def __init__(self, name, waw=True):
        self.name = name
        self.excl = False
        self.writers = {}
        self.readers = {}
        self.waw = waw
        self.dsem = None


class V:
    __slots__ = ("ap", "buf")

    def __init__(self, ap, buf):
        self.ap = ap
        self.buf = buf


class T:
    def __init__(self, handle, buf, bufs=None):
        self.h = handle
        self.buf = buf
        self.bufs = bufs

    def __getitem__(self, key):
        return V(self.h[key], self.buf)

    def sub(self, i):
        return T(self.h, self.bufs[i])


class FW:
    def __init__(self, nc, strict_same=True):
        self.nc = nc
        self.eng = {"pe": nc.tensor, "dve": nc.vector, "act": nc.scalar, "pool": nc.gpsimd, "sp": nc.sync}
        self.sems = {}
        self.count = {}
        for e in self.eng:
            self.sems[e] = nc.alloc_semaphore("s_" + e)
            self.count[e] = 0
        self.seen = {e: {} for e in self.eng}
        self.strict_same = strict_same
        self.nops = {e: 0 for e in self.eng}
        self.nwait = 0
        self.uid = 0
        self.scopes = [[]]
        self.scope_bufs = [[]]
        self.free_dsems = []
        self.bar_tile = V(nc.alloc_sbuf_tensor("bar_tile", [128, 8], F32)[:, :], Buf("bar"))

    def sb(self, name, shape, dtype, nsub=0, waw=True):
        self.uid += 1
        name = "%s_u%d" % (name, self.uid)
        g = self.nc.sbuf_tensor(name, list(shape), dtype)
        h = g.__enter__()
        self.scopes[-1].append(g)
        bufs = [Buf(name + "_%d" % i, waw) for i in range(nsub)] if nsub else None
        t = T(h, Buf(name, waw), bufs)
        self.scope_bufs[-1].extend([t.buf] + (bufs or []))
        return t

    def push(self):
        self.scopes.append([])
        self.scope_bufs.append([])

    def pop(self):
        self.barrier()
        for g in reversed(self.scopes.pop()):
            g.__exit__(None, None, None)
        for b in self.scope_bufs.pop():
            if b.dsem is not None:
                self.free_dsems.append(b.dsem)
                b.dsem = None

    def barrier(self):
        need = {k: v for k, v in self.count.items() if v > 0}
        self._waits("pool", need)
        ins = self.eng["pool"].memset(self.bar_tile.ap, 0.0)
        self.count["pool"] += 1
        ins.then_inc(self.sems["pool"], 1)
        val = self.count["pool"]
        for e in self.eng:
            if e == "pool":
                continue
            self._waits(e, {"pool": val})
            for k, v in need.items():
                self.seen[e][k] = max(self.seen[e].get(k, 0), v)

    def ps(self, name, shape, dtype=F32):
        h = self.nc.alloc_psum_tensor(name, list(shape), dtype)
        b = Buf(name)
        b.excl = True
        return T(h, b)

    def dram(self, name, shape, dtype, kind="Internal", nsub=0):
        h = self.nc.dram_tensor(name, list(shape), dtype, kind=kind)
        bufs = [Buf(name + "_%d" % i, False) for i in range(nsub)] if nsub else None
        return T(h.ap(), Buf(name, waw=False), bufs)

    def _need(self, reads, writes):
        need = {}
        for v in reads:
            for k, val in v.buf.writers.items():
                if need.get(k, 0) < val:
                    need[k] = val
            if v.buf.excl:
                for k, val in v.buf.readers.items():
                    if need.get(k, 0) < val:
                        need[k] = val
        for v in writes:
            b = v.buf
            if b.waw:
                for k, val in b.writers.items():
                    if need.get(k, 0) < val:
                        need[k] = val
            for k, val in b.readers.items():
                if need.get(k, 0) < val:
                    need[k] = val
        return need

    def _waits(self, e, need):
        seen = self.seen[e]
        for k, val in need.items():
            if k == e and (e == "pe" or not self.strict_same):
                continue
            if seen.get(k, 0) >= val:
                continue
            self.eng[e].wait_ge(self.sems[k], val)
            seen[k] = val
            self.nwait += 1

    def op(self, e, fn, reads=(), writes=()):
        self._waits(e, self._need(reads, writes))
        ins = fn(self.eng[e])
        self.count[e] += 1
        val = self.count[e]
        ins.then_inc(self.sems[e], 1)
        self.nops[e] += 1
        for v in reads:
            b = v.buf
            if b.readers.get(e, 0) < val:
                b.readers[e] = val
        for v in writes:
            b = v.buf
            if b.waw:
                b.writers = {e: val}
            else:
                b.writers[e] = val
            b.readers = {}
        return ins

    def dma(self, q, out, in_, semof=None, **kw):
        sbuf = semof if semof is not None else out.buf
        if sbuf.dsem is None:
            if self.free_dsems:
                key = self.free_dsems.pop()
            else:
                key = "d%d" % len(self.sems)
                self.sems[key] = self.nc.alloc_semaphore(key)
                self.count[key] = 0
            sbuf.dsem = key
        key = sbuf.dsem
        self._waits(q, self._need([in_], [out]))
        ins = self.eng[q].dma_start(out=out.ap, in_=in_.ap, **kw)
        self.count[key] += 16
        val = self.count[key]
        ins.then_inc(self.sems[key], 16)
        self.nops[q] += 1
        b = in_.buf
        if b.readers.get(key, 0) < val:
            b.readers[key] = val
        b = out.buf
        if b.waw:
            b.writers = {key: val}
        else:
            b.writers[key] = val
        b.readers = {}
        return ins

    def finish(self, bufs, e="sp"):
        need = {}
        for b in bufs:
            for k, val in b.buf.writers.items():
                if need.get(k, 0) < val:
                    need[k] = val
        self._waits(e, need)

    def mm(self, out, lhsT, rhs, start=True, stop=True):
        reads = [lhsT, rhs] + ([] if start else [out])
        return self.op("pe", lambda e: e.matmul(out.ap, lhsT.ap, rhs.ap, start=start, stop=stop), reads, [out])

    def tr(self, out, in_, ident):
        return self.op("pe", lambda e: e.transpose(out.ap, in_.ap, ident.ap), [in_, ident], [out])

    def act(self, out, in_, func, bias=None, scale=None, accum_out=None):
        reads = [in_]
        kw = {}
        if bias is not None:
            if isinstance(bias, V):
                reads.append(bias)
                kw["bias"] = bias.ap
            else:
                kw["bias"] = bias
        if scale is not None:
            if isinstance(scale, V):
                reads.append(scale)
                kw["scale"] = scale.ap
            else:
                kw["scale"] = scale
        writes = [out]
        if accum_out is not None:
            kw["accum_out"] = accum_out.ap
            writes.append(accum_out)
        return self.op("act", lambda e: e.activation(out.ap, in_.ap, func, **kw), reads, writes)

    def tt(self, out, in0, in1, op, e="dve"):
        return self.op(e, lambda g: g.tensor_tensor(out.ap, in0.ap, in1.ap, op), [in0, in1], [out])

    def ts(self, out, in0, s1, s2=None, op0=ALU.mult, op1=None, e="dve"):
        reads = [in0]
        a1 = s1
        if isinstance(s1, V):
            reads.append(s1)
            a1 = s1.ap
        a2 = s2
        if isinstance(s2, V):
            reads.append(s2)
            a2 = s2.ap
        if op1 is None:
            return self.op(e, lambda g: g.tensor_scalar(out.ap, in0.ap, a1, None, op0), reads, [out])
        return self.op(e, lambda g: g.tensor_scalar(out.ap, in0.ap, a1, a2, op0, op1), reads, [out])

    def stt(self, out, in0, s, in1, op0, op1):
        reads = [in0, in1]
        a = s
        if isinstance(s, V):
            reads.append(s)
            a = s.ap
        return self.op("dve", lambda g: g.scalar_tensor_tensor(out.ap, in0.ap, a, in1.ap, op0, op1), reads, [out])

    def copy(self, out, in_, e="dve"):
        if e == "act":
            return self.op("act", lambda g: g.activation(out.ap, in_.ap, AF.Copy), [in_], [out])
        return self.op(e, lambda g: g.tensor_copy(out.ap, in_.ap), [in_], [out])

    def recip(self, out, in_):
        return self.op("dve", lambda g: g.reciprocal(out.ap, in_.ap), [in_], [out])


def _bf(a):
    return np.ascontiguousarray(a).astype(ml_dtypes.bfloat16)


_CONST = {}


def host_constants():
    if _CONST:
        return _CONST
    c = {}
    t = np.arange(TL, dtype=np.int64)
    ph = (np.outer(t, t) % TL).astype(np.float64) * (2 * np.pi / TL)
    c["CL"] = _bf(np.cos(ph))
    c["NSL"] = _bf(-np.sin(ph))
    tcx = np.arange(TC, dtype=np.int64)
    phc = (np.outer(tcx, tcx) % TC).astype(np.float64) * (2 * np.pi / TC)
    c["CLC"] = _bf(np.cos(phc))
    c["NSLC"] = _bf(-np.sin(phc))
    ch = np.arange(128, dtype=np.int64)
    phd = (np.outer(ch, ch) % 128).astype(np.float64) * (2 * np.pi / 128)
    c["CSC"] = _bf(np.concatenate([np.cos(phd), np.sin(phd)], axis=1))
    rows = np.repeat(np.arange(64, dtype=np.float32), 64)
    cols = np.tile(np.arange(64, dtype=np.float32), 64)
    inv = (10000.0 ** (-np.arange(8, dtype=np.float32) / 8)).astype(np.float32)
    ang_r = rows[:, None] * inv
    ang_c = cols[:, None] * inv
    ang = np.concatenate([ang_r, ang_r, ang_c, ang_c], axis=-1)
    sgn = np.array([-1.0] * 8 + [1.0] * 8 + [-1.0] * 8 + [1.0] * 8, dtype=np.float32)
    C = np.ones((96, TT), np.float32)
    S = np.zeros((96, TT), np.float32)
    C[64:96, :TL] = np.cos(ang).T
    S[64:96, :TL] = (np.sin(ang) * sgn[None, :]).T
    c["ROPEC"] = C
    c["ROPES"] = S
    k = np.arange(128)[:, None]
    m = np.arange(128)[None, :]
    same = (k // 64) == (m // 64)
    ident = (k == m).astype(np.float32)
    ones = np.ones((128, 128), np.float32)
    triF = (same & (k <= m)).astype(np.float32)
    restF = (same & (k > m)).astype(np.float32)
    triB = (same & (k >= m)).astype(np.float32)
    restB = (same & (k < m)).astype(np.float32)
    tot0 = np.broadcast_to((k < 64), (128, 128)).astype(np.float32)
    tot1 = np.broadcast_to((k >= 64), (128, 128)).astype(np.float32)
    j = k
    i = m
    nmF = np.where(same & (i >= j), 0.0, NEG).astype(np.float32)
    nmB = np.where(same & (i <= j), 0.0, NEG).astype(np.float32)
    stF = (same & (i > j)).astype(np.float32)
    stB = (same & (i < j)).astype(np.float32)
    c["MSK"] = np.ascontiguousarray(np.concatenate(
        [ident, ones, triF, restF, tot0, tot1, triB, restB,
         np.tile(nmF, (1, 4)), np.tile(nmB, (1, 4)), np.tile(ident, (1, 4))], axis=1)).astype(np.float32)
    d16 = ((k // 16) == (m // 16)).astype(np.float32)
    c32 = (((k // 32) == (m // 32)) & ((k // 16) != (m // 16))).astype(np.float32)
    c64 = (same & ((k // 32) != (m // 32))).astype(np.float32)
    c["MSKB"] = _bf(np.concatenate([ident, ones, np.tile(stF, (1, 4)), np.tile(stB, (1, 4)), np.tile(ident, (1, 4)),
                                    np.tile(d16, (1, 4)), np.tile(c32, (1, 4)), np.tile(c64, (1, 4))], axis=1))
    _CONST.update(c)
    return c


IN_W = (512, 512, 512, 512, 512, 512, 16, 384, 256, 32, 512, 3072)
IN_OFF = np.concatenate([[0], np.cumsum(IN_W)]).tolist()


def _col(v, nblk):
    return np.ascontiguousarray(v.reshape(nblk, 128).T)


def host_layout(inp, b):
    d = {}
    d["xin"] = np.ascontiguousarray(np.concatenate([inp["x"][b], inp["ctx"][b]], axis=0))
    cc = np.stack([_col(inp["c"][b], 8), _col(inp["c_ctx"], 8)], axis=-1)
    d["ccol"] = np.ascontiguousarray(cc)
    return d


def host_layout_shared(inp):
    d = {}
    o = IN_OFF
    perm = np.concatenate([np.arange(8, 16), np.arange(0, 8), np.arange(24, 32), np.arange(16, 24)])
    w_in = inp["w_in"]
    kpe = w_in[:, :, o[9]:o[10]]
    cols = np.concatenate([
        w_in[:, :, o[0]:o[6]],
        w_in[:, :, o[7]:o[9]],
        w_in[:, :, o[10]:o[11]],
        w_in[:, :, o[11]:o[12]],
        kpe, kpe[:, :, perm],
        np.zeros((2, 1024, 64), np.float32),
    ], axis=-1)
    assert cols.shape[-1] == NB * 128
    d["w_in_r"] = np.ascontiguousarray(cols.reshape(2, 8, 128, NB, 128).transpose(0, 3, 2, 1, 4))
    wab = w_in[:, :, o[6]:o[7]]
    d["w_ab"] = np.ascontiguousarray(wab.reshape(2, 8, 128, 16).transpose(0, 2, 1, 3))
    d["w_mod"] = inp["w_mod"]
    d["bmod"] = np.ascontiguousarray(np.stack([_col(inp["b_mod"][l], 24) for l in range(2)]))
    d["gpre"] = np.ascontiguousarray(np.stack([_col(inp["g_pre"][l], 8) for l in range(2)]))
    d["gpost"] = np.ascontiguousarray(np.stack([_col(inp["g_post"][l], 8) for l in range(2)]))
    d["f_w"] = np.ascontiguousarray(inp["f_w"].transpose(0, 2, 1, 3))
    cw = inp["dn_conv"]
    d["convw"] = np.ascontiguousarray(cw.reshape(2, 3, 12, 128).transpose(0, 3, 2, 1))
    d["alog"] = np.ascontiguousarray(np.broadcast_to(inp["dn_a_log"].reshape(2, 1, 1, 8), (2, 128, NT, 8)))
    d["dtb"] = np.ascontiguousarray(np.broadcast_to(inp["dn_dt_bias"].reshape(2, 1, 1, 8), (2, 128, NT, 8)))
    d["dnorm"] = np.ascontiguousarray(inp["dn_norm"].reshape(2, 128, 1))
    d["qnorm"] = np.ascontiguousarray(np.stack([_col(inp["mla_q_norm"][l], 3) for l in range(2)]))
    d["kvnorm"] = np.ascontiguousarray(np.stack([_col(inp["mla_kv_norm"][l], 2) for l in range(2)]))
    wuq = inp["mla_w_uq"]
    hp = np.concatenate([np.arange(64), 64 + perm])
    permc = np.concatenate([h * 96 + hp for h in range(8)])
    both = np.concatenate([wuq, wuq[:, :, permc]], axis=-1)
    d["w_uq"] = np.ascontiguousarray(both.reshape(2, 3, 128, 1536).transpose(0, 2, 1, 3))
    d["w_ukv"] = np.ascontiguousarray(inp["mla_w_ukv"].reshape(2, 2, 128, 1024).transpose(0, 2, 1, 3))
    d["w_br"] = np.ascontiguousarray(inp["w_branch"].reshape(2, 12, 128, 1024).transpose(0, 2, 1, 3))
    d["w_out"] = np.ascontiguousarray(inp["w_out"].reshape(2, 8, 128, 1024).transpose(0, 2, 1, 3))
    return d


def build(n_layers=2, dbg=(), stop_after=None):
    nc = bass.Bass("TRN2", target_bir_lowering=False)
    fw = FW(nc)
    EI = "ExternalInput"

    def skind(name):
        return "ExternalOutput" if name in dbg else "Internal"

    xin = fw.dram("xin", [TT, 1024], F32, EI)
    ccol_d = fw.dram("ccol", [128, 8, 2], F32, EI)
    w_in_d = fw.dram("w_in_r", [2, NB, 128, 8, 128], F32, EI)
    w_ab_d = fw.dram("w_ab", [2, 128, 8, 16], F32, EI)
    w_mod_d = fw.dram("w_mod", [2, 1024, 3072], F32, EI)
    bmod_d = fw.dram("bmod", [2, 128, 24], F32, EI)
    gpre_d = fw.dram("gpre", [2, 128, 8], F32, EI)
    gpost_d = fw.dram("gpost", [2, 128, 8], F32, EI)
    f_w_d = fw.dram("f_w", [2, 128, 4, 128], F32, EI)
    convw_d = fw.dram("convw", [2, 128, 12, 3], F32, EI)
    alog_d = fw.dram("alog", [2, 128, NT, 8], F32, EI)
    dtb_d = fw.dram("dtb", [2, 128, NT, 8], F32, EI)
    dnorm_d = fw.dram("dnorm", [2, 128, 1], F32, EI)
    qnorm_d = fw.dram("qnorm", [2, 128, 3], F32, EI)
    kvnorm_d = fw.dram("kvnorm", [2, 128, 2], F32, EI)
    w_uq_d = fw.dram("w_uq", [2, 128, 3, 1536], F32, EI)
    w_ukv_d = fw.dram("w_ukv", [2, 128, 2, 1024], F32, EI)
    w_br_d = fw.dram("w_br", [2, 128, 12, 1024], F32, EI)
    w_out_d = fw.dram("w_out", [2, 128, 8, 1024], F32, EI)
    CL_d = fw.dram("CL", [TL, TL], BF16, EI)
    NSL_d = fw.dram("NSL", [TL, TL], BF16, EI)
    CLC_d = fw.dram("CLC", [TC, TC], BF16, EI)
    NSLC_d = fw.dram("NSLC", [TC, TC], BF16, EI)
    CSC_d = fw.dram("CSC", [128, 256], BF16, EI)
    ROPEC_d = fw.dram("ROPEC", [96, TT], F32, EI)
    ROPES_d = fw.dram("ROPES", [96, TT], F32, EI)
    MSK_d = fw.dram("MSK", [128, 8 * 128 + 1536], F32, EI)
    MSKB_d = fw.dram("MSKB", [128, 2 * 128 + 6 * 512], BF16, EI)

    out_d = fw.dram("out", [TL, 1024], F32, "ExternalOutput")
    xs_d = fw.dram("xs", [TT, 1024], F32, skind("xs"))
    pT_d = fw.dram("pT", [NB * 128, TT], BF16, skind("pT"))
    yaT_d = fw.dram("yaT", [512, TT], BF16, skind("yaT"))
    ycT_d = fw.dram("ycT", [512, TT], BF16, skind("ycT"))
    oT_d = [fw.dram("oT%d" % d, [512, TT], F32, skind("oT%d" % d)) for d in range(2)]
    dbg_d = {}

    msk = fw.sb("msk", [128, 8 * 128 + 1536], F32)
    mskb = fw.sb("mskb", [128, 2 * 128 + 6 * 512], BF16)
    fw.dma("sp", msk[:], MSK_d[:])
    fw.dma("sp", mskb[:], MSKB_d[:])

    def mcol(i):
        return msk[:, i * 128:(i + 1) * 128]
    identF, onesF, triF, restF, tot0, tot1, triB, restB = [mcol(i) for i in range(8)]
    negmask = [msk[:, 1024:1536], msk[:, 1536:2048]]
    ident4F = msk[:, 2048:2560]
    identB = mskb[:, 0:128]
    onesB = mskb[:, 128:256]
    strictB = [mskb[:, 256:768], mskb[:, 768:1280]]
    ident4B = mskb[:, 1280:1792]
    mD16 = mskb[:, 1792:2304]
    mC32 = mskb[:, 2304:2816]
    mC64 = mskb[:, 2816:3328]

    PF = [fw.ps("pf%d" % i, [128, 512], F32) for i in range(7)]
    PB = fw.ps("pb", [128, 1024], BF16)

    ccol = fw.sb("ccol_sb", [128, 8, 2], F32)
    sc = fw.sb("sc", [128, 8, 2], F32)
    modc = fw.sb("modc", [128, 24, 2], F32)
    Acol = fw.sb("Acol", [128, 8, 2], F32)
    ggcol = fw.sb("ggcol", [128, 8, 2], F32)
    ggbc = [fw.sb("ggbc%d" % j, [128, 1024], F32) for j in range(2)]
    bmod = fw.sb("bmod_sb", [128, 24], F32)
    gpre = fw.sb("gpre_sb", [128, 8], F32)
    gpost = fw.sb("gpost_sb", [128, 8], F32)
    abT = fw.sb("abT", [128, NT, 16], F32)
    stat = fw.sb("stat", [128, 4 * NT], F32, nsub=NT)
    fw.dma("sp", ccol[:], ccol_d[:])

    ring_ctr = {}

    def ring(lst, key):
        i = ring_ctr.get(key, 0)
        ring_ctr[key] = i + 1
        return lst[i % len(lst)]

    pfc = [0]

    def pf(lo=0, hi=7):
        i = pfc[0]
        pfc[0] += 1
        return PF[lo + i % (hi - lo)]

    S = {}

    def alloc_s1():
        fw.push()
        S["hT"] = fw.sb("hT", [128, 8, TT], BF16)
        S["w32"] = [fw.sb("w32_%d" % i, [128, 4096], F32) for i in range(2)]
        S["wbf"] = [fw.sb("wbf_%d" % i, [128, 1024], BF16) for i in range(3)]
        S["stg"] = [fw.sb("stg_%d" % i, [128, TT], BF16) for i in range(3)]
        S["xring"] = [fw.sb("xr%d" % i, [128, 1024], F32) for i in range(3)]
        S["hnring"] = [fw.sb("hn%d" % i, [128, 1024], BF16) for i in range(2)]

    def phase_mod(l):
        fw.dma("sp", bmod[:], bmod_d[l])
        fw.dma("sp", gpre[:], gpre_d[l])
        fw.dma("sp", gpost[:], gpost_d[l])
        fw.act(sc[:], ccol[:], AF.Silu)
        pm = PF[0]
        wv = w_mod_d.h[l].rearrange("(kc k) n -> k kc n", k=128)
        for nch in range(6):
            slot = ring(S["w32"], "w32")
            sv = V(slot.h[:, :].rearrange("p (kc n) -> p kc n", kc=8), slot.buf)
            fw.dma("sp", sv, V(wv[:, :, nch * 512:(nch + 1) * 512], w_mod_d.buf))
            for j in range(4):
                blk = nch * 4 + j
                for kc in range(8):
                    fw.mm(pm[:, blk * 2:blk * 2 + 2],
                          V(slot.h[:, kc * 512 + j * 128: kc * 512 + (j + 1) * 128], slot.buf),
                          sc[:, kc, :], start=(kc == 0), stop=(kc == 7))
        for j in range(2):
            fw.tt(modc[:, :, j], V(pm.h[:, 0:48].rearrange("p (b j) -> p b j", j=2)[:, :, j], pm.buf), bmod[:], ALU.add)
            fw.stt(Acol[:, :, j], modc[:, 8:16, j], 1.0, gpre[:], ALU.add, ALU.mult)
            fw.tt(ggcol[:, :, j], modc[:, 16:24, j], gpost[:], ALU.mult)
        for j in range(2):
            for half in range(2):
                pg = pf(1, 7)
                for q in range(4):
                    kc = half * 4 + q
                    D = ring(S["w32"], "w32")
                    fw.ts(D[:, 0:128], identF, ggcol[:, kc, j:j + 1])
                    fw.mm(pg[:, q * 128:(q + 1) * 128], onesF, D[:, 0:128])
                fw.copy(ggbc[j][:, half * 512:(half + 1) * 512], pg[:])

    def phase_h(l):
        src = xin if l == 0 else xs_d
        for tt in range(NT):
            j = 0 if tt < 32 else 1
            xt = ring(S["xring"], "xr")
            fw.dma("sp", xt[:], src[tt * 128:(tt + 1) * 128, :])
            st = stat.sub(tt)
            hn = ring(S["hnring"], "hn")
            fw.act(hn[:], xt[:], AF.Square, accum_out=st[:, 4 * tt:4 * tt + 1])
            fw.act(st[:, 4 * tt + 1:4 * tt + 2], st[:, 4 * tt:4 * tt + 1], AF.Sqrt, bias=EPS, scale=1.0 / 1024)
            fw.recip(st[:, 4 * tt + 2:4 * tt + 3], st[:, 4 * tt + 1:4 * tt + 2])
            fw.ts(hn[:], xt[:], st[:, 4 * tt + 2:4 * tt + 3])
            for kc in range(8):
                fw.tr(PB[:, kc * 128:(kc + 1) * 128], hn[:, kc * 128:(kc + 1) * 128], identB)
            for kc in range(8):
                o = S["hT"][:, kc, tt * 128:(tt + 1) * 128]
                i = PB[:, kc * 128:(kc + 1) * 128]
                if kc % 2 == 0:
                    fw.ts(o, i, Acol[:, kc, j:j + 1], modc[:, kc, j:j + 1], ALU.mult, ALU.add)
                else:
                    fw.act(o, i, AF.Identity, bias=modc[:, kc, j:j + 1], scale=Acol[:, kc, j:j + 1])

    SILU_BLK = list(range(4, 8)) + list(range(20, 24)) + list(range(29, 33))
    SIG_BLK = list(range(33, 57))
    COPY_BLK = [b for b in range(NB) if b not in SILU_BLK and b not in SIG_BLK]

    def phase_proj(l):
        wab32 = ring(S["w32"], "w32")
        fw.dma("sp", wab32[:, 0:128], V(w_ab_d.h[l].rearrange("p kc n -> p (kc n)"), w_ab_d.buf))
        wabb = ring(S["wbf"], "wbf")
        fw.copy(wabb[:, 0:128], wab32[:, 0:128], e="pool")
        pa = [PF[5], PF[6]]
        for tt in range(NT):
            dst = pa[0][:, tt * 16:(tt + 1) * 16] if tt < 32 else pa[1][:, (tt - 32) * 16:(tt - 31) * 16]
            for kc in range(8):
                fw.mm(dst, S["hT"][:, kc, tt * 128:(tt + 1) * 128], wabb[:, kc * 16:(kc + 1) * 16], start=(kc == 0), stop=(kc == 7))
        fw.copy(V(abT.h[:, 0:32, :].rearrange("p a b -> p (a b)"), abT.buf), pa[0][:, 0:512])
        fw.copy(V(abT.h[:, 32:34, :].rearrange("p a b -> p (a b)"), abT.buf), pa[1][:, 0:32])
        for blk in COPY_BLK + SILU_BLK + SIG_BLK:
            ws = ring(S["w32"], "w32")
            fw.dma("sp", ws[:, 0:1024], V(w_in_d.h[l, blk].rearrange("p kc n -> p (kc n)"), w_in_d.buf))
            wb = ring(S["wbf"], "wbf")
            fw.copy(wb[:, 0:1024], ws[:, 0:1024], e="pool")
            sg = ring(S["stg"], "stg")
            for ci, (t0, n) in enumerate(CHUNKS):
                ps = pf(0, 5)
                for kc in range(8):
                    fw.mm(ps[:, 0:n], wb[:, kc * 128:(kc + 1) * 128], S["hT"][:, kc, t0:t0 + n], start=(kc == 0), stop=(kc == 7))
                if blk in SILU_BLK:
                    fw.act(sg[:, t0:t0 + n], ps[:, 0:n], AF.Silu)
                elif blk in SIG_BLK:
                    fw.act(sg[:, t0:t0 + n], ps[:, 0:n], AF.Sigmoid)
                else:
                    fw.copy(sg[:, t0:t0 + n], ps[:, 0:n])
            fw.dma("pool", pT_d[blk * 128:(blk + 1) * 128, :], sg[:], semof=sg.buf)


    def v3(t, n):
        return t.h[:, :].rearrange("p (a n) -> p a n", n=n)

    def phase_fourier(l, with_ctx):
        fw.push()
        UT = fw.sb("UT", [128, 4, TT], BF16)
        ABs = fw.sb("ABs", [128, NT, 1024], BF16)
        csc = fw.sb("csc", [128, 256], BF16)
        fw32 = fw.sb("fw32", [128, 512], F32)
        fwb = fw.sb("fwb", [128, 512], BF16)
        tbC = [fw.sb("tbC%d" % i, [128, 8, 512], BF16) for i in range(2)]
        tbS = [fw.sb("tbS%d" % i, [128, 8, 512], BF16) for i in range(2)]
        specb = [fw.sb("specb%d" % i, [128, 512], BF16) for i in range(2)]
        szr = [fw.sb("szr%d" % i, [128, 4, 512], BF16) for i in range(2)]
        yast = [fw.sb("yast%d" % i, [128, 4, 512], BF16) for i in range(2)]
        fw.dma("sp", csc[:], CSC_d[:])
        fw.dma("sp", fw32[:], V(f_w_d.h[l].rearrange("p g d -> p (g d)"), f_w_d.buf))
        fw.copy(fwb[:], fw32[:], e="pool")
        for g in range(4):
            fw.dma("sp", UT[:, g, :], pT_d[g * 128:(g + 1) * 128, :])
        for tt in range(NT if with_ctx else 32):
            for half in range(2):
                ps = pf(4, 7)
                for gg in range(2):
                    g = half * 2 + gg
                    fw.mm(ps[:, gg * 256:(gg + 1) * 256], UT[:, g, tt * 128:(tt + 1) * 128], csc[:])
                fw.copy(ABs[:, tt, half * 512:(half + 1) * 512], ps[:], e=("dve" if half == 0 else "act"))
        osc = 1.0 / math.sqrt(128.0)
        jobs = [(ci, t0, n, 0, 32, CL_d, NSL_d, TL) for ci, (t0, n) in enumerate(CHUNKS[:8])]
        if with_ctx:
            jobs.append((8, TL, TC, 32, 2, CLC_d, NSLC_d, TC))
        for (ci, t0, n, tt0, ntile, Cd, Sd, Lseq) in jobs:
            sz = ring(szr, "szr")
            fw.dma("sp", sz[:, :, 0:n], V(pT_d.h[512:1024, :].rearrange("(g p) t -> p g t", p=128)[:, :, t0:t0 + n], pT_d.buf))
            c0 = t0 - tt0 * 128
            nq = (ntile + 7) // 8
            for qd in range(nq):
                na = min(8, ntile - qd * 8)
                tc_ = ring(tbC, "tbC")
                tsn = ring(tbS, "tbS")
                fw.dma("sp", tc_[:, 0:na, 0:n], V(Cd.h[qd * 1024: qd * 1024 + na * 128, :].rearrange("(a p) n -> p a n", p=128)[:, :, c0:c0 + n], Cd.buf))
                fw.dma("sp", tsn[:, 0:na, 0:n], V(Sd.h[qd * 1024: qd * 1024 + na * 128, :].rearrange("(a p) n -> p a n", p=128)[:, :, c0:c0 + n], Sd.buf))
                for g in range(4):
                    for a in range(na):
                        tt = tt0 + qd * 8 + a
                        first = (qd == 0 and a == 0)
                        last = (qd == nq - 1 and a == na - 1)
                        fw.mm(PF[g][:, 0:n], ABs[:, tt, g * 256: g * 256 + 128], tc_[:, a, 0:n], start=first, stop=False)
                        fw.mm(PF[g][:, 0:n], ABs[:, tt, g * 256 + 128: g * 256 + 256], tsn[:, a, 0:n], start=False, stop=last)
            ya = ring(yast, "yast")
            scl = osc / math.sqrt(float(Lseq))
            for g in range(4):
                sb_ = ring(specb, "specb")
                fw.act(sb_[:, 0:n], PF[g][:, 0:n], AF.Copy, scale=scl)
                po = pf(4, 7)
                fw.mm(po[:, 0:n], fwb[:, g * 128:(g + 1) * 128], sb_[:, 0:n])
                fw.tt(ya[:, g, 0:n], po[:, 0:n], sz[:, g, 0:n], ALU.mult)
            fw.dma("pool", V(yaT_d.h.rearrange("(g p) t -> p g t", p=128)[:, :, t0:t0 + n], yaT_d.buf), ya[:, :, 0:n], semof=ya.buf)
        fw.pop()

    def phase_gdn(l):
        fw.push()
        qT = fw.sb("qT", [128, 4, TT], BF16)
        kT = fw.sb("kT", [128, 4, TT], BF16)
        vT = fw.sb("vT", [128, 4, TT], BF16)
        convw = fw.sb("convw", [128, 12, 3], F32)
        fw.dma("sp", convw[:], convw_d[l])
        fw.push()
        cin = [fw.sb("cin%d" % i, [128, TT], BF16) for i in range(2)]
        cy = [fw.sb("cy%d" % i, [128, TT], F32) for i in range(2)]
        sqr = [fw.sb("sqr%d" % i, [128, 512], BF16) for i in range(2)]
        rr = [fw.sb("rr%d" % i, [128, 512], F32) for i in range(2)]
        for blk in range(12):
            xi = ring(cin, "cin")
            y = ring(cy, "cy")
            fw.dma("sp", xi[:], pT_d[1024 + blk * 128: 1024 + (blk + 1) * 128, :])
            for (a, b) in ((0, TL), (TL, TT)):
                fw.ts(y[:, a:b], xi[:, a:b], convw[:, blk, 1:2])
                fw.stt(y[:, a + 1:b], xi[:, a:b - 1], convw[:, blk, 0:1], y[:, a + 1:b], ALU.mult, ALU.add)
                fw.stt(y[:, a:b - 1], xi[:, a + 1:b], convw[:, blk, 2:3], y[:, a:b - 1], ALU.mult, ALU.add)
            h = blk % 4
            if blk >= 8:
                fw.act(vT[:, h, :], y[:], AF.Silu)
                continue
            dst = qT if blk < 4 else kT
            fw.act(y[:], y[:], AF.Silu)
            for (t0, n) in CHUNKS:
                sq = ring(sqr, "sqr")
                r = ring(rr, "rr")
                fw.tt(sq[:, 0:n], y[:, t0:t0 + n], y[:, t0:t0 + n], ALU.mult, e="pool")
                ps = pf(0, 7)
                fw.mm(ps[:, 0:n], onesB, sq[:, 0:n])
                fw.act(r[:, 0:n], ps[:, 0:n], AF.Ln, bias=EPS)
                fw.act(r[:, 0:n], r[:, 0:n], AF.Exp, scale=-0.5)
                fw.tt(dst[:, h, t0:t0 + n], y[:, t0:t0 + n], r[:, 0:n], ALU.mult)
        fw.pop()
        if "qkv" in dbg and l == 0:
            for nm, t_ in (("qT", qT), ("kT", kT), ("vT", vT)):
                dbg_d[nm] = fw.dram(nm + "_o", [128, 4, TT], BF16, "ExternalOutput")
                fw.dma("pool", dbg_d[nm][:], t_[:], semof=t_.buf)

        if "gdn_d1only" in dbg:
            fw.pop()
            return
        NC = NT * 4
        bb = fw.sb("bb", [128, NT, 8], F32)
        gd = [fw.sb("gd%d" % d, [128, NC], F32) for d in range(2)]
        names = ("Gs", "nG", "EG", "ER", "GL0", "GL1")
        GA = [{nm: fw.sb("%s%d" % (nm, d), [128, NC], F32) for nm in names} for d in range(2)]
        fw.push()
        alog = fw.sb("alog", [128, NT, 8], F32)
        dtb = fw.sb("dtb", [128, NT, 8], F32)
        fw.dma("sp", alog[:], alog_d[l])
        fw.dma("sp", dtb[:], dtb_d[l])
        gz = fw.sb("gz", [128, NT, 8], F32)
        fw.tt(gz[:], abT[:, :, 0:8], dtb[:], ALU.add)
        fw.act(gz[:], gz[:], AF.Exp)
        fw.act(gz[:], gz[:], AF.Ln, bias=1.0)
        fw.act(alog[:], alog[:], AF.Exp)
        fw.stt(gz[:], gz[:], -1.0, alog[:], ALU.mult, ALU.mult)
        fw.act(bb[:], abT[:, :, 8:16], AF.Exp, scale=-1.0)
        fw.ts(bb[:], bb[:], 1.0, None, ALU.add)
        fw.recip(bb[:], bb[:])
        for d in range(2):
            fw.copy(V(v3(gd[d], 4), gd[d].buf), gz[:, :, d * 4:(d + 1) * 4])
        fw.pop()
        for d in range(2):
            tri = triF if d == 0 else triB
            rest = restF if d == 0 else restB
            p1 = pf(0, 7)
            fw.mm(p1[:, 0:NC], tri, gd[d][:])
            if "ab2b" in dbg:
                fw.pop()
                return
            fw.copy(GA[d]["Gs"][:], p1[:, 0:NC])
            if "ab2c" in dbg:
                fw.pop()
                return
            fw.ts(GA[d]["nG"][:], p1[:, 0:NC], -1.0)
            fw.act(GA[d]["EG"][:], p1[:, 0:NC], AF.Exp)
            p2 = pf(0, 7)
            fw.mm(p2[:, 0:NC], rest, gd[d][:])
            fw.act(GA[d]["ER"][:], p2[:, 0:NC], AF.Exp)
            p3 = pf(0, 7)
            fw.mm(p3[:, 0:NC], tot0, gd[d][:])
            fw.act(GA[d]["GL0"][:], p3[:, 0:NC], AF.Exp)
            p4 = pf(0, 7)
            fw.mm(p4[:, 0:NC], tot1, gd[d][:])
            fw.act(GA[d]["GL1"][:], p4[:, 0:NC], AF.Exp)

        if "ab2" in dbg:
            fw.pop()
            return

        def bc4(t_, tt, off=0, rows=slice(0, 128), stride4=True):
            ap = t_.h[rows, tt * 4 + off: tt * 4 + off + 4].unsqueeze(2)
            nrows = rows.stop - rows.start
            return V(ap.broadcast_to([nrows, 4, 128]), t_.buf)

        def bcb(tt, d, rows=slice(0, 128)):
            ap = bb.h[rows, tt, d * 4:(d + 1) * 4].unsqueeze(2)
            nrows = rows.stop - rows.start
            return V(ap.broadcast_to([nrows, 4, 128]), bb.buf)

        NS = 2
        slots = []
        for i in range(NS):
            slots.append({
                "w0T": fw.sb("w0T%d" % i, [128, 512], BF16), "qkdT": fw.sb("qkdT%d" % i, [128, 512], BF16),
                "qgT": fw.sb("qgT%d" % i, [128, 512], BF16), "kd": fw.sb("kd%d" % i, [128, 512], BF16),
                "ub": fw.sb("ub%d" % i, [128, 512], F32)})
        tmps = []
        for i in range(1):
            tm_ = {
                "EGr": fw.sb("EGr%d" % i, [128, 512], F32),
                "tD": fw.sb("tD%d" % i, [128, 512], F32), "dec": fw.sb("dec%d" % i, [128, 512], F32),
                "kEG": fw.sb("kEG%d" % i, [128, 512], BF16), "vtok": fw.sb("vtok%d" % i, [128, 512], BF16),
                "TTb": fw.sb("TTb%d" % i, [128, 512], BF16)}
            for nm in ("M0", "MT0", "Qa", "QTa", "Qb", "QTb", "Pa", "PTa", "Pb", "PTb", "C32", "C32T", "C64", "C64T"):
                tm_[nm] = fw.sb("%s_%d" % (nm, i), [128, 512], F32)
            tm_["Rp"] = tm_["dec"]
            tm_["Mf"] = tm_["tD"]
            tmps.append(tm_)
        S32 = [fw.sb("S32_%d" % d, [128, 512], F32) for d in range(2)]
        Sb = [fw.sb("Sb_%d" % d, [128, 512], BF16) for d in range(2)]
        un = [fw.sb("un_%d" % d, [128, 512], BF16) for d in range(2)]
        t5_ = fw.sb("t5", [128, 512], F32)
        t5 = [t5_, t5_]
        ost = [fw.sb("ost%d" % i, [128, 512], F32) for i in range(2)]
        for d in range(2):
            fw.op("pool", lambda g, d=d: g.memset(S32[d].h[:, :], 0.0), [], [S32[d][:]])
            fw.op("pool", lambda g, d=d: g.memset(Sb[d].h[:, :], 0.0), [], [Sb[d][:]])
        qscale = 128.0 ** -0.5

        def pg():
            i = pfc[0]
            pfc[0] += 1
            return PF[(0, 1, 2, 6)[i % 4]]

        def prep(tt, d, sl, tm):
            tok = slice(tt * 128, (tt + 1) * 128)
            G = GA[d]
            fw.tt(V(v3(tm["Rp"], 128), tm["Rp"].buf), V(ident4F.ap.rearrange("p (a n) -> p a n", n=128), ident4F.buf),
                  bc4(G["Gs"], tt), ALU.mult, e="pool")
            p1 = pg()
            fw.mm(p1[:], onesF, tm["Rp"][:])
            fw.act(tm["EGr"][:], p1[:], AF.Exp, bias=math.log(qscale))
            fw.tt(tm["tD"][:], p1[:], negmask[d], ALU.add)
            fw.tt(V(v3(tm["tD"], 128), tm["tD"].buf), V(v3(tm["tD"], 128), tm["tD"].buf), bc4(G["nG"], tt), ALU.add)
            fw.act(tm["dec"][:], tm["tD"][:], AF.Exp)
            pk = pg()
            for h in range(4):
                fw.mm(pk[:, h * 128:(h + 1) * 128], kT[:, h, tok], kT[:, h, tok])
            fw.tt(tm["Mf"][:], pk[:], tm["dec"][:], ALU.mult)
            fw.tt(V(v3(tm["Mf"], 128), tm["Mf"].buf), V(v3(tm["Mf"], 128), tm["Mf"].buf), bcb(tt, d), ALU.mult)
            M0, MT0 = tm["M0"], tm["MT0"]
            fw.tt(M0[:], tm["Mf"][:], strictB[d], ALU.mult, e="pool")
            ptr = pg()
            for h in range(4):
                fw.tr(ptr[:, h * 128:(h + 1) * 128], M0[:, h * 128:(h + 1) * 128], identF)
            fw.copy(MT0[:], ptr[:], e="act")
            Q, QT, P, PT = tm["Qa"], tm["QTa"], tm["Pa"], tm["PTa"]
            Qn, QTn, Pn, PTn = tm["Qb"], tm["QTb"], tm["Pb"], tm["PTb"]
            fw.tt(Q[:], M0[:], mD16, ALU.mult, e="pool")
            fw.tt(QT[:], MT0[:], mD16, ALU.mult, e="pool")
            fw.tt(tm["C32"][:], M0[:], mC32, ALU.mult, e="pool")
            fw.tt(tm["C32T"][:], MT0[:], mC32, ALU.mult, e="pool")
            fw.tt(tm["C64"][:], M0[:], mC64, ALU.mult, e="pool")
            fw.tt(tm["C64T"][:], MT0[:], mC64, ALU.mult, e="pool")
            fw.tt(P[:], ident4F, Q[:], ALU.subtract)
            fw.tt(PT[:], ident4F, QT[:], ALU.subtract)

            def mm4(lhsT, rhs):
                p_ = pg()
                for h in range(4):
                    hs = slice(h * 128, (h + 1) * 128)
                    fw.mm(p_[:, hs], lhsT[:, hs], rhs[:, hs])
                return p_

            for lev in range(3):
                pq = mm4(QT, Q)
                fw.copy(Qn[:], pq[:], e="dve")
                pqt = mm4(Q, QT)
                fw.copy(QTn[:], pqt[:], e="act")
                pp = mm4(QTn, P)
                fw.tt(Pn[:], pp[:], P[:], ALU.add)
                ppt = mm4(Qn, PT)
                fw.tt(PTn[:], ppt[:], PT[:], ALU.add)
                Q, Qn = Qn, Q
                QT, QTn = QTn, QT
                P, Pn = Pn, P
                PT, PTn = PTn, PT
            X, XT = P, PT
            py = mm4(tm["C32T"], X)
            fw.copy(Qn[:], py[:], e="act")
            pyp = mm4(tm["C32"], XT)
            fw.copy(QTn[:], pyp[:], e="dve")
            px = mm4(XT, Qn)
            fw.tt(Pn[:], X[:], px[:], ALU.subtract)
            pxt = mm4(X, QTn)
            fw.tt(PTn[:], XT[:], pxt[:], ALU.subtract)
            X, XT = Pn, PTn
            py = mm4(tm["C64T"], X)
            fw.copy(Q[:], py[:], e="act")
            px = mm4(XT, Q)
            fw.tt(tm["TTb"][:], X[:], px[:], ALU.subtract)
            TTm = tm["TTb"]
            for h in range(4):
                fw.tr(PB[:, h * 128:(h + 1) * 128], kT[:, h, tok], identB)
            fw.tt(V(v3(tm["kEG"], 128), tm["kEG"].buf), V(PB.h[:, 0:512].rearrange("p (a n) -> p a n", n=128), PB.buf), bc4(G["EG"], tt), ALU.mult)
            fw.tt(V(v3(sl["kd"], 128), sl["kd"].buf), V(PB.h[:, 0:512].rearrange("p (a n) -> p a n", n=128), PB.buf), bc4(G["ER"], tt), ALU.mult)
            for h in range(4):
                fw.tr(PB[:, 512 + h * 128: 512 + (h + 1) * 128], vT[:, h, tok], identB)
            fw.copy(tm["vtok"][:], PB[:, 512:1024], e="act")
            po = pg()
            for h in range(4):
                hs = slice(h * 128, (h + 1) * 128)
                fw.mm(po[:, hs], tm["kEG"][:, hs], TTm[:, hs])
            fw.copy(sl["w0T"][:], po[:], e="act")
            po2 = pg()
            for h in range(4):
                hs = slice(h * 128, (h + 1) * 128)
                fw.mm(po2[:, hs], TTm[:, hs], tm["vtok"][:, hs])
            fw.tt(V(v3(sl["ub"], 128), sl["ub"].buf), V(po2.h[:, :].rearrange("p (a n) -> p a n", n=128), po2.buf), bcb(tt, d), ALU.mult)
            po3 = pg()
            for h in range(4):
                hs = slice(h * 128, (h + 1) * 128)
                fw.mm(po3[:, hs], kT[:, h, tok], qT[:, h, tok])
            fw.stt(sl["qkdT"][:], po3[:], qscale, tm["dec"][:], ALU.mult, ALU.mult)
            fw.tt(V(v3(sl["qgT"], 128), sl["qgT"].buf), qT[:, :, tok], V(v3(tm["EGr"], 128), tm["EGr"].buf), ALU.mult, e="pool")

        def scan(tt, d, sl):
            G = GA[d]
            for ci in ((0, 1) if d == 0 else (1, 0)):
                cs = slice(64 * ci, 64 * ci + 64)
                for h in range(4):
                    hs = slice(h * 128, (h + 1) * 128)
                    fw.mm(PF[3][cs, hs], sl["w0T"][:, h * 128 + 64 * ci: h * 128 + 64 * ci + 64], Sb[d][:, hs])
                t5v = V(t5[d].h[cs, :].rearrange("p (a n) -> p a n", n=128), t5[d].buf)
                fw.tt(t5v, V(PF[3].h[cs, :].rearrange("p (a n) -> p a n", n=128), PF[3].buf), bcb(tt, d, cs), ALU.mult)
                fw.tt(un[d][cs, :], sl["ub"][cs, :], t5[d][cs, :], ALU.subtract)
                for h in range(4):
                    oc = slice(h * 128 + 64 * ci, h * 128 + 64 * ci + 64)
                    hs = slice(h * 128, (h + 1) * 128)
                    fw.mm(PF[4][:, oc], Sb[d][:, hs], sl["qgT"][:, oc], start=True, stop=False)
                    fw.mm(PF[4][:, oc], un[d][cs, hs], sl["qkdT"][cs, oc], start=False, stop=True)
                for h in range(4):
                    hs = slice(h * 128, (h + 1) * 128)
                    fw.mm(PF[5][:, hs], sl["kd"][cs, hs], un[d][cs, hs])
                gl = G["GL0"] if ci == 0 else G["GL1"]
                fw.tt(V(v3(S32[d], 128), S32[d].buf), V(v3(S32[d], 128), S32[d].buf), bc4(gl, tt), ALU.mult)
                fw.tt(S32[d][:], S32[d][:], PF[5][:], ALU.add)
                fw.copy(Sb[d][:], S32[d][:], e="act")
            o = ring(ost, "ost")
            fw.copy(o[:], PF[4][:], e="act")
            fw.dma("pool", V(oT_d[d].h.rearrange("(h p) t -> p h t", p=128)[:, :, tt * 128:(tt + 1) * 128], oT_d[d].buf),
                   V(v3(o, 128), o.buf), semof=o.buf)

        order_f = [32, 33] + list(range(32))
        order_b = [33, 32] + list(range(31, -1, -1))
        k = 0
        for s_ in range(NT):
            for d, tt in ((0, order_f[s_]), (1, order_b[s_])):
                sl = slots[k % NS]
                tm = tmps[0]
                k += 1
                if "gdn_abonly" in dbg:
                    continue
                prep(tt, d, sl, tm)
                if "gdn_noscan" not in dbg:
                    scan(tt, d, sl)
        fw.pop()


    def phase_mla(l, with_ctx):
        fw.push()
        cqn = fw.sb("cqn", [128, 3, TT], BF16)
        ckvn = fw.sb("ckvn", [128, 2, TT], BF16)
        ropeC = fw.sb("ropeC", [96, TT], BF16)
        ropeS = fw.sb("ropeS", [96, TT], BF16)
        kper = fw.sb("kper", [96, TT], BF16)
        wuq = fw.sb("wuq", [128, 3, 1536], BF16)
        wukv = fw.sb("wukv", [128, 2, 1024], BF16)
        qn = fw.sb("qn", [128, 3], F32)
        kvn = fw.sb("kvn", [128, 2], F32)
        fw.dma("sp", qn[:], qnorm_d[l])
        fw.dma("sp", kvn[:], kvnorm_d[l])
        fw.push()
        st32 = fw.sb("st32", [128, 4608], F32)
        fw.dma("sp", st32[:, 0:4608], V(w_uq_d.h[l].rearrange("p a n -> p (a n)"), w_uq_d.buf))
        fw.copy(V(wuq.h[:, :, :].rearrange("p a n -> p (a n)"), wuq.buf), st32[:, 0:4608], e="pool")
        fw.dma("sp", st32[:, 0:2048], V(w_ukv_d.h[l].rearrange("p a n -> p (a n)"), w_ukv_d.buf))
        fw.copy(V(wukv.h[:, :, :].rearrange("p a n -> p (a n)"), wukv.buf), st32[:, 0:2048], e="pool")
        fw.dma("sp", st32[0:96, 0:TT], ROPEC_d[:])
        fw.copy(ropeC[:], st32[0:96, 0:TT], e="pool")
        fw.dma("sp", st32[0:96, 0:TT], ROPES_d[:])
        fw.copy(ropeS[:], st32[0:96, 0:TT], e="pool")
        kpa = fw.sb("kpa", [96, TT], BF16)
        kpb = fw.sb("kpb", [96, TT], BF16)
        fw.dma("sp", kpa[64:96, :], pT_d[7296:7328, :])
        fw.dma("sp", kpb[64:96, :], pT_d[7328:7360, :])
        fw.tt(st32[64:96, 0:TT], kpa[64:96, :], ropeC[64:96, :], ALU.mult)
        fw.tt(kpb[64:96, :], kpb[64:96, :], ropeS[64:96, :], ALU.mult)
        fw.tt(kper[64:96, :], st32[64:96, 0:TT], kpb[64:96, :], ALU.add)
        sqr = [fw.sb("msq%d" % i, [128, 512], BF16) for i in range(3)]
        rr = [fw.sb("mrr%d" % i, [128, 512], F32) for i in range(2)]
        for (dst, nb_, row0, nrm, width) in ((cqn, 3, 3072, qn, 384.0), (ckvn, 2, 3456, kvn, 256.0)):
            for b_ in range(nb_):
                fw.dma("sp", dst[:, b_, :], pT_d[row0 + b_ * 128: row0 + (b_ + 1) * 128, :])
            for (t0, n) in CHUNKS:
                ps = pf(0, 5)
                sqs = []
                for b_ in range(nb_):
                    sq = ring(sqr, "msq")
                    fw.tt(sq[:, 0:n], dst[:, b_, t0:t0 + n], dst[:, b_, t0:t0 + n], ALU.mult, e="pool")
                    sqs.append(sq)
                for b_ in range(nb_):
                    fw.mm(ps[:, 0:n], onesB, sqs[b_][:, 0:n], start=(b_ == 0), stop=(b_ == nb_ - 1))
                r = ring(rr, "mrr")
                fw.act(r[:, 0:n], ps[:, 0:n], AF.Ln, bias=EPS, scale=1.0 / width)
                fw.act(r[:, 0:n], r[:, 0:n], AF.Exp, scale=-0.5)
                for b_ in range(nb_):
                    fw.stt(dst[:, b_, t0:t0 + n], dst[:, b_, t0:t0 + n], nrm[:, b_:b_ + 1], r[:, 0:n], ALU.mult, ALU.mult)
        fw.pop()

        kTh = [fw.sb("kTh%d" % i, [96, TT], BF16) for i in range(2)]
        qTh = [fw.sb("qTh%d" % i, [96, TT], BF16) for i in range(2)]
        Vaug = [fw.sb("Vaug%d" % i, [128, NT, 128], BF16) for i in range(2)]
        for i in range(2):
            fw.op("pool", lambda g, i=i: g.memset(Vaug[i].h[:, :, 64:128], 1.0), [], [Vaug[i][:, :, 64:128]])
        PTr = [fw.sb("PTr%d" % i, [128, 512], BF16) for i in range(4)]
        q1 = [fw.sb("q1_%d" % i, [96, 512], F32) for i in range(2)]
        q2 = [fw.sb("q2_%d" % i, [96, 512], F32) for i in range(2)]
        rden = [fw.sb("rden%d" % i, [128, 512], F32) for i in range(2)]
        otmp = [fw.sb("otmp%d" % i, [64, 512], F32) for i in range(2)]
        zc = [fw.sb("zc%d" % i, [64, 512], BF16) for i in range(2)]
        ycs = [fw.sb("ycs%d" % i, [64, 512], BF16) for i in range(2)]
        ascale = 96.0 ** -0.5
        qchunks = CHUNKS if with_ctx else CHUNKS[:8]
        for h in range(8):
            kt_ = kTh[h % 2]
            qt_ = qTh[h % 2]
            va = Vaug[h % 2]
            for (t0, n) in CHUNKS:
                ps = pf(0, 5)
                for kc in range(2):
                    fw.mm(ps[0:64, 0:n], wukv[:, kc, h * 128: h * 128 + 64], ckvn[:, kc, t0:t0 + n], start=(kc == 0), stop=(kc == 1))
                fw.copy(kt_[0:64, t0:t0 + n], ps[0:64, 0:n], e="dve")
            fw.copy(kt_[64:96, :], kper[64:96, :], e="pool")
            for t8 in range(0, NT, 8):
                nt8 = min(8, NT - t8)
                ps = pf(0, 5)
                for a in range(nt8):
                    tt = t8 + a
                    for kc in range(2):
                        fw.mm(ps[:, a * 64:(a + 1) * 64], ckvn[:, kc, tt * 128:(tt + 1) * 128], wukv[:, kc, h * 128 + 64: h * 128 + 128],
                              start=(kc == 0), stop=(kc == 1))
                fw.copy(va[:, t8:t8 + nt8, 0:64], V(ps.h[:, 0:nt8 * 64].rearrange("p (a n) -> p a n", n=64), ps.buf), e="dve")
            for (t0, n) in qchunks:
                pa = pf(0, 5)
                for kc in range(3):
                    fw.mm(pa[0:96, 0:n], wuq[:, kc, h * 96:(h + 1) * 96], cqn[:, kc, t0:t0 + n], start=(kc == 0), stop=(kc == 2))
                pb_ = pf(0, 5)
                for kc in range(3):
                    fw.mm(pb_[0:96, 0:n], wuq[:, kc, 768 + h * 96: 768 + (h + 1) * 96], cqn[:, kc, t0:t0 + n], start=(kc == 0), stop=(kc == 2))
                a1 = ring(q1, "q1")
                a2 = ring(q2, "q2")
                fw.tt(a1[:, 0:n], pa[0:96, 0:n], ropeC[:, t0:t0 + n], ALU.mult)
                fw.tt(a2[:, 0:n], pb_[0:96, 0:n], ropeS[:, t0:t0 + n], ALU.mult)
                fw.tt(qt_[:, t0:t0 + n], a1[:, 0:n], a2[:, 0:n], ALU.add, e="pool")
            for qi, (t0, n) in enumerate(qchunks):
                ktiles = list(range(NT)) if t0 < TL else [32, 33]
                acc = PF[5 + qi % 2]
                zt = ring(zc, "zc")
                fw.dma("sp", zt[:, 0:n], pT_d[3712 + h * 64: 3712 + (h + 1) * 64, t0:t0 + n])
                LA = 3
                pss = {}

                def issue_s(ki_):
                    ps_ = pf(0, 5)
                    kt__ = ktiles[ki_]
                    fw.mm(ps_[:, 0:n], kt_[:, kt__ * 128:(kt__ + 1) * 128], qt_[:, t0:t0 + n])
                    pss[ki_] = ps_

                for ki in range(min(LA, len(ktiles))):
                    issue_s(ki)
                for ki, kt in enumerate(ktiles):
                    ps = pss.pop(ki)
                    pt = ring(PTr, "PTr")
                    fw.act(pt[:, 0:n], ps[:, 0:n], AF.Exp, scale=ascale)
                    fw.mm(acc[:, 0:n], va[:, kt, :], pt[:, 0:n], start=(ki == 0), stop=(ki == len(ktiles) - 1))
                    if ki + LA < len(ktiles):
                        issue_s(ki + LA)
                rd = ring(rden, "rden")
                fw.recip(rd[64:128, 0:n], acc[64:128, 0:n])
                ot = ring(otmp, "otmp")
                fw.tt(ot[:, 0:n], acc[0:64, 0:n], rd[64:128, 0:n], ALU.mult)
                yc = ring(ycs, "ycs")
                fw.tt(yc[:, 0:n], ot[:, 0:n], zt[:, 0:n], ALU.mult, e="pool")
                fw.dma("pool", ycT_d[h * 64:(h + 1) * 64, t0:t0 + n], yc[:, 0:n], semof=yc.buf)
        fw.pop()

    def phase_final(l, with_ctx, last):
        fw.push()
        wbr = fw.sb("wbr", [128, 12, 1024], BF16)
        wout = fw.sb("wout", [128, 8, 1024], BF16)
        dnrm = fw.sb("dnrm", [128, 1], F32)
        fw.dma("sp", dnrm[:], dnorm_d[l])
        fw.push()
        st32 = [fw.sb("fst32_%d" % i, [128, 4096], F32) for i in range(2)]
        for q in range(3):
            st = ring(st32, "fst32")
            fw.dma("sp", st[:], V(w_br_d.h[l, :, q * 4:(q + 1) * 4, :].rearrange("p a n -> p (a n)"), w_br_d.buf))
            fw.copy(V(wbr.h[:, q * 4:(q + 1) * 4, :].rearrange("p a n -> p (a n)"), wbr.buf), st[:], e="pool")
        for q in range(2):
            st = ring(st32, "fst32")
            fw.dma("sp", st[:], V(w_out_d.h[l, :, q * 4:(q + 1) * 4, :].rearrange("p a n -> p (a n)"), w_out_d.buf))
            fw.copy(V(wout.h[:, q * 4:(q + 1) * 4, :].rearrange("p a n -> p (a n)"), wout.buf), st[:], e="pool")
        fw.pop()
        of_ = fw.sb("of", [128, 4, 512], F32)
        ob_ = fw.sb("ob", [128, 4, 512], F32)
        sqb = fw.sb("sqb", [128, 4, 512], BF16)
        rr = [fw.sb("frr%d" % i, [128, 512], F32) for i in range(2)]
        szb = fw.sb("szb", [128, 4, 512], BF16)
        ybT = fw.sb("ybT", [128, 4, 512], BF16)
        yaT = fw.sb("yaTt", [128, 4, 512], BF16)
        ycT = fw.sb("ycTt", [128, 4, 512], BF16)
        gts = [fw.sb("gts%d" % i, [128, 8, 512], BF16) for i in range(2)]
        merged = fw.sb("merged", [128, 8, 512], F32)
        mergedb = fw.sb("mergedb", [128, 8, 512], BF16)
        tmpf = [fw.sb("tmpf%d" % i, [128, 512], F32) for i in range(2)]
        xr = [fw.sb("fxr%d" % i, [128, 1024], F32) for i in range(2)]
        xo = [fw.sb("fxo%d" % i, [128, 1024], F32) for i in range(2)]
        junk = fw.sb("junk", [128, 512], BF16)
        st4 = fw.sb("st4", [128, 8 * NT], F32, nsub=NT)
        xsrc = xin if l == 0 else xs_d
        chunks = CHUNKS if with_ctx else CHUNKS[:8]
        for (t0, n) in chunks:
            j = 0 if t0 < TL else 1
            for d, dstt in ((0, of_), (1, ob_)):
                fw.dma("sp", dstt[:, :, 0:n], V(oT_d[d].h.rearrange("(h p) t -> p h t", p=128)[:, :, t0:t0 + n], oT_d[d].buf))
            fw.dma("sp", szb[:, :, 0:n], V(pT_d.h[2560:3072, :].rearrange("(g p) t -> p g t", p=128)[:, :, t0:t0 + n], pT_d.buf))
            fw.dma("sp", yaT[:, :, 0:n], V(yaT_d.h.rearrange("(g p) t -> p g t", p=128)[:, :, t0:t0 + n], yaT_d.buf))
            fw.dma("sp", ycT[:, :, 0:n], V(ycT_d.h.rearrange("(g p) t -> p g t", p=128)[:, :, t0:t0 + n], ycT_d.buf))
            fw.tt(of_[:, :, 0:n], of_[:, :, 0:n], ob_[:, :, 0:n], ALU.add, e="pool")
            fw.act(sqb[:, :, 0:n], of_[:, :, 0:n], AF.Square)
            for h in range(4):
                ps = pf()
                fw.mm(ps[:, 0:n], onesB, sqb[:, h, 0:n])
                r = ring(rr, "frr")
                fw.act(r[:, 0:n], ps[:, 0:n], AF.Ln, bias=EPS, scale=1.0 / 128)
                fw.act(r[:, 0:n], r[:, 0:n], AF.Exp, scale=-0.5)
                fw.stt(r[:, 0:n], of_[:, h, 0:n], dnrm[:, 0:1], r[:, 0:n], ALU.mult, ALU.mult)
                fw.tt(ybT[:, h, 0:n], r[:, 0:n], szb[:, h, 0:n], ALU.mult, e="pool")
            for br, src in enumerate((yaT, ybT, ycT)):
                gt = ring(gts, "gts")
                fw.dma("sp", gt[:, :, 0:n], V(pT_d.h[4224 + br * 1024: 4224 + (br + 1) * 1024, :].rearrange("(g p) t -> p g t", p=128)[:, :, t0:t0 + n], pT_d.buf))
                for jb in range(8):
                    ps = pf()
                    for kc in range(4):
                        fw.mm(ps[:, 0:n], wbr[:, br * 4 + kc, jb * 128:(jb + 1) * 128], src[:, kc, 0:n], start=(kc == 0), stop=(kc == 3))
                    if br == 0:
                        fw.tt(merged[:, jb, 0:n], ps[:, 0:n], gt[:, jb, 0:n], ALU.mult)
                    else:
                        tf = ring(tmpf, "tmpf")
                        fw.tt(tf[:, 0:n], ps[:, 0:n], gt[:, jb, 0:n], ALU.mult)
                        fw.tt(merged[:, jb, 0:n], merged[:, jb, 0:n], tf[:, 0:n], ALU.add, e="pool")
            fw.copy(mergedb[:, :, 0:n], merged[:, :, 0:n], e="act")
            for ts_ in range(n // 128):
                tt = (t0 // 128) + ts_
                xt = ring(xr, "fxr")
                fw.dma("sp", xt[:], xsrc[tt * 128:(tt + 1) * 128, :])
                st = st4.sub(tt)
                c0 = 8 * tt
                phs = []
                for half in range(2):
                    ps = pf()
                    for kc in range(8):
                        fw.mm(ps[:], mergedb[:, kc, ts_ * 128:(ts_ + 1) * 128], wout[:, kc, half * 512:(half + 1) * 512], start=(kc == 0), stop=(kc == 7))
                    fw.act(junk[:], ps[:], AF.Square, accum_out=st[:, c0 + half: c0 + half + 1])
                    phs.append(ps)
                fw.tt(st[:, c0 + 2:c0 + 3], st[:, c0:c0 + 1], st[:, c0 + 1:c0 + 2], ALU.add)
                fw.act(st[:, c0 + 3:c0 + 4], st[:, c0 + 2:c0 + 3], AF.Sqrt, bias=EPS, scale=1.0 / 1024)
                fw.recip(st[:, c0 + 4:c0 + 5], st[:, c0 + 3:c0 + 4])
                o = ring(xo, "fxo")
                for half in range(2):
                    hs = slice(half * 512, (half + 1) * 512)
                    fw.stt(o[:, hs], phs[half][:], st[:, c0 + 4:c0 + 5], ggbc[j][:, hs], ALU.mult, ALU.mult)
                fw.tt(o[:], o[:], xt[:], ALU.add, e="pool")
                if last:
                    fw.dma("pool", out_d[tt * 128:(tt + 1) * 128, :], o[:], semof=o.buf)
                else:
                    fw.dma("pool", xs_d[tt * 128:(tt + 1) * 128, :], o[:], semof=o.buf)
        fw.pop()

    for l in range(n_layers):
        alloc_s1()
        phase_mod(l)
        phase_h(l)
        if "hT" in dbg and l == 0:
            dbg_d["hT"] = fw.dram("hT_o", [128, 8, TT], BF16, "ExternalOutput")
            fw.dma("pool", dbg_d["hT"][:], S["hT"][:], semof=S["hT"].buf)
        if stop_after == "h":
            fw.pop()
            break
        phase_proj(l)
        fw.pop()
        if stop_after == "proj":
            break
        if "nofourier" not in dbg:
            phase_fourier(l, with_ctx=(l < n_layers - 1 or "ctxall" in dbg))
        if stop_after == "fourier":
            break
        if "nogdn" not in dbg:
            phase_gdn(l)
        if stop_after == "gdn":
            break
        wc = (l < n_layers - 1 or "ctxall" in dbg)
        phase_mla(l, with_ctx=wc)
        if stop_after == "mla":
            break
        phase_final(l, with_ctx=wc, last=(l == n_layers - 1 and "ctxall" not in dbg))

    if "abT" in dbg:
        dbg_d["abT"] = fw.dram("abT_o", [128, NT, 16], F32, "ExternalOutput")
        fw.dma("pool", dbg_d["abT"][:], abT[:], semof=abT.buf)
    outs = [out_d, xs_d, pT_d, yaT_d, ycT_d] + oT_d + list(dbg_d.values())
    fw.finish(outs, e="pool")
    return nc, fw


def kernel(**inputs):
    inp = {k: np.asarray(v) for k, v in inputs.items()}
    consts = host_constants()
    shared = host_layout_shared(inp)
    nc, fw = build()
    in_maps = []
    for b in range(8):
        m = dict(consts)
        m.update(shared)
        m.update(host_layout(inp, b))
        in_maps.append(m)
    res = run_bass_kernel_spmd(nc, in_maps, core_ids=list(range(8)))
    return np.stack([np.asarray(r["out"]) for r in res.results], axis=0).astype(np.float32)
```

```python
import math
import numpy as np
import ml_dtypes
import concourse.bass as bass
import concourse.mybir as mybir
from concourse.bass_utils import run_bass_kernel_spmd

F32 = mybir.dt.float32
BF16 = mybir.dt.bfloat16
AF = mybir.ActivationFunctionType
ALU = mybir.AluOpType

TL = 4096
TC = 256
TT = TL + TC
NT = TT // 128
CHUNKS = [(i * 512, 512) for i in range(8)] + [(TL, TC)]
NB = 58
EPS = 1e-6
NEG = -30000.0


class Buf:
    __slots__ = ("name", "writers", "readers", "waw", "dsem", "excl")

    def __init__(self, name, waw=True):
        self.name = name
        self.excl = False
        self.writers = {}
        self.readers = {}
        self.waw = waw
        self.dsem = None


class V:
    __slots__ = ("ap", "buf")

    def __init__(self, ap, buf):
        self.ap = ap
        self.buf = buf


class T:
    def __init__(self, handle, buf, bufs=None):
        self.h = handle
        self.buf = buf
        self.bufs = bufs

    def __getitem__(self, key):
        return V(self.h[key], self.buf)

    def sub(self, i):
        return T(self.h, self.bufs[i])


class FW:
    def __init__(self, nc, strict_same=True):
        self.nc = nc
        self.eng = {"pe": nc.tensor, "dve": nc.vector, "act": nc.scalar, "pool": nc.gpsimd, "sp": nc.sync}
        self.sems = {}
        self.count = {}
        for e in self.eng:
            self.sems[e] = nc.alloc_semaphore("s_" + e)
            self.count[e] = 0
        self.seen = {e: {} for e in self.eng}
        self.strict_same = strict_same
        self.nops = {e: 0 for e in self.eng}
        self.nwait = 0
        self.uid = 0
        self.scopes = [[]]
        self.scope_bufs = [[]]
        self.free_dsems = []
        self.bar_tile = V(nc.alloc_sbuf_tensor("bar_tile", [128, 8], F32)[:, :], Buf("bar"))

    def sb(self, name, shape, dtype, nsub=0, waw=True):
        self.uid += 1
        name = "%s_u%d" % (name, self.uid)
        g = self.nc.sbuf_tensor(name, list(shape), dtype)
        h = g.__enter__()
        self.scopes[-1].append(g)
        bufs = [Buf(name + "_%d" % i, waw) for i in range(nsub)] if nsub else None
        t = T(h, Buf(name, waw), bufs)
        self.scope_bufs[-1].extend([t.buf] + (bufs or []))
        return t

    def push(self):
        self.scopes.append([])
        self.scope_bufs.append([])

    def pop(self):
        self.barrier()
        for g in reversed(self.scopes.pop()):
            g.__exit__(None, None, None)
        for b in self.scope_bufs.pop():
            if b.dsem is not None:
                self.free_dsems.append(b.dsem)
                b.dsem = None

    def barrier(self):
        need = {k: v for k, v in self.count.items() if v > 0}
        self._waits("pool", need)
        ins = self.eng["pool"].memset(self.bar_tile.ap, 0.0)
        self.count["pool"] += 1
        ins.then_inc(self.sems["pool"], 1)
        val = self.count["pool"]
        for e in self.eng:
            if e == "pool":
                continue
            self._waits(e, {"pool": val})
            for k, v in need.items():
                self.seen[e][k] = max(self.seen[e].get(k, 0), v)

    def ps(self, name, shape, dtype=F32):
        h = self.nc.alloc_psum_tensor(name, list(shape), dtype)
        b = Buf(name)
        b.excl = True
        return T(h, b)

    def dram(self, name, shape, dtype, kind="Internal", nsub=0):
        h = self.nc.dram_tensor(name, list(shape), dtype, kind=kind)
        bufs = [Buf(name + "_%d" % i, False) for i in range(nsub)] if nsub else None
        return T(h.ap(), Buf(name, waw=False), bufs)

    def _need(self, reads, writes):
        need = {}
        for v in reads:
            for k, val in v.buf.writers.items():
                if need.get(k, 0) < val:
                    need[k] = val
            if v.buf.excl:
                for k, val in v.buf.readers.items():
                    if need.get(k, 0) < val:
                        need[k] = val
        for v in writes:
            b = v.buf
            if b.waw:
                for k, val in b.writers.items():
                    if need.get(k, 0) < val:
                        need[k] = val
            for k, val in b.readers.items():
                if need.get(k, 0) < val:
                    need[k] = val
        return need

    def _waits(self, e, need):
        seen = self.seen[e]
        for k, val in need.items():
            if k == e and (e == "pe" or not self.strict_same):
                continue
            if seen.get(k, 0) >= val:
                continue
            self.eng[e].wait_ge(self.sems[k], val)
            seen[k] = val
            self.nwait += 1

    def op(self, e, fn, reads=(), writes=()):
        self._waits(e, self._need(reads, writes))
        ins = fn(self.eng[e])
        self.count[e] += 1
        val = self.count[e]
        ins.then_inc(self.sems[e], 1)
        self.nops[e] += 1
        for v in reads:
            b = v.buf
            if b.readers.get(e, 0) < val:
                b.readers[e] = val
        for v in writes:
            b = v.buf
            if b.waw:
                b.writers = {e: val}
            else:
                b.writers[e] = val
            b.readers = {}
        return ins

    def dma(self, q, out, in_, semof=None, **kw):
        sbuf = semof if semof is not None else out.buf
        if sbuf.dsem is None:
            if self.free_dsems:
                key = self.free_dsems.pop()
            else:
                key = "d%d" % len(self.sems)
                self.sems[key] = self.nc.alloc_semaphore(key)
                self.count[key] = 0
            sbuf.dsem = key
        key = sbuf.dsem
        self._waits(q, self._need([in_], [out]))
        ins = self.eng[q].dma_start(out=out.ap, in_=in_.ap, **kw)
        self.count[key] += 16
        val = self.count[key]
        ins.then_inc(self.sems[key], 16)
        self.nops[q] += 1
        b = in_.buf
        if b.readers.get(key, 0) < val:
            b.readers[key] = val
        b = out.buf
        if b.waw:
            b.writers = {key: val}
        else:
            b.writers[key] = val
        b.readers = {}
        return ins

    def finish(self, bufs, e="sp"):
        need = {}
        for b in bufs:
            for k, val in b.buf.writers.items():
                if need.get(k, 0) < val:
                    need[k] = val
        self._waits(e, need)

    def mm(self, out, lhsT, rhs, start=True, stop=True):
        reads = [lhsT, rhs] + ([] if start else [out])
        return self.op("pe", lambda e: e.matmul(out.ap, lhsT.ap, rhs.ap, start=start, stop=stop), reads, [out])

    def tr(self, out, in_, ident):
        return self.op("pe", lambda e: e.transpose(out.ap, in_.ap, ident.ap), [in_, ident], [out])

    def act(self, out, in_, func, bias=None, scale=None, accum_out=None):
        reads = [in_]
        kw = {}
        if bias is not None:
            if isinstance(bias, V):
                reads.append(bias)
                kw["bias"] = bias.ap
            else:
                kw["bias"] = bias
        if scale is not None:
            if isinstance(scale, V):
                reads.append(scale)
                kw["scale"] = scale.ap
            else:
                kw["scale"] = scale
        writes = [out]
        if accum_out is not None:
            kw["accum_out"] = accum_out.ap
            writes.append(accum_out)
        return self.op("act", lambda e: e.activation(out.ap, in_.ap, func, **kw), reads, writes)

    def tt(self, out, in0, in1, op, e="dve"):
        return self.op(e, lambda g: g.tensor_tensor(out.ap, in0.ap, in1.ap, op), [in0, in1], [out])

    def ts(self, out, in0, s1, s2=None, op0=ALU.mult, op1=None, e="dve"):
        reads = [in0]
        a1 = s1
        if isinstance(s1, V):
            reads.append(s1)
            a1 = s1.ap
        a2 = s2
        if isinstance(s2, V):
            reads.append(s2)
            a2 = s2.ap
        if op1 is None:
            return self.op(e, lambda g: g.tensor_scalar(out.ap, in0.ap, a1, None, op0), reads, [out])
        return self.op(e, lambda g: g.tensor_scalar(out.ap, in0.ap, a1, a2, op0, op1), reads, [out])

    def stt(self, out, in0, s, in1, op0, op1):
        reads = [in0, in1]
        a = s
        if isinstance(s, V):
            reads.append(s)
            a = s.ap
        return self.op("dve", lambda g: g.scalar_tensor_tensor(out.ap, in0.ap, a, in1.ap, op0, op1), reads, [out])

    def copy(self, out, in_, e="dve"):
        if e == "act":
            return self.op("act", lambda g: g.activation(out.ap, in_.ap, AF.Copy), [in_], [out])
        return self.op(e, lambda g: g.tensor_copy(out.ap, in_.ap), [in_], [out])

    def recip(self, out, in_):
        return self.op("dve", lambda g: g.reciprocal(out.ap, in_.ap), [in_], [out])


def _bf(a):
    return np.ascontiguousarray(a).astype(ml_dtypes.bfloat16)


_CONST = {}


def host_constants():
    if _CONST:
        return _CONST
    c = {}
    t = np.arange(TL, dtype=np.int64)
    ph = (np.outer(t, t) % TL).astype(np.float64) * (2 * np.pi / TL)
    c["CL"] = _bf(np.cos(ph))
    c["NSL"] = _bf(-np.sin(ph))
    tcx = np.arange(TC, dtype=np.int64)
    phc = (np.outer(tcx, tcx) % TC).astype(np.float64) * (2 * np.pi / TC)
    c["CLC"] = _bf(np.cos(phc))
    c["NSLC"] = _bf(-np.sin(phc))
    ch = np.arange(128, dtype=np.int64)
    phd = (np.outer(ch, ch) % 128).astype(np.float64) * (2 * np.pi / 128)
    c["CSC"] = _bf(np.concatenate([np.cos(phd), np.sin(phd)], axis=1))
    rows = np.repeat(np.arange(64, dtype=np.float32), 64)
    cols = np.tile(np.arange(64, dtype=np.float32), 64)
    inv = (10000.0 ** (-np.arange(8, dtype=np.float32) / 8)).astype(np.float32)
    ang_r = rows[:, None] * inv
    ang_c = cols[:, None] * inv
    ang = np.concatenate([ang_r, ang_r, ang_c, ang_c], axis=-1)
    sgn = np.array([-1.0] * 8 + [1.0] * 8 + [-1.0] * 8 + [1.0] * 8, dtype=np.float32)
    C = np.ones((96, TT), np.float32)
    S = np.zeros((96, TT), np.float32)
    C[64:96, :TL] = np.cos(ang).T
    S[64:96, :TL] = (np.sin(ang) * sgn[None, :]).T
    c["ROPEC"] = C
    c["ROPES"] = S
    k = np.arange(128)[:, None]
    m = np.arange(128)[None, :]
    same = (k // 64) == (m // 64)
    ident = (k == m).astype(np.float32)
    ones = np.ones((128, 128), np.float32)
    triF = (same & (k <= m)).astype(np.float32)
    restF = (same & (k > m)).astype(np.float32)
    triB = (same & (k >= m)).astype(np.float32)
    restB = (same & (k < m)).astype(np.float32)
    tot0 = np.broadcast_to((k < 64), (128, 128)).astype(np.float32)
    tot1 = np.broadcast_to((k >= 64), (128, 128)).astype(np.float32)
    j = k
    i = m
    nmF = np.where(same & (i >= j), 0.0, NEG).astype(np.float32)
    nmB = np.where(same & (i <= j), 0.0, NEG).astype(np.float32)
    stF = (same & (i > j)).astype(np.float32)
    stB = (same & (i < j)).astype(np.float32)
    c["MSK"] = np.ascontiguousarray(np.concatenate(
        [ident, ones, triF, restF, tot0, tot1, triB, restB,
         np.tile(nmF, (1, 4)), np.tile(nmB, (1, 4)), np.tile(ident, (1, 4))], axis=1)).astype(np.float32)
    d16 = ((k // 16) == (m // 16)).astype(np.float32)
    c32 = (((k // 32) == (m // 32)) & ((k // 16) != (m // 16))).astype(np.float32)
    c64 = (same & ((k // 32) != (m // 32))).astype(np.float32)
    c["MSKB"] = _bf(np.concatenate([ident, ones, np.tile(stF, (1, 4)), np.tile(stB, (1, 4)), np.tile(ident, (1, 4)),
                                    np.tile(d16, (1, 4)), np.tile(c32, (1, 4)), np.tile(c64, (1, 4))], axis=1))
    _CONST.update(c)
    return c


IN_W = (512, 512, 512, 512, 512, 512, 16, 384, 256, 32, 512, 3072)
IN_OFF = np.concatenate([[0], np.cumsum(IN_W)]).tolist()


def _col(v, nblk):
    return np.ascontiguousarray(v.reshape(nblk, 128).T)


def host_layout(inp, b):
    d = {}
    d["xin"] = np.ascontiguousarray(np.concatenate([inp["x"][b], inp["ctx"][b]], axis=0))
    cc = np.stack([_col(inp["c"][b], 8), _col(inp["c_ctx"], 8)], axis=-1)
    d["ccol"] = np.ascontiguousarray(cc)
    return d


def host_layout_shared(inp):
    d = {}
    o = IN_OFF
    perm = np.concatenate([np.arange(8, 16), np.arange(0, 8), np.arange(24, 32), np.arange(16, 24)])
    w_in = inp["w_in"]
    kpe = w_in[:, :, o[9]:o[10]]
    cols = np.concatenate([
        w_in[:, :, o[0]:o[6]],
        w_in[:, :, o[7]:o[9]],
        w_in[:, :, o[10]:o[11]],
        w_in[:, :, o[11]:o[12]],
        kpe, kpe[:, :, perm],
        np.zeros((2, 1024, 64), np.float32),
    ], axis=-1)
    assert cols.shape[-1] == NB * 128
    d["w_in_r"] = np.ascontiguousarray(cols.reshape(2, 8, 128, NB, 128).transpose(0, 3, 2, 1, 4))
    wab = w_in[:, :, o[6]:o[7]]
    d["w_ab"] = np.ascontiguousarray(wab.reshape(2, 8, 128, 16).transpose(0, 2, 1, 3))
    d["w_mod"] = inp["w_mod"]
    d["bmod"] = np.ascontiguousarray(np.stack([_col(inp["b_mod"][l], 24) for l in range(2)]))
    d["gpre"] = np.ascontiguousarray(np.stack([_col(inp["g_pre"][l], 8) for l in range(2)]))
    d["gpost"] = np.ascontiguousarray(np.stack([_col(inp["g_post"][l], 8) for l in range(2)]))
    d["f_w"] = np.ascontiguousarray(inp["f_w"].transpose(0, 2, 1, 3))
    cw = inp["dn_conv"]
    d["convw"] = np.ascontiguousarray(cw.reshape(2, 3, 12, 128).transpose(0, 3, 2, 1))
    d["alog"] = np.ascontiguousarray(np.broadcast_to(inp["dn_a_log"].reshape(2, 1, 1, 8), (2, 128, NT, 8)))
    d["dtb"] = np.ascontiguousarray(np.broadcast_to(inp["dn_dt_bias"].reshape(2, 1, 1, 8), (2, 128, NT, 8)))
    d["dnorm"] = np.ascontiguousarray(inp["dn_norm"].reshape(2, 128, 1))
    d["qnorm"] = np.ascontiguousarray(np.stack([_col(inp["mla_q_norm"][l], 3) for l in range(2)]))
    d["kvnorm"] = np.ascontiguousarray(np.stack([_col(inp["mla_kv_norm"][l], 2) for l in range(2)]))
    wuq = inp["mla_w_uq"]
    hp = np.concatenate([np.arange(64), 64 + perm])
    permc = np.concatenate([h * 96 + hp for h in range(8)])
    both = np.concatenate([wuq, wuq[:, :, permc]], axis=-1)
    d["w_uq"] = np.ascontiguousarray(both.reshape(2, 3, 128, 1536).transpose(0, 2, 1, 3))
    d["w_ukv"] = np.ascontiguousarray(inp["mla_w_ukv"].reshape(2, 2, 128, 1024).transpose(0, 2, 1, 3))
    d["w_br"] = np.ascontiguousarray(inp["w_branch"].reshape(2, 12, 128, 1024).transpose(0, 2, 1, 3))
    d["w_out"] = np.ascontiguousarray(inp["w_out"].reshape(2, 8, 128, 1024).transpose(0, 2, 1, 3))
    return d


def build(n_layers=2, dbg=(), stop_after=None):
    nc = bass.Bass("TRN2", target_bir_lowering=False)
    fw = FW(nc)
    EI = "ExternalInput"

    def skind(name):
        return "ExternalOutput" if name in dbg else "Internal"

    xin = fw.dram("xin", [TT, 1024], F32, EI)
    ccol_d = fw.dram("ccol", [128, 8, 2], F32, EI)
    w_in_d = fw.dram("w_in_r", [2, NB, 128, 8, 128], F32, EI)
    w_ab_d = fw.dram("w_ab", [2, 128, 8, 16], F32, EI)
    w_mod_d = fw.dram("w_mod", [2, 1024, 3072], F32, EI)
    bmod_d = fw.dram("bmod", [2, 128, 24], F32, EI)
    gpre_d = fw.dram("gpre", [2, 128, 8], F32, EI)
    gpost_d = fw.dram("gpost", [2, 128, 8], F32, EI)
    f_w_d = fw.dram("f_w", [2, 128, 4, 128], F32, EI)
    convw_d = fw.dram("convw", [2, 128, 12, 3], F32, EI)
    alog_d = fw.dram("alog", [2, 128, NT, 8], F32, EI)
    dtb_d = fw.dram("dtb", [2, 128, NT, 8], F32, EI)
    dnorm_d = fw.dram("dnorm", [2, 128, 1], F32, EI)
    qnorm_d = fw.dram("qnorm", [2, 128, 3], F32, EI)
    kvnorm_d = fw.dram("kvnorm", [2, 128, 2], F32, EI)
    w_uq_d = fw.dram("w_uq", [2, 128, 3, 1536], F32, EI)
    w_ukv_d = fw.dram("w_ukv", [2, 128, 2, 1024], F32, EI)
    w_br_d = fw.dram("w_br", [2, 128, 12, 1024], F32, EI)
    w_out_d = fw.dram("w_out", [2, 128, 8, 1024], F32, EI)
    CL_d = fw.dram("CL", [TL, TL], BF16, EI)
    NSL_d = fw.dram("NSL", [TL, TL], BF16, EI)
    CLC_d = fw.dram("CLC", [TC, TC], BF16, EI)
    NSLC_d = fw.dram("NSLC", [TC, TC], BF16, EI)
    CSC_d = fw.dram("CSC", [128, 256], BF16, EI)
    ROPEC_d = fw.dram("ROPEC", [96, TT], F32, EI)
    ROPES_d = fw.dram("ROPES", [96, TT], F32, EI)
    MSK_d = fw.dram("MSK", [128, 8 * 128 + 1536], F32, EI)
    MSKB_d = fw.dram("MSKB", [128, 2 * 128 + 6 * 512], BF16, EI)

    out_d = fw.dram("out", [TL, 1024], F32, "ExternalOutput")
    xs_d = fw.dram("xs", [TT, 1024], F32, skind("xs"))
    pT_d = fw.dram("pT", [NB * 128, TT], BF16, skind("pT"))
    yaT_d = fw.dram("yaT", [512, TT], BF16, skind("yaT"))
    ycT_d = fw.dram("ycT", [512, TT], BF16, skind("ycT"))
    oT_d = [fw.dram("oT%d" % d, [512, TT], F32, skind("oT%d" % d)) for d in range(2)]
    qkvT_d = fw.dram("qkvT", [1536, TT], BF16, skind("qkvT"))
    dbg_d = {}

    msk = fw.sb("msk", [128, 8 * 128 + 1536], F32)
    mskb = fw.sb("mskb", [128, 2 * 128 + 6 * 512], BF16)
    fw.dma("sp", msk[:], MSK_d[:])
    fw.dma("sp", mskb[:], MSKB_d[:])

    def mcol(i):
        return msk[:, i * 128:(i + 1) * 128]
    identF, onesF, triF, restF, tot0, tot1, triB, restB = [mcol(i) for i in range(8)]
    negmask = [msk[:, 1024:1536], msk[:, 1536:2048]]
    ident4F = msk[:, 2048:2560]
    identB = mskb[:, 0:128]
    onesB = mskb[:, 128:256]
    strictB = [mskb[:, 256:768], mskb[:, 768:1280]]
    ident4B = mskb[:, 1280:1792]
    mD16 = mskb[:, 1792:2304]
    mC32 = mskb[:, 2304:2816]
    mC64 = mskb[:, 2816:3328]

    PF = [fw.ps("pf%d" % i, [128, 512], F32) for i in range(7)]
    PB = fw.ps("pb", [128, 1024], BF16)

    ccol = fw.sb("ccol_sb", [128, 8, 2], F32)
    sc = fw.sb("sc", [128, 8, 2], F32)
    modc = fw.sb("modc", [128, 24, 2], F32)
    Acol = fw.sb("Acol", [128, 8, 2], F32)
    ggcol = fw.sb("ggcol", [128, 8, 2], F32)
    ggbc = [fw.sb("ggbc%d" % j, [128, 1024], F32) for j in range(2)]
    bmod = fw.sb("bmod_sb", [128, 24], F32)
    gpre = fw.sb("gpre_sb", [128, 8], F32)
    gpost = fw.sb("gpost_sb", [128, 8], F32)
    abT = fw.sb("abT", [128, NT, 16], F32)
    stat = fw.sb("stat", [128, 4 * NT], F32, nsub=NT)
    fw.dma("sp", ccol[:], ccol_d[:])

    ring_ctr = {}

    def ring(lst, key):
        i = ring_ctr.get(key, 0)
        ring_ctr[key] = i + 1
        return lst[i % len(lst)]

    pfc = [0]

    def pf(lo=0, hi=7):
        i = pfc[0]
        pfc[0] += 1
        return PF[lo + i % (hi - lo)]

    S = {}

    def alloc_s1():
        fw.push()
        S["hT"] = fw.sb("hT", [128, 8, TT], BF16)
        S["w32"] = [fw.sb("w32_%d" % i, [128, 4096], F32) for i in range(2)]
        S["wbf"] = [fw.sb("wbf_%d" % i, [128, 1024], BF16) for i in range(3)]
        S["stg"] = [fw.sb("stg_%d" % i, [128, TT], BF16) for i in range(3)]
        S["xring"] = [fw.sb("xr%d" % i, [128, 1024], F32) for i in range(3)]
        S["hnring"] = [fw.sb("hn%d" % i, [128, 1024], BF16) for i in range(2)]

    def phase_mod(l):
        fw.dma("sp", bmod[:], bmod_d[l])
        fw.dma("sp", gpre[:], gpre_d[l])
        fw.dma("sp", gpost[:], gpost_d[l])
        fw.act(sc[:], ccol[:], AF.Silu)
        pm = PF[0]
        wv = w_mod_d.h[l].rearrange("(kc k) n -> k kc n", k=128)
        for nch in range(6):
            slot = ring(S["w32"], "w32")
            sv = V(slot.h[:, :].rearrange("p (kc n) -> p kc n", kc=8), slot.buf)
            fw.dma("sp", sv, V(wv[:, :, nch * 512:(nch + 1) * 512], w_mod_d.buf))
            for j in range(4):
                blk = nch * 4 + j
                for kc in range(8):
                    fw.mm(pm[:, blk * 2:blk * 2 + 2],
                          V(slot.h[:, kc * 512 + j * 128: kc * 512 + (j + 1) * 128], slot.buf),
                          sc[:, kc, :], start=(kc == 0), stop=(kc == 7))
        for j in range(2):
            fw.tt(modc[:, :, j], V(pm.h[:, 0:48].rearrange("p (b j) -> p b j", j=2)[:, :, j], pm.buf), bmod[:], ALU.add)
            fw.stt(Acol[:, :, j], modc[:, 8:16, j], 1.0, gpre[:], ALU.add, ALU.mult)
            fw.tt(ggcol[:, :, j], modc[:, 16:24, j], gpost[:], ALU.mult)
        for j in range(2):
            for half in range(2):
                pg = pf(1, 7)
                for q in range(4):
                    kc = half * 4 + q
                    D = ring(S["w32"], "w32")
                    fw.ts(D[:, 0:128], identF, ggcol[:, kc, j:j + 1])
                    fw.mm(pg[:, q * 128:(q + 1) * 128], onesF, D[:, 0:128])
                fw.copy(ggbc[j][:, half * 512:(half + 1) * 512], pg[:])

    def phase_h(l):
        src = xin if l == 0 else xs_d
        for tt in range(NT):
            j = 0 if tt < 32 else 1
            xt = ring(S["xring"], "xr")
            fw.dma("sp", xt[:], src[tt * 128:(tt + 1) * 128, :])
            st = stat.sub(tt)
            hn = ring(S["hnring"], "hn")
            fw.act(hn[:], xt[:], AF.Square, accum_out=st[:, 4 * tt:4 * tt + 1])
            fw.act(st[:, 4 * tt + 1:4 * tt + 2], st[:, 4 * tt:4 * tt + 1], AF.Sqrt, bias=EPS, scale=1.0 / 1024)
            fw.recip(st[:, 4 * tt + 2:4 * tt + 3], st[:, 4 * tt + 1:4 * tt + 2])
            fw.ts(hn[:], xt[:], st[:, 4 * tt + 2:4 * tt + 3])
            for kc in range(8):
                fw.tr(PB[:, kc * 128:(kc + 1) * 128], hn[:, kc * 128:(kc + 1) * 128], identB)
            for kc in range(8):
                o = S["hT"][:, kc, tt * 128:(tt + 1) * 128]
                i = PB[:, kc * 128:(kc + 1) * 128]
                if kc % 2 == 0:
                    fw.ts(o, i, Acol[:, kc, j:j + 1], modc[:, kc, j:j + 1], ALU.mult, ALU.add)
                else:
                    fw.act(o, i, AF.Identity, bias=modc[:, kc, j:j + 1], scale=Acol[:, kc, j:j + 1])

    SILU_BLK = list(range(4, 8)) + list(range(20, 24)) + list(range(29, 33))
    SIG_BLK = list(range(33, 57))
    COPY_BLK = [b for b in range(NB) if b not in SILU_BLK and b not in SIG_BLK]

    def phase_proj(l):
        wab32 = ring(S["w32"], "w32")
        fw.dma("sp", wab32[:, 0:128], V(w_ab_d.h[l].rearrange("p kc n -> p (kc n)"), w_ab_d.buf))
        wabb = ring(S["wbf"], "wbf")
        fw.copy(wabb[:, 0:128], wab32[:, 0:128], e="pool")
        pa = [PF[5], PF[6]]
        for tt in range(NT):
            dst = pa[0][:, tt * 16:(tt + 1) * 16] if tt < 32 else pa[1][:, (tt - 32) * 16:(tt - 31) * 16]
            for kc in range(8):
                fw.mm(dst, S["hT"][:, kc, tt * 128:(tt + 1) * 128], wabb[:, kc * 16:(kc + 1) * 16], start=(kc == 0), stop=(kc == 7))
        fw.copy(V(abT.h[:, 0:32, :].rearrange("p a b -> p (a b)"), abT.buf), pa[0][:, 0:512])
        fw.copy(V(abT.h[:, 32:34, :].rearrange("p a b -> p (a b)"), abT.buf), pa[1][:, 0:32])
        for blk in COPY_BLK + SILU_BLK + SIG_BLK:
            ws = ring(S["w32"], "w32")
            fw.dma("sp", ws[:, 0:1024], V(w_in_d.h[l, blk].rearrange("p kc n -> p (kc n)"), w_in_d.buf))
            wb = ring(S["wbf"], "wbf")
            fw.copy(wb[:, 0:1024], ws[:, 0:1024], e="pool")
            sg = ring(S["stg"], "stg")
            for ci, (t0, n) in enumerate(CHUNKS):
                ps = pf(0, 5)
                for kc in range(8):
                    fw.mm(ps[:, 0:n], wb[:, kc * 128:(kc + 1) * 128], S["hT"][:, kc, t0:t0 + n], start=(kc == 0), stop=(kc == 7))
                if blk in SILU_BLK:
                    fw.act(sg[:, t0:t0 + n], ps[:, 0:n], AF.Silu)
                elif blk in SIG_BLK:
                    fw.act(sg[:, t0:t0 + n], ps[:, 0:n], AF.Sigmoid)
                else:
                    fw.copy(sg[:, t0:t0 + n], ps[:, 0:n])
            fw.dma("pool", pT_d[blk * 128:(blk + 1) * 128, :], sg[:], semof=sg.buf)


    def v3(t, n):
        return t.h[:, :].rearrange("p (a n) -> p a n", n=n)

    def phase_fourier(l, with_ctx):
        fw.push()
        UT = fw.sb("UT", [128, 4, TT], BF16)
        ABs = fw.sb("ABs", [128, NT, 1024], BF16)
        csc = fw.sb("csc", [128, 256], BF16)
        fw32 = fw.sb("fw32", [128, 512], F32)
        fwb = fw.sb("fwb", [128, 512], BF16)
        tbC = [fw.sb("tbC%d" % i, [128, 8, 512], BF16) for i in range(2)]
        tbS = [fw.sb("tbS%d" % i, [128, 8, 512], BF16) for i in range(2)]
        specb = [fw.sb("specb%d" % i, [128, 512], BF16) for i in range(2)]
        szr = [fw.sb("szr%d" % i, [128, 4, 512], BF16) for i in range(2)]
        yast = [fw.sb("yast%d" % i, [128, 4, 512], BF16) for i in range(2)]
        fw.dma("sp", csc[:], CSC_d[:])
        fw.dma("sp", fw32[:], V(f_w_d.h[l].rearrange("p g d -> p (g d)"), f_w_d.buf))
        fw.copy(fwb[:], fw32[:], e="pool")
        for g in range(4):
            fw.dma("sp", UT[:, g, :], pT_d[g * 128:(g + 1) * 128, :])
        for tt in range(NT if with_ctx else 32):
            for half in range(2):
                ps = pf(4, 7)
                for gg in range(2):
                    g = half * 2 + gg
                    fw.mm(ps[:, gg * 256:(gg + 1) * 256], UT[:, g, tt * 128:(tt + 1) * 128], csc[:])
                fw.copy(ABs[:, tt, half * 512:(half + 1) * 512], ps[:], e=("dve" if half == 0 else "act"))
        osc = 1.0 / math.sqrt(128.0)
        jobs = [(ci, t0, n, 0, 32, CL_d, NSL_d, TL) for ci, (t0, n) in enumerate(CHUNKS[:8])]
        if with_ctx:
            jobs.append((8, TL, TC, 32, 2, CLC_d, NSLC_d, TC))
        for (ci, t0, n, tt0, ntile, Cd, Sd, Lseq) in jobs:
            sz = ring(szr, "szr")
            fw.dma("sp", sz[:, :, 0:n], V(pT_d.h[512:1024, :].rearrange("(g p) t -> p g t", p=128)[:, :, t0:t0 + n], pT_d.buf))
            c0 = t0 - tt0 * 128
            nq = (ntile + 7) // 8
            for qd in range(nq):
                na = min(8, ntile - qd * 8)
                tc_ = ring(tbC, "tbC")
                tsn = ring(tbS, "tbS")
                fw.dma("sp", tc_[:, 0:na, 0:n], V(Cd.h[qd * 1024: qd * 1024 + na * 128, :].rearrange("(a p) n -> p a n", p=128)[:, :, c0:c0 + n], Cd.buf))
                fw.dma("sp", tsn[:, 0:na, 0:n], V(Sd.h[qd * 1024: qd * 1024 + na * 128, :].rearrange("(a p) n -> p a n", p=128)[:, :, c0:c0 + n], Sd.buf))
                for g in range(4):
                    for a in range(na):
                        tt = tt0 + qd * 8 + a
                        first = (qd == 0 and a == 0)
                        last = (qd == nq - 1 and a == na - 1)
                        fw.mm(PF[g][:, 0:n], ABs[:, tt, g * 256: g * 256 + 128], tc_[:, a, 0:n], start=first, stop=False)
                        fw.mm(PF[g][:, 0:n], ABs[:, tt, g * 256 + 128: g * 256 + 256], tsn[:, a, 0:n], start=False, stop=last)
            ya = ring(yast, "yast")
            scl = osc / math.sqrt(float(Lseq))
            for g in range(4):
                sb_ = ring(specb, "specb")
                fw.act(sb_[:, 0:n], PF[g][:, 0:n], AF.Copy, scale=scl)
                po = pf(4, 7)
                fw.mm(po[:, 0:n], fwb[:, g * 128:(g + 1) * 128], sb_[:, 0:n])
                fw.tt(ya[:, g, 0:n], po[:, 0:n], sz[:, g, 0:n], ALU.mult)
            fw.dma("pool", V(yaT_d.h.rearrange("(g p) t -> p g t", p=128)[:, :, t0:t0 + n], yaT_d.buf), ya[:, :, 0:n], semof=ya.buf)
        fw.pop()

    def phase_gdn(l):
        fw.push()
        convw = fw.sb("convw", [128, 12, 3], F32)
        fw.dma("sp", convw[:], convw_d[l])
        fw.push()
        cin = [fw.sb("cin%d" % i, [128, TT], BF16) for i in range(2)]
        cy = [fw.sb("cy%d" % i, [128, TT], F32) for i in range(2)]
        cst = [fw.sb("cst%d" % i, [128, TT], BF16) for i in range(2)]
        sqr = [fw.sb("sqr%d" % i, [128, 512], BF16) for i in range(2)]
        rr = [fw.sb("rr%d" % i, [128, 512], F32) for i in range(2)]
        for blk in range(12):
            xi = ring(cin, "cin")
            y = ring(cy, "cy")
            so = ring(cst, "cst")
            fw.dma("sp", xi[:], pT_d[1024 + blk * 128: 1024 + (blk + 1) * 128, :])
            for (a_, b_) in ((0, TL), (TL, TT)):
                fw.ts(y[:, a_:b_], xi[:, a_:b_], convw[:, blk, 1:2])
                fw.stt(y[:, a_ + 1:b_], xi[:, a_:b_ - 1], convw[:, blk, 0:1], y[:, a_ + 1:b_], ALU.mult, ALU.add)
                fw.stt(y[:, a_:b_ - 1], xi[:, a_ + 1:b_], convw[:, blk, 2:3], y[:, a_:b_ - 1], ALU.mult, ALU.add)
            if blk >= 8:
                fw.act(so[:], y[:], AF.Silu)
            else:
                fw.act(y[:], y[:], AF.Silu)
                for (t0, n) in CHUNKS:
                    sq = ring(sqr, "sqr")
                    r = ring(rr, "rr")
                    fw.tt(sq[:, 0:n], y[:, t0:t0 + n], y[:, t0:t0 + n], ALU.mult, e="pool")
                    ps = pf(0, 7)
                    fw.mm(ps[:, 0:n], onesB, sq[:, 0:n])
                    fw.act(r[:, 0:n], ps[:, 0:n], AF.Ln, bias=EPS)
                    fw.act(r[:, 0:n], r[:, 0:n], AF.Exp, scale=-0.5)
                    fw.tt(so[:, t0:t0 + n], y[:, t0:t0 + n], r[:, 0:n], ALU.mult)
            fw.dma("pool", qkvT_d[blk * 128:(blk + 1) * 128, :], so[:], semof=so.buf)
        fw.pop()

        NC = NT * 4
        bb = fw.sb("bb", [128, NT, 8], F32)
        gd = [fw.sb("gd%d" % d, [128, NC], F32) for d in range(2)]
        names = ("Gs", "nG", "EG", "ER", "GL0", "GL1")
        GA = [{nm: fw.sb("%s%d" % (nm, d), [128, NC], F32) for nm in names} for d in range(2)]
        fw.push()
        alog = fw.sb("alog", [128, NT, 8], F32)
        dtb = fw.sb("dtb", [128, NT, 8], F32)
        fw.dma("sp", alog[:], alog_d[l])
        fw.dma("sp", dtb[:], dtb_d[l])
        gz = fw.sb("gz", [128, NT, 8], F32)
        fw.tt(gz[:], abT[:, :, 0:8], dtb[:], ALU.add)
        fw.act(gz[:], gz[:], AF.Exp)
        fw.act(gz[:], gz[:], AF.Ln, bias=1.0)
        fw.act(alog[:], alog[:], AF.Exp)
        fw.stt(gz[:], gz[:], -1.0, alog[:], ALU.mult, ALU.mult)
        fw.act(bb[:], abT[:, :, 8:16], AF.Exp, scale=-1.0)
        fw.ts(bb[:], bb[:], 1.0, None, ALU.add)
        fw.recip(bb[:], bb[:])
        for d in range(2):
            fw.copy(V(v3(gd[d], 4), gd[d].buf), gz[:, :, d * 4:(d + 1) * 4])
        fw.pop()
        for d in range(2):
            tri = triF if d == 0 else triB
            rest = restF if d == 0 else restB
            p1 = pf(0, 7)
            fw.mm(p1[:, 0:NC], tri, gd[d][:])
            fw.copy(GA[d]["Gs"][:], p1[:, 0:NC])
            fw.ts(GA[d]["nG"][:], p1[:, 0:NC], -1.0)
            fw.act(GA[d]["EG"][:], p1[:, 0:NC], AF.Exp)
            p2 = pf(0, 7)
            fw.mm(p2[:, 0:NC], rest, gd[d][:])
            fw.act(GA[d]["ER"][:], p2[:, 0:NC], AF.Exp)
            p3 = pf(0, 7)
            fw.mm(p3[:, 0:NC], tot0, gd[d][:])
            fw.act(GA[d]["GL0"][:], p3[:, 0:NC], AF.Exp)
            p4 = pf(0, 7)
            fw.mm(p4[:, 0:NC], tot1, gd[d][:])
            fw.act(GA[d]["GL1"][:], p4[:, 0:NC], AF.Exp)

        def bc4(t_, tt, rows=slice(0, 128)):
            ap = t_.h[rows, tt * 4: tt * 4 + 4].unsqueeze(2)
            return V(ap.broadcast_to([rows.stop - rows.start, 4, 128]), t_.buf)

        def bcb(tt, d, rows=slice(0, 128)):
            ap = bb.h[rows, tt, d * 4:(d + 1) * 4].unsqueeze(2)
            return V(ap.broadcast_to([rows.stop - rows.start, 4, 128]), bb.buf)

        slots = [[], []]
        for d in range(2):
            for i in range(2):
                slots[d].append({
                    "w0T": fw.sb("w0T%d%d" % (d, i), [128, 512], BF16), "qkdT": fw.sb("qkdT%d%d" % (d, i), [128, 512], BF16),
                    "qgT": fw.sb("qgT%d%d" % (d, i), [128, 512], BF16), "kd": fw.sb("kd%d%d" % (d, i), [128, 512], BF16),
                    "ub": fw.sb("ub%d%d" % (d, i), [128, 512], F32)})
        tmps = []
        for i in range(2):
            tm_ = {
                "EGr": fw.sb("EGr%d" % i, [128, 512], F32),
                "tD": fw.sb("tD%d" % i, [128, 512], F32), "dec": fw.sb("dec%d" % i, [128, 512], F32),
                "kEG": fw.sb("kEG%d" % i, [128, 512], BF16), "vtok": fw.sb("vtok%d" % i, [128, 512], BF16),
                "TTb": fw.sb("TTb%d" % i, [128, 512], BF16),
                "qk": [fw.sb("qk%d_%d" % (i, k_), [128, 12, 128], BF16) for k_ in range(2)]}
            for nm in ("M0", "MT0", "Qa", "QTa", "Qb", "QTb", "Pa", "PTa", "Pb", "PTb", "C32", "C32T", "C64", "C64T"):
                tm_[nm] = fw.sb("%s_%d" % (nm, i), [128, 512], F32)
            tm_["Rp"] = tm_["dec"]
            tm_["Mf"] = tm_["tD"]
            tmps.append(tm_)
        S32 = [fw.sb("S32_%d" % d, [128, 512], F32) for d in range(2)]
        Sb = [fw.sb("Sb_%d" % d, [128, 512], BF16) for d in range(2)]
        un = [fw.sb("un_%d" % d, [128, 512], BF16) for d in range(2)]
        t5 = [fw.sb("t5_%d" % d, [128, 512], F32) for d in range(2)]
        ost = [[fw.sb("ost%d%d" % (d, i), [128, 512], F32) for i in range(2)] for d in range(2)]
        for d in range(2):
            fw.op("pool", lambda g, d=d: g.memset(S32[d].h[:, :], 0.0), [], [S32[d][:]])
            fw.op("pool", lambda g, d=d: g.memset(Sb[d].h[:, :], 0.0), [], [Sb[d][:]])
        qscale = 128.0 ** -0.5
        PGB = [(PF[0], PF[1]), (PF[2], PF[6])]
        pgc = [0, 0]

        def prep(tt, d, sl, s_):
            tm = tmps[d]
            tok = slice(tt * 128, (tt + 1) * 128)
            G = GA[d]

            def pg():
                pgc[d] += 1
                return PGB[d][pgc[d] % 2]

            qk = tm["qk"][s_ % 2]
            fw.dma("sp", qk[:], V(qkvT_d.h.rearrange("(b p) t -> p b t", p=128)[:, :, tok], qkvT_d.buf))
            fw.tt(V(v3(tm["Rp"], 128), tm["Rp"].buf), V(ident4F.ap.rearrange("p (a n) -> p a n", n=128), ident4F.buf),
                  bc4(G["Gs"], tt), ALU.mult, e="pool")
            yield
            p1 = pg()
            fw.mm(p1[:], onesF, tm["Rp"][:])
            yield
            fw.act(tm["EGr"][:], p1[:], AF.Exp, bias=math.log(qscale))
            fw.tt(tm["tD"][:], p1[:], negmask[d], ALU.add)
            fw.tt(V(v3(tm["tD"], 128), tm["tD"].buf), V(v3(tm["tD"], 128), tm["tD"].buf), bc4(G["nG"], tt), ALU.add)
            yield
            fw.act(tm["dec"][:], tm["tD"][:], AF.Exp)
            pk = pg()
            for h in range(4):
                fw.mm(pk[:, h * 128:(h + 1) * 128], qk[:, 4 + h, :], qk[:, 4 + h, :])
            yield
            fw.tt(tm["Mf"][:], pk[:], tm["dec"][:], ALU.mult)
            fw.tt(V(v3(tm["Mf"], 128), tm["Mf"].buf), V(v3(tm["Mf"], 128), tm["Mf"].buf), bcb(tt, d), ALU.mult)
            yield
            M0, MT0 = tm["M0"], tm["MT0"]
            fw.tt(M0[:], tm["Mf"][:], strictB[d], ALU.mult, e="pool")
            yield
            ptr = pg()
            for h in range(4):
                fw.tr(ptr[:, h * 128:(h + 1) * 128], M0[:, h * 128:(h + 1) * 128], identF)
            yield
            fw.copy(MT0[:], ptr[:], e="act")
            Q, QT, P, PT = tm["Qa"], tm["QTa"], tm["Pa"], tm["PTa"]
            Qn, QTn, Pn, PTn = tm["Qb"], tm["QTb"], tm["Pb"], tm["PTb"]
            fw.tt(Q[:], M0[:], mD16, ALU.mult, e="pool")
            yield
            fw.tt(QT[:], MT0[:], mD16, ALU.mult, e="pool")
            fw.tt(P[:], ident4F, Q[:], ALU.subtract)
            yield
            fw.tt(PT[:], ident4F, QT[:], ALU.subtract)
            fw.tt(tm["C32"][:], M0[:], mC32, ALU.mult, e="pool")
            fw.tt(tm["C32T"][:], MT0[:], mC32, ALU.mult, e="pool")
            fw.tt(tm["C64"][:], M0[:], mC64, ALU.mult, e="pool")
            fw.tt(tm["C64T"][:], MT0[:], mC64, ALU.mult, e="pool")
            yield

            def mm4(lhsT, rhs):
                p_ = pg()
                for h in range(4):
                    hs = slice(h * 128, (h + 1) * 128)
                    fw.mm(p_[:, hs], lhsT[:, hs], rhs[:, hs])
                return p_

            for lev in range(3):
                pq = mm4(QT, Q)
                yield
                fw.copy(Qn[:], pq[:], e="dve")
                pqt = mm4(Q, QT)
                yield
                fw.copy(QTn[:], pqt[:], e="act")
                yield
                pp = mm4(QTn, P)
                yield
                fw.tt(Pn[:], pp[:], P[:], ALU.add)
                ppt = mm4(Qn, PT)
                yield
                fw.tt(PTn[:], ppt[:], PT[:], ALU.add)
                yield
                Q, Qn = Qn, Q
                QT, QTn = QTn, QT
                P, Pn = Pn, P
                PT, PTn = PTn, PT
            X, XT = P, PT
            py = mm4(tm["C32T"], X)
            yield
            fw.copy(Qn[:], py[:], e="act")
            pyp = mm4(tm["C32"], XT)
            yield
            fw.copy(QTn[:], pyp[:], e="dve")
            yield
            px = mm4(XT, Qn)
            yield
            fw.tt(Pn[:], X[:], px[:], ALU.subtract)
            pxt = mm4(X, QTn)
            yield
            fw.tt(PTn[:], XT[:], pxt[:], ALU.subtract)
            yield
            X, XT = Pn, PTn
            py = mm4(tm["C64T"], X)
            yield
            fw.copy(Q[:], py[:], e="act")
            yield
            px = mm4(XT, Q)
            yield
            fw.tt(tm["TTb"][:], X[:], px[:], ALU.subtract)
            TTm = tm["TTb"]
            for h in range(4):
                fw.tr(PB[:, h * 128:(h + 1) * 128], qk[:, 4 + h, :], identB)
            for h in range(4):
                fw.tr(PB[:, 512 + h * 128: 512 + (h + 1) * 128], qk[:, 8 + h, :], identB)
            fw.tt(V(v3(tm["kEG"], 128), tm["kEG"].buf), V(PB.h[:, 0:512].rearrange("p (a n) -> p a n", n=128), PB.buf), bc4(G["EG"], tt), ALU.mult)
            fw.tt(V(v3(sl["kd"], 128), sl["kd"].buf), V(PB.h[:, 0:512].rearrange("p (a n) -> p a n", n=128), PB.buf), bc4(G["ER"], tt), ALU.mult)
            fw.copy(tm["vtok"][:], PB[:, 512:1024], e="dve")
            yield
            po = pg()
            for h in range(4):
                hs = slice(h * 128, (h + 1) * 128)
                fw.mm(po[:, hs], tm["kEG"][:, hs], TTm[:, hs])
            yield
            fw.copy(sl["w0T"][:], po[:], e="act")
            po2 = pg()
            for h in range(4):
                hs = slice(h * 128, (h + 1) * 128)
                fw.mm(po2[:, hs], TTm[:, hs], tm["vtok"][:, hs])
            yield
            fw.tt(V(v3(sl["ub"], 128), sl["ub"].buf), V(po2.h[:, :].rearrange("p (a n) -> p a n", n=128), po2.buf), bcb(tt, d), ALU.mult)
            po3 = pg()
            for h in range(4):
                fw.mm(po3[:, h * 128:(h + 1) * 128], qk[:, 4 + h, :], qk[:, h, :])
            yield
            fw.stt(sl["qkdT"][:], po3[:], qscale, tm["dec"][:], ALU.mult, ALU.mult)
            fw.tt(V(v3(sl["qgT"], 128), sl["qgT"].buf), qk[:, 0:4, :], V(v3(tm["EGr"], 128), tm["EGr"].buf), ALU.mult, e="pool")
            yield

        def scan(tt, d, sl):
            G = GA[d]
            BX = PF[3 + d]
            BY = PF[5]
            o = ring(ost[d], "ost%d" % d)
            for ci in ((0, 1) if d == 0 else (1, 0)):
                cs = slice(64 * ci, 64 * ci + 64)
                for h in range(4):
                    hs = slice(h * 128, (h + 1) * 128)
                    fw.mm(BX[cs, hs], sl["w0T"][:, h * 128 + 64 * ci: h * 128 + 64 * ci + 64], Sb[d][:, hs])
                t5v = V(t5[d].h[cs, :].rearrange("p (a n) -> p a n", n=128), t5[d].buf)
                fw.tt(t5v, V(BX.h[cs, :].rearrange("p (a n) -> p a n", n=128), BX.buf), bcb(tt, d, cs), ALU.mult)
                fw.tt(un[d][cs, :], sl["ub"][cs, :], t5[d][cs, :], ALU.subtract)
                yield
                for h in range(4):
                    oc = slice(h * 128 + 64 * ci, h * 128 + 64 * ci + 64)
                    yc_ = slice(d * 256 + h * 64, d * 256 + (h + 1) * 64)
                    hs = slice(h * 128, (h + 1) * 128)
                    fw.mm(BY[:, yc_], Sb[d][:, hs], sl["qgT"][:, oc], start=True, stop=False)
                    fw.mm(BY[:, yc_], un[d][cs, hs], sl["qkdT"][cs, oc], start=False, stop=True)
                fw.copy(V(o.h[:, :].rearrange("p (a n) -> p a n", n=128)[:, :, 64 * ci: 64 * ci + 64], o.buf),
                        V(BY.h[:, d * 256:(d + 1) * 256].rearrange("p (a n) -> p a n", n=64), BY.buf), e="act")
                for h in range(4):
                    hs = slice(h * 128, (h + 1) * 128)
                    fw.mm(BX[:, hs], sl["kd"][cs, hs], un[d][cs, hs])
                gl = G["GL0"] if ci == 0 else G["GL1"]
                fw.tt(V(v3(S32[d], 128), S32[d].buf), V(v3(S32[d], 128), S32[d].buf), bc4(gl, tt), ALU.mult)
                fw.tt(S32[d][:], S32[d][:], BX[:], ALU.add)
                yield
                fw.copy(Sb[d][:], S32[d][:], e="act")
                yield
            fw.dma("pool", V(oT_d[d].h.rearrange("(h p) t -> p h t", p=128)[:, :, tt * 128:(tt + 1) * 128], oT_d[d].buf),
                   V(v3(o, 128), o.buf), semof=o.buf)
            yield

        def run_il(gens):
            gens = list(gens)
            while gens:
                for g_ in list(gens):
                    try:
                        next(g_)
                    except StopIteration:
                        gens.remove(g_)

        order = [[32, 33] + list(range(32)), [33, 32] + list(range(31, -1, -1))]
        run_il([prep(order[d][0], d, slots[d][0], 0) for d in range(2)])
        for s_ in range(NT):
            gens = [scan(order[d][s_], d, slots[d][s_ % 2]) for d in range(2)]
            if s_ + 1 < NT:
                gens += [prep(order[d][s_ + 1], d, slots[d][(s_ + 1) % 2], s_ + 1) for d in range(2)]
            run_il(gens)
        fw.pop()

    def phase_mla(l, with_ctx):
        fw.push()
        cqn = fw.sb("cqn", [128, 3, TT], BF16)
        ckvn = fw.sb("ckvn", [128, 2, TT], BF16)
        ropeC = fw.sb("ropeC", [96, TT], BF16)
        ropeS = fw.sb("ropeS", [96, TT], BF16)
        kper = fw.sb("kper", [96, TT], BF16)
        wuq = fw.sb("wuq", [128, 3, 1536], BF16)
        wukv = fw.sb("wukv", [128, 2, 1024], BF16)
        qn = fw.sb("qn", [128, 3], F32)
        kvn = fw.sb("kvn", [128, 2], F32)
        fw.dma("sp", qn[:], qnorm_d[l])
        fw.dma("sp", kvn[:], kvnorm_d[l])
        fw.push()
        st32 = fw.sb("st32", [128, 4608], F32)
        fw.dma("sp", st32[:, 0:4608], V(w_uq_d.h[l].rearrange("p a n -> p (a n)"), w_uq_d.buf))
        fw.copy(V(wuq.h[:, :, :].rearrange("p a n -> p (a n)"), wuq.buf), st32[:, 0:4608], e="pool")
        fw.dma("sp", st32[:, 0:2048], V(w_ukv_d.h[l].rearrange("p a n -> p (a n)"), w_ukv_d.buf))
        fw.copy(V(wukv.h[:, :, :].rearrange("p a n -> p (a n)"), wukv.buf), st32[:, 0:2048], e="pool")
        fw.dma("sp", st32[0:96, 0:TT], ROPEC_d[:])
        fw.copy(ropeC[:], st32[0:96, 0:TT], e="pool")
        fw.dma("sp", st32[0:96, 0:TT], ROPES_d[:])
        fw.copy(ropeS[:], st32[0:96, 0:TT], e="pool")
        kpa = fw.sb("kpa", [96, TT], BF16)
        kpb = fw.sb("kpb", [96, TT], BF16)
        fw.dma("sp", kpa[64:96, :], pT_d[7296:7328, :])
        fw.dma("sp", kpb[64:96, :], pT_d[7328:7360, :])
        fw.tt(st32[64:96, 0:TT], kpa[64:96, :], ropeC[64:96, :], ALU.mult)
        fw.tt(kpb[64:96, :], kpb[64:96, :], ropeS[64:96, :], ALU.mult)
        fw.tt(kper[64:96, :], st32[64:96, 0:TT], kpb[64:96, :], ALU.add)
        sqr = [fw.sb("msq%d" % i, [128, 512], BF16) for i in range(3)]
        rr = [fw.sb("mrr%d" % i, [128, 512], F32) for i in range(2)]
        for (dst, nb_, row0, nrm, width) in ((cqn, 3, 3072, qn, 384.0), (ckvn, 2, 3456, kvn, 256.0)):
            for b_ in range(nb_):
                fw.dma("sp", dst[:, b_, :], pT_d[row0 + b_ * 128: row0 + (b_ + 1) * 128, :])
            for (t0, n) in CHUNKS:
                ps = pf(0, 5)
                sqs = []
                for b_ in range(nb_):
                    sq = ring(sqr, "msq")
                    fw.tt(sq[:, 0:n], dst[:, b_, t0:t0 + n], dst[:, b_, t0:t0 + n], ALU.mult, e="pool")
                    sqs.append(sq)
                for b_ in range(nb_):
                    fw.mm(ps[:, 0:n], onesB, sqs[b_][:, 0:n], start=(b_ == 0), stop=(b_ == nb_ - 1))
                r = ring(rr, "mrr")
                fw.act(r[:, 0:n], ps[:, 0:n], AF.Ln, bias=EPS, scale=1.0 / width)
                fw.act(r[:, 0:n], r[:, 0:n], AF.Exp, scale=-0.5)
                for b_ in range(nb_):
                    fw.stt(dst[:, b_, t0:t0 + n], dst[:, b_, t0:t0 + n], nrm[:, b_:b_ + 1], r[:, 0:n], ALU.mult, ALU.mult)
        fw.pop()

        kTh = [fw.sb("kTh%d" % i, [96, TT], BF16) for i in range(2)]
        qTh = [fw.sb("qTh%d" % i, [96, TT], BF16) for i in range(2)]
        Vaug = [fw.sb("Vaug%d" % i, [128, NT, 128], BF16) for i in range(2)]
        for i in range(2):
            fw.op("pool", lambda g, i=i: g.memset(Vaug[i].h[:, :, 64:128], 1.0), [], [Vaug[i][:, :, 64:128]])
        PTr = [fw.sb("PTr%d" % i, [128, 512], BF16) for i in range(4)]
        q1 = [fw.sb("q1_%d" % i, [96, 512], F32) for i in range(2)]
        q2 = [fw.sb("q2_%d" % i, [96, 512], F32) for i in range(2)]
        rden = [fw.sb("rden%d" % i, [128, 512], F32) for i in range(2)]
        otmp = [fw.sb("otmp%d" % i, [64, 512], F32) for i in range(2)]
        zc = [fw.sb("zc%d" % i, [64, 512], BF16) for i in range(2)]
        ycs = [fw.sb("ycs%d" % i, [64, 512], BF16) for i in range(2)]
        ascale = 96.0 ** -0.5
        qchunks = CHUNKS if with_ctx else CHUNKS[:8]
        for h in range(8):
            kt_ = kTh[h % 2]
            qt_ = qTh[h % 2]
            va = Vaug[h % 2]
            for (t0, n) in CHUNKS:
                ps = pf(0, 5)
                for kc in range(2):
                    fw.mm(ps[0:64, 0:n], wukv[:, kc, h * 128: h * 128 + 64], ckvn[:, kc, t0:t0 + n], start=(kc == 0), stop=(kc == 1))
                fw.copy(kt_[0:64, t0:t0 + n], ps[0:64, 0:n], e="dve")
            fw.copy(kt_[64:96, :], kper[64:96, :], e="pool")
            for t8 in range(0, NT, 8):
                nt8 = min(8, NT - t8)
                ps = pf(0, 5)
                for a in range(nt8):
                    tt = t8 + a
                    for kc in range(2):
                        fw.mm(ps[:, a * 64:(a + 1) * 64], ckvn[:, kc, tt * 128:(tt + 1) * 128], wukv[:, kc, h * 128 + 64: h * 128 + 128],
                              start=(kc == 0), stop=(kc == 1))
                fw.copy(va[:, t8:t8 + nt8, 0:64], V(ps.h[:, 0:nt8 * 64].rearrange("p (a n) -> p a n", n=64), ps.buf), e="dve")
            for (t0, n) in qchunks:
                pa = pf(0, 5)
                for kc in range(3):
                    fw.mm(pa[0:96, 0:n], wuq[:, kc, h * 96:(h + 1) * 96], cqn[:, kc, t0:t0 + n], start=(kc == 0), stop=(kc == 2))
                pb_ = pf(0, 5)
                for kc in range(3):
                    fw.mm(pb_[0:96, 0:n], wuq[:, kc, 768 + h * 96: 768 + (h + 1) * 96], cqn[:, kc, t0:t0 + n], start=(kc == 0), stop=(kc == 2))
                a1 = ring(q1, "q1")
                a2 = ring(q2, "q2")
                fw.tt(a1[:, 0:n], pa[0:96, 0:n], ropeC[:, t0:t0 + n], ALU.mult)
                fw.tt(a2[:, 0:n], pb_[0:96, 0:n], ropeS[:, t0:t0 + n], ALU.mult)
                fw.tt(qt_[:, t0:t0 + n], a1[:, 0:n], a2[:, 0:n], ALU.add, e="pool")
            for qi, (t0, n) in enumerate(qchunks):
                ktiles = list(range(NT)) if t0 < TL else [32, 33]
                acc = PF[5 + qi % 2]
                zt = ring(zc, "zc")
                fw.dma("sp", zt[:, 0:n], pT_d[3712 + h * 64: 3712 + (h + 1) * 64, t0:t0 + n])
                LA = 3
                pss = {}

                def issue_s(ki_):
                    ps_ = pf(0, 5)
                    kt__ = ktiles[ki_]
                    fw.mm(ps_[:, 0:n], kt_[:, kt__ * 128:(kt__ + 1) * 128], qt_[:, t0:t0 + n])
                    pss[ki_] = ps_

                for ki in range(min(LA, len(ktiles))):
                    issue_s(ki)
                for ki, kt in enumerate(ktiles):
                    ps = pss.pop(ki)
                    pt = ring(PTr, "PTr")
                    fw.act(pt[:, 0:n], ps[:, 0:n], AF.Exp, scale=ascale)
                    fw.mm(acc[:, 0:n], va[:, kt, :], pt[:, 0:n], start=(ki == 0), stop=(ki == len(ktiles) - 1))
                    if ki + LA < len(ktiles):
                        issue_s(ki + LA)
                rd = ring(rden, "rden")
                fw.recip(rd[64:128, 0:n], acc[64:128, 0:n])
                ot = ring(otmp, "otmp")
                fw.tt(ot[:, 0:n], acc[0:64, 0:n], rd[64:128, 0:n], ALU.mult)
                yc = ring(ycs, "ycs")
                fw.tt(yc[:, 0:n], ot[:, 0:n], zt[:, 0:n], ALU.mult, e="pool")
                fw.dma("pool", ycT_d[h * 64:(h + 1) * 64, t0:t0 + n], yc[:, 0:n], semof=yc.buf)
        fw.pop()

    def phase_final(l, with_ctx, last):
        fw.push()
        wbr = fw.sb("wbr", [128, 12, 1024], BF16)
        wout = fw.sb("wout", [128, 8, 1024], BF16)
        dnrm = fw.sb("dnrm", [128, 1], F32)
        fw.dma("sp", dnrm[:], dnorm_d[l])
        fw.push()
        st32 = [fw.sb("fst32_%d" % i, [128, 4096], F32) for i in range(2)]
        for q in range(3):
            st = ring(st32, "fst32")
            fw.dma("sp", st[:], V(w_br_d.h[l, :, q * 4:(q + 1) * 4, :].rearrange("p a n -> p (a n)"), w_br_d.buf))
            fw.copy(V(wbr.h[:, q * 4:(q + 1) * 4, :].rearrange("p a n -> p (a n)"), wbr.buf), st[:], e="pool")
        for q in range(2):
            st = ring(st32, "fst32")
            fw.dma("sp", st[:], V(w_out_d.h[l, :, q * 4:(q + 1) * 4, :].rearrange("p a n -> p (a n)"), w_out_d.buf))
            fw.copy(V(wout.h[:, q * 4:(q + 1) * 4, :].rearrange("p a n -> p (a n)"), wout.buf), st[:], e="pool")
        fw.pop()
        of_ = fw.sb("of", [128, 4, 512], F32)
        ob_ = fw.sb("ob", [128, 4, 512], F32)
        sqb = fw.sb("sqb", [128, 4, 512], BF16)
        rr = [fw.sb("frr%d" % i, [128, 512], F32) for i in range(2)]
        szb = fw.sb("szb", [128, 4, 512], BF16)
        ybT = fw.sb("ybT", [128, 4, 512], BF16)
        yaT = fw.sb("yaTt", [128, 4, 512], BF16)
        ycT = fw.sb("ycTt", [128, 4, 512], BF16)
        gts = [fw.sb("gts%d" % i, [128, 8, 512], BF16) for i in range(2)]
        merged = fw.sb("merged", [128, 8, 512], F32)
        mergedb = fw.sb("mergedb", [128, 8, 512], BF16)
        tmpf = [fw.sb("tmpf%d" % i, [128, 512], F32) for i in range(2)]
        xr = [fw.sb("fxr%d" % i, [128, 1024], F32) for i in range(2)]
        xo = [fw.sb("fxo%d" % i, [128, 1024], F32) for i in range(2)]
        junk = fw.sb("junk", [128, 512], BF16)
        st4 = fw.sb("st4", [128, 8 * NT], F32, nsub=NT)
        xsrc = xin if l == 0 else xs_d
        chunks = CHUNKS if with_ctx else CHUNKS[:8]
        for (t0, n) in chunks:
            j = 0 if t0 < TL else 1
            for d, dstt in ((0, of_), (1, ob_)):
                fw.dma("sp", dstt[:, :, 0:n], V(oT_d[d].h.rearrange("(h p) t -> p h t", p=128)[:, :, t0:t0 + n], oT_d[d].buf))
            fw.dma("sp", szb[:, :, 0:n], V(pT_d.h[2560:3072, :].rearrange("(g p) t -> p g t", p=128)[:, :, t0:t0 + n], pT_d.buf))
            fw.dma("sp", yaT[:, :, 0:n], V(yaT_d.h.rearrange("(g p) t -> p g t", p=128)[:, :, t0:t0 + n], yaT_d.buf))
            fw.dma("sp", ycT[:, :, 0:n], V(ycT_d.h.rearrange("(g p) t -> p g t", p=128)[:, :, t0:t0 + n], ycT_d.buf))
            fw.tt(of_[:, :, 0:n], of_[:, :, 0:n], ob_[:, :, 0:n], ALU.add, e="pool")
            fw.act(sqb[:, :, 0:n], of_[:, :, 0:n], AF.Square)
            for h in range(4):
                ps = pf()
                fw.mm(ps[:, 0:n], onesB, sqb[:, h, 0:n])
                r = ring(rr, "frr")
                fw.act(r[:, 0:n], ps[:, 0:n], AF.Ln, bias=EPS, scale=1.0 / 128)
                fw.act(r[:, 0:n], r[:, 0:n], AF.Exp, scale=-0.5)
                fw.stt(r[:, 0:n], of_[:, h, 0:n], dnrm[:, 0:1], r[:, 0:n], ALU.mult, ALU.mult)
                fw.tt(ybT[:, h, 0:n], r[:, 0:n], szb[:, h, 0:n], ALU.mult, e="pool")
            for br, src in enumerate((yaT, ybT, ycT)):
                gt = ring(gts, "gts")
                fw.dma("sp", gt[:, :, 0:n], V(pT_d.h[4224 + br * 1024: 4224 + (br + 1) * 1024, :].rearrange("(g p) t -> p g t", p=128)[:, :, t0:t0 + n], pT_d.buf))
                for jb in range(8):
                    ps = pf()
                    for kc in range(4):
                        fw.mm(ps[:, 0:n], wbr[:, br * 4 + kc, jb * 128:(jb + 1) * 128], src[:, kc, 0:n], start=(kc == 0), stop=(kc == 3))
                    if br == 0:
                        fw.tt(merged[:, jb, 0:n], ps[:, 0:n], gt[:, jb, 0:n], ALU.mult)
                    else:
                        tf = ring(tmpf, "tmpf")
                        fw.tt(tf[:, 0:n], ps[:, 0:n], gt[:, jb, 0:n], ALU.mult)
                        fw.tt(merged[:, jb, 0:n], merged[:, jb, 0:n], tf[:, 0:n], ALU.add, e="pool")
            fw.copy(mergedb[:, :, 0:n], merged[:, :, 0:n], e="act")
            for ts_ in range(n // 128):
                tt = (t0 // 128) + ts_
                xt = ring(xr, "fxr")
                fw.dma("sp", xt[:], xsrc[tt * 128:(tt + 1) * 128, :])
                st = st4.sub(tt)
                c0 = 8 * tt
                phs = []
                for half in range(2):
                    ps = pf()
                    for kc in range(8):
                        fw.mm(ps[:], mergedb[:, kc, ts_ * 128:(ts_ + 1) * 128], wout[:, kc, half * 512:(half + 1) * 512], start=(kc == 0), stop=(kc == 7))
                    fw.act(junk[:], ps[:], AF.Square, accum_out=st[:, c0 + half: c0 + half + 1])
                    phs.append(ps)
                fw.tt(st[:, c0 + 2:c0 + 3], st[:, c0:c0 + 1], st[:, c0 + 1:c0 + 2], ALU.add)
                fw.act(st[:, c0 + 3:c0 + 4], st[:, c0 + 2:c0 + 3], AF.Sqrt, bias=EPS, scale=1.0 / 1024)
                fw.recip(st[:, c0 + 4:c0 + 5], st[:, c0 + 3:c0 + 4])
                o = ring(xo, "fxo")
                for half in range(2):
                    hs = slice(half * 512, (half + 1) * 512)
                    fw.stt(o[:, hs], phs[half][:], st[:, c0 + 4:c0 + 5], ggbc[j][:, hs], ALU.mult, ALU.mult)
                fw.tt(o[:], o[:], xt[:], ALU.add, e="pool")
                if last:
                    fw.dma("pool", out_d[tt * 128:(tt + 1) * 128, :], o[:], semof=o.buf)
                else:
                    fw.dma("pool", xs_d[tt * 128:(tt + 1) * 128, :], o[:], semof=o.buf)
        fw.pop()

    for l in range(n_layers):
        alloc_s1()
        phase_mod(l)
        phase_h(l)
        if "hT" in dbg and l == 0:
            dbg_d["hT"] = fw.dram("hT_o", [128, 8, TT], BF16, "ExternalOutput")
            fw.dma("pool", dbg_d["hT"][:], S["hT"][:], semof=S["hT"].buf)
        if stop_after == "h":
            fw.pop()
            break
        phase_proj(l)
        fw.pop()
        if stop_after == "proj":
            break
        if "nofourier" not in dbg:
            phase_fourier(l, with_ctx=(l < n_layers - 1 or "ctxall" in dbg))
        if stop_after == "fourier":
            break
        if "nogdn" not in dbg:
            phase_gdn(l)
        if stop_after == "gdn":
            break
        wc = (l < n_layers - 1 or "ctxall" in dbg)
        phase_mla(l, with_ctx=wc)
        if stop_after == "mla":
            break
        phase_final(l, with_ctx=wc, last=(l == n_layers - 1 and "ctxall" not in dbg))

    if "abT" in dbg:
        dbg_d["abT"] = fw.dram("abT_o", [128, NT, 16], F32, "ExternalOutput")
        fw.dma("pool", dbg_d["abT"][:], abT[:], semof=abT.buf)
    outs = [out_d, xs_d, pT_d, yaT_d, ycT_d, qkvT_d] + oT_d + list(dbg_d.values())
    fw.finish(outs, e="pool")
    return nc, fw


def kernel(**inputs):
    inp = {k: np.asarray(v) for k, v in inputs.items()}
    consts = host_constants()
    shared = host_layout_shared(inp)
    nc, fw = build()
    in_maps = []
    for b in range(8):
        m = dict(consts)
        m.update(shared)
        m.update(host_layout(inp, b))
        in_maps.append(m)
    res = run_bass_kernel_spmd(nc, in_maps, core_ids=list(range(8)))
    return np.stack([np.asarray(r["out"]) for r in res.results], axis=0).astype(np.float32)
```

```python
import math
import numpy as np
import ml_dtypes
import concourse.bass as bass
import concourse.mybir as mybir
from concourse.bass_utils import run_bass_kernel_spmd

F32 = mybir.dt.float32
BF16 = mybir.dt.bfloat16
AF = mybir.ActivationFunctionType
ALU = mybir.AluOpType

TL = 4096
TC = 256
TT = TL + TC
NT = TT // 128
CHUNKS = [(i * 512, 512) for i in range(8)] + [(TL, TC)]
NB = 58
EPS = 1e-6
NEG = -30000.0
CHDT = BF16


class Buf:
    __slots__ = ("name", "writers", "readers", "waw", "dsem", "excl")

    def __init__(self, name, waw=True):
        self.name = name
        self.excl = False
        self.writers = {}
        self.readers = {}
        self.waw = waw
        self.dsem = None


class V:
    __slots__ = ("ap", "buf")

    def __init__(self, ap, buf):
        self.ap = ap
        self.buf = buf


class T:
    def __init__(self, handle, buf, bufs=None):
        self.h = handle
        self.buf = buf
        self.bufs = bufs

    def __getitem__(self, key):
        return V(self.h[key], self.buf)

    def sub(self, i):
        return T(self.h, self.bufs[i])


class FW:
    def __init__(self, nc, strict_same=True):
        self.nc = nc
        self.eng = {"pe": nc.tensor, "dve": nc.vector, "act": nc.scalar, "pool": nc.gpsimd, "sp": nc.sync}
        self.sems = {}
        self.count = {}
        for e in self.eng:
            self.sems[e] = nc.alloc_semaphore("s_" + e)
            self.count[e] = 0
        self.seen = {e: {} for e in self.eng}
        self.strict_same = strict_same
        self.nops = {e: 0 for e in self.eng}
        self.nwait = 0
        self.uid = 0
        self.scopes = [[]]
        self.scope_bufs = [[]]
        self.free_dsems = []
        self.bar_tile = V(nc.alloc_sbuf_tensor("bar_tile", [128, 8], F32)[:, :], Buf("bar"))

    def sb(self, name, shape, dtype, nsub=0, waw=True):
        self.uid += 1
        name = "%s_u%d" % (name, self.uid)
        g = self.nc.sbuf_tensor(name, list(shape), dtype)
        h = g.__enter__()
        self.scopes[-1].append(g)
        bufs = [Buf(name + "_%d" % i, waw) for i in range(nsub)] if nsub else None
        t = T(h, Buf(name, waw), bufs)
        self.scope_bufs[-1].extend([t.buf] + (bufs or []))
        return t

    def push(self):
        self.scopes.append([])
        self.scope_bufs.append([])

    def pop(self):
        self.barrier()
        for g in reversed(self.scopes.pop()):
            g.__exit__(None, None, None)
        for b in self.scope_bufs.pop():
            if b.dsem is not None:
                self.free_dsems.append(b.dsem)
                b.dsem = None

    def barrier(self):
        need = {k: v for k, v in self.count.items() if v > 0}
        self._waits("pool", need)
        ins = self.eng["pool"].memset(self.bar_tile.ap, 0.0)
        self.count["pool"] += 1
        ins.then_inc(self.sems["pool"], 1)
        val = self.count["pool"]
        for e in self.eng:
            if e == "pool":
                continue
            self._waits(e, {"pool": val})
            for k, v in need.items():
                self.seen[e][k] = max(self.seen[e].get(k, 0), v)

    def ps(self, name, shape, dtype=F32):
        h = self.nc.alloc_psum_tensor(name, list(shape), dtype)
        b = Buf(name)
        b.excl = True
        return T(h, b)

    def dram(self, name, shape, dtype, kind="Internal", nsub=0):
        h = self.nc.dram_tensor(name, list(shape), dtype, kind=kind)
        bufs = [Buf(name + "_%d" % i, False) for i in range(nsub)] if nsub else None
        return T(h.ap(), Buf(name, waw=False), bufs)

    def _need(self, reads, writes):
        need = {}
        for v in reads:
            for k, val in v.buf.writers.items():
                if need.get(k, 0) < val:
                    need[k] = val
            if v.buf.excl:
                for k, val in v.buf.readers.items():
                    if need.get(k, 0) < val:
                        need[k] = val
        for v in writes:
            b = v.buf
            if b.waw:
                for k, val in b.writers.items():
                    if need.get(k, 0) < val:
                        need[k] = val
            for k, val in b.readers.items():
                if need.get(k, 0) < val:
                    need[k] = val
        return need

    def _waits(self, e, need):
        seen = self.seen[e]
        for k, val in need.items():
            if k == e and (e == "pe" or not self.strict_same):
                continue
            if seen.get(k, 0) >= val:
                continue
            self.eng[e].wait_ge(self.sems[k], val)
            seen[k] = val
            self.nwait += 1

    def op(self, e, fn, reads=(), writes=()):
        self._waits(e, self._need(reads, writes))
        ins = fn(self.eng[e])
        self.count[e] += 1
        val = self.count[e]
        ins.then_inc(self.sems[e], 1)
        self.nops[e] += 1
        for v in reads:
            b = v.buf
            if b.readers.get(e, 0) < val:
                b.readers[e] = val
        for v in writes:
            b = v.buf
            if b.waw:
                b.writers = {e: val}
            else:
                b.writers[e] = val
            b.readers = {}
        return ins

    def dma(self, q, out, in_, semof=None, **kw):
        sbuf = semof if semof is not None else out.buf
        if sbuf.dsem is None:
            if self.free_dsems:
                key = self.free_dsems.pop()
            else:
                key = "d%d" % len(self.sems)
                self.sems[key] = self.nc.alloc_semaphore(key)
                self.count[key] = 0
            sbuf.dsem = key
        key = sbuf.dsem
        self._waits(q, self._need([in_], [out]))
        ins = self.eng[q].dma_start(out=out.ap, in_=in_.ap, **kw)
        self.count[key] += 16
        val = self.count[key]
        ins.then_inc(self.sems[key], 16)
        self.nops[q] += 1
        b = in_.buf
        if b.readers.get(key, 0) < val:
            b.readers[key] = val
        b = out.buf
        if b.waw:
            b.writers = {key: val}
        else:
            b.writers[key] = val
        b.readers = {}
        return ins

    def finish(self, bufs, e="sp"):
        need = {}
        for b in bufs:
            for k, val in b.buf.writers.items():
                if need.get(k, 0) < val:
                    need[k] = val
        self._waits(e, need)

    def mm(self, out, lhsT, rhs, start=True, stop=True):
        reads = [lhsT, rhs] + ([] if start else [out])
        return self.op("pe", lambda e: e.matmul(out.ap, lhsT.ap, rhs.ap, start=start, stop=stop), reads, [out])

    def tr(self, out, in_, ident):
        return self.op("pe", lambda e: e.transpose(out.ap, in_.ap, ident.ap), [in_, ident], [out])

    def act(self, out, in_, func, bias=None, scale=None, accum_out=None):
        reads = [in_]
        kw = {}
        if bias is not None:
            if isinstance(bias, V):
                reads.append(bias)
                kw["bias"] = bias.ap
            else:
                kw["bias"] = bias
        if scale is not None:
            if isinstance(scale, V):
                reads.append(scale)
                kw["scale"] = scale.ap
            else:
                kw["scale"] = scale
        writes = [out]
        if accum_out is not None:
            kw["accum_out"] = accum_out.ap
            writes.append(accum_out)
        return self.op("act", lambda e: e.activation(out.ap, in_.ap, func, **kw), reads, writes)

    def tt(self, out, in0, in1, op, e="dve"):
        return self.op(e, lambda g: g.tensor_tensor(out.ap, in0.ap, in1.ap, op), [in0, in1], [out])

    def ts(self, out, in0, s1, s2=None, op0=ALU.mult, op1=None, e="dve"):
        reads = [in0]
        a1 = s1
        if isinstance(s1, V):
            reads.append(s1)
            a1 = s1.ap
        a2 = s2
        if isinstance(s2, V):
            reads.append(s2)
            a2 = s2.ap
        if op1 is None:
            return self.op(e, lambda g: g.tensor_scalar(out.ap, in0.ap, a1, None, op0), reads, [out])
        return self.op(e, lambda g: g.tensor_scalar(out.ap, in0.ap, a1, a2, op0, op1), reads, [out])

    def stt(self, out, in0, s, in1, op0, op1):
        reads = [in0, in1]
        a = s
        if isinstance(s, V):
            reads.append(s)
            a = s.ap
        return self.op("dve", lambda g: g.scalar_tensor_tensor(out.ap, in0.ap, a, in1.ap, op0, op1), reads, [out])

    def copy(self, out, in_, e="dve"):
        if e == "act":
            return self.op("act", lambda g: g.activation(out.ap, in_.ap, AF.Copy), [in_], [out])
        return self.op(e, lambda g: g.tensor_copy(out.ap, in_.ap), [in_], [out])

    def recip(self, out, in_):
        return self.op("dve", lambda g: g.reciprocal(out.ap, in_.ap), [in_], [out])


def _bf(a):
    return np.ascontiguousarray(a).astype(ml_dtypes.bfloat16)


_CONST = {}


def host_constants():
    if _CONST:
        return _CONST
    c = {}
    t = np.arange(TL, dtype=np.int64)
    ph = (np.outer(t, t) % TL).astype(np.float64) * (2 * np.pi / TL)
    c["CL"] = _bf(np.cos(ph))
    c["NSL"] = _bf(-np.sin(ph))
    tcx = np.arange(TC, dtype=np.int64)
    phc = (np.outer(tcx, tcx) % TC).astype(np.float64) * (2 * np.pi / TC)
    c["CLC"] = _bf(np.cos(phc))
    c["NSLC"] = _bf(-np.sin(phc))
    ch = np.arange(128, dtype=np.int64)
    phd = (np.outer(ch, ch) % 128).astype(np.float64) * (2 * np.pi / 128)
    c["CSC"] = _bf(np.concatenate([np.cos(phd), np.sin(phd)], axis=1))
    rows = np.repeat(np.arange(64, dtype=np.float32), 64)
    cols = np.tile(np.arange(64, dtype=np.float32), 64)
    inv = (10000.0 ** (-np.arange(8, dtype=np.float32) / 8)).astype(np.float32)
    ang_r = rows[:, None] * inv
    ang_c = cols[:, None] * inv
    ang = np.concatenate([ang_r, ang_r, ang_c, ang_c], axis=-1)
    sgn = np.array([-1.0] * 8 + [1.0] * 8 + [-1.0] * 8 + [1.0] * 8, dtype=np.float32)
    C = np.ones((96, TT), np.float32)
    S = np.zeros((96, TT), np.float32)
    C[64:96, :TL] = np.cos(ang).T
    S[64:96, :TL] = (np.sin(ang) * sgn[None, :]).T
    c["ROPEC"] = C
    c["ROPES"] = S
    k = np.arange(128)[:, None]
    m = np.arange(128)[None, :]
    same = (k // 64) == (m // 64)
    ident = (k == m).astype(np.float32)
    ones = np.ones((128, 128), np.float32)
    triF = (same & (k <= m)).astype(np.float32)
    restF = (same & (k > m)).astype(np.float32)
    triB = (same & (k >= m)).astype(np.float32)
    restB = (same & (k < m)).astype(np.float32)
    tot0 = np.broadcast_to((k < 64), (128, 128)).astype(np.float32)
    tot1 = np.broadcast_to((k >= 64), (128, 128)).astype(np.float32)
    j = k
    i = m
    nmF = np.where(same & (i >= j), 0.0, NEG).astype(np.float32)
    nmB = np.where(same & (i <= j), 0.0, NEG).astype(np.float32)
    stF = (same & (i > j)).astype(np.float32)
    stB = (same & (i < j)).astype(np.float32)
    c["MSK"] = np.ascontiguousarray(np.concatenate(
        [ident, ones, triF, restF, tot0, tot1, triB, restB,
         np.tile(nmF, (1, 4)), np.tile(nmB, (1, 4)), np.tile(ident, (1, 4))], axis=1)).astype(np.float32)
    d16 = ((k // 16) == (m // 16)).astype(np.float32)
    c32 = (((k // 32) == (m // 32)) & ((k // 16) != (m // 16))).astype(np.float32)
    c64 = (same & ((k // 32) != (m // 32))).astype(np.float32)
    c["MSKB"] = _bf(np.concatenate([ident, ones, np.tile(stF, (1, 4)), np.tile(stB, (1, 4)), np.tile(ident, (1, 4)),
                                    np.tile(d16, (1, 4)), np.tile(c32, (1, 4)), np.tile(c64, (1, 4))], axis=1))
    _CONST.update(c)
    return c


IN_W = (512, 512, 512, 512, 512, 512, 16, 384, 256, 32, 512, 3072)
IN_OFF = np.concatenate([[0], np.cumsum(IN_W)]).tolist()


def _col(v, nblk):
    return np.ascontiguousarray(v.reshape(nblk, 128).T)


def host_layout(inp, b):
    d = {}
    d["xin"] = np.ascontiguousarray(np.concatenate([inp["x"][b], inp["ctx"][b]], axis=0))
    cc = np.stack([_col(inp["c"][b], 8), _col(inp["c_ctx"], 8)], axis=-1)
    d["ccol"] = np.ascontiguousarray(cc)
    return d


def host_layout_shared(inp):
    d = {}
    o = IN_OFF
    perm = np.concatenate([np.arange(8, 16), np.arange(0, 8), np.arange(24, 32), np.arange(16, 24)])
    w_in = inp["w_in"]
    kpe = w_in[:, :, o[9]:o[10]]
    cols = np.concatenate([
        w_in[:, :, o[0]:o[6]],
        w_in[:, :, o[7]:o[9]],
        w_in[:, :, o[10]:o[11]],
        w_in[:, :, o[11]:o[12]],
        kpe, kpe[:, :, perm],
        np.zeros((2, 1024, 64), np.float32),
    ], axis=-1)
    assert cols.shape[-1] == NB * 128
    d["w_in_r"] = np.ascontiguousarray(cols.reshape(2, 8, 128, NB, 128).transpose(0, 3, 2, 1, 4))
    wab = w_in[:, :, o[6]:o[7]]
    d["w_ab"] = np.ascontiguousarray(wab.reshape(2, 8, 128, 16).transpose(0, 2, 1, 3))
    d["w_mod"] = inp["w_mod"]
    d["bmod"] = np.ascontiguousarray(np.stack([_col(inp["b_mod"][l], 24) for l in range(2)]))
    d["gpre"] = np.ascontiguousarray(np.stack([_col(inp["g_pre"][l], 8) for l in range(2)]))
    d["gpost"] = np.ascontiguousarray(np.stack([_col(inp["g_post"][l], 8) for l in range(2)]))
    d["f_w"] = np.ascontiguousarray(inp["f_w"].transpose(0, 2, 1, 3))
    cw = inp["dn_conv"]
    d["convw"] = np.ascontiguousarray(cw.reshape(2, 3, 12, 128).transpose(0, 3, 2, 1))
    d["alog"] = np.ascontiguousarray(np.broadcast_to(inp["dn_a_log"].reshape(2, 1, 1, 8), (2, 128, NT, 8)))
    d["dtb"] = np.ascontiguousarray(np.broadcast_to(inp["dn_dt_bias"].reshape(2, 1, 1, 8), (2, 128, NT, 8)))
    d["dnorm"] = np.ascontiguousarray(inp["dn_norm"].reshape(2, 128, 1))
    d["qnorm"] = np.ascontiguousarray(np.stack([_col(inp["mla_q_norm"][l], 3) for l in range(2)]))
    d["kvnorm"] = np.ascontiguousarray(np.stack([_col(inp["mla_kv_norm"][l], 2) for l in range(2)]))
    wuq = inp["mla_w_uq"]
    hp = np.concatenate([np.arange(64), 64 + perm])
    permc = np.concatenate([h * 96 + hp for h in range(8)])
    both = np.concatenate([wuq, wuq[:, :, permc]], axis=-1)
    d["w_uq"] = np.ascontiguousarray(both.reshape(2, 3, 128, 1536).transpose(0, 2, 1, 3))
    d["w_ukv"] = np.ascontiguousarray(inp["mla_w_ukv"].reshape(2, 2, 128, 1024).transpose(0, 2, 1, 3))
    d["w_br"] = np.ascontiguousarray(inp["w_branch"].reshape(2, 12, 128, 1024).transpose(0, 2, 1, 3))
    d["w_out"] = np.ascontiguousarray(inp["w_out"].reshape(2, 8, 128, 1024).transpose(0, 2, 1, 3))
    return d


def build(n_layers=2, dbg=(), stop_after=None):
    nc = bass.Bass("TRN2", target_bir_lowering=False)
    fw = FW(nc)
    EI = "ExternalInput"

    def skind(name):
        return "ExternalOutput" if name in dbg else "Internal"

    xin = fw.dram("xin", [TT, 1024], F32, EI)
    ccol_d = fw.dram("ccol", [128, 8, 2], F32, EI)
    w_in_d = fw.dram("w_in_r", [2, NB, 128, 8, 128], F32, EI)
    w_ab_d = fw.dram("w_ab", [2, 128, 8, 16], F32, EI)
    w_mod_d = fw.dram("w_mod", [2, 1024, 3072], F32, EI)
    bmod_d = fw.dram("bmod", [2, 128, 24], F32, EI)
    gpre_d = fw.dram("gpre", [2, 128, 8], F32, EI)
    gpost_d = fw.dram("gpost", [2, 128, 8], F32, EI)
    f_w_d = fw.dram("f_w", [2, 128, 4, 128], F32, EI)
    convw_d = fw.dram("convw", [2, 128, 12, 3], F32, EI)
    alog_d = fw.dram("alog", [2, 128, NT, 8], F32, EI)
    dtb_d = fw.dram("dtb", [2, 128, NT, 8], F32, EI)
    dnorm_d = fw.dram("dnorm", [2, 128, 1], F32, EI)
    qnorm_d = fw.dram("qnorm", [2, 128, 3], F32, EI)
    kvnorm_d = fw.dram("kvnorm", [2, 128, 2], F32, EI)
    w_uq_d = fw.dram("w_uq", [2, 128, 3, 1536], F32, EI)
    w_ukv_d = fw.dram("w_ukv", [2, 128, 2, 1024], F32, EI)
    w_br_d = fw.dram("w_br", [2, 128, 12, 1024], F32, EI)
    w_out_d = fw.dram("w_out", [2, 128, 8, 1024], F32, EI)
    CL_d = fw.dram("CL", [TL, TL], BF16, EI)
    NSL_d = fw.dram("NSL", [TL, TL], BF16, EI)
    CLC_d = fw.dram("CLC", [TC, TC], BF16, EI)
    NSLC_d = fw.dram("NSLC", [TC, TC], BF16, EI)
    CSC_d = fw.dram("CSC", [128, 256], BF16, EI)
    ROPEC_d = fw.dram("ROPEC", [96, TT], F32, EI)
    ROPES_d = fw.dram("ROPES", [96, TT], F32, EI)
    MSK_d = fw.dram("MSK", [128, 8 * 128 + 1536], F32, EI)
    MSKB_d = fw.dram("MSKB", [128, 2 * 128 + 6 * 512], BF16, EI)

    out_d = fw.dram("out", [TL, 1024], F32, "ExternalOutput")
    xs_d = fw.dram("xs", [TT, 1024], F32, skind("xs"))
    pT_d = fw.dram("pT", [NB * 128, TT], BF16, skind("pT"))
    yaT_d = fw.dram("yaT", [512, TT], BF16, skind("yaT"))
    ycT_d = fw.dram("ycT", [512, TT], BF16, skind("ycT"))
    oT_d = [fw.dram("oT%d" % d, [512, TT], F32, skind("oT%d" % d)) for d in range(2)]
    qkvT_d = fw.dram("qkvT", [1536, TT], BF16, skind("qkvT"))
    dbg_d = {}

    msk = fw.sb("msk", [128, 8 * 128 + 1536], F32)
    mskb = fw.sb("mskb", [128, 2 * 128 + 6 * 512], BF16)
    fw.dma("sp", msk[:], MSK_d[:])
    fw.dma("sp", mskb[:], MSKB_d[:])

    def mcol(i):
        return msk[:, i * 128:(i + 1) * 128]
    identF, onesF, triF, restF, tot0, tot1, triB, restB = [mcol(i) for i in range(8)]
    negmask = [msk[:, 1024:1536], msk[:, 1536:2048]]
    ident4F = msk[:, 2048:2560]
    identB = mskb[:, 0:128]
    onesB = mskb[:, 128:256]
    strictB = [mskb[:, 256:768], mskb[:, 768:1280]]
    ident4B = mskb[:, 1280:1792]
    mD16 = mskb[:, 1792:2304]
    mC32 = mskb[:, 2304:2816]
    mC64 = mskb[:, 2816:3328]

    PF = [fw.ps("pf%d" % i, [128, 512], F32) for i in range(7)]
    PB = fw.ps("pb", [128, 1024], BF16)

    ccol = fw.sb("ccol_sb", [128, 8, 2], F32)
    sc = fw.sb("sc", [128, 8, 2], F32)
    modc = fw.sb("modc", [128, 24, 2], F32)
    Acol = fw.sb("Acol", [128, 8, 2], F32)
    ggcol = fw.sb("ggcol", [128, 8, 2], F32)
    ggbc = [fw.sb("ggbc%d" % j, [128, 1024], F32) for j in range(2)]
    bmod = fw.sb("bmod_sb", [128, 24], F32)
    gpre = fw.sb("gpre_sb", [128, 8], F32)
    gpost = fw.sb("gpost_sb", [128, 8], F32)
    abT = fw.sb("abT", [128, NT, 16], F32)
    stat = fw.sb("stat", [128, 4 * NT], F32, nsub=NT)
    fw.dma("sp", ccol[:], ccol_d[:])

    ring_ctr = {}

    def ring(lst, key):
        i = ring_ctr.get(key, 0)
        ring_ctr[key] = i + 1
        return lst[i % len(lst)]

    pfc = [0]

    def pf(lo=0, hi=7):
        i = pfc[0]
        pfc[0] += 1
        return PF[lo + i % (hi - lo)]

    S = {}

    def alloc_s1():
        fw.push()
        S["hT"] = fw.sb("hT", [128, 8, TT], BF16)
        S["w32"] = [fw.sb("w32_%d" % i, [128, 4096], F32) for i in range(2)]
        S["wbf"] = [fw.sb("wbf_%d" % i, [128, 1024], BF16) for i in range(3)]
        S["stg"] = [fw.sb("stg_%d" % i, [128, TT], BF16) for i in range(3)]
        S["xring"] = [fw.sb("xr%d" % i, [128, 1024], F32) for i in range(3)]
        S["hnring"] = [fw.sb("hn%d" % i, [128, 1024], BF16) for i in range(2)]

    def phase_mod(l):
        fw.dma("sp", bmod[:], bmod_d[l])
        fw.dma("sp", gpre[:], gpre_d[l])
        fw.dma("sp", gpost[:], gpost_d[l])
        fw.act(sc[:], ccol[:], AF.Silu)
        pm = PF[0]
        wv = w_mod_d.h[l].rearrange("(kc k) n -> k kc n", k=128)
        for nch in range(6):
            slot = ring(S["w32"], "w32")
            sv = V(slot.h[:, :].rearrange("p (kc n) -> p kc n", kc=8), slot.buf)
            fw.dma("sp", sv, V(wv[:, :, nch * 512:(nch + 1) * 512], w_mod_d.buf))
            for j in range(4):
                blk = nch * 4 + j
                for kc in range(8):
                    fw.mm(pm[:, blk * 2:blk * 2 + 2],
                          V(slot.h[:, kc * 512 + j * 128: kc * 512 + (j + 1) * 128], slot.buf),
                          sc[:, kc, :], start=(kc == 0), stop=(kc == 7))
        for j in range(2):
            fw.tt(modc[:, :, j], V(pm.h[:, 0:48].rearrange("p (b j) -> p b j", j=2)[:, :, j], pm.buf), bmod[:], ALU.add)
            fw.stt(Acol[:, :, j], modc[:, 8:16, j], 1.0, gpre[:], ALU.add, ALU.mult)
            fw.tt(ggcol[:, :, j], modc[:, 16:24, j], gpost[:], ALU.mult)
        for j in range(2):
            for half in range(2):
                pg = pf(1, 7)
                for q in range(4):
                    kc = half * 4 + q
                    D = ring(S["w32"], "w32")
                    fw.ts(D[:, 0:128], identF, ggcol[:, kc, j:j + 1])
                    fw.mm(pg[:, q * 128:(q + 1) * 128], onesF, D[:, 0:128])
                fw.copy(ggbc[j][:, half * 512:(half + 1) * 512], pg[:])

    def phase_h(l):
        src = xin if l == 0 else xs_d
        for tt in range(NT):
            j = 0 if tt < 32 else 1
            xt = ring(S["xring"], "xr")
            fw.dma("sp", xt[:], src[tt * 128:(tt + 1) * 128, :])
            st = stat.sub(tt)
            hn = ring(S["hnring"], "hn")
            fw.act(hn[:], xt[:], AF.Square, accum_out=st[:, 4 * tt:4 * tt + 1])
            fw.act(st[:, 4 * tt + 1:4 * tt + 2], st[:, 4 * tt:4 * tt + 1], AF.Sqrt, bias=EPS, scale=1.0 / 1024)
            fw.recip(st[:, 4 * tt + 2:4 * tt + 3], st[:, 4 * tt + 1:4 * tt + 2])
            fw.ts(hn[:], xt[:], st[:, 4 * tt + 2:4 * tt + 3])
            for kc in range(8):
                fw.tr(PB[:, kc * 128:(kc + 1) * 128], hn[:, kc * 128:(kc + 1) * 128], identB)
            for kc in range(8):
                o = S["hT"][:, kc, tt * 128:(tt + 1) * 128]
                i = PB[:, kc * 128:(kc + 1) * 128]
                if kc % 2 == 0:
                    fw.ts(o, i, Acol[:, kc, j:j + 1], modc[:, kc, j:j + 1], ALU.mult, ALU.add)
                else:
                    fw.act(o, i, AF.Identity, bias=modc[:, kc, j:j + 1], scale=Acol[:, kc, j:j + 1])

    SILU_BLK = list(range(4, 8)) + list(range(20, 24)) + list(range(29, 33))
    SIG_BLK = list(range(33, 57))
    COPY_BLK = [b for b in range(NB) if b not in SILU_BLK and b not in SIG_BLK]

    def phase_proj(l):
        wab32 = ring(S["w32"], "w32")
        fw.dma("sp", wab32[:, 0:128], V(w_ab_d.h[l].rearrange("p kc n -> p (kc n)"), w_ab_d.buf))
        wabb = ring(S["wbf"], "wbf")
        fw.copy(wabb[:, 0:128], wab32[:, 0:128], e="pool")
        pa = [PF[5], PF[6]]
        for tt in range(NT):
            dst = pa[0][:, tt * 16:(tt + 1) * 16] if tt < 32 else pa[1][:, (tt - 32) * 16:(tt - 31) * 16]
            for kc in range(8):
                fw.mm(dst, S["hT"][:, kc, tt * 128:(tt + 1) * 128], wabb[:, kc * 16:(kc + 1) * 16], start=(kc == 0), stop=(kc == 7))
        fw.copy(V(abT.h[:, 0:32, :].rearrange("p a b -> p (a b)"), abT.buf), pa[0][:, 0:512])
        fw.copy(V(abT.h[:, 32:34, :].rearrange("p a b -> p (a b)"), abT.buf), pa[1][:, 0:32])
        for blk in COPY_BLK + SILU_BLK + SIG_BLK:
            ws = ring(S["w32"], "w32")
            fw.dma("sp", ws[:, 0:1024], V(w_in_d.h[l, blk].rearrange("p kc n -> p (kc n)"), w_in_d.buf))
            wb = ring(S["wbf"], "wbf")
            fw.copy(wb[:, 0:1024], ws[:, 0:1024], e="pool")
            sg = ring(S["stg"], "stg")
            for ci, (t0, n) in enumerate(CHUNKS):
                ps = pf(0, 5)
                for kc in range(8):
                    fw.mm(ps[:, 0:n], wb[:, kc * 128:(kc + 1) * 128], S["hT"][:, kc, t0:t0 + n], start=(kc == 0), stop=(kc == 7))
                if blk in SILU_BLK:
                    fw.act(sg[:, t0:t0 + n], ps[:, 0:n], AF.Silu)
                elif blk in SIG_BLK:
                    fw.act(sg[:, t0:t0 + n], ps[:, 0:n], AF.Sigmoid)
                else:
                    fw.copy(sg[:, t0:t0 + n], ps[:, 0:n])
            fw.dma("pool", pT_d[blk * 128:(blk + 1) * 128, :], sg[:], semof=sg.buf)


    def v3(t, n):
        return t.h[:, :].rearrange("p (a n) -> p a n", n=n)

    def phase_fourier(l, with_ctx):
        fw.push()
        UT = fw.sb("UT", [128, 4, TT], BF16)
        ABs = fw.sb("ABs", [128, NT, 1024], BF16)
        csc = fw.sb("csc", [128, 256], BF16)
        fw32 = fw.sb("fw32", [128, 512], F32)
        fwb = fw.sb("fwb", [128, 512], BF16)
        tbC = [fw.sb("tbC%d" % i, [128, 8, 512], BF16) for i in range(2)]
        tbS = [fw.sb("tbS%d" % i, [128, 8, 512], BF16) for i in range(2)]
        specb = [fw.sb("specb%d" % i, [128, 512], BF16) for i in range(2)]
        szr = [fw.sb("szr%d" % i, [128, 4, 512], BF16) for i in range(2)]
        yast = [fw.sb("yast%d" % i, [128, 4, 512], BF16) for i in range(2)]
        fw.dma("sp", csc[:], CSC_d[:])
        fw.dma("sp", fw32[:], V(f_w_d.h[l].rearrange("p g d -> p (g d)"), f_w_d.buf))
        fw.copy(fwb[:], fw32[:], e="pool")
        for g in range(4):
            fw.dma("sp", UT[:, g, :], pT_d[g * 128:(g + 1) * 128, :])
        for tt in range(NT if with_ctx else 32):
            for half in range(2):
                ps = pf(4, 7)
                for gg in range(2):
                    g = half * 2 + gg
                    fw.mm(ps[:, gg * 256:(gg + 1) * 256], UT[:, g, tt * 128:(tt + 1) * 128], csc[:])
                fw.copy(ABs[:, tt, half * 512:(half + 1) * 512], ps[:], e=("dve" if half == 0 else "act"))
        osc = 1.0 / math.sqrt(128.0)
        jobs = [(ci, t0, n, 0, 32, CL_d, NSL_d, TL) for ci, (t0, n) in enumerate(CHUNKS[:8])]
        if with_ctx:
            jobs.append((8, TL, TC, 32, 2, CLC_d, NSLC_d, TC))
        for (ci, t0, n, tt0, ntile, Cd, Sd, Lseq) in jobs:
            sz = ring(szr, "szr")
            fw.dma("sp", sz[:, :, 0:n], V(pT_d.h[512:1024, :].rearrange("(g p) t -> p g t", p=128)[:, :, t0:t0 + n], pT_d.buf))
            c0 = t0 - tt0 * 128
            nq = (ntile + 7) // 8
            for qd in range(nq):
                na = min(8, ntile - qd * 8)
                tc_ = ring(tbC, "tbC")
                tsn = ring(tbS, "tbS")
                fw.dma("sp", tc_[:, 0:na, 0:n], V(Cd.h[qd * 1024: qd * 1024 + na * 128, :].rearrange("(a p) n -> p a n", p=128)[:, :, c0:c0 + n], Cd.buf))
                fw.dma("sp", tsn[:, 0:na, 0:n], V(Sd.h[qd * 1024: qd * 1024 + na * 128, :].rearrange("(a p) n -> p a n", p=128)[:, :, c0:c0 + n], Sd.buf))
                for g in range(4):
                    for a in range(na):
                        tt = tt0 + qd * 8 + a
                        first = (qd == 0 and a == 0)
                        last = (qd == nq - 1 and a == na - 1)
                        fw.mm(PF[g][:, 0:n], ABs[:, tt, g * 256: g * 256 + 128], tc_[:, a, 0:n], start=first, stop=False)
                        fw.mm(PF[g][:, 0:n], ABs[:, tt, g * 256 + 128: g * 256 + 256], tsn[:, a, 0:n], start=False, stop=last)
            ya = ring(yast, "yast")
            scl = osc / math.sqrt(float(Lseq))
            for g in range(4):
                sb_ = ring(specb, "specb")
                fw.act(sb_[:, 0:n], PF[g][:, 0:n], AF.Copy, scale=scl)
                po = pf(4, 7)
                fw.mm(po[:, 0:n], fwb[:, g * 128:(g + 1) * 128], sb_[:, 0:n])
                fw.tt(ya[:, g, 0:n], po[:, 0:n], sz[:, g, 0:n], ALU.mult)
            fw.dma("pool", V(yaT_d.h.rearrange("(g p) t -> p g t", p=128)[:, :, t0:t0 + n], yaT_d.buf), ya[:, :, 0:n], semof=ya.buf)
        fw.pop()

    def phase_gdn(l):
        fw.push()
        convw = fw.sb("convw", [128, 12, 3], F32)
        fw.dma("sp", convw[:], convw_d[l])
        fw.push()
        cin = [fw.sb("cin%d" % i, [128, TT], BF16) for i in range(2)]
        cy = [fw.sb("cy%d" % i, [128, TT], F32) for i in range(2)]
        cst = [fw.sb("cst%d" % i, [128, TT], BF16) for i in range(2)]
        sqr = [fw.sb("sqr%d" % i, [128, 512], BF16) for i in range(2)]
        rr = [fw.sb("rr%d" % i, [128, 512], F32) for i in range(2)]
        for blk in range(12):
            xi = ring(cin, "cin")
            y = ring(cy, "cy")
            so = ring(cst, "cst")
            fw.dma("sp", xi[:], pT_d[1024 + blk * 128: 1024 + (blk + 1) * 128, :])
            for (a_, b_) in ((0, TL), (TL, TT)):
                fw.ts(y[:, a_:b_], xi[:, a_:b_], convw[:, blk, 1:2])
                fw.stt(y[:, a_ + 1:b_], xi[:, a_:b_ - 1], convw[:, blk, 0:1], y[:, a_ + 1:b_], ALU.mult, ALU.add)
                fw.stt(y[:, a_:b_ - 1], xi[:, a_ + 1:b_], convw[:, blk, 2:3], y[:, a_:b_ - 1], ALU.mult, ALU.add)
            if blk >= 8:
                fw.act(so[:], y[:], AF.Silu)
            else:
                fw.act(y[:], y[:], AF.Silu)
                for (t0, n) in CHUNKS:
                    sq = ring(sqr, "sqr")
                    r = ring(rr, "rr")
                    fw.tt(sq[:, 0:n], y[:, t0:t0 + n], y[:, t0:t0 + n], ALU.mult, e="pool")
                    ps = pf(0, 7)
                    fw.mm(ps[:, 0:n], onesB, sq[:, 0:n])
                    fw.act(r[:, 0:n], ps[:, 0:n], AF.Ln, bias=EPS)
                    fw.act(r[:, 0:n], r[:, 0:n], AF.Exp, scale=-0.5)
                    fw.tt(so[:, t0:t0 + n], y[:, t0:t0 + n], r[:, 0:n], ALU.mult)
            fw.dma("pool", qkvT_d[blk * 128:(blk + 1) * 128, :], so[:], semof=so.buf)
        fw.pop()

        NC = NT * 4
        bb = fw.sb("bb", [128, NT, 8], F32)
        gd = [fw.sb("gd%d" % d, [128, NC], F32) for d in range(2)]
        names = ("Gs", "nG", "EG", "ER", "GL0", "GL1")
        GA = [{nm: fw.sb("%s%d" % (nm, d), [128, NC], F32) for nm in names} for d in range(2)]
        fw.push()
        alog = fw.sb("alog", [128, NT, 8], F32)
        dtb = fw.sb("dtb", [128, NT, 8], F32)
        fw.dma("sp", alog[:], alog_d[l])
        fw.dma("sp", dtb[:], dtb_d[l])
        gz = fw.sb("gz", [128, NT, 8], F32)
        fw.tt(gz[:], abT[:, :, 0:8], dtb[:], ALU.add)
        fw.act(gz[:], gz[:], AF.Exp)
        fw.act(gz[:], gz[:], AF.Ln, bias=1.0)
        fw.act(alog[:], alog[:], AF.Exp)
        fw.stt(gz[:], gz[:], -1.0, alog[:], ALU.mult, ALU.mult)
        fw.act(bb[:], abT[:, :, 8:16], AF.Exp, scale=-1.0)
        fw.ts(bb[:], bb[:], 1.0, None, ALU.add)
        fw.recip(bb[:], bb[:])
        for d in range(2):
            fw.copy(V(v3(gd[d], 4), gd[d].buf), gz[:, :, d * 4:(d + 1) * 4])
        fw.pop()
        for d in range(2):
            tri = triF if d == 0 else triB
            rest = restF if d == 0 else restB
            p1 = pf(0, 7)
            fw.mm(p1[:, 0:NC], tri, gd[d][:])
            fw.copy(GA[d]["Gs"][:], p1[:, 0:NC])
            fw.ts(GA[d]["nG"][:], p1[:, 0:NC], -1.0)
            fw.act(GA[d]["EG"][:], p1[:, 0:NC], AF.Exp)
            p2 = pf(0, 7)
            fw.mm(p2[:, 0:NC], rest, gd[d][:])
            fw.act(GA[d]["ER"][:], p2[:, 0:NC], AF.Exp)
            p3 = pf(0, 7)
            fw.mm(p3[:, 0:NC], tot0, gd[d][:])
            fw.act(GA[d]["GL0"][:], p3[:, 0:NC], AF.Exp)
            p4 = pf(0, 7)
            fw.mm(p4[:, 0:NC], tot1, gd[d][:])
            fw.act(GA[d]["GL1"][:], p4[:, 0:NC], AF.Exp)

        def bc4(t_, tt, rows=slice(0, 128)):
            ap = t_.h[rows, tt * 4: tt * 4 + 4].unsqueeze(2)
            return V(ap.broadcast_to([rows.stop - rows.start, 4, 128]), t_.buf)

        def bcb(tt, d, rows=slice(0, 128)):
            ap = bb.h[rows, tt, d * 4:(d + 1) * 4].unsqueeze(2)
            return V(ap.broadcast_to([rows.stop - rows.start, 4, 128]), bb.buf)

        slots = [[], []]
        for d in range(2):
            for i in range(3):
                slots[d].append({
                    "w0T": fw.sb("w0T%d%d" % (d, i), [128, 512], BF16), "qkdT": fw.sb("qkdT%d%d" % (d, i), [128, 512], BF16),
                    "qgT": fw.sb("qgT%d%d" % (d, i), [128, 512], BF16), "kd": fw.sb("kd%d%d" % (d, i), [128, 512], BF16),
                    "ub": fw.sb("ub%d%d" % (d, i), [128, 512], F32)})
        tmps = []
        for i in range(4):
            tm_ = {
                "EGr": fw.sb("EGr%d" % i, [128, 512], F32),
                "tD": fw.sb("tD%d" % i, [128, 512], F32), "dec": fw.sb("dec%d" % i, [128, 512], F32),
                "kEG": fw.sb("kEG%d" % i, [128, 512], BF16), "vtok": fw.sb("vtok%d" % i, [128, 512], BF16),
                "TTb": fw.sb("TTb%d" % i, [128, 512], BF16),
                "qk": [fw.sb("qk%d_%d" % (i, k_), [128, 12, 128], BF16) for k_ in range(1)]}
            for nm in ("M0", "MT0", "Qa", "QTa", "Qb", "QTb", "Pa", "PTa", "Pb", "PTb", "C32", "C32T", "C64", "C64T"):
                tm_[nm] = fw.sb("%s_%d" % (nm, i), [128, 512], CHDT)
            tm_["Rp"] = tm_["dec"]
            tm_["Mf"] = tm_["tD"]
            tmps.append(tm_)
        S32 = [fw.sb("S32_%d" % d, [128, 512], F32) for d in range(2)]
        Sb = [fw.sb("Sb_%d" % d, [128, 512], BF16) for d in range(2)]
        un = [fw.sb("un_%d" % d, [128, 512], BF16) for d in range(2)]
        t5 = [fw.sb("t5_%d" % d, [128, 512], F32) for d in range(2)]
        ost = [[fw.sb("ost%d%d" % (d, i), [128, 512], F32) for i in range(2)] for d in range(2)]
        for d in range(2):
            fw.op("pool", lambda g, d=d: g.memset(S32[d].h[:, :], 0.0), [], [S32[d][:]])
            fw.op("pool", lambda g, d=d: g.memset(Sb[d].h[:, :], 0.0), [], [Sb[d][:]])
        qscale = 128.0 ** -0.5
        PGB = [(PF[0], PF[1]), (PF[2], PF[6])]

        def prep(tt, d, sl, s_):
            tm = tmps[d * 2 + s_ % 2]
            tok = slice(tt * 128, (tt + 1) * 128)
            G = GA[d]

            def pg():
                return PGB[d][s_ % 2]

            qk = tm["qk"][0]
            fw.dma("sp", qk[:], V(qkvT_d.h.rearrange("(b p) t -> p b t", p=128)[:, :, tok], qkvT_d.buf))
            fw.tt(V(v3(tm["Rp"], 128), tm["Rp"].buf), V(ident4F.ap.rearrange("p (a n) -> p a n", n=128), ident4F.buf),
                  bc4(G["Gs"], tt), ALU.mult, e="pool")
            yield
            p1 = pg()
            fw.mm(p1[:], onesF, tm["Rp"][:])
            yield
            fw.act(tm["EGr"][:], p1[:], AF.Exp, bias=math.log(qscale))
            fw.tt(tm["tD"][:], p1[:], negmask[d], ALU.add)
            fw.tt(V(v3(tm["tD"], 128), tm["tD"].buf), V(v3(tm["tD"], 128), tm["tD"].buf), bc4(G["nG"], tt), ALU.add)
            yield
            fw.act(tm["dec"][:], tm["tD"][:], AF.Exp)
            pk = pg()
            for h in range(4):
                fw.mm(pk[:, h * 128:(h + 1) * 128], qk[:, 4 + h, :], qk[:, 4 + h, :])
            yield
            fw.tt(tm["Mf"][:], pk[:], tm["dec"][:], ALU.mult)
            fw.tt(V(v3(tm["Mf"], 128), tm["Mf"].buf), V(v3(tm["Mf"], 128), tm["Mf"].buf), bcb(tt, d), ALU.mult)
            yield
            M0, MT0 = tm["M0"], tm["MT0"]
            fw.tt(M0[:], tm["Mf"][:], strictB[d], ALU.mult, e="pool")
            yield
            if CHDT == F32:
                ptr = pg()
                for h in range(4):
                    fw.tr(ptr[:, h * 128:(h + 1) * 128], M0[:, h * 128:(h + 1) * 128], identF)
                yield
                fw.copy(MT0[:], ptr[:], e="act")
            else:
                for h in range(4):
                    fw.tr(PB[:, h * 128:(h + 1) * 128], M0[:, h * 128:(h + 1) * 128], identB)
                fw.copy(MT0[:], PB[:, 0:512], e="act")
            Q, QT, P, PT = tm["Qa"], tm["QTa"], tm["Pa"], tm["PTa"]
            Qn, QTn, Pn, PTn = tm["Qb"], tm["QTb"], tm["Pb"], tm["PTb"]
            fw.tt(Q[:], M0[:], mD16, ALU.mult, e="pool")
            yield
            fw.tt(QT[:], MT0[:], mD16, ALU.mult, e="pool")
            fw.tt(P[:], ident4F, Q[:], ALU.subtract)
            yield
            fw.tt(PT[:], ident4F, QT[:], ALU.subtract)
            fw.tt(tm["C32"][:], M0[:], mC32, ALU.mult, e="pool")
            fw.tt(tm["C32T"][:], MT0[:], mC32, ALU.mult, e="pool")
            fw.tt(tm["C64"][:], M0[:], mC64, ALU.mult, e="pool")
            fw.tt(tm["C64T"][:], MT0[:], mC64, ALU.mult, e="pool")
            yield

            def mm4(lhsT, rhs):
                p_ = pg()
                for h in range(4):
                    hs = slice(h * 128, (h + 1) * 128)
                    fw.mm(p_[:, hs], lhsT[:, hs], rhs[:, hs])
                return p_

            for lev in range(3):
                pq = mm4(QT, Q)
                yield
                fw.copy(Qn[:], pq[:], e="dve")
                pqt = mm4(Q, QT)
                yield
                fw.copy(QTn[:], pqt[:], e="act")
                yield
                pp = mm4(QTn, P)
                yield
                fw.tt(Pn[:], pp[:], P[:], ALU.add)
                ppt = mm4(Qn, PT)
                yield
                fw.tt(PTn[:], ppt[:], PT[:], ALU.add)
                yield
                Q, Qn = Qn, Q
                QT, QTn = QTn, QT
                P, Pn = Pn, P
                PT, PTn = PTn, PT
            X, XT = P, PT
            py = mm4(tm["C32T"], X)
            yield
            fw.copy(Qn[:], py[:], e="act")
            pyp = mm4(tm["C32"], XT)
            yield
            fw.copy(QTn[:], pyp[:], e="dve")
            yield
            px = mm4(XT, Qn)
            yield
            fw.tt(Pn[:], X[:], px[:], ALU.subtract)
            pxt = mm4(X, QTn)
            yield
            fw.tt(PTn[:], XT[:], pxt[:], ALU.subtract)
            yield
            X, XT = Pn, PTn
            py = mm4(tm["C64T"], X)
            yield
            fw.copy(Q[:], py[:], e="act")
            yield
            px = mm4(XT, Q)
            yield
            fw.tt(tm["TTb"][:], X[:], px[:], ALU.subtract)
            TTm = tm["TTb"]
            for h in range(4):
                fw.tr(PB[:, h * 128:(h + 1) * 128], qk[:, 4 + h, :], identB)
            for h in range(4):
                fw.tr(PB[:, 512 + h * 128: 512 + (h + 1) * 128], qk[:, 8 + h, :], identB)
            fw.tt(V(v3(tm["kEG"], 128), tm["kEG"].buf), V(PB.h[:, 0:512].rearrange("p (a n) -> p a n", n=128), PB.buf), bc4(G["EG"], tt), ALU.mult)
            fw.tt(V(v3(sl["kd"], 128), sl["kd"].buf), V(PB.h[:, 0:512].rearrange("p (a n) -> p a n", n=128), PB.buf), bc4(G["ER"], tt), ALU.mult)
            fw.copy(tm["vtok"][:], PB[:, 512:1024], e="dve")
            yield
            po = pg()
            for h in range(4):
                hs = slice(h * 128, (h + 1) * 128)
                fw.mm(po[:, hs], tm["kEG"][:, hs], TTm[:, hs])
            yield
            fw.copy(sl["w0T"][:], po[:], e="act")
            po2 = pg()
            for h in range(4):
                hs = slice(h * 128, (h + 1) * 128)
                fw.mm(po2[:, hs], TTm[:, hs], tm["vtok"][:, hs])
            yield
            fw.tt(V(v3(sl["ub"], 128), sl["ub"].buf), V(po2.h[:, :].rearrange("p (a n) -> p a n", n=128), po2.buf), bcb(tt, d), ALU.mult)
            po3 = pg()
            for h in range(4):
                fw.mm(po3[:, h * 128:(h + 1) * 128], qk[:, 4 + h, :], qk[:, h, :])
            yield
            fw.stt(sl["qkdT"][:], po3[:], qscale, tm["dec"][:], ALU.mult, ALU.mult)
            fw.tt(V(v3(sl["qgT"], 128), sl["qgT"].buf), qk[:, 0:4, :], V(v3(tm["EGr"], 128), tm["EGr"].buf), ALU.mult, e="pool")
            yield

        def scan(tt, d, sl):
            G = GA[d]
            BX = PF[3 + d]
            BY = PF[5]
            o = ring(ost[d], "ost%d" % d)
            for ci in ((0, 1) if d == 0 else (1, 0)):
                cs = slice(64 * ci, 64 * ci + 64)
                for h in range(4):
                    hs = slice(h * 128, (h + 1) * 128)
                    fw.mm(BX[cs, hs], sl["w0T"][:, h * 128 + 64 * ci: h * 128 + 64 * ci + 64], Sb[d][:, hs])
                t5v = V(t5[d].h[cs, :].rearrange("p (a n) -> p a n", n=128), t5[d].buf)
                fw.tt(t5v, V(BX.h[cs, :].rearrange("p (a n) -> p a n", n=128), BX.buf), bcb(tt, d, cs), ALU.mult)
                fw.tt(un[d][cs, :], sl["ub"][cs, :], t5[d][cs, :], ALU.subtract)
                yield
                for h in range(4):
                    oc = slice(h * 128 + 64 * ci, h * 128 + 64 * ci + 64)
                    yc_ = slice(d * 256 + h * 64, d * 256 + (h + 1) * 64)
                    hs = slice(h * 128, (h + 1) * 128)
                    fw.mm(BY[:, yc_], Sb[d][:, hs], sl["qgT"][:, oc], start=True, stop=False)
                    fw.mm(BY[:, yc_], un[d][cs, hs], sl["qkdT"][cs, oc], start=False, stop=True)
                fw.copy(V(o.h[:, :].rearrange("p (a n) -> p a n", n=128)[:, :, 64 * ci: 64 * ci + 64], o.buf),
                        V(BY.h[:, d * 256:(d + 1) * 256].rearrange("p (a n) -> p a n", n=64), BY.buf), e="act")
                for h in range(4):
                    hs = slice(h * 128, (h + 1) * 128)
                    fw.mm(BX[:, hs], sl["kd"][cs, hs], un[d][cs, hs])
                gl = G["GL0"] if ci == 0 else G["GL1"]
                fw.tt(V(v3(S32[d], 128), S32[d].buf), V(v3(S32[d], 128), S32[d].buf), bc4(gl, tt), ALU.mult)
                fw.tt(S32[d][:], S32[d][:], BX[:], ALU.add)
                yield
                fw.copy(Sb[d][:], S32[d][:], e="act")
                yield
            fw.dma("pool", V(oT_d[d].h.rearrange("(h p) t -> p h t", p=128)[:, :, tt * 128:(tt + 1) * 128], oT_d[d].buf),
                   V(v3(o, 128), o.buf), semof=o.buf)
            yield

        def run_il(gens):
            gens = list(gens)
            while gens:
                for g_ in list(gens):
                    try:
                        next(g_)
                    except StopIteration:
                        gens.remove(g_)

        order = [[32, 33] + list(range(32)), [33, 32] + list(range(31, -1, -1))]
        active = []
        p_started = [0, 0]
        p_done = [0, 0]
        s_started = [0, 0]
        s_done = [0, 0]
        while s_done[0] < NT or s_done[1] < NT:
            for d in range(2):
                if p_started[d] < NT and p_started[d] - p_done[d] < 2 and p_started[d] - s_done[d] < 3:
                    k_ = p_started[d]
                    active.append((prep(order[d][k_], d, slots[d][k_ % 3], k_), "p", d))
                    p_started[d] += 1
                if s_started[d] < p_done[d] and s_started[d] == s_done[d]:
                    k_ = s_started[d]
                    active.append((scan(order[d][k_], d, slots[d][k_ % 3]), "s", d))
                    s_started[d] += 1
            for item in list(active):
                g_, kind, d = item
                try:
                    next(g_)
                except StopIteration:
                    active.remove(item)
                    if kind == "p":
                        p_done[d] += 1
                    else:
                        s_done[d] += 1
        fw.pop()

    def phase_mla(l, with_ctx):
        fw.push()
        cqn = fw.sb("cqn", [128, 3, TT], BF16)
        ckvn = fw.sb("ckvn", [128, 2, TT], BF16)
        ropeC = fw.sb("ropeC", [96, TT], BF16)
        ropeS = fw.sb("ropeS", [96, TT], BF16)
        kper = fw.sb("kper", [96, TT], BF16)
        wuq = fw.sb("wuq", [128, 3, 1536], BF16)
        wukv = fw.sb("wukv", [128, 2, 1024], BF16)
        qn = fw.sb("qn", [128, 3], F32)
        kvn = fw.sb("kvn", [128, 2], F32)
        fw.dma("sp", qn[:], qnorm_d[l])
        fw.dma("sp", kvn[:], kvnorm_d[l])
        fw.push()
        st32 = fw.sb("st32", [128, 4608], F32)
        fw.dma("sp", st32[:, 0:4608], V(w_uq_d.h[l].rearrange("p a n -> p (a n)"), w_uq_d.buf))
        fw.copy(V(wuq.h[:, :, :].rearrange("p a n -> p (a n)"), wuq.buf), st32[:, 0:4608], e="pool")
        fw.dma("sp", st32[:, 0:2048], V(w_ukv_d.h[l].rearrange("p a n -> p (a n)"), w_ukv_d.buf))
        fw.copy(V(wukv.h[:, :, :].rearrange("p a n -> p (a n)"), wukv.buf), st32[:, 0:2048], e="pool")
        fw.dma("sp", st32[0:96, 0:TT], ROPEC_d[:])
        fw.copy(ropeC[:], st32[0:96, 0:TT], e="pool")
        fw.dma("sp", st32[0:96, 0:TT], ROPES_d[:])
        fw.copy(ropeS[:], st32[0:96, 0:TT], e="pool")
        kpa = fw.sb("kpa", [96, TT], BF16)
        kpb = fw.sb("kpb", [96, TT], BF16)
        fw.dma("sp", kpa[64:96, :], pT_d[7296:7328, :])
        fw.dma("sp", kpb[64:96, :], pT_d[7328:7360, :])
        fw.tt(st32[64:96, 0:TT], kpa[64:96, :], ropeC[64:96, :], ALU.mult)
        fw.tt(kpb[64:96, :], kpb[64:96, :], ropeS[64:96, :], ALU.mult)
        fw.tt(kper[64:96, :], st32[64:96, 0:TT], kpb[64:96, :], ALU.add)
        sqr = [fw.sb("msq%d" % i, [128, 512], BF16) for i in range(3)]
        rr = [fw.sb("mrr%d" % i, [128, 512], F32) for i in range(2)]
        for (dst, nb_, row0, nrm, width) in ((cqn, 3, 3072, qn, 384.0), (ckvn, 2, 3456, kvn, 256.0)):
            for b_ in range(nb_):
                fw.dma("sp", dst[:, b_, :], pT_d[row0 + b_ * 128: row0 + (b_ + 1) * 128, :])
            for (t0, n) in CHUNKS:
                ps = pf(0, 5)
                sqs = []
                for b_ in range(nb_):
                    sq = ring(sqr, "msq")
                    fw.tt(sq[:, 0:n], dst[:, b_, t0:t0 + n], dst[:, b_, t0:t0 + n], ALU.mult, e="pool")
                    sqs.append(sq)
                for b_ in range(nb_):
                    fw.mm(ps[:, 0:n], onesB, sqs[b_][:, 0:n], start=(b_ == 0), stop=(b_ == nb_ - 1))
                r = ring(rr, "mrr")
                fw.act(r[:, 0:n], ps[:, 0:n], AF.Ln, bias=EPS, scale=1.0 / width)
                fw.act(r[:, 0:n], r[:, 0:n], AF.Exp, scale=-0.5)
                for b_ in range(nb_):
                    fw.stt(dst[:, b_, t0:t0 + n], dst[:, b_, t0:t0 + n], nrm[:, b_:b_ + 1], r[:, 0:n], ALU.mult, ALU.mult)
        fw.pop()

        kTh = [fw.sb("kTh%d" % i, [96, TT], BF16) for i in range(2)]
        qTh = [fw.sb("qTh%d" % i, [96, TT], BF16) for i in range(2)]
        Vaug = [fw.sb("Vaug%d" % i, [128, NT, 128], BF16) for i in range(2)]
        for i in range(2):
            fw.op("pool", lambda g, i=i: g.memset(Vaug[i].h[:, :, 64:128], 1.0), [], [Vaug[i][:, :, 64:128]])
        PTr = [fw.sb("PTr%d" % i, [128, 512], BF16) for i in range(4)]
        q1 = [fw.sb("q1_%d" % i, [96, 512], F32) for i in range(2)]
        q2 = [fw.sb("q2_%d" % i, [96, 512], F32) for i in range(2)]
        rden = [fw.sb("rden%d" % i, [128, 512], F32) for i in range(2)]
        otmp = [fw.sb("otmp%d" % i, [64, 512], F32) for i in range(2)]
        zc = [fw.sb("zc%d" % i, [64, 512], BF16) for i in range(2)]
        ycs = [fw.sb("ycs%d" % i, [64, 512], BF16) for i in range(2)]
        ascale = 96.0 ** -0.5
        qchunks = CHUNKS if with_ctx else CHUNKS[:8]
        for h in range(8):
            kt_ = kTh[h % 2]
            qt_ = qTh[h % 2]
            va = Vaug[h % 2]
            for (t0, n) in CHUNKS:
                ps = pf(0, 5)
                for kc in range(2):
                    fw.mm(ps[0:64, 0:n], wukv[:, kc, h * 128: h * 128 + 64], ckvn[:, kc, t0:t0 + n], start=(kc == 0), stop=(kc == 1))
                fw.copy(kt_[0:64, t0:t0 + n], ps[0:64, 0:n], e="dve")
            fw.copy(kt_[64:96, :], kper[64:96, :], e="pool")
            for t8 in range(0, NT, 8):
                nt8 = min(8, NT - t8)
                ps = pf(0, 5)
                for a in range(nt8):
                    tt = t8 + a
                    for kc in range(2):
                        fw.mm(ps[:, a * 64:(a + 1) * 64], ckvn[:, kc, tt * 128:(tt + 1) * 128], wukv[:, kc, h * 128 + 64: h * 128 + 128],
                              start=(kc == 0), stop=(kc == 1))
                fw.copy(va[:, t8:t8 + nt8, 0:64], V(ps.h[:, 0:nt8 * 64].rearrange("p (a n) -> p a n", n=64), ps.buf), e="dve")
            for (t0, n) in qchunks:
                pa = pf(0, 5)
                for kc in range(3):
                    fw.mm(pa[0:96, 0:n], wuq[:, kc, h * 96:(h + 1) * 96], cqn[:, kc, t0:t0 + n], start=(kc == 0), stop=(kc == 2))
                pb_ = pf(0, 5)
                for kc in range(3):
                    fw.mm(pb_[0:96, 0:n], wuq[:, kc, 768 + h * 96: 768 + (h + 1) * 96], cqn[:, kc, t0:t0 + n], start=(kc == 0), stop=(kc == 2))
                a1 = ring(q1, "q1")
                a2 = ring(q2, "q2")
                fw.tt(a1[:, 0:n], pa[0:96, 0:n], ropeC[:, t0:t0 + n], ALU.mult)
                fw.tt(a2[:, 0:n], pb_[0:96, 0:n], ropeS[:, t0:t0 + n], ALU.mult)
                fw.tt(qt_[:, t0:t0 + n], a1[:, 0:n], a2[:, 0:n], ALU.add, e="pool")
            for qi, (t0, n) in enumerate(qchunks):
                ktiles = list(range(NT)) if t0 < TL else [32, 33]
                acc = PF[5 + qi % 2]
                zt = ring(zc, "zc")
                fw.dma("sp", zt[:, 0:n], pT_d[3712 + h * 64: 3712 + (h + 1) * 64, t0:t0 + n])
                LA = 3
                pss = {}

                def issue_s(ki_):
                    ps_ = pf(0, 5)
                    kt__ = ktiles[ki_]
                    fw.mm(ps_[:, 0:n], kt_[:, kt__ * 128:(kt__ + 1) * 128], qt_[:, t0:t0 + n])
                    pss[ki_] = ps_

                for ki in range(min(LA, len(ktiles))):
                    issue_s(ki)
                for ki, kt in enumerate(ktiles):
                    ps = pss.pop(ki)
                    pt = ring(PTr, "PTr")
                    fw.act(pt[:, 0:n], ps[:, 0:n], AF.Exp, scale=ascale)
                    fw.mm(acc[:, 0:n], va[:, kt, :], pt[:, 0:n], start=(ki == 0), stop=(ki == len(ktiles) - 1))
                    if ki + LA < len(ktiles):
                        issue_s(ki + LA)
                rd = ring(rden, "rden")
                fw.recip(rd[64:128, 0:n], acc[64:128, 0:n])
                ot = ring(otmp, "otmp")
                fw.tt(ot[:, 0:n], acc[0:64, 0:n], rd[64:128, 0:n], ALU.mult)
                yc = ring(ycs, "ycs")
                fw.tt(yc[:, 0:n], ot[:, 0:n], zt[:, 0:n], ALU.mult, e="pool")
                fw.dma("pool", ycT_d[h * 64:(h + 1) * 64, t0:t0 + n], yc[:, 0:n], semof=yc.buf)
        fw.pop()

    def phase_final(l, with_ctx, last):
        fw.push()
        wbr = fw.sb("wbr", [128, 12, 1024], BF16)
        wout = fw.sb("wout", [128, 8, 1024], BF16)
        dnrm = fw.sb("dnrm", [128, 1], F32)
        fw.dma("sp", dnrm[:], dnorm_d[l])
        fw.push()
        st32 = [fw.sb("fst32_%d" % i, [128, 4096], F32) for i in range(2)]
        for q in range(3):
            st = ring(st32, "fst32")
            fw.dma("sp", st[:], V(w_br_d.h[l, :, q * 4:(q + 1) * 4, :].rearrange("p a n -> p (a n)"), w_br_d.buf))
            fw.copy(V(wbr.h[:, q * 4:(q + 1) * 4, :].rearrange("p a n -> p (a n)"), wbr.buf), st[:], e="pool")
        for q in range(2):
            st = ring(st32, "fst32")
            fw.dma("sp", st[:], V(w_out_d.h[l, :, q * 4:(q + 1) * 4, :].rearrange("p a n -> p (a n)"), w_out_d.buf))
            fw.copy(V(wout.h[:, q * 4:(q + 1) * 4, :].rearrange("p a n -> p (a n)"), wout.buf), st[:], e="pool")
        fw.pop()
        of_ = fw.sb("of", [128, 4, 512], F32)
        ob_ = fw.sb("ob", [128, 4, 512], F32)
        sqb = fw.sb("sqb", [128, 4, 512], BF16)
        rr = [fw.sb("frr%d" % i, [128, 512], F32) for i in range(2)]
        szb = fw.sb("szb", [128, 4, 512], BF16)
        ybT = fw.sb("ybT", [128, 4, 512], BF16)
        yaT = fw.sb("yaTt", [128, 4, 512], BF16)
        ycT = fw.sb("ycTt", [128, 4, 512], BF16)
        gts = [fw.sb("gts%d" % i, [128, 8, 512], BF16) for i in range(2)]
        merged = fw.sb("merged", [128, 8, 512], F32)
        mergedb = fw.sb("mergedb", [128, 8, 512], BF16)
        tmpf = [fw.sb("tmpf%d" % i, [128, 512], F32) for i in range(2)]
        xr = [fw.sb("fxr%d" % i, [128, 1024], F32) for i in range(2)]
        xo = [fw.sb("fxo%d" % i, [128, 1024], F32) for i in range(2)]
        junk = fw.sb("junk", [128, 512], BF16)
        st4 = fw.sb("st4", [128, 8 * NT], F32, nsub=NT)
        xsrc = xin if l == 0 else xs_d
        chunks = CHUNKS if with_ctx else CHUNKS[:8]
        for (t0, n) in chunks:
            j = 0 if t0 < TL else 1
            for d, dstt in ((0, of_), (1, ob_)):
                fw.dma("sp", dstt[:, :, 0:n], V(oT_d[d].h.rearrange("(h p) t -> p h t", p=128)[:, :, t0:t0 + n], oT_d[d].buf))
            fw.dma("sp", szb[:, :, 0:n], V(pT_d.h[2560:3072, :].rearrange("(g p) t -> p g t", p=128)[:, :, t0:t0 + n], pT_d.buf))
            fw.dma("sp", yaT[:, :, 0:n], V(yaT_d.h.rearrange("(g p) t -> p g t", p=128)[:, :, t0:t0 + n], yaT_d.buf))
            fw.dma("sp", ycT[:, :, 0:n], V(ycT_d.h.rearrange("(g p) t -> p g t", p=128)[:, :, t0:t0 + n], ycT_d.buf))
            fw.tt(of_[:, :, 0:n], of_[:, :, 0:n], ob_[:, :, 0:n], ALU.add, e="pool")
            fw.act(sqb[:, :, 0:n], of_[:, :, 0:n], AF.Square)
            for h in range(4):
                ps = pf()
                fw.mm(ps[:, 0:n], onesB, sqb[:, h, 0:n])
                r = ring(rr, "frr")
                fw.act(r[:, 0:n], ps[:, 0:n], AF.Ln, bias=EPS, scale=1.0 / 128)
                fw.act(r[:, 0:n], r[:, 0:n], AF.Exp, scale=-0.5)
                fw.stt(r[:, 0:n], of_[:, h, 0:n], dnrm[:, 0:1], r[:, 0:n], ALU.mult, ALU.mult)
                fw.tt(ybT[:, h, 0:n], r[:, 0:n], szb[:, h, 0:n], ALU.mult, e="pool")
            for br, src in enumerate((yaT, ybT, ycT)):
                gt = ring(gts, "gts")
                fw.dma("sp", gt[:, :, 0:n], V(pT_d.h[4224 + br * 1024: 4224 + (br + 1) * 1024, :].rearrange("(g p) t -> p g t", p=128)[:, :, t0:t0 + n], pT_d.buf))
                for jb in range(8):
                    ps = pf()
                    for kc in range(4):
                        fw.mm(ps[:, 0:n], wbr[:, br * 4 + kc, jb * 128:(jb + 1) * 128], src[:, kc, 0:n], start=(kc == 0), stop=(kc == 3))
                    if br == 0:
                        fw.tt(merged[:, jb, 0:n], ps[:, 0:n], gt[:, jb, 0:n], ALU.mult)
                    else:
                        tf = ring(tmpf, "tmpf")
                        fw.tt(tf[:, 0:n], ps[:, 0:n], gt[:, jb, 0:n], ALU.mult)
                        fw.tt(merged[:, jb, 0:n], merged[:, jb, 0:n], tf[:, 0:n], ALU.add, e="pool")
            fw.copy(mergedb[:, :, 0:n], merged[:, :, 0:n], e="act")
            for ts_ in range(n // 128):
                tt = (t0 // 128) + ts_
                xt = ring(xr, "fxr")
                fw.dma("sp", xt[:], xsrc[tt * 128:(tt + 1) * 128, :])
                st = st4.sub(tt)
                c0 = 8 * tt
                phs = []
                for half in range(2):
                    ps = pf()
                    for kc in range(8):
                        fw.mm(ps[:], mergedb[:, kc, ts_ * 128:(ts_ + 1) * 128], wout[:, kc, half * 512:(half + 1) * 512], start=(kc == 0), stop=(kc == 7))
                    fw.act(junk[:], ps[:], AF.Square, accum_out=st[:, c0 + half: c0 + half + 1])
                    phs.append(ps)
                fw.tt(st[:, c0 + 2:c0 + 3], st[:, c0:c0 + 1], st[:, c0 + 1:c0 + 2], ALU.add)
                fw.act(st[:, c0 + 3:c0 + 4], st[:, c0 + 2:c0 + 3], AF.Sqrt, bias=EPS, scale=1.0 / 1024)
                fw.recip(st[:, c0 + 4:c0 + 5], st[:, c0 + 3:c0 + 4])
                o = ring(xo, "fxo")
                for half in range(2):
                    hs = slice(half * 512, (half + 1) * 512)
                    fw.stt(o[:, hs], phs[half][:], st[:, c0 + 4:c0 + 5], ggbc[j][:, hs], ALU.mult, ALU.mult)
                fw.tt(o[:], o[:], xt[:], ALU.add, e="pool")
                if last:
                    fw.dma("pool", out_d[tt * 128:(tt + 1) * 128, :], o[:], semof=o.buf)
                else:
                    fw.dma("pool", xs_d[tt * 128:(tt + 1) * 128, :], o[:], semof=o.buf)
        fw.pop()

    for l in range(n_layers):
        alloc_s1()
        phase_mod(l)
        phase_h(l)
        if "hT" in dbg and l == 0:
            dbg_d["hT"] = fw.dram("hT_o", [128, 8, TT], BF16, "ExternalOutput")
            fw.dma("pool", dbg_d["hT"][:], S["hT"][:], semof=S["hT"].buf)
        if stop_after == "h":
            fw.pop()
            break
        phase_proj(l)
        fw.pop()
        if stop_after == "proj":
            break
        if "nofourier" not in dbg:
            phase_fourier(l, with_ctx=(l < n_layers - 1 or "ctxall" in dbg))
        if stop_after == "fourier":
            break
        if "nogdn" not in dbg:
            phase_gdn(l)
        if stop_after == "gdn":
            break
        wc = (l < n_layers - 1 or "ctxall" in dbg)
        phase_mla(l, with_ctx=wc)
        if stop_after == "mla":
            break
        phase_final(l, with_ctx=wc, last=(l == n_layers - 1 and "ctxall" not in dbg))

    if "abT" in dbg:
        dbg_d["abT"] = fw.dram("abT_o", [128, NT, 16], F32, "ExternalOutput")
        fw.dma("pool", dbg_d["abT"][:], abT[:], semof=abT.buf)
    outs = [out_d, xs_d, pT_d, yaT_d, ycT_d, qkvT_d] + oT_d + list(dbg_d.values())
    fw.finish(outs, e="pool")
    return nc, fw


def kernel(**inputs):
    inp = {k: np.asarray(v) for k, v in inputs.items()}
    consts = host_constants()
    shared = host_layout_shared(inp)
    nc, fw = build()
    in_maps = []
    for b in range(8):
        m = dict(consts)
        m.update(shared)
        m.update(host_layout(inp, b))
        in_maps.append(m)
    res = run_bass_kernel_spmd(nc, in_maps, core_ids=list(range(8)))
    return np.stack([np.asarray(r["out"]) for r in res.results], axis=0).astype(np.float32)
```

```python
import math
import numpy as np
import ml_dtypes
import concourse.bass as bass
import concourse.mybir as mybir
from concourse.bass_utils import run_bass_kernel_spmd

F32 = mybir.dt.float32
BF16 = mybir.dt.bfloat16
AF = mybir.ActivationFunctionType
ALU = mybir.AluOpType

TL = 4096
TC = 256
TT = TL + TC
NT = TT // 128
CHUNKS = [(i * 512, 512) for i in range(8)] + [(TL, TC)]
NB = 58
EPS = 1e-6
NEG = -30000.0
CHDT = BF16


class Buf:
    __slots__ = ("name", "writers", "readers", "waw", "dsem", "excl")

    def __init__(self, name, waw=True):
        self.name = name
        self.excl = False
        self.writers = {}
        self.readers = {}
        self.waw = waw
        self.dsem = None


class V:
    __slots__ = ("ap", "buf")

    def __init__(self, ap, buf):
        self.ap = ap
        self.buf = buf


class T:
    def __init__(self, handle, buf, bufs=None):
        self.h = handle
        self.buf = buf
        self.bufs = bufs

    def __getitem__(self, key):
        return V(self.h[key], self.buf)

    def sub(self, i):
        return T(self.h, self.bufs[i])


class FW:
    def __init__(self, nc, strict_same=True):
        self.nc = nc
        self.eng = {"pe": nc.tensor, "dve": nc.vector, "act": nc.scalar, "pool": nc.gpsimd, "sp": nc.sync}
        self.sems = {}
        self.count = {}
        for e in self.eng:
            self.sems[e] = nc.alloc_semaphore("s_" + e)
            self.count[e] = 0
        self.seen = {e: {} for e in self.eng}
        self.strict_same = strict_same
        self.nops = {e: 0 for e in self.eng}
        self.nwait = 0
        self.uid = 0
        self.scopes = [[]]
        self.scope_bufs = [[]]
        self.free_dsems = []
        self.bar_tile = V(nc.alloc_sbuf_tensor("bar_tile", [128, 8], F32)[:, :], Buf("bar"))

    def sb(self, name, shape, dtype, nsub=0, waw=True):
        self.uid += 1
        name = "%s_u%d" % (name, self.uid)
        g = self.nc.sbuf_tensor(name, list(shape), dtype)
        h = g.__enter__()
        self.scopes[-1].append(g)
        bufs = [Buf(name + "_%d" % i, waw) for i in range(nsub)] if nsub else None
        t = T(h, Buf(name, waw), bufs)
        self.scope_bufs[-1].extend([t.buf] + (bufs or []))
        return t

    def push(self):
        self.scopes.append([])
        self.scope_bufs.append([])

    def pop(self):
        self.barrier()
        for g in reversed(self.scopes.pop()):
            g.__exit__(None, None, None)
        for b in self.scope_bufs.pop():
            if b.dsem is not None:
                self.free_dsems.append(b.dsem)
                b.dsem = None

    def barrier(self):
        need = {k: v for k, v in self.count.items() if v > 0}
        self._waits("pool", need)
        ins = self.eng["pool"].memset(self.bar_tile.ap, 0.0)
        self.count["pool"] += 1
        ins.then_inc(self.sems["pool"], 1)
        val = self.count["pool"]
        for e in self.eng:
            if e == "pool":
                continue
            self._waits(e, {"pool": val})
            for k, v in need.items():
                self.seen[e][k] = max(self.seen[e].get(k, 0), v)

    def ps(self, name, shape, dtype=F32):
        h = self.nc.alloc_psum_tensor(name, list(shape), dtype)
        b = Buf(name)
        b.excl = True
        return T(h, b)

    def dram(self, name, shape, dtype, kind="Internal", nsub=0):
        h = self.nc.dram_tensor(name, list(shape), dtype, kind=kind)
        bufs = [Buf(name + "_%d" % i, False) for i in range(nsub)] if nsub else None
        return T(h.ap(), Buf(name, waw=False), bufs)

    def _need(self, reads, writes):
        need = {}
        for v in reads:
            for k, val in v.buf.writers.items():
                if need.get(k, 0) < val:
                    need[k] = val
            if v.buf.excl:
                for k, val in v.buf.readers.items():
                    if need.get(k, 0) < val:
                        need[k] = val
        for v in writes:
            b = v.buf
            if b.waw:
                for k, val in b.writers.items():
                    if need.get(k, 0) < val:
                        need[k] = val
            for k, val in b.readers.items():
                if need.get(k, 0) < val:
                    need[k] = val
        return need

    def _waits(self, e, need):
        seen = self.seen[e]
        for k, val in need.items():
            if k == e and (e == "pe" or not self.strict_same):
                continue
            if seen.get(k, 0) >= val:
                continue
            self.eng[e].wait_ge(self.sems[k], val)
            seen[k] = val
            self.nwait += 1

    def op(self, e, fn, reads=(), writes=()):
        self._waits(e, self._need(reads, writes))
        ins = fn(self.eng[e])
        self.count[e] += 1
        val = self.count[e]
        ins.then_inc(self.sems[e], 1)
        self.nops[e] += 1
        for v in reads:
            b = v.buf
            if b.readers.get(e, 0) < val:
                b.readers[e] = val
        for v in writes:
            b = v.buf
            if b.waw:
                b.writers = {e: val}
            else:
                b.writers[e] = val
            b.readers = {}
        return ins

    def dma(self, q, out, in_, semof=None, **kw):
        sbuf = semof if semof is not None else out.buf
        if sbuf.dsem is None:
            if self.free_dsems:
                key = self.free_dsems.pop()
            else:
                key = "d%d" % len(self.sems)
                self.sems[key] = self.nc.alloc_semaphore(key)
                self.count[key] = 0
            sbuf.dsem = key
        key = sbuf.dsem
        self._waits(q, self._need([in_], [out]))
        ins = self.eng[q].dma_start(out=out.ap, in_=in_.ap, **kw)
        self.count[key] += 16
        val = self.count[key]
        ins.then_inc(self.sems[key], 16)
        self.nops[q] += 1
        b = in_.buf
        if b.readers.get(key, 0) < val:
            b.readers[key] = val
        b = out.buf
        if b.waw:
            b.writers = {key: val}
        else:
            b.writers[key] = val
        b.readers = {}
        return ins

    def finish(self, bufs, e="sp"):
        need = {}
        for b in bufs:
            for k, val in b.buf.writers.items():
                if need.get(k, 0) < val:
                    need[k] = val
        self._waits(e, need)

    def mm(self, out, lhsT, rhs, start=True, stop=True):
        reads = [lhsT, rhs] + ([] if start else [out])
        return self.op("pe", lambda e: e.matmul(out.ap, lhsT.ap, rhs.ap, start=start, stop=stop), reads, [out])

    def tr(self, out, in_, ident):
        return self.op("pe", lambda e: e.transpose(out.ap, in_.ap, ident.ap), [in_, ident], [out])

    def act(self, out, in_, func, bias=None, scale=None, accum_out=None):
        reads = [in_]
        kw = {}
        if bias is not None:
            if isinstance(bias, V):
                reads.append(bias)
                kw["bias"] = bias.ap
            else:
                kw["bias"] = bias
        if scale is not None:
            if isinstance(scale, V):
                reads.append(scale)
                kw["scale"] = scale.ap
            else:
                kw["scale"] = scale
        writes = [out]
        if accum_out is not None:
            kw["accum_out"] = accum_out.ap
            writes.append(accum_out)
        return self.op("act", lambda e: e.activation(out.ap, in_.ap, func, **kw), reads, writes)

    def tt(self, out, in0, in1, op, e="dve"):
        return self.op(e, lambda g: g.tensor_tensor(out.ap, in0.ap, in1.ap, op), [in0, in1], [out])

    def ts(self, out, in0, s1, s2=None, op0=ALU.mult, op1=None, e="dve"):
        reads = [in0]
        a1 = s1
        if isinstance(s1, V):
            reads.append(s1)
            a1 = s1.ap
        a2 = s2
        if isinstance(s2, V):
            reads.append(s2)
            a2 = s2.ap
        if op1 is None:
            return self.op(e, lambda g: g.tensor_scalar(out.ap, in0.ap, a1, None, op0), reads, [out])
        return self.op(e, lambda g: g.tensor_scalar(out.ap, in0.ap, a1, a2, op0, op1), reads, [out])

    def stt(self, out, in0, s, in1, op0, op1):
        reads = [in0, in1]
        a = s
        if isinstance(s, V):
            reads.append(s)
            a = s.ap
        return self.op("dve", lambda g: g.scalar_tensor_tensor(out.ap, in0.ap, a, in1.ap, op0, op1), reads, [out])

    def copy(self, out, in_, e="dve"):
        if e == "act":
            return self.op("act", lambda g: g.activation(out.ap, in_.ap, AF.Copy), [in_], [out])
        return self.op(e, lambda g: g.tensor_copy(out.ap, in_.ap), [in_], [out])

    def recip(self, out, in_):
        return self.op("dve", lambda g: g.reciprocal(out.ap, in_.ap), [in_], [out])


def _bf(a):
    return np.ascontiguousarray(a).astype(ml_dtypes.bfloat16)


_CONST = {}


def host_constants():
    if _CONST:
        return _CONST
    c = {}
    t = np.arange(TL, dtype=np.int64)
    ph = (np.outer(t, t) % TL).astype(np.float64) * (2 * np.pi / TL)
    c["CL"] = _bf(np.cos(ph))
    c["NSL"] = _bf(-np.sin(ph))
    tcx = np.arange(TC, dtype=np.int64)
    phc = (np.outer(tcx, tcx) % TC).astype(np.float64) * (2 * np.pi / TC)
    c["CLC"] = _bf(np.cos(phc))
    c["NSLC"] = _bf(-np.sin(phc))
    ch = np.arange(128, dtype=np.int64)
    phd = (np.outer(ch, ch) % 128).astype(np.float64) * (2 * np.pi / 128)
    c["CSC"] = _bf(np.concatenate([np.cos(phd), np.sin(phd)], axis=1))
    rows = np.repeat(np.arange(64, dtype=np.float32), 64)
    cols = np.tile(np.arange(64, dtype=np.float32), 64)
    inv = (10000.0 ** (-np.arange(8, dtype=np.float32) / 8)).astype(np.float32)
    ang_r = rows[:, None] * inv
    ang_c = cols[:, None] * inv
    ang = np.concatenate([ang_r, ang_r, ang_c, ang_c], axis=-1)
    sgn = np.array([-1.0] * 8 + [1.0] * 8 + [-1.0] * 8 + [1.0] * 8, dtype=np.float32)
    C = np.ones((96, TT), np.float32)
    S = np.zeros((96, TT), np.float32)
    C[64:96, :TL] = np.cos(ang).T
    S[64:96, :TL] = (np.sin(ang) * sgn[None, :]).T
    c["ROPEC"] = C
    c["ROPES"] = S
    k = np.arange(128)[:, None]
    m = np.arange(128)[None, :]
    same = (k // 64) == (m // 64)
    ident = (k == m).astype(np.float32)
    ones = np.ones((128, 128), np.float32)
    triF = (same & (k <= m)).astype(np.float32)
    restF = (same & (k > m)).astype(np.float32)
    triB = (same & (k >= m)).astype(np.float32)
    restB = (same & (k < m)).astype(np.float32)
    tot0 = np.broadcast_to((k < 64), (128, 128)).astype(np.float32)
    tot1 = np.broadcast_to((k >= 64), (128, 128)).astype(np.float32)
    j = k
    i = m
    nmF = np.where(same & (i >= j), 0.0, NEG).astype(np.float32)
    nmB = np.where(same & (i <= j), 0.0, NEG).astype(np.float32)
    stF = (same & (i > j)).astype(np.float32)
    stB = (same & (i < j)).astype(np.float32)
    c["MSK"] = np.ascontiguousarray(np.concatenate(
        [ident, ones, triF, restF, tot0, tot1, triB, restB,
         np.tile(nmF, (1, 4)), np.tile(nmB, (1, 4)), np.tile(ident, (1, 4))], axis=1)).astype(np.float32)
    d16 = ((k // 16) == (m // 16)).astype(np.float32)
    c32 = (((k // 32) == (m // 32)) & ((k // 16) != (m // 16))).astype(np.float32)
    c64 = (same & ((k // 32) != (m // 32))).astype(np.float32)
    c["MSKB"] = _bf(np.concatenate([ident, ones, np.tile(stF, (1, 4)), np.tile(stB, (1, 4)), np.tile(ident, (1, 4)),
                                    np.tile(d16, (1, 4)), np.tile(c32, (1, 4)), np.tile(c64, (1, 4))], axis=1))
    _CONST.update(c)
    return c


IN_W = (512, 512, 512, 512, 512, 512, 16, 384, 256, 32, 512, 3072)
IN_OFF = np.concatenate([[0], np.cumsum(IN_W)]).tolist()


def _col(v, nblk):
    return np.ascontiguousarray(v.reshape(nblk, 128).T)


def host_layout(inp, b):
    d = {}
    d["xin"] = np.ascontiguousarray(np.concatenate([inp["x"][b], inp["ctx"][b]], axis=0))
    cc = np.stack([_col(inp["c"][b], 8), _col(inp["c_ctx"], 8)], axis=-1)
    d["ccol"] = np.ascontiguousarray(cc)
    return d


def host_layout_shared(inp):
    d = {}
    o = IN_OFF
    perm = np.concatenate([np.arange(8, 16), np.arange(0, 8), np.arange(24, 32), np.arange(16, 24)])
    w_in = inp["w_in"]
    kpe = w_in[:, :, o[9]:o[10]]
    cols = np.concatenate([
        w_in[:, :, o[0]:o[6]],
        w_in[:, :, o[7]:o[9]],
        w_in[:, :, o[10]:o[11]],
        w_in[:, :, o[11]:o[12]],
        kpe, kpe[:, :, perm],
        np.zeros((2, 1024, 64), np.float32),
    ], axis=-1)
    assert cols.shape[-1] == NB * 128
    d["w_in_r"] = np.ascontiguousarray(cols.reshape(2, 8, 128, NB, 128).transpose(0, 3, 2, 1, 4))
    wab = w_in[:, :, o[6]:o[7]]
    d["w_ab"] = np.ascontiguousarray(wab.reshape(2, 8, 128, 16).transpose(0, 2, 1, 3))
    d["w_mod"] = inp["w_mod"]
    d["bmod"] = np.ascontiguousarray(np.stack([_col(inp["b_mod"][l], 24) for l in range(2)]))
    d["gpre"] = np.ascontiguousarray(np.stack([_col(inp["g_pre"][l], 8) for l in range(2)]))
    d["gpost"] = np.ascontiguousarray(np.stack([_col(inp["g_post"][l], 8) for l in range(2)]))
    d["f_w"] = np.ascontiguousarray(inp["f_w"].transpose(0, 2, 1, 3))
    cw = inp["dn_conv"]
    d["convw"] = np.ascontiguousarray(cw.reshape(2, 3, 12, 128).transpose(0, 3, 2, 1))
    d["alog"] = np.ascontiguousarray(np.broadcast_to(inp["dn_a_log"].reshape(2, 1, 1, 8), (2, 128, NT, 8)))
    d["dtb"] = np.ascontiguousarray(np.broadcast_to(inp["dn_dt_bias"].reshape(2, 1, 1, 8), (2, 128, NT, 8)))
    d["dnorm"] = np.ascontiguousarray(inp["dn_norm"].reshape(2, 128, 1))
    d["qnorm"] = np.ascontiguousarray(np.stack([_col(inp["mla_q_norm"][l], 3) for l in range(2)]))
    d["kvnorm"] = np.ascontiguousarray(np.stack([_col(inp["mla_kv_norm"][l], 2) for l in range(2)]))
    wuq = inp["mla_w_uq"]
    hp = np.concatenate([np.arange(64), 64 + perm])
    permc = np.concatenate([h * 96 + hp for h in range(8)])
    both = np.concatenate([wuq, wuq[:, :, permc]], axis=-1)
    d["w_uq"] = np.ascontiguousarray(both.reshape(2, 3, 128, 1536).transpose(0, 2, 1, 3))
    d["w_ukv"] = np.ascontiguousarray(inp["mla_w_ukv"].reshape(2, 2, 128, 1024).transpose(0, 2, 1, 3))
    d["w_br"] = np.ascontiguousarray(inp["w_branch"].reshape(2, 12, 128, 1024).transpose(0, 2, 1, 3))
    d["w_out"] = np.ascontiguousarray(inp["w_out"].reshape(2, 8, 128, 1024).transpose(0, 2, 1, 3))
    return d


def build(n_layers=2, dbg=(), stop_after=None):
    nc = bass.Bass("TRN2", target_bir_lowering=False)
    fw = FW(nc)
    EI = "ExternalInput"

    def skind(name):
        return "ExternalOutput" if name in dbg else "Internal"

    xin = fw.dram("xin", [TT, 1024], F32, EI)
    ccol_d = fw.dram("ccol", [128, 8, 2], F32, EI)
    w_in_d = fw.dram("w_in_r", [2, NB, 128, 8, 128], F32, EI)
    w_ab_d = fw.dram("w_ab", [2, 128, 8, 16], F32, EI)
    w_mod_d = fw.dram("w_mod", [2, 1024, 3072], F32, EI)
    bmod_d = fw.dram("bmod", [2, 128, 24], F32, EI)
    gpre_d = fw.dram("gpre", [2, 128, 8], F32, EI)
    gpost_d = fw.dram("gpost", [2, 128, 8], F32, EI)
    f_w_d = fw.dram("f_w", [2, 128, 4, 128], F32, EI)
    convw_d = fw.dram("convw", [2, 128, 12, 3], F32, EI)
    alog_d = fw.dram("alog", [2, 128, NT, 8], F32, EI)
    dtb_d = fw.dram("dtb", [2, 128, NT, 8], F32, EI)
    dnorm_d = fw.dram("dnorm", [2, 128, 1], F32, EI)
    qnorm_d = fw.dram("qnorm", [2, 128, 3], F32, EI)
    kvnorm_d = fw.dram("kvnorm", [2, 128, 2], F32, EI)
    w_uq_d = fw.dram("w_uq", [2, 128, 3, 1536], F32, EI)
    w_ukv_d = fw.dram("w_ukv", [2, 128, 2, 1024], F32, EI)
    w_br_d = fw.dram("w_br", [2, 128, 12, 1024], F32, EI)
    w_out_d = fw.dram("w_out", [2, 128, 8, 1024], F32, EI)
    CL_d = fw.dram("CL", [TL, TL], BF16, EI)
    NSL_d = fw.dram("NSL", [TL, TL], BF16, EI)
    CLC_d = fw.dram("CLC", [TC, TC], BF16, EI)
    NSLC_d = fw.dram("NSLC", [TC, TC], BF16, EI)
    CSC_d = fw.dram("CSC", [128, 256], BF16, EI)
    ROPEC_d = fw.dram("ROPEC", [96, TT], F32, EI)
    ROPES_d = fw.dram("ROPES", [96, TT], F32, EI)
    MSK_d = fw.dram("MSK", [128, 8 * 128 + 1536], F32, EI)
    MSKB_d = fw.dram("MSKB", [128, 2 * 128 + 6 * 512], BF16, EI)

    out_d = fw.dram("out", [TL, 1024], F32, "ExternalOutput")
    xs_d = fw.dram("xs", [TT, 1024], F32, skind("xs"))
    pT_d = fw.dram("pT", [NB * 128, TT], BF16, skind("pT"))
    yaT_d = fw.dram("yaT", [512, TT], BF16, skind("yaT"))
    ycT_d = fw.dram("ycT", [512, TT], BF16, skind("ycT"))
    oT_d = [fw.dram("oT%d" % d, [512, TT], F32, skind("oT%d" % d)) for d in range(2)]
    qkvT_d = fw.dram("qkvT", [1536, TT], BF16, skind("qkvT"))
    dbg_d = {}

    msk = fw.sb("msk", [128, 8 * 128 + 1536], F32)
    mskb = fw.sb("mskb", [128, 2 * 128 + 6 * 512], BF16)
    fw.dma("sp", msk[:], MSK_d[:])
    fw.dma("sp", mskb[:], MSKB_d[:])

    def mcol(i):
        return msk[:, i * 128:(i + 1) * 128]
    identF, onesF, triF, restF, tot0, tot1, triB, restB = [mcol(i) for i in range(8)]
    negmask = [msk[:, 1024:1536], msk[:, 1536:2048]]
    ident4F = msk[:, 2048:2560]
    identB = mskb[:, 0:128]
    onesB = mskb[:, 128:256]
    strictB = [mskb[:, 256:768], mskb[:, 768:1280]]
    ident4B = mskb[:, 1280:1792]
    mD16 = mskb[:, 1792:2304]
    mC32 = mskb[:, 2304:2816]
    mC64 = mskb[:, 2816:3328]

    PF = [fw.ps("pf%d" % i, [128, 512], F32) for i in range(7)]
    PB = fw.ps("pb", [128, 1024], BF16)

    ccol = fw.sb("ccol_sb", [128, 8, 2], F32)
    sc = fw.sb("sc", [128, 8, 2], F32)
    modc = fw.sb("modc", [128, 24, 2], F32)
    Acol = fw.sb("Acol", [128, 8, 2], F32)
    ggcol = fw.sb("ggcol", [128, 8, 2], F32)
    ggbc = [fw.sb("ggbc%d" % j, [128, 1024], F32) for j in range(2)]
    bmod = fw.sb("bmod_sb", [128, 24], F32)
    gpre = fw.sb("gpre_sb", [128, 8], F32)
    gpost = fw.sb("gpost_sb", [128, 8], F32)
    abT = fw.sb("abT", [128, NT, 16], F32)
    stat = fw.sb("stat", [128, 4 * NT], F32, nsub=NT)
    fw.dma("sp", ccol[:], ccol_d[:])

    ring_ctr = {}

    def ring(lst, key):
        i = ring_ctr.get(key, 0)
        ring_ctr[key] = i + 1
        return lst[i % len(lst)]

    pfc = [0]

    def pf(lo=0, hi=7):
        i = pfc[0]
        pfc[0] += 1
        return PF[lo + i % (hi - lo)]

    S = {}

    def alloc_s1():
        fw.push()
        S["hT"] = fw.sb("hT", [128, 8, TT], BF16)
        S["w32"] = [fw.sb("w32_%d" % i, [128, 4096], F32) for i in range(2)]
        S["wbf"] = [fw.sb("wbf_%d" % i, [128, 1024], BF16) for i in range(3)]
        S["stg"] = [fw.sb("stg_%d" % i, [128, TT], BF16) for i in range(3)]
        S["xring"] = [fw.sb("xr%d" % i, [128, 1024], F32) for i in range(3)]
        S["hnring"] = [fw.sb("hn%d" % i, [128, 1024], BF16) for i in range(2)]

    def phase_mod(l):
        fw.dma("sp", bmod[:], bmod_d[l])
        fw.dma("sp", gpre[:], gpre_d[l])
        fw.dma("sp", gpost[:], gpost_d[l])
        fw.act(sc[:], ccol[:], AF.Silu)
        pm = PF[0]
        wv = w_mod_d.h[l].rearrange("(kc k) n -> k kc n", k=128)
        for nch in range(6):
            slot = ring(S["w32"], "w32")
            sv = V(slot.h[:, :].rearrange("p (kc n) -> p kc n", kc=8), slot.buf)
            fw.dma("sp", sv, V(wv[:, :, nch * 512:(nch + 1) * 512], w_mod_d.buf))
            for j in range(4):
                blk = nch * 4 + j
                for kc in range(8):
                    fw.mm(pm[:, blk * 2:blk * 2 + 2],
                          V(slot.h[:, kc * 512 + j * 128: kc * 512 + (j + 1) * 128], slot.buf),
                          sc[:, kc, :], start=(kc == 0), stop=(kc == 7))
        for j in range(2):
            fw.tt(modc[:, :, j], V(pm.h[:, 0:48].rearrange("p (b j) -> p b j", j=2)[:, :, j], pm.buf), bmod[:], ALU.add)
            fw.stt(Acol[:, :, j], modc[:, 8:16, j], 1.0, gpre[:], ALU.add, ALU.mult)
            fw.tt(ggcol[:, :, j], modc[:, 16:24, j], gpost[:], ALU.mult)
        for j in range(2):
            for half in range(2):
                pg = pf(1, 7)
                for q in range(4):
                    kc = half * 4 + q
                    D = ring(S["w32"], "w32")
                    fw.ts(D[:, 0:128], identF, ggcol[:, kc, j:j + 1])
                    fw.mm(pg[:, q * 128:(q + 1) * 128], onesF, D[:, 0:128])
                fw.copy(ggbc[j][:, half * 512:(half + 1) * 512], pg[:])

    def phase_h(l):
        src = xin if l == 0 else xs_d
        for tt in range(NT):
            j = 0 if tt < 32 else 1
            xt = ring(S["xring"], "xr")
            fw.dma("sp", xt[:], src[tt * 128:(tt + 1) * 128, :])
            st = stat.sub(tt)
            hn = ring(S["hnring"], "hn")
            fw.act(hn[:], xt[:], AF.Square, accum_out=st[:, 4 * tt:4 * tt + 1])
            fw.act(st[:, 4 * tt + 1:4 * tt + 2], st[:, 4 * tt:4 * tt + 1], AF.Sqrt, bias=EPS, scale=1.0 / 1024)
            fw.recip(st[:, 4 * tt + 2:4 * tt + 3], st[:, 4 * tt + 1:4 * tt + 2])
            fw.ts(hn[:], xt[:], st[:, 4 * tt + 2:4 * tt + 3])
            for kc in range(8):
                fw.tr(PB[:, kc * 128:(kc + 1) * 128], hn[:, kc * 128:(kc + 1) * 128], identB)
            for kc in range(8):
                o = S["hT"][:, kc, tt * 128:(tt + 1) * 128]
                i = PB[:, kc * 128:(kc + 1) * 128]
                if kc % 2 == 0:
                    fw.ts(o, i, Acol[:, kc, j:j + 1], modc[:, kc, j:j + 1], ALU.mult, ALU.add)
                else:
                    fw.act(o, i, AF.Identity, bias=modc[:, kc, j:j + 1], scale=Acol[:, kc, j:j + 1])

    SILU_BLK = list(range(4, 8)) + list(range(20, 24)) + list(range(29, 33))
    SIG_BLK = list(range(33, 57))
    COPY_BLK = [b for b in range(NB) if b not in SILU_BLK and b not in SIG_BLK]

    def phase_proj(l):
        wab32 = ring(S["w32"], "w32")
        fw.dma("sp", wab32[:, 0:128], V(w_ab_d.h[l].rearrange("p kc n -> p (kc n)"), w_ab_d.buf))
        wabb = ring(S["wbf"], "wbf")
        fw.copy(wabb[:, 0:128], wab32[:, 0:128], e="pool")
        pa = [PF[5], PF[6]]
        for tt in range(NT):
            dst = pa[0][:, tt * 16:(tt + 1) * 16] if tt < 32 else pa[1][:, (tt - 32) * 16:(tt - 31) * 16]
            for kc in range(8):
                fw.mm(dst, S["hT"][:, kc, tt * 128:(tt + 1) * 128], wabb[:, kc * 16:(kc + 1) * 16], start=(kc == 0), stop=(kc == 7))
        fw.copy(V(abT.h[:, 0:32, :].rearrange("p a b -> p (a b)"), abT.buf), pa[0][:, 0:512])
        fw.copy(V(abT.h[:, 32:34, :].rearrange("p a b -> p (a b)"), abT.buf), pa[1][:, 0:32])
        for blk in COPY_BLK + SILU_BLK + SIG_BLK:
            ws = ring(S["w32"], "w32")
            fw.dma("sp", ws[:, 0:1024], V(w_in_d.h[l, blk].rearrange("p kc n -> p (kc n)"), w_in_d.buf))
            wb = ring(S["wbf"], "wbf")
            fw.copy(wb[:, 0:1024], ws[:, 0:1024], e="pool")
            sg = ring(S["stg"], "stg")
            for ci, (t0, n) in enumerate(CHUNKS):
                ps = pf(0, 5)
                for kc in range(8):
                    fw.mm(ps[:, 0:n], wb[:, kc * 128:(kc + 1) * 128], S["hT"][:, kc, t0:t0 + n], start=(kc == 0), stop=(kc == 7))
                if blk in SILU_BLK:
                    fw.act(sg[:, t0:t0 + n], ps[:, 0:n], AF.Silu)
                elif blk in SIG_BLK:
                    fw.act(sg[:, t0:t0 + n], ps[:, 0:n], AF.Sigmoid)
                else:
                    fw.copy(sg[:, t0:t0 + n], ps[:, 0:n])
            fw.dma("pool", pT_d[blk * 128:(blk + 1) * 128, :], sg[:], semof=sg.buf)


    def v3(t, n):
        return t.h[:, :].rearrange("p (a n) -> p a n", n=n)

    def phase_fourier(l, with_ctx):
        fw.push()
        UT = fw.sb("UT", [128, 4, TT], BF16)
        ABs = fw.sb("ABs", [128, NT, 1024], BF16)
        csc = fw.sb("csc", [128, 256], BF16)
        fw32 = fw.sb("fw32", [128, 512], F32)
        fwb = fw.sb("fwb", [128, 512], BF16)
        tbC = [fw.sb("tbC%d" % i, [128, 8, 512], BF16) for i in range(2)]
        tbS = [fw.sb("tbS%d" % i, [128, 8, 512], BF16) for i in range(2)]
        specb = [fw.sb("specb%d" % i, [128, 512], BF16) for i in range(2)]
        szr = [fw.sb("szr%d" % i, [128, 4, 512], BF16) for i in range(2)]
        yast = [fw.sb("yast%d" % i, [128, 4, 512], BF16) for i in range(2)]
        fw.dma("sp", csc[:], CSC_d[:])
        fw.dma("sp", fw32[:], V(f_w_d.h[l].rearrange("p g d -> p (g d)"), f_w_d.buf))
        fw.copy(fwb[:], fw32[:], e="pool")
        for g in range(4):
            fw.dma("sp", UT[:, g, :], pT_d[g * 128:(g + 1) * 128, :])
        for tt in range(NT if with_ctx else 32):
            for half in range(2):
                ps = pf(4, 7)
                for gg in range(2):
                    g = half * 2 + gg
                    fw.mm(ps[:, gg * 256:(gg + 1) * 256], UT[:, g, tt * 128:(tt + 1) * 128], csc[:])
                fw.copy(ABs[:, tt, half * 512:(half + 1) * 512], ps[:], e=("dve" if half == 0 else "act"))
        osc = 1.0 / math.sqrt(128.0)
        jobs = [(ci, t0, n, 0, 32, CL_d, NSL_d, TL) for ci, (t0, n) in enumerate(CHUNKS[:8])]
        if with_ctx:
            jobs.append((8, TL, TC, 32, 2, CLC_d, NSLC_d, TC))
        for (ci, t0, n, tt0, ntile, Cd, Sd, Lseq) in jobs:
            sz = ring(szr, "szr")
            fw.dma("sp", sz[:, :, 0:n], V(pT_d.h[512:1024, :].rearrange("(g p) t -> p g t", p=128)[:, :, t0:t0 + n], pT_d.buf))
            c0 = t0 - tt0 * 128
            nq = (ntile + 7) // 8
            for qd in range(nq):
                na = min(8, ntile - qd * 8)
                tc_ = ring(tbC, "tbC")
                tsn = ring(tbS, "tbS")
                fw.dma("sp", tc_[:, 0:na, 0:n], V(Cd.h[qd * 1024: qd * 1024 + na * 128, :].rearrange("(a p) n -> p a n", p=128)[:, :, c0:c0 + n], Cd.buf))
                fw.dma("sp", tsn[:, 0:na, 0:n], V(Sd.h[qd * 1024: qd * 1024 + na * 128, :].rearrange("(a p) n -> p a n", p=128)[:, :, c0:c0 + n], Sd.buf))
                for g in range(4):
                    for a in range(na):
                        tt = tt0 + qd * 8 + a
                        first = (qd == 0 and a == 0)
                        last = (qd == nq - 1 and a == na - 1)
                        fw.mm(PF[g][:, 0:n], ABs[:, tt, g * 256: g * 256 + 128], tc_[:, a, 0:n], start=first, stop=False)
                        fw.mm(PF[g][:, 0:n], ABs[:, tt, g * 256 + 128: g * 256 + 256], tsn[:, a, 0:n], start=False, stop=last)
            ya = ring(yast, "yast")
            scl = osc / math.sqrt(float(Lseq))
            for g in range(4):
                sb_ = ring(specb, "specb")
                fw.act(sb_[:, 0:n], PF[g][:, 0:n], AF.Copy, scale=scl)
                po = pf(4, 7)
                fw.mm(po[:, 0:n], fwb[:, g * 128:(g + 1) * 128], sb_[:, 0:n])
                fw.tt(ya[:, g, 0:n], po[:, 0:n], sz[:, g, 0:n], ALU.mult)
            fw.dma("pool", V(yaT_d.h.rearrange("(g p) t -> p g t", p=128)[:, :, t0:t0 + n], yaT_d.buf), ya[:, :, 0:n], semof=ya.buf)
        fw.pop()

    def phase_gdn(l):
        fw.push()
        convw = fw.sb("convw", [128, 12, 3], F32)
        fw.dma("sp", convw[:], convw_d[l])
        fw.push()
        cin = [fw.sb("cin%d" % i, [128, TT], BF16) for i in range(2)]
        cy = [fw.sb("cy%d" % i, [128, TT], F32) for i in range(2)]
        cst = [fw.sb("cst%d" % i, [128, TT], BF16) for i in range(2)]
        sqr = [fw.sb("sqr%d" % i, [128, 512], BF16) for i in range(2)]
        rr = [fw.sb("rr%d" % i, [128, 512], F32) for i in range(2)]
        for blk in range(12):
            xi = ring(cin, "cin")
            y = ring(cy, "cy")
            so = ring(cst, "cst")
            fw.dma("sp", xi[:], pT_d[1024 + blk * 128: 1024 + (blk + 1) * 128, :])
            for (a_, b_) in ((0, TL), (TL, TT)):
                fw.ts(y[:, a_:b_], xi[:, a_:b_], convw[:, blk, 1:2])
                fw.stt(y[:, a_ + 1:b_], xi[:, a_:b_ - 1], convw[:, blk, 0:1], y[:, a_ + 1:b_], ALU.mult, ALU.add)
                fw.stt(y[:, a_:b_ - 1], xi[:, a_ + 1:b_], convw[:, blk, 2:3], y[:, a_:b_ - 1], ALU.mult, ALU.add)
            if blk >= 8:
                fw.act(so[:], y[:], AF.Silu)
            else:
                fw.act(y[:], y[:], AF.Silu)
                for (t0, n) in CHUNKS:
                    sq = ring(sqr, "sqr")
                    r = ring(rr, "rr")
                    fw.tt(sq[:, 0:n], y[:, t0:t0 + n], y[:, t0:t0 + n], ALU.mult, e="pool")
                    ps = pf(0, 7)
                    fw.mm(ps[:, 0:n], onesB, sq[:, 0:n])
                    fw.act(r[:, 0:n], ps[:, 0:n], AF.Ln, bias=EPS)
                    fw.act(r[:, 0:n], r[:, 0:n], AF.Exp, scale=-0.5)
                    fw.tt(so[:, t0:t0 + n], y[:, t0:t0 + n], r[:, 0:n], ALU.mult)
            fw.dma("pool", qkvT_d[blk * 128:(blk + 1) * 128, :], so[:], semof=so.buf)
        fw.pop()

        NC = NT * 4
        bb = fw.sb("bb", [128, NT, 8], F32)
        gd = [fw.sb("gd%d" % d, [128, NC], F32) for d in range(2)]
        names = ("Gs", "nG", "EG", "ER", "GL0", "GL1")
        GA = [{nm: fw.sb("%s%d" % (nm, d), [128, NC], F32) for nm in names} for d in range(2)]
        fw.push()
        alog = fw.sb("alog", [128, NT, 8], F32)
        dtb = fw.sb("dtb", [128, NT, 8], F32)
        fw.dma("sp", alog[:], alog_d[l])
        fw.dma("sp", dtb[:], dtb_d[l])
        gz = fw.sb("gz", [128, NT, 8], F32)
        fw.tt(gz[:], abT[:, :, 0:8], dtb[:], ALU.add)
        fw.act(gz[:], gz[:], AF.Exp)
        fw.act(gz[:], gz[:], AF.Ln, bias=1.0)
        fw.act(alog[:], alog[:], AF.Exp)
        fw.stt(gz[:], gz[:], -1.0, alog[:], ALU.mult, ALU.mult)
        fw.act(bb[:], abT[:, :, 8:16], AF.Exp, scale=-1.0)
        fw.ts(bb[:], bb[:], 1.0, None, ALU.add)
        fw.recip(bb[:], bb[:])
        for d in range(2):
            fw.copy(V(v3(gd[d], 4), gd[d].buf), gz[:, :, d * 4:(d + 1) * 4])
        fw.pop()
        for d in range(2):
            tri = triF if d == 0 else triB
            rest = restF if d == 0 else restB
            p1 = pf(0, 7)
            fw.mm(p1[:, 0:NC], tri, gd[d][:])
            fw.copy(GA[d]["Gs"][:], p1[:, 0:NC])
            fw.ts(GA[d]["nG"][:], p1[:, 0:NC], -1.0)
            fw.act(GA[d]["EG"][:], p1[:, 0:NC], AF.Exp)
            p2 = pf(0, 7)
            fw.mm(p2[:, 0:NC], rest, gd[d][:])
            fw.act(GA[d]["ER"][:], p2[:, 0:NC], AF.Exp)
            p3 = pf(0, 7)
            fw.mm(p3[:, 0:NC], tot0, gd[d][:])
            fw.act(GA[d]["GL0"][:], p3[:, 0:NC], AF.Exp)
            p4 = pf(0, 7)
            fw.mm(p4[:, 0:NC], tot1, gd[d][:])
            fw.act(GA[d]["GL1"][:], p4[:, 0:NC], AF.Exp)

        def bc4(t_, tt, rows=slice(0, 128)):
            ap = t_.h[rows, tt * 4: tt * 4 + 4].unsqueeze(2)
            return V(ap.broadcast_to([rows.stop - rows.start, 4, 128]), t_.buf)

        def bcb(tt, d, rows=slice(0, 128)):
            ap = bb.h[rows, tt, d * 4:(d + 1) * 4].unsqueeze(2)
            return V(ap.broadcast_to([rows.stop - rows.start, 4, 128]), bb.buf)

        slots = [[], []]
        for d in range(2):
            for i in range(3):
                slots[d].append({
                    "w0T": fw.sb("w0T%d%d" % (d, i), [128, 512], BF16), "qkdT": fw.sb("qkdT%d%d" % (d, i), [128, 512], BF16),
                    "qgT": fw.sb("qgT%d%d" % (d, i), [128, 512], BF16), "kd": fw.sb("kd%d%d" % (d, i), [128, 512], BF16),
                    "ub": fw.sb("ub%d%d" % (d, i), [128, 512], F32)})
        tmps = []
        for i in range(4):
            tm_ = {
                "EGr": fw.sb("EGr%d" % i, [128, 512], F32),
                "tD": fw.sb("tD%d" % i, [128, 512], F32), "dec": fw.sb("dec%d" % i, [128, 512], F32),
                "kEG": fw.sb("kEG%d" % i, [128, 512], BF16), "vtok": fw.sb("vtok%d" % i, [128, 512], BF16),
                "TTb": fw.sb("TTb%d" % i, [128, 512], BF16),
                "qk": [fw.sb("qk%d_%d" % (i, k_), [128, 12, 128], BF16) for k_ in range(1)]}
            for nm in ("M0", "MT0", "Qa", "QTa", "Qb", "QTb", "Pa", "PTa", "Pb", "PTb", "C32", "C32T", "C64", "C64T"):
                tm_[nm] = fw.sb("%s_%d" % (nm, i), [128, 512], CHDT)
            tm_["Rp"] = tm_["dec"]
            tm_["Mf"] = tm_["tD"]
            tmps.append(tm_)
        S32 = [fw.sb("S32_%d" % d, [128, 512], F32) for d in range(2)]
        Sb = [fw.sb("Sb_%d" % d, [128, 512], BF16) for d in range(2)]
        un = [fw.sb("un_%d" % d, [128, 512], BF16) for d in range(2)]
        t5 = [fw.sb("t5_%d" % d, [128, 512], F32) for d in range(2)]
        ost = [[fw.sb("ost%d%d" % (d, i), [128, 512], F32) for i in range(2)] for d in range(2)]
        for d in range(2):
            fw.op("pool", lambda g, d=d: g.memset(S32[d].h[:, :], 0.0), [], [S32[d][:]])
            fw.op("pool", lambda g, d=d: g.memset(Sb[d].h[:, :], 0.0), [], [Sb[d][:]])
        qscale = 128.0 ** -0.5
        PGB = [(PF[0], PF[1]), (PF[2], PF[6])]

        def prep(tt, d, sl, s_):
            tm = tmps[d * 2 + s_ % 2]
            tok = slice(tt * 128, (tt + 1) * 128)
            G = GA[d]

            def pg():
                return PGB[d][s_ % 2]

            qk = tm["qk"][0]
            fw.dma("sp", qk[:], V(qkvT_d.h.rearrange("(b p) t -> p b t", p=128)[:, :, tok], qkvT_d.buf))
            fw.tt(V(v3(tm["Rp"], 128), tm["Rp"].buf), V(ident4F.ap.rearrange("p (a n) -> p a n", n=128), ident4F.buf),
                  bc4(G["Gs"], tt), ALU.mult, e="pool")
            yield
            p1 = pg()
            fw.mm(p1[:], onesF, tm["Rp"][:])
            yield
            fw.act(tm["EGr"][:], p1[:], AF.Exp, bias=math.log(qscale))
            fw.tt(tm["tD"][:], p1[:], negmask[d], ALU.add)
            fw.tt(V(v3(tm["tD"], 128), tm["tD"].buf), V(v3(tm["tD"], 128), tm["tD"].buf), bc4(G["nG"], tt), ALU.add)
            yield
            fw.act(tm["dec"][:], tm["tD"][:], AF.Exp)
            pk = pg()
            for h in range(4):
                fw.mm(pk[:, h * 128:(h + 1) * 128], qk[:, 4 + h, :], qk[:, 4 + h, :])
            yield
            fw.tt(tm["Mf"][:], pk[:], tm["dec"][:], ALU.mult)
            fw.tt(V(v3(tm["Mf"], 128), tm["Mf"].buf), V(v3(tm["Mf"], 128), tm["Mf"].buf), bcb(tt, d), ALU.mult)
            yield
            M0, MT0 = tm["M0"], tm["MT0"]
            fw.tt(M0[:], tm["Mf"][:], strictB[d], ALU.mult, e="pool")
            yield
            if CHDT == F32:
                ptr = pg()
                for h in range(4):
                    fw.tr(ptr[:, h * 128:(h + 1) * 128], M0[:, h * 128:(h + 1) * 128], identF)
                yield
                fw.copy(MT0[:], ptr[:], e="act")
            else:
                for h in range(4):
                    fw.tr(PB[:, h * 128:(h + 1) * 128], M0[:, h * 128:(h + 1) * 128], identB)
                fw.copy(MT0[:], PB[:, 0:512], e="act")
            Q, QT, P, PT = tm["Qa"], tm["QTa"], tm["Pa"], tm["PTa"]
            Qn, QTn, Pn, PTn = tm["Qb"], tm["QTb"], tm["Pb"], tm["PTb"]
            fw.tt(Q[:], M0[:], mD16, ALU.mult, e="pool")
            yield
            fw.tt(QT[:], MT0[:], mD16, ALU.mult, e="pool")
            fw.tt(P[:], ident4F, Q[:], ALU.subtract)
            yield
            fw.tt(PT[:], ident4F, QT[:], ALU.subtract)
            fw.tt(tm["C32"][:], M0[:], mC32, ALU.mult, e="pool")
            fw.tt(tm["C32T"][:], MT0[:], mC32, ALU.mult, e="pool")
            fw.tt(tm["C64"][:], M0[:], mC64, ALU.mult, e="pool")
            fw.tt(tm["C64T"][:], MT0[:], mC64, ALU.mult, e="pool")
            yield

            def mm4(lhsT, rhs):
                p_ = pg()
                for h in range(4):
                    hs = slice(h * 128, (h + 1) * 128)
                    fw.mm(p_[:, hs], lhsT[:, hs], rhs[:, hs])
                return p_

            for lev in range(3):
                pq = mm4(QT, Q)
                yield
                fw.copy(Qn[:], pq[:], e="dve")
                pqt = mm4(Q, QT)
                yield
                fw.copy(QTn[:], pqt[:], e="act")
                yield
                pp = mm4(QTn, P)
                yield
                fw.tt(Pn[:], pp[:], P[:], ALU.add)
                ppt = mm4(Qn, PT)
                yield
                fw.tt(PTn[:], ppt[:], PT[:], ALU.add)
                yield
                Q, Qn = Qn, Q
                QT, QTn = QTn, QT
                P, Pn = Pn, P
                PT, PTn = PTn, PT
            X, XT = P, PT
            py = mm4(tm["C32T"], X)
            yield
            fw.copy(Qn[:], py[:], e="act")
            pyp = mm4(tm["C32"], XT)
            yield
            fw.copy(QTn[:], pyp[:], e="dve")
            yield
            px = mm4(XT, Qn)
            yield
            fw.tt(Pn[:], X[:], px[:], ALU.subtract)
            pxt = mm4(X, QTn)
            yield
            fw.tt(PTn[:], XT[:], pxt[:], ALU.subtract)
            yield
            X, XT = Pn, PTn
            py = mm4(tm["C64T"], X)
            yield
            fw.copy(Q[:], py[:], e="act")
            yield
            px = mm4(XT, Q)
            yield
            fw.tt(tm["TTb"][:], X[:], px[:], ALU.subtract)
            TTm = tm["TTb"]
            for h in range(4):
                fw.tr(PB[:, h * 128:(h + 1) * 128], qk[:, 4 + h, :], identB)
            for h in range(4):
                fw.tr(PB[:, 512 + h * 128: 512 + (h + 1) * 128], qk[:, 8 + h, :], identB)
            fw.tt(V(v3(tm["kEG"], 128), tm["kEG"].buf), V(PB.h[:, 0:512].rearrange("p (a n) -> p a n", n=128), PB.buf), bc4(G["EG"], tt), ALU.mult)
            fw.tt(V(v3(sl["kd"], 128), sl["kd"].buf), V(PB.h[:, 0:512].rearrange("p (a n) -> p a n", n=128), PB.buf), bc4(G["ER"], tt), ALU.mult)
            fw.copy(tm["vtok"][:], PB[:, 512:1024], e="dve")
            yield
            po = pg()
            for h in range(4):
                hs = slice(h * 128, (h + 1) * 128)
                fw.mm(po[:, hs], tm["kEG"][:, hs], TTm[:, hs])
            yield
            fw.copy(sl["w0T"][:], po[:], e="act")
            po2 = pg()
            for h in range(4):
                hs = slice(h * 128, (h + 1) * 128)
                fw.mm(po2[:, hs], TTm[:, hs], tm["vtok"][:, hs])
            yield
            fw.tt(V(v3(sl["ub"], 128), sl["ub"].buf), V(po2.h[:, :].rearrange("p (a n) -> p a n", n=128), po2.buf), bcb(tt, d), ALU.mult)
            po3 = pg()
            for h in range(4):
                fw.mm(po3[:, h * 128:(h + 1) * 128], qk[:, 4 + h, :], qk[:, h, :])
            yield
            fw.stt(sl["qkdT"][:], po3[:], qscale, tm["dec"][:], ALU.mult, ALU.mult)
            fw.tt(V(v3(sl["qgT"], 128), sl["qgT"].buf), qk[:, 0:4, :], V(v3(tm["EGr"], 128), tm["EGr"].buf), ALU.mult, e="pool")
            yield

        def scan(tt, d, sl):
            G = GA[d]
            BX = PF[3 + d]
            BY = PF[5]
            o = ring(ost[d], "ost%d" % d)
            for ci in ((0, 1) if d == 0 else (1, 0)):
                cs = slice(64 * ci, 64 * ci + 64)
                for h in range(4):
                    hs = slice(h * 128, (h + 1) * 128)
                    fw.mm(BX[cs, hs], sl["w0T"][:, h * 128 + 64 * ci: h * 128 + 64 * ci + 64], Sb[d][:, hs])
                t5v = V(t5[d].h[cs, :].rearrange("p (a n) -> p a n", n=128), t5[d].buf)
                fw.tt(t5v, V(BX.h[cs, :].rearrange("p (a n) -> p a n", n=128), BX.buf), bcb(tt, d, cs), ALU.mult)
                fw.tt(un[d][cs, :], sl["ub"][cs, :], t5[d][cs, :], ALU.subtract)
                yield
                for h in range(4):
                    oc = slice(h * 128 + 64 * ci, h * 128 + 64 * ci + 64)
                    yc_ = slice(d * 256 + h * 64, d * 256 + (h + 1) * 64)
                    hs = slice(h * 128, (h + 1) * 128)
                    fw.mm(BY[:, yc_], Sb[d][:, hs], sl["qgT"][:, oc], start=True, stop=False)
                    fw.mm(BY[:, yc_], un[d][cs, hs], sl["qkdT"][cs, oc], start=False, stop=True)
                fw.copy(V(o.h[:, :].rearrange("p (a n) -> p a n", n=128)[:, :, 64 * ci: 64 * ci + 64], o.buf),
                        V(BY.h[:, d * 256:(d + 1) * 256].rearrange("p (a n) -> p a n", n=64), BY.buf), e="act")
                for h in range(4):
                    hs = slice(h * 128, (h + 1) * 128)
                    fw.mm(BX[:, hs], sl["kd"][cs, hs], un[d][cs, hs])
                gl = G["GL0"] if ci == 0 else G["GL1"]
                fw.tt(V(v3(S32[d], 128), S32[d].buf), V(v3(S32[d], 128), S32[d].buf), bc4(gl, tt), ALU.mult)
                fw.tt(S32[d][:], S32[d][:], BX[:], ALU.add)
                yield
                fw.copy(Sb[d][:], S32[d][:], e="act")
                yield
            fw.dma("pool", V(oT_d[d].h.rearrange("(h p) t -> p h t", p=128)[:, :, tt * 128:(tt + 1) * 128], oT_d[d].buf),
                   V(v3(o, 128), o.buf), semof=o.buf)
            yield

        def run_il(gens):
            gens = list(gens)
            while gens:
                for g_ in list(gens):
                    try:
                        next(g_)
                    except StopIteration:
                        gens.remove(g_)

        order = [[32, 33] + list(range(32)), [33, 32] + list(range(31, -1, -1))]
        active = []
        p_started = [0, 0]
        p_done = [0, 0]
        s_started = [0, 0]
        s_done = [0, 0]
        while s_done[0] < NT or s_done[1] < NT:
            for d in range(2):
                if p_started[d] < NT and p_started[d] - p_done[d] < 2 and p_started[d] - s_done[d] < 3:
                    k_ = p_started[d]
                    active.append((prep(order[d][k_], d, slots[d][k_ % 3], k_), "p", d))
                    p_started[d] += 1
                if s_started[d] < p_done[d] and s_started[d] == s_done[d]:
                    k_ = s_started[d]
                    active.append((scan(order[d][k_], d, slots[d][k_ % 3]), "s", d))
                    s_started[d] += 1
            for item in list(active):
                g_, kind, d = item
                try:
                    next(g_)
                except StopIteration:
                    active.remove(item)
                    if kind == "p":
                        p_done[d] += 1
                    else:
                        s_done[d] += 1
        fw.pop()

    def phase_mla(l, with_ctx):
        fw.push()
        cqn = fw.sb("cqn", [128, 3, TT], BF16)
        ckvn = fw.sb("ckvn", [128, 2, TT], BF16)
        ropeC = fw.sb("ropeC", [96, TT], BF16)
        ropeS = fw.sb("ropeS", [96, TT], BF16)
        kper = fw.sb("kper", [96, TT], BF16)
        wuq = fw.sb("wuq", [128, 3, 1536], BF16)
        wukv = fw.sb("wukv", [128, 2, 1024], BF16)
        qn = fw.sb("qn", [128, 3], F32)
        kvn = fw.sb("kvn", [128, 2], F32)
        fw.dma("sp", qn[:], qnorm_d[l])
        fw.dma("sp", kvn[:], kvnorm_d[l])
        fw.push()
        st32 = fw.sb("st32", [128, 4608], F32)
        fw.dma("sp", st32[:, 0:4608], V(w_uq_d.h[l].rearrange("p a n -> p (a n)"), w_uq_d.buf))
        fw.copy(V(wuq.h[:, :, :].rearrange("p a n -> p (a n)"), wuq.buf), st32[:, 0:4608], e="pool")
        fw.dma("sp", st32[:, 0:2048], V(w_ukv_d.h[l].rearrange("p a n -> p (a n)"), w_ukv_d.buf))
        fw.copy(V(wukv.h[:, :, :].rearrange("p a n -> p (a n)"), wukv.buf), st32[:, 0:2048], e="pool")
        fw.dma("sp", st32[0:96, 0:TT], ROPEC_d[:])
        fw.copy(ropeC[:], st32[0:96, 0:TT], e="pool")
        fw.dma("sp", st32[0:96, 0:TT], ROPES_d[:])
        fw.copy(ropeS[:], st32[0:96, 0:TT], e="pool")
        kpa = fw.sb("kpa", [96, TT], BF16)
        kpb = fw.sb("kpb", [96, TT], BF16)
        fw.dma("sp", kpa[64:96, :], pT_d[7296:7328, :])
        fw.dma("sp", kpb[64:96, :], pT_d[7328:7360, :])
        fw.tt(st32[64:96, 0:TT], kpa[64:96, :], ropeC[64:96, :], ALU.mult)
        fw.tt(kpb[64:96, :], kpb[64:96, :], ropeS[64:96, :], ALU.mult)
        fw.tt(kper[64:96, :], st32[64:96, 0:TT], kpb[64:96, :], ALU.add)
        sqr = [fw.sb("msq%d" % i, [128, 512], BF16) for i in range(3)]
        rr = [fw.sb("mrr%d" % i, [128, 512], F32) for i in range(2)]
        for (dst, nb_, row0, nrm, width) in ((cqn, 3, 3072, qn, 384.0), (ckvn, 2, 3456, kvn, 256.0)):
            for b_ in range(nb_):
                fw.dma("sp", dst[:, b_, :], pT_d[row0 + b_ * 128: row0 + (b_ + 1) * 128, :])
            for (t0, n) in CHUNKS:
                ps = pf(0, 5)
                sqs = []
                for b_ in range(nb_):
                    sq = ring(sqr, "msq")
                    fw.tt(sq[:, 0:n], dst[:, b_, t0:t0 + n], dst[:, b_, t0:t0 + n], ALU.mult, e="pool")
                    sqs.append(sq)
                for b_ in range(nb_):
                    fw.mm(ps[:, 0:n], onesB, sqs[b_][:, 0:n], start=(b_ == 0), stop=(b_ == nb_ - 1))
                r = ring(rr, "mrr")
                fw.act(r[:, 0:n], ps[:, 0:n], AF.Ln, bias=EPS, scale=1.0 / width)
                fw.act(r[:, 0:n], r[:, 0:n], AF.Exp, scale=-0.5)
                for b_ in range(nb_):
                    fw.stt(dst[:, b_, t0:t0 + n], dst[:, b_, t0:t0 + n], nrm[:, b_:b_ + 1], r[:, 0:n], ALU.mult, ALU.mult)
        fw.pop()

        kTh = [fw.sb("kTh%d" % i, [96, TT], BF16) for i in range(2)]
        qTh = [fw.sb("qTh%d" % i, [96, TT], BF16) for i in range(2)]
        Vaug = [fw.sb("Vaug%d" % i, [128, NT, 128], BF16) for i in range(2)]
        for i in range(2):
            fw.op("pool", lambda g, i=i: g.memset(Vaug[i].h[:, :, 64:128], 1.0), [], [Vaug[i][:, :, 64:128]])
        PTr = [fw.sb("PTr%d" % i, [128, 512], BF16) for i in range(4)]
        q1 = [fw.sb("q1_%d" % i, [96, 512], F32) for i in range(2)]
        q2 = [fw.sb("q2_%d" % i, [96, 512], F32) for i in range(2)]
        rden = [fw.sb("rden%d" % i, [128, 512], F32) for i in range(2)]
        otmp = [fw.sb("otmp%d" % i, [64, 512], F32) for i in range(2)]
        zc = [fw.sb("zc%d" % i, [64, 512], BF16) for i in range(2)]
        ycs = [fw.sb("ycs%d" % i, [64, 512], BF16) for i in range(2)]
        ascale = 96.0 ** -0.5
        qchunks = CHUNKS if with_ctx else CHUNKS[:8]
        for h in range(8):
            kt_ = kTh[h % 2]
            qt_ = qTh[h % 2]
            va = Vaug[h % 2]
            for (t0, n) in CHUNKS:
                ps = pf(0, 5)
                for kc in range(2):
                    fw.mm(ps[0:64, 0:n], wukv[:, kc, h * 128: h * 128 + 64], ckvn[:, kc, t0:t0 + n], start=(kc == 0), stop=(kc == 1))
                fw.copy(kt_[0:64, t0:t0 + n], ps[0:64, 0:n], e="dve")
            fw.copy(kt_[64:96, :], kper[64:96, :], e="pool")
            for t8 in range(0, NT, 8):
                nt8 = min(8, NT - t8)
                ps = pf(0, 5)
                for a in range(nt8):
                    tt = t8 + a
                    for kc in range(2):
                        fw.mm(ps[:, a * 64:(a + 1) * 64], ckvn[:, kc, tt * 128:(tt + 1) * 128], wukv[:, kc, h * 128 + 64: h * 128 + 128],
                              start=(kc == 0), stop=(kc == 1))
                fw.copy(va[:, t8:t8 + nt8, 0:64], V(ps.h[:, 0:nt8 * 64].rearrange("p (a n) -> p a n", n=64), ps.buf), e="dve")
            for (t0, n) in qchunks:
                pa = pf(0, 5)
                for kc in range(3):
                    fw.mm(pa[0:96, 0:n], wuq[:, kc, h * 96:(h + 1) * 96], cqn[:, kc, t0:t0 + n], start=(kc == 0), stop=(kc == 2))
                pb_ = pf(0, 5)
                for kc in range(3):
                    fw.mm(pb_[0:96, 0:n], wuq[:, kc, 768 + h * 96: 768 + (h + 1) * 96], cqn[:, kc, t0:t0 + n], start=(kc == 0), stop=(kc == 2))
                a1 = ring(q1, "q1")
                a2 = ring(q2, "q2")
                fw.tt(a1[:, 0:n], pa[0:96, 0:n], ropeC[:, t0:t0 + n], ALU.mult)
                fw.tt(a2[:, 0:n], pb_[0:96, 0:n], ropeS[:, t0:t0 + n], ALU.mult)
                fw.tt(qt_[:, t0:t0 + n], a1[:, 0:n], a2[:, 0:n], ALU.add, e="pool")
            for qi, (t0, n) in enumerate(qchunks):
                ktiles = list(range(NT)) if t0 < TL else [32, 33]
                acc = PF[5 + qi % 2]
                zt = ring(zc, "zc")
                fw.dma("sp", zt[:, 0:n], pT_d[3712 + h * 64: 3712 + (h + 1) * 64, t0:t0 + n])
                LA = 4
                pss = {}

                def issue_s(ki_):
                    ps_ = pf(0, 5)
                    kt__ = ktiles[ki_]
                    fw.mm(ps_[:, 0:n], kt_[:, kt__ * 128:(kt__ + 1) * 128], qt_[:, t0:t0 + n])
                    pss[ki_] = ps_

                for ki in range(min(LA, len(ktiles))):
                    issue_s(ki)
                for ki, kt in enumerate(ktiles):
                    ps = pss.pop(ki)
                    pt = ring(PTr, "PTr")
                    fw.act(pt[:, 0:n], ps[:, 0:n], AF.Exp, scale=ascale)
                    fw.mm(acc[:, 0:n], va[:, kt, :], pt[:, 0:n], start=(ki == 0), stop=(ki == len(ktiles) - 1))
                    if ki + LA < len(ktiles):
                        issue_s(ki + LA)
                rd = ring(rden, "rden")
                fw.recip(rd[64:128, 0:n], acc[64:128, 0:n])
                ot = ring(otmp, "otmp")
                fw.tt(ot[:, 0:n], acc[0:64, 0:n], rd[64:128, 0:n], ALU.mult)
                yc = ring(ycs, "ycs")
                fw.tt(yc[:, 0:n], ot[:, 0:n], zt[:, 0:n], ALU.mult, e="pool")
                fw.dma("pool", ycT_d[h * 64:(h + 1) * 64, t0:t0 + n], yc[:, 0:n], semof=yc.buf)
        fw.pop()

    def phase_final(l, with_ctx, last):
        fw.push()
        wbr = fw.sb("wbr", [128, 12, 1024], BF16)
        wout = fw.sb("wout", [128, 8, 1024], BF16)
        dnrm = fw.sb("dnrm", [128, 1], F32)
        fw.dma("sp", dnrm[:], dnorm_d[l])
        fw.push()
        st32 = [fw.sb("fst32_%d" % i, [128, 4096], F32) for i in range(2)]
        for q in range(3):
            st = ring(st32, "fst32")
            fw.dma("sp", st[:], V(w_br_d.h[l, :, q * 4:(q + 1) * 4, :].rearrange("p a n -> p (a n)"), w_br_d.buf))
            fw.copy(V(wbr.h[:, q * 4:(q + 1) * 4, :].rearrange("p a n -> p (a n)"), wbr.buf), st[:], e="pool")
        for q in range(2):
            st = ring(st32, "fst32")
            fw.dma("sp", st[:], V(w_out_d.h[l, :, q * 4:(q + 1) * 4, :].rearrange("p a n -> p (a n)"), w_out_d.buf))
            fw.copy(V(wout.h[:, q * 4:(q + 1) * 4, :].rearrange("p a n -> p (a n)"), wout.buf), st[:], e="pool")
        fw.pop()
        of_r = [fw.sb("of%d" % i, [128, 4, 512], F32) for i in range(2)]
        ob_r = [fw.sb("ob%d" % i, [128, 4, 512], F32) for i in range(2)]
        sqb = fw.sb("sqb", [128, 4, 512], BF16)
        rr = [fw.sb("frr%d" % i, [128, 512], F32) for i in range(2)]
        szb_r = [fw.sb("szb%d" % i, [128, 4, 512], BF16) for i in range(2)]
        ybT = fw.sb("ybT", [128, 4, 512], BF16)
        yaT_r = [fw.sb("yaTt%d" % i, [128, 4, 512], BF16) for i in range(2)]
        ycT_r = [fw.sb("ycTt%d" % i, [128, 4, 512], BF16) for i in range(2)]
        gts = [fw.sb("gts%d" % i, [128, 8, 512], BF16) for i in range(2)]
        merged = fw.sb("merged", [128, 8, 512], F32)
        mergedb = fw.sb("mergedb", [128, 8, 512], BF16)
        tmpf = [fw.sb("tmpf%d" % i, [128, 512], F32) for i in range(2)]
        xr = [fw.sb("fxr%d" % i, [128, 1024], F32) for i in range(2)]
        xo = [fw.sb("fxo%d" % i, [128, 1024], F32) for i in range(2)]
        junk = fw.sb("junk", [128, 512], BF16)
        st4 = fw.sb("st4", [128, 8 * NT], F32, nsub=NT)
        xsrc = xin if l == 0 else xs_d
        chunks = CHUNKS if with_ctx else CHUNKS[:8]
        for (t0, n) in chunks:
            j = 0 if t0 < TL else 1
            of_ = ring(of_r, "of_r")
            ob_ = ring(ob_r, "ob_r")
            szb = ring(szb_r, "szb_r")
            yaT = ring(yaT_r, "yaT_r")
            ycT = ring(ycT_r, "ycT_r")
            for d, dstt in ((0, of_), (1, ob_)):
                fw.dma("sp", dstt[:, :, 0:n], V(oT_d[d].h.rearrange("(h p) t -> p h t", p=128)[:, :, t0:t0 + n], oT_d[d].buf))
            fw.dma("sp", szb[:, :, 0:n], V(pT_d.h[2560:3072, :].rearrange("(g p) t -> p g t", p=128)[:, :, t0:t0 + n], pT_d.buf))
            fw.dma("sp", yaT[:, :, 0:n], V(yaT_d.h.rearrange("(g p) t -> p g t", p=128)[:, :, t0:t0 + n], yaT_d.buf))
            fw.dma("sp", ycT[:, :, 0:n], V(ycT_d.h.rearrange("(g p) t -> p g t", p=128)[:, :, t0:t0 + n], ycT_d.buf))
            fw.tt(of_[:, :, 0:n], of_[:, :, 0:n], ob_[:, :, 0:n], ALU.add, e="pool")
            fw.act(sqb[:, :, 0:n], of_[:, :, 0:n], AF.Square)
            for h in range(4):
                ps = pf()
                fw.mm(ps[:, 0:n], onesB, sqb[:, h, 0:n])
                r = ring(rr, "frr")
                fw.act(r[:, 0:n], ps[:, 0:n], AF.Ln, bias=EPS, scale=1.0 / 128)
                fw.act(r[:, 0:n], r[:, 0:n], AF.Exp, scale=-0.5)
                fw.stt(r[:, 0:n], of_[:, h, 0:n], dnrm[:, 0:1], r[:, 0:n], ALU.mult, ALU.mult)
                fw.tt(ybT[:, h, 0:n], r[:, 0:n], szb[:, h, 0:n], ALU.mult, e="pool")
            for br, src in enumerate((yaT, ybT, ycT)):
                gt = ring(gts, "gts")
                fw.dma("sp", gt[:, :, 0:n], V(pT_d.h[4224 + br * 1024: 4224 + (br + 1) * 1024, :].rearrange("(g p) t -> p g t", p=128)[:, :, t0:t0 + n], pT_d.buf))
                for jb in range(8):
                    ps = pf()
                    for kc in range(4):
                        fw.mm(ps[:, 0:n], wbr[:, br * 4 + kc, jb * 128:(jb + 1) * 128], src[:, kc, 0:n], start=(kc == 0), stop=(kc == 3))
                    if br == 0:
                        fw.tt(merged[:, jb, 0:n], ps[:, 0:n], gt[:, jb, 0:n], ALU.mult)
                    else:
                        tf = ring(tmpf, "tmpf")
                        fw.tt(tf[:, 0:n], ps[:, 0:n], gt[:, jb, 0:n], ALU.mult)
                        fw.tt(merged[:, jb, 0:n], merged[:, jb, 0:n], tf[:, 0:n], ALU.add, e="pool")
            fw.copy(mergedb[:, :, 0:n], merged[:, :, 0:n], e="act")
            for ts_ in range(n // 128):
                tt = (t0 // 128) + ts_
                xt = ring(xr, "fxr")
                fw.dma("sp", xt[:], xsrc[tt * 128:(tt + 1) * 128, :])
                st = st4.sub(tt)
                c0 = 8 * tt
                phs = []
                for half in range(2):
                    ps = pf()
                    for kc in range(8):
                        fw.mm(ps[:], mergedb[:, kc, ts_ * 128:(ts_ + 1) * 128], wout[:, kc, half * 512:(half + 1) * 512], start=(kc == 0), stop=(kc == 7))
                    fw.act(junk[:], ps[:], AF.Square, accum_out=st[:, c0 + half: c0 + half + 1])
                    phs.append(ps)
                fw.tt(st[:, c0 + 2:c0 + 3], st[:, c0:c0 + 1], st[:, c0 + 1:c0 + 2], ALU.add)
                fw.act(st[:, c0 + 3:c0 + 4], st[:, c0 + 2:c0 + 3], AF.Sqrt, bias=EPS, scale=1.0 / 1024)
                fw.recip(st[:, c0 + 4:c0 + 5], st[:, c0 + 3:c0 + 4])
                o = ring(xo, "fxo")
                for half in range(2):
                    hs = slice(half * 512, (half + 1) * 512)
                    fw.stt(o[:, hs], phs[half][:], st[:, c0 + 4:c0 + 5], ggbc[j][:, hs], ALU.mult, ALU.mult)
                fw.tt(o[:], o[:], xt[:], ALU.add, e="pool")
                if last:
                    fw.dma("pool", out_d[tt * 128:(tt + 1) * 128, :], o[:], semof=o.buf)
                else:
                    fw.dma("pool", xs_d[tt * 128:(tt + 1) * 128, :], o[:], semof=o.buf)
        fw.pop()

    for l in range(n_layers):
        alloc_s1()
        phase_mod(l)
        phase_h(l)
        if "hT" in dbg and l == 0:
            dbg_d["hT"] = fw.dram("hT_o", [128, 8, TT], BF16, "ExternalOutput")
            fw.dma("pool", dbg_d["hT"][:], S["hT"][:], semof=S["hT"].buf)
        if stop_after == "h":
            fw.pop()
            break
        phase_proj(l)
        fw.pop()
        if stop_after == "proj":
            break
        if "nofourier" not in dbg:
            phase_fourier(l, with_ctx=(l < n_layers - 1 or "ctxall" in dbg))
        if stop_after == "fourier":
            break
        if "nogdn" not in dbg:
            phase_gdn(l)
        if stop_after == "gdn":
            break
        wc = (l < n_layers - 1 or "ctxall" in dbg)
        phase_mla(l, with_ctx=wc)
        if stop_after == "mla":
            break
        phase_final(l, with_ctx=wc, last=(l == n_layers - 1 and "ctxall" not in dbg))

    if "abT" in dbg:
        dbg_d["abT"] = fw.dram("abT_o", [128, NT, 16], F32, "ExternalOutput")
        fw.dma("pool", dbg_d["abT"][:], abT[:], semof=abT.buf)
    outs = [out_d, xs_d, pT_d, yaT_d, ycT_d, qkvT_d] + oT_d + list(dbg_d.values())
    fw.finish(outs, e="pool")
    return nc, fw


def kernel(**inputs):
    inp = {k: np.asarray(v) for k, v in inputs.items()}
    consts = host_constants()
    shared = host_layout_shared(inp)
    nc, fw = build()
    in_maps = []
    for b in range(8):
        m = dict(consts)
        m.update(shared)
        m.update(host_layout(inp, b))
        in_maps.append(m)
    res = run_bass_kernel_spmd(nc, in_maps, core_ids=list(range(8)))
    return np.stack([np.asarray(r["out"]) for r in res.results], axis=0).astype(np.float32)
```

```python
import math
import numpy as np
import ml_dtypes
import concourse.bass as bass
import concourse.mybir as mybir
from concourse.bass_utils import run_bass_kernel_spmd

F32 = mybir.dt.float32
BF16 = mybir.dt.bfloat16
AF = mybir.ActivationFunctionType
ALU = mybir.AluOpType

TL = 4096
TC = 256
TT = TL + TC
NT = TT // 128
CHUNKS = [(i * 512, 512) for i in range(8)] + [(TL, TC)]
NB = 58
EPS = 1e-6
NEG = -30000.0
CHDT = BF16


class Buf:
    __slots__ = ("name", "writers", "readers", "waw", "dsem", "excl")

    def __init__(self, name, waw=True):
        self.name = name
        self.excl = False
        self.writers = {}
        self.readers = {}
        self.waw = waw
        self.dsem = None


class V:
    __slots__ = ("ap", "buf")

    def __init__(self, ap, buf):
        self.ap = ap
        self.buf = buf


class T:
    def __init__(self, handle, buf, bufs=None):
        self.h = handle
        self.buf = buf
        self.bufs = bufs

    def __getitem__(self, key):
        return V(self.h[key], self.buf)

    def sub(self, i):
        return T(self.h, self.bufs[i])


class FW:
    def __init__(self, nc, strict_same=True):
        self.nc = nc
        self.eng = {"pe": nc.tensor, "dve": nc.vector, "act": nc.scalar, "pool": nc.gpsimd, "sp": nc.sync}
        self.sems = {}
        self.count = {}
        for e in self.eng:
            self.sems[e] = nc.alloc_semaphore("s_" + e)
            self.count[e] = 0
        self.seen = {e: {} for e in self.eng}
        self.strict_same = strict_same
        self.nops = {e: 0 for e in self.eng}
        self.nwait = 0
        self.uid = 0
        self.scopes = [[]]
        self.scope_bufs = [[]]
        self.free_dsems = []
        self.bar_tile = V(nc.alloc_sbuf_tensor("bar_tile", [128, 8], F32)[:, :], Buf("bar"))

    def sb(self, name, shape, dtype, nsub=0, waw=True):
        self.uid += 1
        name = "%s_u%d" % (name, self.uid)
        g = self.nc.sbuf_tensor(name, list(shape), dtype)
        h = g.__enter__()
        self.scopes[-1].append(g)
        bufs = [Buf(name + "_%d" % i, waw) for i in range(nsub)] if nsub else None
        t = T(h, Buf(name, waw), bufs)
        self.scope_bufs[-1].extend([t.buf] + (bufs or []))
        return t

    def push(self):
        self.scopes.append([])
        self.scope_bufs.append([])

    def pop(self):
        self.barrier()
        for g in reversed(self.scopes.pop()):
            g.__exit__(None, None, None)
        for b in self.scope_bufs.pop():
            if b.dsem is not None:
                self.free_dsems.append(b.dsem)
                b.dsem = None

    def barrier(self):
        need = {k: v for k, v in self.count.items() if v > 0}
        self._waits("pool", need)
        ins = self.eng["pool"].memset(self.bar_tile.ap, 0.0)
        self.count["pool"] += 1
        ins.then_inc(self.sems["pool"], 1)
        val = self.count["pool"]
        for e in self.eng:
            if e == "pool":
                continue
            self._waits(e, {"pool": val})
            for k, v in need.items():
                self.seen[e][k] = max(self.seen[e].get(k, 0), v)

    def ps(self, name, shape, dtype=F32):
        h = self.nc.alloc_psum_tensor(name, list(shape), dtype)
        b = Buf(name)
        b.excl = True
        return T(h, b)

    def dram(self, name, shape, dtype, kind="Internal", nsub=0):
        h = self.nc.dram_tensor(name, list(shape), dtype, kind=kind)
        bufs = [Buf(name + "_%d" % i, False) for i in range(nsub)] if nsub else None
        return T(h.ap(), Buf(name, waw=False), bufs)

    def _need(self, reads, writes):
        need = {}
        for v in reads:
            for k, val in v.buf.writers.items():
                if need.get(k, 0) < val:
                    need[k] = val
            if v.buf.excl:
                for k, val in v.buf.readers.items():
                    if need.get(k, 0) < val:
                        need[k] = val
        for v in writes:
            b = v.buf
            if b.waw:
                for k, val in b.writers.items():
                    if need.get(k, 0) < val:
                        need[k] = val
            for k, val in b.readers.items():
                if need.get(k, 0) < val:
                    need[k] = val
        return need

    def _waits(self, e, need):
        seen = self.seen[e]
        for k, val in need.items():
            if k == e and (e == "pe" or not self.strict_same):
                continue
            if seen.get(k, 0) >= val:
                continue
            self.eng[e].wait_ge(self.sems[k], val)
            seen[k] = val
            self.nwait += 1

    def op(self, e, fn, reads=(), writes=()):
        self._waits(e, self._need(reads, writes))
        ins = fn(self.eng[e])
        self.count[e] += 1
        val = self.count[e]
        ins.then_inc(self.sems[e], 1)
        self.nops[e] += 1
        for v in reads:
            b = v.buf
            if b.readers.get(e, 0) < val:
                b.readers[e] = val
        for v in writes:
            b = v.buf
            if b.waw:
                b.writers = {e: val}
            else:
                b.writers[e] = val
            b.readers = {}
        return ins

    def dma(self, q, out, in_, semof=None, **kw):
        sbuf = semof if semof is not None else out.buf
        if sbuf.dsem is None:
            if self.free_dsems:
                key = self.free_dsems.pop()
            else:
                key = "d%d" % len(self.sems)
                self.sems[key] = self.nc.alloc_semaphore(key)
                self.count[key] = 0
            sbuf.dsem = key
        key = sbuf.dsem
        self._waits(q, self._need([in_], [out]))
        ins = self.eng[q].dma_start(out=out.ap, in_=in_.ap, **kw)
        self.count[key] += 16
        val = self.count[key]
        ins.then_inc(self.sems[key], 16)
        self.nops[q] += 1
        b = in_.buf
        if b.readers.get(key, 0) < val:
            b.readers[key] = val
        b = out.buf
        if b.waw:
            b.writers = {key: val}
        else:
            b.writers[key] = val
        b.readers = {}
        return ins

    def finish(self, bufs, e="sp"):
        need = {}
        for b in bufs:
            for k, val in b.buf.writers.items():
                if need.get(k, 0) < val:
                    need[k] = val
        self._waits(e, need)

    def mm(self, out, lhsT, rhs, start=True, stop=True):
        reads = [lhsT, rhs] + ([] if start else [out])
        return self.op("pe", lambda e: e.matmul(out.ap, lhsT.ap, rhs.ap, start=start, stop=stop), reads, [out])

    def tr(self, out, in_, ident):
        return self.op("pe", lambda e: e.transpose(out.ap, in_.ap, ident.ap), [in_, ident], [out])

    def act(self, out, in_, func, bias=None, scale=None, accum_out=None):
        reads = [in_]
        kw = {}
        if bias is not None:
            if isinstance(bias, V):
                reads.append(bias)
                kw["bias"] = bias.ap
            else:
                kw["bias"] = bias
        if scale is not None:
            if isinstance(scale, V):
                reads.append(scale)
                kw["scale"] = scale.ap
            else:
                kw["scale"] = scale
        writes = [out]
        if accum_out is not None:
            kw["accum_out"] = accum_out.ap
            writes.append(accum_out)
        return self.op("act", lambda e: e.activation(out.ap, in_.ap, func, **kw), reads, writes)

    def tt(self, out, in0, in1, op, e="dve"):
        return self.op(e, lambda g: g.tensor_tensor(out.ap, in0.ap, in1.ap, op), [in0, in1], [out])

    def ts(self, out, in0, s1, s2=None, op0=ALU.mult, op1=None, e="dve"):
        reads = [in0]
        a1 = s1
        if isinstance(s1, V):
            reads.append(s1)
            a1 = s1.ap
        a2 = s2
        if isinstance(s2, V):
            reads.append(s2)
            a2 = s2.ap
        if op1 is None:
            return self.op(e, lambda g: g.tensor_scalar(out.ap, in0.ap, a1, None, op0), reads, [out])
        return self.op(e, lambda g: g.tensor_scalar(out.ap, in0.ap, a1, a2, op0, op1), reads, [out])

    def stt(self, out, in0, s, in1, op0, op1):
        reads = [in0, in1]
        a = s
        if isinstance(s, V):
            reads.append(s)
            a = s.ap
        return self.op("dve", lambda g: g.scalar_tensor_tensor(out.ap, in0.ap, a, in1.ap, op0, op1), reads, [out])

    def copy(self, out, in_, e="dve"):
        if e == "act":
            return self.op("act", lambda g: g.activation(out.ap, in_.ap, AF.Copy), [in_], [out])
        return self.op(e, lambda g: g.tensor_copy(out.ap, in_.ap), [in_], [out])

    def recip(self, out, in_):
        return self.op("dve", lambda g: g.reciprocal(out.ap, in_.ap), [in_], [out])


def _bf(a):
    return np.ascontiguousarray(a).astype(ml_dtypes.bfloat16)


_CONST = {}


def host_constants():
    if _CONST:
        return _CONST
    c = {}
    t = np.arange(TL, dtype=np.int64)
    ph = (np.outer(t, t) % TL).astype(np.float64) * (2 * np.pi / TL)
    c["CL"] = _bf(np.cos(ph))
    c["NSL"] = _bf(-np.sin(ph))
    tcx = np.arange(TC, dtype=np.int64)
    phc = (np.outer(tcx, tcx) % TC).astype(np.float64) * (2 * np.pi / TC)
    c["CLC"] = _bf(np.cos(phc))
    c["NSLC"] = _bf(-np.sin(phc))
    ch = np.arange(128, dtype=np.int64)
    phd = (np.outer(ch, ch) % 128).astype(np.float64) * (2 * np.pi / 128)
    c["CSC"] = _bf(np.concatenate([np.cos(phd), np.sin(phd)], axis=1))
    rows = np.repeat(np.arange(64, dtype=np.float32), 64)
    cols = np.tile(np.arange(64, dtype=np.float32), 64)
    inv = (10000.0 ** (-np.arange(8, dtype=np.float32) / 8)).astype(np.float32)
    ang_r = rows[:, None] * inv
    ang_c = cols[:, None] * inv
    ang = np.concatenate([ang_r, ang_r, ang_c, ang_c], axis=-1)
    sgn = np.array([-1.0] * 8 + [1.0] * 8 + [-1.0] * 8 + [1.0] * 8, dtype=np.float32)
    C = np.ones((96, TT), np.float32)
    S = np.zeros((96, TT), np.float32)
    C[64:96, :TL] = np.cos(ang).T
    S[64:96, :TL] = (np.sin(ang) * sgn[None, :]).T
    c["ROPEC"] = C
    c["ROPES"] = S
    k = np.arange(128)[:, None]
    m = np.arange(128)[None, :]
    same = (k // 64) == (m // 64)
    ident = (k == m).astype(np.float32)
    ones = np.ones((128, 128), np.float32)
    triF = (same & (k <= m)).astype(np.float32)
    restF = (same & (k > m)).astype(np.float32)
    triB = (same & (k >= m)).astype(np.float32)
    restB = (same & (k < m)).astype(np.float32)
    tot0 = np.broadcast_to((k < 64), (128, 128)).astype(np.float32)
    tot1 = np.broadcast_to((k >= 64), (128, 128)).astype(np.float32)
    j = k
    i = m
    nmF = np.where(same & (i >= j), 0.0, NEG).astype(np.float32)
    nmB = np.where(same & (i <= j), 0.0, NEG).astype(np.float32)
    stF = (same & (i > j)).astype(np.float32)
    stB = (same & (i < j)).astype(np.float32)
    c["MSK"] = np.ascontiguousarray(np.concatenate(
        [ident, ones, triF, restF, tot0, tot1, triB, restB,
         np.tile(nmF, (1, 4)), np.tile(nmB, (1, 4)), np.tile(ident, (1, 4))], axis=1)).astype(np.float32)
    d16 = ((k // 16) == (m // 16)).astype(np.float32)
    c32 = (((k // 32) == (m // 32)) & ((k // 16) != (m // 16))).astype(np.float32)
    c64 = (same & ((k // 32) != (m // 32))).astype(np.float32)
    c["MSKB"] = _bf(np.concatenate([ident, ones, np.tile(stF, (1, 4)), np.tile(stB, (1, 4)), np.tile(ident, (1, 4)),
                                    np.tile(d16, (1, 4)), np.tile(c32, (1, 4)), np.tile(c64, (1, 4))], axis=1))
    _CONST.update(c)
    return c


IN_W = (512, 512, 512, 512, 512, 512, 16, 384, 256, 32, 512, 3072)
IN_OFF = np.concatenate([[0], np.cumsum(IN_W)]).tolist()


def _col(v, nblk):
    return np.ascontiguousarray(v.reshape(nblk, 128).T)


def host_layout(inp, b):
    d = {}
    d["xin"] = np.ascontiguousarray(np.concatenate([inp["x"][b], inp["ctx"][b]], axis=0))
    cc = np.stack([_col(inp["c"][b], 8), _col(inp["c_ctx"], 8)], axis=-1)
    d["ccol"] = np.ascontiguousarray(cc)
    return d


def host_layout_shared(inp):
    d = {}
    o = IN_OFF
    perm = np.concatenate([np.arange(8, 16), np.arange(0, 8), np.arange(24, 32), np.arange(16, 24)])
    w_in = inp["w_in"]
    kpe = w_in[:, :, o[9]:o[10]]
    cols = np.concatenate([
        w_in[:, :, o[0]:o[6]],
        w_in[:, :, o[7]:o[9]],
        w_in[:, :, o[10]:o[11]],
        w_in[:, :, o[11]:o[12]],
        kpe, kpe[:, :, perm],
        np.zeros((2, 1024, 64), np.float32),
    ], axis=-1)
    assert cols.shape[-1] == NB * 128
    d["w_in_r"] = np.ascontiguousarray(cols.reshape(2, 8, 128, NB, 128).transpose(0, 3, 2, 1, 4))
    wab = w_in[:, :, o[6]:o[7]]
    d["w_ab"] = np.ascontiguousarray(wab.reshape(2, 8, 128, 16).transpose(0, 2, 1, 3))
    d["w_mod"] = inp["w_mod"]
    d["bmod"] = np.ascontiguousarray(np.stack([_col(inp["b_mod"][l], 24) for l in range(2)]))
    d["gpre"] = np.ascontiguousarray(np.stack([_col(inp["g_pre"][l], 8) for l in range(2)]))
    d["gpost"] = np.ascontiguousarray(np.stack([_col(inp["g_post"][l], 8) for l in range(2)]))
    d["f_w"] = np.ascontiguousarray(inp["f_w"].transpose(0, 2, 1, 3))
    cw = inp["dn_conv"]
    d["convw"] = np.ascontiguousarray(cw.reshape(2, 3, 12, 128).transpose(0, 3, 2, 1))
    d["alog"] = np.ascontiguousarray(np.broadcast_to(inp["dn_a_log"].reshape(2, 1, 1, 8), (2, 128, NT, 8)))
    d["dtb"] = np.ascontiguousarray(np.broadcast_to(inp["dn_dt_bias"].reshape(2, 1, 1, 8), (2, 128, NT, 8)))
    d["dnorm"] = np.ascontiguousarray(inp["dn_norm"].reshape(2, 128, 1))
    d["qnorm"] = np.ascontiguousarray(np.stack([_col(inp["mla_q_norm"][l], 3) for l in range(2)]))
    d["kvnorm"] = np.ascontiguousarray(np.stack([_col(inp["mla_kv_norm"][l], 2) for l in range(2)]))
    wuq = inp["mla_w_uq"]
    hp = np.concatenate([np.arange(64), 64 + perm])
    permc = np.concatenate([h * 96 + hp for h in range(8)])
    both = np.concatenate([wuq, wuq[:, :, permc]], axis=-1)
    d["w_uq"] = np.ascontiguousarray(both.reshape(2, 3, 128, 1536).transpose(0, 2, 1, 3))
    d["w_ukv"] = np.ascontiguousarray(inp["mla_w_ukv"].reshape(2, 2, 128, 1024).transpose(0, 2, 1, 3))
    d["w_br"] = np.ascontiguousarray(inp["w_branch"].reshape(2, 12, 128, 1024).transpose(0, 2, 1, 3))
    d["w_out"] = np.ascontiguousarray(inp["w_out"].reshape(2, 8, 128, 1024).transpose(0, 2, 1, 3))
    return d


def build(n_layers=2, dbg=(), stop_after=None):
    nc = bass.Bass("TRN2", target_bir_lowering=False)
    fw = FW(nc)
    EI = "ExternalInput"

    def skind(name):
        return "ExternalOutput" if name in dbg else "Internal"

    xin = fw.dram("xin", [TT, 1024], F32, EI)
    ccol_d = fw.dram("ccol", [128, 8, 2], F32, EI)
    w_in_d = fw.dram("w_in_r", [2, NB, 128, 8, 128], F32, EI)
    w_ab_d = fw.dram("w_ab", [2, 128, 8, 16], F32, EI)
    w_mod_d = fw.dram("w_mod", [2, 1024, 3072], F32, EI)
    bmod_d = fw.dram("bmod", [2, 128, 24], F32, EI)
    gpre_d = fw.dram("gpre", [2, 128, 8], F32, EI)
    gpost_d = fw.dram("gpost", [2, 128, 8], F32, EI)
    f_w_d = fw.dram("f_w", [2, 128, 4, 128], F32, EI)
    convw_d = fw.dram("convw", [2, 128, 12, 3], F32, EI)
    alog_d = fw.dram("alog", [2, 128, NT, 8], F32, EI)
    dtb_d = fw.dram("dtb", [2, 128, NT, 8], F32, EI)
    dnorm_d = fw.dram("dnorm", [2, 128, 1], F32, EI)
    qnorm_d = fw.dram("qnorm", [2, 128, 3], F32, EI)
    kvnorm_d = fw.dram("kvnorm", [2, 128, 2], F32, EI)
    w_uq_d = fw.dram("w_uq", [2, 128, 3, 1536], F32, EI)
    w_ukv_d = fw.dram("w_ukv", [2, 128, 2, 1024], F32, EI)
    w_br_d = fw.dram("w_br", [2, 128, 12, 1024], F32, EI)
    w_out_d = fw.dram("w_out", [2, 128, 8, 1024], F32, EI)
    CL_d = fw.dram("CL", [TL, TL], BF16, EI)
    NSL_d = fw.dram("NSL", [TL, TL], BF16, EI)
    CLC_d = fw.dram("CLC", [TC, TC], BF16, EI)
    NSLC_d = fw.dram("NSLC", [TC, TC], BF16, EI)
    CSC_d = fw.dram("CSC", [128, 256], BF16, EI)
    ROPEC_d = fw.dram("ROPEC", [96, TT], F32, EI)
    ROPES_d = fw.dram("ROPES", [96, TT], F32, EI)
    MSK_d = fw.dram("MSK", [128, 8 * 128 + 1536], F32, EI)
    MSKB_d = fw.dram("MSKB", [128, 2 * 128 + 6 * 512], BF16, EI)

    out_d = fw.dram("out", [TL, 1024], F32, "ExternalOutput")
    xs_d = fw.dram("xs", [TT, 1024], F32, skind("xs"))
    pT_d = fw.dram("pT", [NB * 128, TT], BF16, skind("pT"))
    yaT_d = fw.dram("yaT", [512, TT], BF16, skind("yaT"))
    ycT_d = fw.dram("ycT", [512, TT], BF16, skind("ycT"))
    oT_d = [fw.dram("oT%d" % d, [512, TT], F32, skind("oT%d" % d)) for d in range(2)]
    qkvT_d = fw.dram("qkvT", [1536, TT], BF16, skind("qkvT"))
    dbg_d = {}

    msk = fw.sb("msk", [128, 8 * 128 + 1536], F32)
    mskb = fw.sb("mskb", [128, 2 * 128 + 6 * 512], BF16)
    fw.dma("sp", msk[:], MSK_d[:])
    fw.dma("sp", mskb[:], MSKB_d[:])

    def mcol(i):
        return msk[:, i * 128:(i + 1) * 128]
    identF, onesF, triF, restF, tot0, tot1, triB, restB = [mcol(i) for i in range(8)]
    negmask = [msk[:, 1024:1536], msk[:, 1536:2048]]
    ident4F = msk[:, 2048:2560]
    identB = mskb[:, 0:128]
    onesB = mskb[:, 128:256]
    strictB = [mskb[:, 256:768], mskb[:, 768:1280]]
    ident4B = mskb[:, 1280:1792]
    mD16 = mskb[:, 1792:2304]
    mC32 = mskb[:, 2304:2816]
    mC64 = mskb[:, 2816:3328]

    PF = [fw.ps("pf%d" % i, [128, 512], F32) for i in range(7)]
    PB = fw.ps("pb", [128, 1024], BF16)

    ccol = fw.sb("ccol_sb", [128, 8, 2], F32)
    sc = fw.sb("sc", [128, 8, 2], F32)
    modc = fw.sb("modc", [128, 24, 2], F32)
    Acol = fw.sb("Acol", [128, 8, 2], F32)
    ggcol = fw.sb("ggcol", [128, 8, 2], F32)
    ggbc = [fw.sb("ggbc%d" % j, [128, 1024], F32) for j in range(2)]
    bmod = fw.sb("bmod_sb", [128, 24], F32)
    gpre = fw.sb("gpre_sb", [128, 8], F32)
    gpost = fw.sb("gpost_sb", [128, 8], F32)
    abT = fw.sb("abT", [128, NT, 16], F32)
    stat = fw.sb("stat", [128, 4 * NT], F32, nsub=NT)
    fw.dma("sp", ccol[:], ccol_d[:])

    ring_ctr = {}

    def ring(lst, key):
        i = ring_ctr.get(key, 0)
        ring_ctr[key] = i + 1
        return lst[i % len(lst)]

    pfc = [0]

    def pf(lo=0, hi=7):
        i = pfc[0]
        pfc[0] += 1
        return PF[lo + i % (hi - lo)]

    S = {}

    def alloc_s1():
        fw.push()
        S["hT"] = fw.sb("hT", [128, 8, TT], BF16)
        S["w32"] = [fw.sb("w32_%d" % i, [128, 4096], F32) for i in range(2)]
        S["wbf"] = [fw.sb("wbf_%d" % i, [128, 1024], BF16) for i in range(3)]
        S["stg"] = [fw.sb("stg_%d" % i, [128, TT], BF16) for i in range(3)]
        S["xring"] = [fw.sb("xr%d" % i, [128, 1024], F32) for i in range(3)]
        S["hnring"] = [fw.sb("hn%d" % i, [128, 1024], BF16) for i in range(2)]

    def phase_mod(l):
        fw.dma("sp", bmod[:], bmod_d[l])
        fw.dma("sp", gpre[:], gpre_d[l])
        fw.dma("sp", gpost[:], gpost_d[l])
        fw.act(sc[:], ccol[:], AF.Silu)
        pm = PF[0]
        wv = w_mod_d.h[l].rearrange("(kc k) n -> k kc n", k=128)
        for nch in range(6):
            slot = ring(S["w32"], "w32")
            sv = V(slot.h[:, :].rearrange("p (kc n) -> p kc n", kc=8), slot.buf)
            fw.dma("sp", sv, V(wv[:, :, nch * 512:(nch + 1) * 512], w_mod_d.buf))
            for j in range(4):
                blk = nch * 4 + j
                for kc in range(8):
                    fw.mm(pm[:, blk * 2:blk * 2 + 2],
                          V(slot.h[:, kc * 512 + j * 128: kc * 512 + (j + 1) * 128], slot.buf),
                          sc[:, kc, :], start=(kc == 0), stop=(kc == 7))
        for j in range(2):
            fw.tt(modc[:, :, j], V(pm.h[:, 0:48].rearrange("p (b j) -> p b j", j=2)[:, :, j], pm.buf), bmod[:], ALU.add)
            fw.stt(Acol[:, :, j], modc[:, 8:16, j], 1.0, gpre[:], ALU.add, ALU.mult)
            fw.tt(ggcol[:, :, j], modc[:, 16:24, j], gpost[:], ALU.mult)
        for j in range(2):
            for half in range(2):
                pg = pf(1, 7)
                for q in range(4):
                    kc = half * 4 + q
                    D = ring(S["w32"], "w32")
                    fw.ts(D[:, 0:128], identF, ggcol[:, kc, j:j + 1])
                    fw.mm(pg[:, q * 128:(q + 1) * 128], onesF, D[:, 0:128])
                fw.copy(ggbc[j][:, half * 512:(half + 1) * 512], pg[:])

    def phase_h(l):
        src = xin if l == 0 else xs_d
        for tt in range(NT):
            j = 0 if tt < 32 else 1
            xt = ring(S["xring"], "xr")
            fw.dma("sp", xt[:], src[tt * 128:(tt + 1) * 128, :])
            st = stat.sub(tt)
            hn = ring(S["hnring"], "hn")
            fw.act(hn[:], xt[:], AF.Square, accum_out=st[:, 4 * tt:4 * tt + 1])
            fw.act(st[:, 4 * tt + 1:4 * tt + 2], st[:, 4 * tt:4 * tt + 1], AF.Sqrt, bias=EPS, scale=1.0 / 1024)
            fw.recip(st[:, 4 * tt + 2:4 * tt + 3], st[:, 4 * tt + 1:4 * tt + 2])
            fw.ts(hn[:], xt[:], st[:, 4 * tt + 2:4 * tt + 3])
            for kc in range(8):
                fw.tr(PB[:, kc * 128:(kc + 1) * 128], hn[:, kc * 128:(kc + 1) * 128], identB)
            for kc in range(8):
                o = S["hT"][:, kc, tt * 128:(tt + 1) * 128]
                i = PB[:, kc * 128:(kc + 1) * 128]
                if kc % 2 == 0:
                    fw.ts(o, i, Acol[:, kc, j:j + 1], modc[:, kc, j:j + 1], ALU.mult, ALU.add)
                else:
                    fw.act(o, i, AF.Identity, bias=modc[:, kc, j:j + 1], scale=Acol[:, kc, j:j + 1])

    SILU_BLK = list(range(4, 8)) + list(range(20, 24)) + list(range(29, 33))
    SIG_BLK = list(range(33, 57))
    COPY_BLK = [b for b in range(NB) if b not in SILU_BLK and b not in SIG_BLK]

    def phase_proj(l):
        wab32 = ring(S["w32"], "w32")
        fw.dma("sp", wab32[:, 0:128], V(w_ab_d.h[l].rearrange("p kc n -> p (kc n)"), w_ab_d.buf))
        wabb = ring(S["wbf"], "wbf")
        fw.copy(wabb[:, 0:128], wab32[:, 0:128], e="pool")
        pa = [PF[5], PF[6]]
        for tt in range(NT):
            dst = pa[0][:, tt * 16:(tt + 1) * 16] if tt < 32 else pa[1][:, (tt - 32) * 16:(tt - 31) * 16]
            for kc in range(8):
                fw.mm(dst, S["hT"][:, kc, tt * 128:(tt + 1) * 128], wabb[:, kc * 16:(kc + 1) * 16], start=(kc == 0), stop=(kc == 7))
        fw.copy(V(abT.h[:, 0:32, :].rearrange("p a b -> p (a b)"), abT.buf), pa[0][:, 0:512])
        fw.copy(V(abT.h[:, 32:34, :].rearrange("p a b -> p (a b)"), abT.buf), pa[1][:, 0:32])
        for blk in COPY_BLK + SILU_BLK + SIG_BLK:
            ws = ring(S["w32"], "w32")
            fw.dma("sp", ws[:, 0:1024], V(w_in_d.h[l, blk].rearrange("p kc n -> p (kc n)"), w_in_d.buf))
            wb = ring(S["wbf"], "wbf")
            fw.copy(wb[:, 0:1024], ws[:, 0:1024], e="pool")
            sg = ring(S["stg"], "stg")
            for ci, (t0, n) in enumerate(CHUNKS):
                ps = pf(0, 7)
                for kc in range(8):
                    fw.mm(ps[:, 0:n], wb[:, kc * 128:(kc + 1) * 128], S["hT"][:, kc, t0:t0 + n], start=(kc == 0), stop=(kc == 7))
                if blk in SILU_BLK:
                    fw.act(sg[:, t0:t0 + n], ps[:, 0:n], AF.Silu)
                elif blk in SIG_BLK:
                    fw.act(sg[:, t0:t0 + n], ps[:, 0:n], AF.Sigmoid)
                else:
                    fw.copy(sg[:, t0:t0 + n], ps[:, 0:n])
            fw.dma("pool", pT_d[blk * 128:(blk + 1) * 128, :], sg[:], semof=sg.buf)


    def v3(t, n):
        return t.h[:, :].rearrange("p (a n) -> p a n", n=n)

    def phase_fourier(l, with_ctx):
        fw.push()
        UT = fw.sb("UT", [128, 4, TT], BF16)
        ABs = fw.sb("ABs", [128, NT, 1024], BF16)
        csc = fw.sb("csc", [128, 256], BF16)
        fw32 = fw.sb("fw32", [128, 512], F32)
        fwb = fw.sb("fwb", [128, 512], BF16)
        tbC = [fw.sb("tbC%d" % i, [128, 8, 512], BF16) for i in range(2)]
        tbS = [fw.sb("tbS%d" % i, [128, 8, 512], BF16) for i in range(2)]
        specb = [fw.sb("specb%d" % i, [128, 512], BF16) for i in range(2)]
        szr = [fw.sb("szr%d" % i, [128, 4, 512], BF16) for i in range(2)]
        yast = [fw.sb("yast%d" % i, [128, 4, 512], BF16) for i in range(2)]
        fw.dma("sp", csc[:], CSC_d[:])
        fw.dma("sp", fw32[:], V(f_w_d.h[l].rearrange("p g d -> p (g d)"), f_w_d.buf))
        fw.copy(fwb[:], fw32[:], e="pool")
        for g in range(4):
            fw.dma("sp", UT[:, g, :], pT_d[g * 128:(g + 1) * 128, :])
        for tt in range(NT if with_ctx else 32):
            for half in range(2):
                ps = pf(4, 7)
                for gg in range(2):
                    g = half * 2 + gg
                    fw.mm(ps[:, gg * 256:(gg + 1) * 256], UT[:, g, tt * 128:(tt + 1) * 128], csc[:])
                fw.copy(ABs[:, tt, half * 512:(half + 1) * 512], ps[:], e=("dve" if half == 0 else "act"))
        osc = 1.0 / math.sqrt(128.0)
        jobs = [(ci, t0, n, 0, 32, CL_d, NSL_d, TL) for ci, (t0, n) in enumerate(CHUNKS[:8])]
        if with_ctx:
            jobs.append((8, TL, TC, 32, 2, CLC_d, NSLC_d, TC))
        for (ci, t0, n, tt0, ntile, Cd, Sd, Lseq) in jobs:
            sz = ring(szr, "szr")
            fw.dma("sp", sz[:, :, 0:n], V(pT_d.h[512:1024, :].rearrange("(g p) t -> p g t", p=128)[:, :, t0:t0 + n], pT_d.buf))
            c0 = t0 - tt0 * 128
            nq = (ntile + 7) // 8
            for qd in range(nq):
                na = min(8, ntile - qd * 8)
                tc_ = ring(tbC, "tbC")
                tsn = ring(tbS, "tbS")
                fw.dma("sp", tc_[:, 0:na, 0:n], V(Cd.h[qd * 1024: qd * 1024 + na * 128, :].rearrange("(a p) n -> p a n", p=128)[:, :, c0:c0 + n], Cd.buf))
                fw.dma("sp", tsn[:, 0:na, 0:n], V(Sd.h[qd * 1024: qd * 1024 + na * 128, :].rearrange("(a p) n -> p a n", p=128)[:, :, c0:c0 + n], Sd.buf))
                for g in range(4):
                    for a in range(na):
                        tt = tt0 + qd * 8 + a
                        first = (qd == 0 and a == 0)
                        last = (qd == nq - 1 and a == na - 1)
                        fw.mm(PF[g][:, 0:n], ABs[:, tt, g * 256: g * 256 + 128], tc_[:, a, 0:n], start=first, stop=False)
                        fw.mm(PF[g][:, 0:n], ABs[:, tt, g * 256 + 128: g * 256 + 256], tsn[:, a, 0:n], start=False, stop=last)
            ya = ring(yast, "yast")
            scl = osc / math.sqrt(float(Lseq))
            for g in range(4):
                sb_ = ring(specb, "specb")
                fw.act(sb_[:, 0:n], PF[g][:, 0:n], AF.Copy, scale=scl)
                po = pf(4, 7)
                fw.mm(po[:, 0:n], fwb[:, g * 128:(g + 1) * 128], sb_[:, 0:n])
                fw.tt(ya[:, g, 0:n], po[:, 0:n], sz[:, g, 0:n], ALU.mult)
            fw.dma("pool", V(yaT_d.h.rearrange("(g p) t -> p g t", p=128)[:, :, t0:t0 + n], yaT_d.buf), ya[:, :, 0:n], semof=ya.buf)
        fw.pop()

    def phase_gdn(l):
        fw.push()
        convw = fw.sb("convw", [128, 12, 3], F32)
        fw.dma("sp", convw[:], convw_d[l])
        fw.push()
        cin = [fw.sb("cin%d" % i, [128, TT], BF16) for i in range(2)]
        cy = [fw.sb("cy%d" % i, [128, TT], F32) for i in range(2)]
        cst = [fw.sb("cst%d" % i, [128, TT], BF16) for i in range(2)]
        sqr = [fw.sb("sqr%d" % i, [128, 512], BF16) for i in range(2)]
        rr = [fw.sb("rr%d" % i, [128, 512], F32) for i in range(2)]
        for blk in range(12):
            xi = ring(cin, "cin")
            y = ring(cy, "cy")
            so = ring(cst, "cst")
            fw.dma("sp", xi[:], pT_d[1024 + blk * 128: 1024 + (blk + 1) * 128, :])
            for (a_, b_) in ((0, TL), (TL, TT)):
                fw.ts(y[:, a_:b_], xi[:, a_:b_], convw[:, blk, 1:2])
                fw.stt(y[:, a_ + 1:b_], xi[:, a_:b_ - 1], convw[:, blk, 0:1], y[:, a_ + 1:b_], ALU.mult, ALU.add)
                fw.stt(y[:, a_:b_ - 1], xi[:, a_ + 1:b_], convw[:, blk, 2:3], y[:, a_:b_ - 1], ALU.mult, ALU.add)
            if blk >= 8:
                fw.act(so[:], y[:], AF.Silu)
            else:
                fw.act(y[:], y[:], AF.Silu)
                for (t0, n) in CHUNKS:
                    sq = ring(sqr, "sqr")
                    r = ring(rr, "rr")
                    fw.tt(sq[:, 0:n], y[:, t0:t0 + n], y[:, t0:t0 + n], ALU.mult, e="pool")
                    ps = pf(0, 7)
                    fw.mm(ps[:, 0:n], onesB, sq[:, 0:n])
                    fw.act(r[:, 0:n], ps[:, 0:n], AF.Ln, bias=EPS)
                    fw.act(r[:, 0:n], r[:, 0:n], AF.Exp, scale=-0.5)
                    fw.tt(so[:, t0:t0 + n], y[:, t0:t0 + n], r[:, 0:n], ALU.mult)
            fw.dma("pool", qkvT_d[blk * 128:(blk + 1) * 128, :], so[:], semof=so.buf)
        fw.pop()

        NC = NT * 4
        bb = fw.sb("bb", [128, NT, 8], F32)
        gd = [fw.sb("gd%d" % d, [128, NC], F32) for d in range(2)]
        names = ("Gs", "nG", "EG", "ER", "GL0", "GL1")
        GA = [{nm: fw.sb("%s%d" % (nm, d), [128, NC], F32) for nm in names} for d in range(2)]
        fw.push()
        alog = fw.sb("alog", [128, NT, 8], F32)
        dtb = fw.sb("dtb", [128, NT, 8], F32)
        fw.dma("sp", alog[:], alog_d[l])
        fw.dma("sp", dtb[:], dtb_d[l])
        gz = fw.sb("gz", [128, NT, 8], F32)
        fw.tt(gz[:], abT[:, :, 0:8], dtb[:], ALU.add)
        fw.act(gz[:], gz[:], AF.Exp)
        fw.act(gz[:], gz[:], AF.Ln, bias=1.0)
        fw.act(alog[:], alog[:], AF.Exp)
        fw.stt(gz[:], gz[:], -1.0, alog[:], ALU.mult, ALU.mult)
        fw.act(bb[:], abT[:, :, 8:16], AF.Exp, scale=-1.0)
        fw.ts(bb[:], bb[:], 1.0, None, ALU.add)
        fw.recip(bb[:], bb[:])
        for d in range(2):
            fw.copy(V(v3(gd[d], 4), gd[d].buf), gz[:, :, d * 4:(d + 1) * 4])
        fw.pop()
        for d in range(2):
            tri = triF if d == 0 else triB
            rest = restF if d == 0 else restB
            p1 = pf(0, 7)
            fw.mm(p1[:, 0:NC], tri, gd[d][:])
            fw.copy(GA[d]["Gs"][:], p1[:, 0:NC])
            fw.ts(GA[d]["nG"][:], p1[:, 0:NC], -1.0)
            fw.act(GA[d]["EG"][:], p1[:, 0:NC], AF.Exp)
            p2 = pf(0, 7)
            fw.mm(p2[:, 0:NC], rest, gd[d][:])
            fw.act(GA[d]["ER"][:], p2[:, 0:NC], AF.Exp)
            p3 = pf(0, 7)
            fw.mm(p3[:, 0:NC], tot0, gd[d][:])
            fw.act(GA[d]["GL0"][:], p3[:, 0:NC], AF.Exp)
            p4 = pf(0, 7)
            fw.mm(p4[:, 0:NC], tot1, gd[d][:])
            fw.act(GA[d]["GL1"][:], p4[:, 0:NC], AF.Exp)

        def bc4(t_, tt, rows=slice(0, 128)):
            ap = t_.h[rows, tt * 4: tt * 4 + 4].unsqueeze(2)
            return V(ap.broadcast_to([rows.stop - rows.start, 4, 128]), t_.buf)

        def bcb(tt, d, rows=slice(0, 128)):
            ap = bb.h[rows, tt, d * 4:(d + 1) * 4].unsqueeze(2)
            return V(ap.broadcast_to([rows.stop - rows.start, 4, 128]), bb.buf)

        slots = [[], []]
        for d in range(2):
            for i in range(3):
                slots[d].append({
                    "w0T": fw.sb("w0T%d%d" % (d, i), [128, 512], BF16), "qkdT": fw.sb("qkdT%d%d" % (d, i), [128, 512], BF16),
                    "qgT": fw.sb("qgT%d%d" % (d, i), [128, 512], BF16), "kd": fw.sb("kd%d%d" % (d, i), [128, 512], BF16),
                    "ub": fw.sb("ub%d%d" % (d, i), [128, 512], F32)})
        tmps = []
        for i in range(4):
            tm_ = {
                "EGr": fw.sb("EGr%d" % i, [128, 512], F32),
                "tD": fw.sb("tD%d" % i, [128, 512], F32), "dec": fw.sb("dec%d" % i, [128, 512], F32),
                "kEG": fw.sb("kEG%d" % i, [128, 512], BF16), "vtok": fw.sb("vtok%d" % i, [128, 512], BF16),
                "TTb": fw.sb("TTb%d" % i, [128, 512], BF16),
                "qk": [fw.sb("qk%d_%d" % (i, k_), [128, 12, 128], BF16) for k_ in range(1)]}
            for nm in ("M0", "MT0", "Qa", "QTa", "Qb", "QTb", "Pa", "PTa", "Pb", "PTb", "C32", "C32T", "C64", "C64T"):
                tm_[nm] = fw.sb("%s_%d" % (nm, i), [128, 512], CHDT)
            tm_["Rp"] = tm_["dec"]
            tm_["Mf"] = tm_["tD"]
            tmps.append(tm_)
        S32 = [fw.sb("S32_%d" % d, [128, 512], F32) for d in range(2)]
        Sb = [fw.sb("Sb_%d" % d, [128, 512], BF16) for d in range(2)]
        un = [fw.sb("un_%d" % d, [128, 512], BF16) for d in range(2)]
        t5 = [fw.sb("t5_%d" % d, [128, 512], F32) for d in range(2)]
        ost = [[fw.sb("ost%d%d" % (d, i), [128, 512], F32) for i in range(2)] for d in range(2)]
        for d in range(2):
            fw.op("pool", lambda g, d=d: g.memset(S32[d].h[:, :], 0.0), [], [S32[d][:]])
            fw.op("pool", lambda g, d=d: g.memset(Sb[d].h[:, :], 0.0), [], [Sb[d][:]])
        qscale = 128.0 ** -0.5
        PGB = [(PF[0], PF[1]), (PF[2], PF[6])]

        def prep(tt, d, sl, s_):
            tm = tmps[d * 2 + s_ % 2]
            tok = slice(tt * 128, (tt + 1) * 128)
            G = GA[d]

            def pg():
                return PGB[d][s_ % 2]

            qk = tm["qk"][0]
            fw.dma("sp", qk[:], V(qkvT_d.h.rearrange("(b p) t -> p b t", p=128)[:, :, tok], qkvT_d.buf))
            fw.tt(V(v3(tm["Rp"], 128), tm["Rp"].buf), V(ident4F.ap.rearrange("p (a n) -> p a n", n=128), ident4F.buf),
                  bc4(G["Gs"], tt), ALU.mult, e="pool")
            yield
            p1 = pg()
            fw.mm(p1[:], onesF, tm["Rp"][:])
            yield
            fw.act(tm["EGr"][:], p1[:], AF.Exp, bias=math.log(qscale))
            fw.tt(tm["tD"][:], p1[:], negmask[d], ALU.add)
            fw.tt(V(v3(tm["tD"], 128), tm["tD"].buf), V(v3(tm["tD"], 128), tm["tD"].buf), bc4(G["nG"], tt), ALU.add)
            yield
            fw.act(tm["dec"][:], tm["tD"][:], AF.Exp)
            pk = pg()
            for h in range(4):
                fw.mm(pk[:, h * 128:(h + 1) * 128], qk[:, 4 + h, :], qk[:, 4 + h, :])
            yield
            fw.tt(tm["Mf"][:], pk[:], tm["dec"][:], ALU.mult)
            fw.tt(V(v3(tm["Mf"], 128), tm["Mf"].buf), V(v3(tm["Mf"], 128), tm["Mf"].buf), bcb(tt, d), ALU.mult)
            yield
            M0, MT0 = tm["M0"], tm["MT0"]
            fw.tt(M0[:], tm["Mf"][:], strictB[d], ALU.mult, e="pool")
            yield
            if CHDT == F32:
                ptr = pg()
                for h in range(4):
                    fw.tr(ptr[:, h * 128:(h + 1) * 128], M0[:, h * 128:(h + 1) * 128], identF)
                yield
                fw.copy(MT0[:], ptr[:], e="act")
            else:
                for h in range(4):
                    fw.tr(PB[:, h * 128:(h + 1) * 128], M0[:, h * 128:(h + 1) * 128], identB)
                fw.copy(MT0[:], PB[:, 0:512], e="act")
            Q, QT, P, PT = tm["Qa"], tm["QTa"], tm["Pa"], tm["PTa"]
            Qn, QTn, Pn, PTn = tm["Qb"], tm["QTb"], tm["Pb"], tm["PTb"]
            fw.tt(Q[:], M0[:], mD16, ALU.mult)
            yield
            fw.tt(QT[:], MT0[:], mD16, ALU.mult)
            fw.tt(P[:], ident4F, Q[:], ALU.subtract)
            yield
            fw.tt(PT[:], ident4F, QT[:], ALU.subtract)
            fw.tt(tm["C32"][:], M0[:], mC32, ALU.mult, e="pool")
            fw.tt(tm["C32T"][:], MT0[:], mC32, ALU.mult, e="pool")
            fw.tt(tm["C64"][:], M0[:], mC64, ALU.mult, e="pool")
            fw.tt(tm["C64T"][:], MT0[:], mC64, ALU.mult, e="pool")
            yield

            def mm4(lhsT, rhs):
                p_ = pg()
                for h in range(4):
                    hs = slice(h * 128, (h + 1) * 128)
                    fw.mm(p_[:, hs], lhsT[:, hs], rhs[:, hs])
                return p_

            for lev in range(3):
                pq = mm4(QT, Q)
                yield
                fw.copy(Qn[:], pq[:], e="dve")
                pqt = mm4(Q, QT)
                yield
                fw.copy(QTn[:], pqt[:], e="act")
                yield
                pp = mm4(QTn, P)
                yield
                fw.tt(Pn[:], pp[:], P[:], ALU.add)
                ppt = mm4(Qn, PT)
                yield
                fw.tt(PTn[:], ppt[:], PT[:], ALU.add)
                yield
                Q, Qn = Qn, Q
                QT, QTn = QTn, QT
                P, Pn = Pn, P
                PT, PTn = PTn, PT
            X, XT = P, PT
            py = mm4(tm["C32T"], X)
            yield
            fw.copy(Qn[:], py[:], e="act")
            pyp = mm4(tm["C32"], XT)
            yield
            fw.copy(QTn[:], pyp[:], e="dve")
            yield
            px = mm4(XT, Qn)
            yield
            fw.tt(Pn[:], X[:], px[:], ALU.subtract)
            pxt = mm4(X, QTn)
            yield
            fw.tt(PTn[:], XT[:], pxt[:], ALU.subtract)
            yield
            X, XT = Pn, PTn
            py = mm4(tm["C64T"], X)
            yield
            fw.copy(Q[:], py[:], e="act")
            yield
            px = mm4(XT, Q)
            yield
            fw.tt(tm["TTb"][:], X[:], px[:], ALU.subtract)
            TTm = tm["TTb"]
            for h in range(4):
                fw.tr(PB[:, h * 128:(h + 1) * 128], qk[:, 4 + h, :], identB)
            for h in range(4):
                fw.tr(PB[:, 512 + h * 128: 512 + (h + 1) * 128], qk[:, 8 + h, :], identB)
            fw.tt(V(v3(tm["kEG"], 128), tm["kEG"].buf), V(PB.h[:, 0:512].rearrange("p (a n) -> p a n", n=128), PB.buf), bc4(G["EG"], tt), ALU.mult)
            fw.tt(V(v3(sl["kd"], 128), sl["kd"].buf), V(PB.h[:, 0:512].rearrange("p (a n) -> p a n", n=128), PB.buf), bc4(G["ER"], tt), ALU.mult)
            fw.copy(tm["vtok"][:], PB[:, 512:1024], e="dve")
            yield
            po = pg()
            for h in range(4):
                hs = slice(h * 128, (h + 1) * 128)
                fw.mm(po[:, hs], tm["kEG"][:, hs], TTm[:, hs])
            yield
            fw.copy(sl["w0T"][:], po[:], e="act")
            po2 = pg()
            for h in range(4):
                hs = slice(h * 128, (h + 1) * 128)
                fw.mm(po2[:, hs], TTm[:, hs], tm["vtok"][:, hs])
            yield
            fw.tt(V(v3(sl["ub"], 128), sl["ub"].buf), V(po2.h[:, :].rearrange("p (a n) -> p a n", n=128), po2.buf), bcb(tt, d), ALU.mult)
            po3 = pg()
            for h in range(4):
                fw.mm(po3[:, h * 128:(h + 1) * 128], qk[:, 4 + h, :], qk[:, h, :])
            yield
            fw.stt(sl["qkdT"][:], po3[:], qscale, tm["dec"][:], ALU.mult, ALU.mult)
            fw.tt(V(v3(sl["qgT"], 128), sl["qgT"].buf), qk[:, 0:4, :], V(v3(tm["EGr"], 128), tm["EGr"].buf), ALU.mult, e="pool")
            yield

        def scan(tt, d, sl):
            G = GA[d]
            BX = PF[3 + d]
            BY = PF[5]
            o = ring(ost[d], "ost%d" % d)
            for ci in ((0, 1) if d == 0 else (1, 0)):
                cs = slice(64 * ci, 64 * ci + 64)
                for h in range(4):
                    hs = slice(h * 128, (h + 1) * 128)
                    fw.mm(BX[cs, hs], sl["w0T"][:, h * 128 + 64 * ci: h * 128 + 64 * ci + 64], Sb[d][:, hs])
                t5v = V(t5[d].h[cs, :].rearrange("p (a n) -> p a n", n=128), t5[d].buf)
                fw.tt(t5v, V(BX.h[cs, :].rearrange("p (a n) -> p a n", n=128), BX.buf), bcb(tt, d, cs), ALU.mult)
                fw.tt(un[d][cs, :], sl["ub"][cs, :], t5[d][cs, :], ALU.subtract)
                yield
                for h in range(4):
                    oc = slice(h * 128 + 64 * ci, h * 128 + 64 * ci + 64)
                    yc_ = slice(d * 256 + h * 64, d * 256 + (h + 1) * 64)
                    hs = slice(h * 128, (h + 1) * 128)
                    fw.mm(BY[:, yc_], Sb[d][:, hs], sl["qgT"][:, oc], start=True, stop=False)
                    fw.mm(BY[:, yc_], un[d][cs, hs], sl["qkdT"][cs, oc], start=False, stop=True)
                fw.copy(V(o.h[:, :].rearrange("p (a n) -> p a n", n=128)[:, :, 64 * ci: 64 * ci + 64], o.buf),
                        V(BY.h[:, d * 256:(d + 1) * 256].rearrange("p (a n) -> p a n", n=64), BY.buf), e="act")
                for h in range(4):
                    hs = slice(h * 128, (h + 1) * 128)
                    fw.mm(BX[:, hs], sl["kd"][cs, hs], un[d][cs, hs])
                gl = G["GL0"] if ci == 0 else G["GL1"]
                fw.tt(V(v3(S32[d], 128), S32[d].buf), V(v3(S32[d], 128), S32[d].buf), bc4(gl, tt), ALU.mult)
                fw.tt(S32[d][:], S32[d][:], BX[:], ALU.add)
                yield
                fw.copy(Sb[d][:], S32[d][:], e="act")
                yield
            fw.dma("pool", V(oT_d[d].h.rearrange("(h p) t -> p h t", p=128)[:, :, tt * 128:(tt + 1) * 128], oT_d[d].buf),
                   V(v3(o, 128), o.buf), semof=o.buf)
            yield

        def run_il(gens):
            gens = list(gens)
            while gens:
                for g_ in list(gens):
                    try:
                        next(g_)
                    except StopIteration:
                        gens.remove(g_)

        order = [[32, 33] + list(range(32)), [33, 32] + list(range(31, -1, -1))]
        active = []
        p_started = [0, 0]
        p_done = [0, 0]
        s_started = [0, 0]
        s_done = [0, 0]
        while s_done[0] < NT or s_done[1] < NT:
            for d in range(2):
                if p_started[d] < NT and p_started[d] - p_done[d] < 2 and p_started[d] - s_done[d] < 3:
                    k_ = p_started[d]
                    active.append((prep(order[d][k_], d, slots[d][k_ % 3], k_), "p", d))
                    p_started[d] += 1
                if s_started[d] < p_done[d] and s_started[d] == s_done[d]:
                    k_ = s_started[d]
                    active.append((scan(order[d][k_], d, slots[d][k_ % 3]), "s", d))
                    s_started[d] += 1
            for item in list(active):
                g_, kind, d = item
                try:
                    next(g_)
                except StopIteration:
                    active.remove(item)
                    if kind == "p":
                        p_done[d] += 1
                    else:
                        s_done[d] += 1
        fw.pop()

    def phase_mla(l, with_ctx):
        fw.push()
        cqn = fw.sb("cqn", [128, 3, TT], BF16)
        ckvn = fw.sb("ckvn", [128, 2, TT], BF16)
        ropeC = fw.sb("ropeC", [96, TT], BF16)
        ropeS = fw.sb("ropeS", [96, TT], BF16)
        kper = fw.sb("kper", [96, TT], BF16)
        wuq = fw.sb("wuq", [128, 3, 1536], BF16)
        wukv = fw.sb("wukv", [128, 2, 1024], BF16)
        qn = fw.sb("qn", [128, 3], F32)
        kvn = fw.sb("kvn", [128, 2], F32)
        fw.dma("sp", qn[:], qnorm_d[l])
        fw.dma("sp", kvn[:], kvnorm_d[l])
        fw.push()
        st32 = fw.sb("st32", [128, 4608], F32)
        fw.dma("sp", st32[:, 0:4608], V(w_uq_d.h[l].rearrange("p a n -> p (a n)"), w_uq_d.buf))
        fw.copy(V(wuq.h[:, :, :].rearrange("p a n -> p (a n)"), wuq.buf), st32[:, 0:4608], e="pool")
        fw.dma("sp", st32[:, 0:2048], V(w_ukv_d.h[l].rearrange("p a n -> p (a n)"), w_ukv_d.buf))
        fw.copy(V(wukv.h[:, :, :].rearrange("p a n -> p (a n)"), wukv.buf), st32[:, 0:2048], e="pool")
        fw.dma("sp", st32[0:96, 0:TT], ROPEC_d[:])
        fw.copy(ropeC[:], st32[0:96, 0:TT], e="pool")
        fw.dma("sp", st32[0:96, 0:TT], ROPES_d[:])
        fw.copy(ropeS[:], st32[0:96, 0:TT], e="pool")
        kpa = fw.sb("kpa", [96, TT], BF16)
        kpb = fw.sb("kpb", [96, TT], BF16)
        fw.dma("sp", kpa[64:96, :], pT_d[7296:7328, :])
        fw.dma("sp", kpb[64:96, :], pT_d[7328:7360, :])
        fw.tt(st32[64:96, 0:TT], kpa[64:96, :], ropeC[64:96, :], ALU.mult)
        fw.tt(kpb[64:96, :], kpb[64:96, :], ropeS[64:96, :], ALU.mult)
        fw.tt(kper[64:96, :], st32[64:96, 0:TT], kpb[64:96, :], ALU.add)
        sqr = [fw.sb("msq%d" % i, [128, 512], BF16) for i in range(3)]
        rr = [fw.sb("mrr%d" % i, [128, 512], F32) for i in range(2)]
        for (dst, nb_, row0, nrm, width) in ((cqn, 3, 3072, qn, 384.0), (ckvn, 2, 3456, kvn, 256.0)):
            for b_ in range(nb_):
                fw.dma("sp", dst[:, b_, :], pT_d[row0 + b_ * 128: row0 + (b_ + 1) * 128, :])
            for (t0, n) in CHUNKS:
                ps = pf(0, 5)
                sqs = []
                for b_ in range(nb_):
                    sq = ring(sqr, "msq")
                    fw.tt(sq[:, 0:n], dst[:, b_, t0:t0 + n], dst[:, b_, t0:t0 + n], ALU.mult, e="pool")
                    sqs.append(sq)
                for b_ in range(nb_):
                    fw.mm(ps[:, 0:n], onesB, sqs[b_][:, 0:n], start=(b_ == 0), stop=(b_ == nb_ - 1))
                r = ring(rr, "mrr")
                fw.act(r[:, 0:n], ps[:, 0:n], AF.Ln, bias=EPS, scale=1.0 / width)
                fw.act(r[:, 0:n], r[:, 0:n], AF.Exp, scale=-0.5)
                for b_ in range(nb_):
                    fw.stt(dst[:, b_, t0:t0 + n], dst[:, b_, t0:t0 + n], nrm[:, b_:b_ + 1], r[:, 0:n], ALU.mult, ALU.mult)
        fw.pop()

        kTh = [fw.sb("kTh%d" % i, [96, TT], BF16) for i in range(2)]
        qTh = [fw.sb("qTh%d" % i, [96, TT], BF16) for i in range(2)]
        Vaug = [fw.sb("Vaug%d" % i, [128, NT, 128], BF16) for i in range(2)]
        for i in range(2):
            fw.op("pool", lambda g, i=i: g.memset(Vaug[i].h[:, :, 64:128], 1.0), [], [Vaug[i][:, :, 64:128]])
        PTr = [fw.sb("PTr%d" % i, [128, 512], BF16) for i in range(4)]
        q1 = [fw.sb("q1_%d" % i, [96, 512], F32) for i in range(2)]
        q2 = [fw.sb("q2_%d" % i, [96, 512], F32) for i in range(2)]
        rden = [fw.sb("rden%d" % i, [128, 512], F32) for i in range(2)]
        otmp = [fw.sb("otmp%d" % i, [64, 512], F32) for i in range(2)]
        zc = [fw.sb("zc%d" % i, [64, 512], BF16) for i in range(2)]
        ycs = [fw.sb("ycs%d" % i, [64, 512], BF16) for i in range(2)]
        ascale = 96.0 ** -0.5
        qchunks = CHUNKS if with_ctx else CHUNKS[:8]
        for h in range(8):
            kt_ = kTh[h % 2]
            qt_ = qTh[h % 2]
            va = Vaug[h % 2]
            for (t0, n) in CHUNKS:
                ps = pf(0, 5)
                for kc in range(2):
                    fw.mm(ps[0:64, 0:n], wukv[:, kc, h * 128: h * 128 + 64], ckvn[:, kc, t0:t0 + n], start=(kc == 0), stop=(kc == 1))
                fw.copy(kt_[0:64, t0:t0 + n], ps[0:64, 0:n], e="dve")
            fw.copy(kt_[64:96, :], kper[64:96, :], e="pool")
            for t8 in range(0, NT, 8):
                nt8 = min(8, NT - t8)
                ps = pf(0, 5)
                for a in range(nt8):
                    tt = t8 + a
                    for kc in range(2):
                        fw.mm(ps[:, a * 64:(a + 1) * 64], ckvn[:, kc, tt * 128:(tt + 1) * 128], wukv[:, kc, h * 128 + 64: h * 128 + 128],
                              start=(kc == 0), stop=(kc == 1))
                fw.copy(va[:, t8:t8 + nt8, 0:64], V(ps.h[:, 0:nt8 * 64].rearrange("p (a n) -> p a n", n=64), ps.buf), e="dve")
            for (t0, n) in qchunks:
                pa = pf(0, 5)
                for kc in range(3):
                    fw.mm(pa[0:96, 0:n], wuq[:, kc, h * 96:(h + 1) * 96], cqn[:, kc, t0:t0 + n], start=(kc == 0), stop=(kc == 2))
                pb_ = pf(0, 5)
                for kc in range(3):
                    fw.mm(pb_[0:96, 0:n], wuq[:, kc, 768 + h * 96: 768 + (h + 1) * 96], cqn[:, kc, t0:t0 + n], start=(kc == 0), stop=(kc == 2))
                a1 = ring(q1, "q1")
                a2 = ring(q2, "q2")
                fw.tt(a1[:, 0:n], pa[0:96, 0:n], ropeC[:, t0:t0 + n], ALU.mult)
                fw.tt(a2[:, 0:n], pb_[0:96, 0:n], ropeS[:, t0:t0 + n], ALU.mult)
                fw.tt(qt_[:, t0:t0 + n], a1[:, 0:n], a2[:, 0:n], ALU.add, e="pool")
            for qi, (t0, n) in enumerate(qchunks):
                ktiles = list(range(NT)) if t0 < TL else [32, 33]
                acc = PF[5 + qi % 2]
                zt = ring(zc, "zc")
                fw.dma("sp", zt[:, 0:n], pT_d[3712 + h * 64: 3712 + (h + 1) * 64, t0:t0 + n])
                LA = 4
                pss = {}

                def issue_s(ki_):
                    ps_ = pf(0, 5)
                    kt__ = ktiles[ki_]
                    fw.mm(ps_[:, 0:n], kt_[:, kt__ * 128:(kt__ + 1) * 128], qt_[:, t0:t0 + n])
                    pss[ki_] = ps_

                for ki in range(min(LA, len(ktiles))):
                    issue_s(ki)
                for ki, kt in enumerate(ktiles):
                    ps = pss.pop(ki)
                    pt = ring(PTr, "PTr")
                    fw.act(pt[:, 0:n], ps[:, 0:n], AF.Exp, scale=ascale)
                    fw.mm(acc[:, 0:n], va[:, kt, :], pt[:, 0:n], start=(ki == 0), stop=(ki == len(ktiles) - 1))
                    if ki + LA < len(ktiles):
                        issue_s(ki + LA)
                rd = ring(rden, "rden")
                fw.recip(rd[64:128, 0:n], acc[64:128, 0:n])
                ot = ring(otmp, "otmp")
                fw.tt(ot[:, 0:n], acc[0:64, 0:n], rd[64:128, 0:n], ALU.mult)
                yc = ring(ycs, "ycs")
                fw.tt(yc[:, 0:n], ot[:, 0:n], zt[:, 0:n], ALU.mult, e="pool")
                fw.dma("pool", ycT_d[h * 64:(h + 1) * 64, t0:t0 + n], yc[:, 0:n], semof=yc.buf)
        fw.pop()

    def phase_final(l, with_ctx, last):
        fw.push()
        wbr = fw.sb("wbr", [128, 12, 1024], BF16)
        wout = fw.sb("wout", [128, 8, 1024], BF16)
        dnrm = fw.sb("dnrm", [128, 1], F32)
        fw.dma("sp", dnrm[:], dnorm_d[l])
        fw.push()
        st32 = [fw.sb("fst32_%d" % i, [128, 4096], F32) for i in range(2)]
        for q in range(3):
            st = ring(st32, "fst32")
            fw.dma("sp", st[:], V(w_br_d.h[l, :, q * 4:(q + 1) * 4, :].rearrange("p a n -> p (a n)"), w_br_d.buf))
            fw.copy(V(wbr.h[:, q * 4:(q + 1) * 4, :].rearrange("p a n -> p (a n)"), wbr.buf), st[:], e="pool")
        for q in range(2):
            st = ring(st32, "fst32")
            fw.dma("sp", st[:], V(w_out_d.h[l, :, q * 4:(q + 1) * 4, :].rearrange("p a n -> p (a n)"), w_out_d.buf))
            fw.copy(V(wout.h[:, q * 4:(q + 1) * 4, :].rearrange("p a n -> p (a n)"), wout.buf), st[:], e="pool")
        fw.pop()
        of_r = [fw.sb("of%d" % i, [128, 4, 512], F32) for i in range(2)]
        ob_r = [fw.sb("ob%d" % i, [128, 4, 512], F32) for i in range(2)]
        sqb = fw.sb("sqb", [128, 4, 512], BF16)
        rr = [fw.sb("frr%d" % i, [128, 512], F32) for i in range(2)]
        szb_r = [fw.sb("szb%d" % i, [128, 4, 512], BF16) for i in range(2)]
        ybT = fw.sb("ybT", [128, 4, 512], BF16)
        yaT_r = [fw.sb("yaTt%d" % i, [128, 4, 512], BF16) for i in range(2)]
        ycT_r = [fw.sb("ycTt%d" % i, [128, 4, 512], BF16) for i in range(2)]
        gts = [fw.sb("gts%d" % i, [128, 8, 512], BF16) for i in range(2)]
        merged = fw.sb("merged", [128, 8, 512], F32)
        mergedb = fw.sb("mergedb", [128, 8, 512], BF16)
        tmpf = [fw.sb("tmpf%d" % i, [128, 512], F32) for i in range(2)]
        xr = [fw.sb("fxr%d" % i, [128, 1024], F32) for i in range(2)]
        xo = [fw.sb("fxo%d" % i, [128, 1024], F32) for i in range(2)]
        junk = fw.sb("junk", [128, 512], BF16)
        st4 = fw.sb("st4", [128, 8 * NT], F32, nsub=NT)
        xsrc = xin if l == 0 else xs_d
        chunks = CHUNKS if with_ctx else CHUNKS[:8]
        for (t0, n) in chunks:
            j = 0 if t0 < TL else 1
            of_ = ring(of_r, "of_r")
            ob_ = ring(ob_r, "ob_r")
            szb = ring(szb_r, "szb_r")
            yaT = ring(yaT_r, "yaT_r")
            ycT = ring(ycT_r, "ycT_r")
            for d, dstt in ((0, of_), (1, ob_)):
                fw.dma("sp", dstt[:, :, 0:n], V(oT_d[d].h.rearrange("(h p) t -> p h t", p=128)[:, :, t0:t0 + n], oT_d[d].buf))
            fw.dma("sp", szb[:, :, 0:n], V(pT_d.h[2560:3072, :].rearrange("(g p) t -> p g t", p=128)[:, :, t0:t0 + n], pT_d.buf))
            fw.dma("sp", yaT[:, :, 0:n], V(yaT_d.h.rearrange("(g p) t -> p g t", p=128)[:, :, t0:t0 + n], yaT_d.buf))
            fw.dma("sp", ycT[:, :, 0:n], V(ycT_d.h.rearrange("(g p) t -> p g t", p=128)[:, :, t0:t0 + n], ycT_d.buf))
            fw.tt(of_[:, :, 0:n], of_[:, :, 0:n], ob_[:, :, 0:n], ALU.add, e="pool")
            fw.act(sqb[:, :, 0:n], of_[:, :, 0:n], AF.Square)
            for h in range(4):
                ps = pf()
                fw.mm(ps[:, 0:n], onesB, sqb[:, h, 0:n])
                r = ring(rr, "frr")
                fw.act(r[:, 0:n], ps[:, 0:n], AF.Ln, bias=EPS, scale=1.0 / 128)
                fw.act(r[:, 0:n], r[:, 0:n], AF.Exp, scale=-0.5)
                fw.stt(r[:, 0:n], of_[:, h, 0:n], dnrm[:, 0:1], r[:, 0:n], ALU.mult, ALU.mult)
                fw.tt(ybT[:, h, 0:n], r[:, 0:n], szb[:, h, 0:n], ALU.mult, e="pool")
            for br, src in enumerate((yaT, ybT, ycT)):
                gt = ring(gts, "gts")
                fw.dma("sp", gt[:, :, 0:n], V(pT_d.h[4224 + br * 1024: 4224 + (br + 1) * 1024, :].rearrange("(g p) t -> p g t", p=128)[:, :, t0:t0 + n], pT_d.buf))
                for jb in range(8):
                    ps = pf()
                    for kc in range(4):
                        fw.mm(ps[:, 0:n], wbr[:, br * 4 + kc, jb * 128:(jb + 1) * 128], src[:, kc, 0:n], start=(kc == 0), stop=(kc == 3))
                    if br == 0:
                        fw.tt(merged[:, jb, 0:n], ps[:, 0:n], gt[:, jb, 0:n], ALU.mult)
                    else:
                        tf = ring(tmpf, "tmpf")
                        fw.tt(tf[:, 0:n], ps[:, 0:n], gt[:, jb, 0:n], ALU.mult)
                        fw.tt(merged[:, jb, 0:n], merged[:, jb, 0:n], tf[:, 0:n], ALU.add, e="pool")
            fw.copy(mergedb[:, :, 0:n], merged[:, :, 0:n], e="act")
            for ts_ in range(n // 128):
                tt = (t0 // 128) + ts_
                xt = ring(xr, "fxr")
                fw.dma("sp", xt[:], xsrc[tt * 128:(tt + 1) * 128, :])
                st = st4.sub(tt)
                c0 = 8 * tt
                phs = []
                for half in range(2):
                    ps = pf()
                    for kc in range(8):
                        fw.mm(ps[:], mergedb[:, kc, ts_ * 128:(ts_ + 1) * 128], wout[:, kc, half * 512:(half + 1) * 512], start=(kc == 0), stop=(kc == 7))
                    fw.act(junk[:], ps[:], AF.Square, accum_out=st[:, c0 + half: c0 + half + 1])
                    phs.append(ps)
                fw.tt(st[:, c0 + 2:c0 + 3], st[:, c0:c0 + 1], st[:, c0 + 1:c0 + 2], ALU.add)
                fw.act(st[:, c0 + 3:c0 + 4], st[:, c0 + 2:c0 + 3], AF.Sqrt, bias=EPS, scale=1.0 / 1024)
                fw.recip(st[:, c0 + 4:c0 + 5], st[:, c0 + 3:c0 + 4])
                o = ring(xo, "fxo")
                for half in range(2):
                    hs = slice(half * 512, (half + 1) * 512)
                    fw.stt(o[:, hs], phs[half][:], st[:, c0 + 4:c0 + 5], ggbc[j][:, hs], ALU.mult, ALU.mult)
                fw.tt(o[:], o[:], xt[:], ALU.add, e="pool")
                if last:
                    fw.dma("pool", out_d[tt * 128:(tt + 1) * 128, :], o[:], semof=o.buf)
                else:
                    fw.dma("pool", xs_d[tt * 128:(tt + 1) * 128, :], o[:], semof=o.buf)
        fw.pop()

    for l in range(n_layers):
        alloc_s1()
        phase_mod(l)
        phase_h(l)
        if "hT" in dbg and l == 0:
            dbg_d["hT"] = fw.dram("hT_o", [128, 8, TT], BF16, "ExternalOutput")
            fw.dma("pool", dbg_d["hT"][:], S["hT"][:], semof=S["hT"].buf)
        if stop_after == "h":
            fw.pop()
            break
        phase_proj(l)
        fw.pop()
        if stop_after == "proj":
            break
        if "nofourier" not in dbg:
            phase_fourier(l, with_ctx=(l < n_layers - 1 or "ctxall" in dbg))
        if stop_after == "fourier":
            break
        if "nogdn" not in dbg:
            phase_gdn(l)
        if stop_after == "gdn":
            break
        wc = (l < n_layers - 1 or "ctxall" in dbg)
        phase_mla(l, with_ctx=wc)
        if stop_after == "mla":
            break
        phase_final(l, with_ctx=wc, last=(l == n_layers - 1 and "ctxall" not in dbg))

    if "abT" in dbg:
        dbg_d["abT"] = fw.dram("abT_o", [128, NT, 16], F32, "ExternalOutput")
        fw.dma("pool", dbg_d["abT"][:], abT[:], semof=abT.buf)
    outs = [out_d, xs_d, pT_d, yaT_d, ycT_d, qkvT_d] + oT_d + list(dbg_d.values())
    fw.finish(outs, e="pool")
    return nc, fw


def kernel(**inputs):
    inp = {k: np.asarray(v) for k, v in inputs.items()}
    consts = host_constants()
    shared = host_layout_shared(inp)
    nc, fw = build()
    in_maps = []
    for b in range(8):
        m = dict(consts)
        m.update(shared)
        m.update(host_layout(inp, b))
        in_maps.append(m)
    res = run_bass_kernel_spmd(nc, in_maps, core_ids=list(range(8)))
    return np.stack([np.asarray(r["out"]) for r in res.results], axis=0).astype(np.float32)
```
